# Optimizing a Trainium2 kernel written in Bass

```python
import math, functools
import jax, jax.numpy as jnp
from jax import lax
import numpy as np

D_MODEL = 1024
BATCH = 32
SEQ = 256
DEPTH = 4
DEC_BATCH = 2
DEC_SEQ = 4096
PAST_LEN = 512

GRID_W = 64
N_MIXERS = 3
N_A = (DEPTH + 2) // 3
N_B = (DEPTH + 1) // 3
N_C = DEPTH // 3
Q_BLOCK = 128
ROPE_THETA = 10000.0
EPS = 1e-6
N_MOD = 9
D_FF = 2816
H_A = 16
Q_LORA = 384
KV_LORA = 256
NOPE_A = 64
ROPE_A = 32
V_A = 64
H_B = 16
KV_B = 4
HD_B = 64
H_C = 8
DK_C = 64
DV_C = 128

kernel_name = "hybrid_diffusion_mla_gqa_diff_macaron"


def _rms(x, g):
    xf = x.astype(jnp.float32)
    y = xf * lax.rsqrt(jnp.mean(xf * xf, axis=-1, keepdims=True) + EPS)
    return (y * g.astype(jnp.float32)).astype(x.dtype)


def _axial_rope_tables(n_tokens, rot_dim, dtype):
    rows = n_tokens // GRID_W
    r, cidx = jnp.meshgrid(jnp.arange(rows, dtype=jnp.float32),
                           jnp.arange(GRID_W, dtype=jnp.float32), indexing="ij")
    r, cidx = r.reshape(-1), cidx.reshape(-1)
    n_f = rot_dim // 4
    freqs = ROPE_THETA ** (-jnp.arange(n_f, dtype=jnp.float32) / n_f)
    ang = jnp.concatenate([r[:, None] * freqs, cidx[:, None] * freqs], axis=-1)
    return jnp.cos(ang).astype(dtype), jnp.sin(ang).astype(dtype)


def _rope(x, cos, sin):
    shp = (cos.shape[0],) + (1,) * (x.ndim - 3) + (cos.shape[1],)
    c, s = cos.reshape(shp), sin.reshape(shp)
    x1, x2 = jnp.split(x, 2, axis=-1)
    return jnp.concatenate([x1 * c - x2 * s, x1 * s + x2 * c], axis=-1)


def _gqa_sdpa(q, k, v):
    scale = q.shape[-1] ** -0.5
    s = jnp.einsum("bqhgd,bkhd->bhgqk", q, k).astype(jnp.float32) * scale
    p = jax.nn.softmax(s, axis=-1).astype(v.dtype)
    return jnp.einsum("bhgqk,bkhd->bqhgd", p, v)


def _diff_sdpa(q, k, v, lam):
    scale = q.shape[-1] ** -0.5
    s = jnp.einsum("bqhmd,bkhmd->bhmqk", q, k).astype(jnp.float32) * scale
    p = jax.nn.softmax(s, axis=-1)
    a = (p[:, :, 0] - lam * p[:, :, 1]).astype(v.dtype)
    return jnp.einsum("bhqk,bkhd->bqhd", a, v)


def _sweep(attn, q, k, v):
    b, sq = q.shape[0], q.shape[1]
    nb = sq // Q_BLOCK
    qb = jnp.moveaxis(q.reshape((b, nb, Q_BLOCK) + q.shape[2:]), 1, 0)
    o = lax.map(lambda blk: attn(blk, k, v), qb)
    return jnp.moveaxis(o, 0, 1).reshape((b, sq) + o.shape[3:])


def _ada(act, w, b):
    return jnp.split(act @ w + b, N_MOD, axis=-1)


def _modulate(x, g, shift, scale):
    return _rms(x, g) * (1 + scale[:, None]) + shift[:, None]


def _swiglu(h, w_in, w_out):
    gt, up = jnp.split(h @ w_in, 2, axis=-1)
    return (jax.nn.silu(gt) * up) @ w_out


def _ffn_sub(y, g, shift, scale, gate, w_in, w_out):
    return y + 0.5 * gate[:, None] * _swiglu(_modulate(y, g, shift, scale), w_in, w_out)


def _mla_down(h, w_down, g_q, g_kv):
    cq, ckv, kr = jnp.split(h @ w_down, [Q_LORA, Q_LORA + KV_LORA], axis=-1)
    return _rms(cq, g_q), _rms(ckv, g_kv), kr


def _mla_attend(cq, ckv, kr, w_uq, w_uk, w_uv, w_o, rope_q):
    b, sq = cq.shape[:2]
    sk = ckv.shape[1]
    q = (cq @ w_uq).reshape(b, sq, H_A, NOPE_A + ROPE_A)
    if rope_q is not None:
        q = jnp.concatenate([q[..., :NOPE_A], _rope(q[..., NOPE_A:], *rope_q)], axis=-1)
    k_nope = (ckv @ w_uk).reshape(b, sk, H_A, NOPE_A)
    k = jnp.concatenate([k_nope, jnp.broadcast_to(kr[:, :, None, :], (b, sk, H_A, ROPE_A))], axis=-1)
    v = (ckv @ w_uv).reshape(b, sk, H_A, V_A)
    o = _sweep(_gqa_sdpa, q[:, :, :, None, :], k, v)
    return o.reshape(b, sq, H_A * V_A) @ w_o


def _mla_context(h, w_down, g_q, g_kv, w_uq, w_uk, w_uv, w_o):
    cq, ckv, kr = _mla_down(h, w_down, g_q, g_kv)
    return _mla_attend(cq, ckv, kr, w_uq, w_uk, w_uv, w_o, None), ckv, kr


def _mla_latent(h, ckv_ctx, kr_ctx, w_down, g_q, g_kv, w_uq, w_uk, w_uv, w_o, rope):
    cq, ckv, kr = _mla_down(h, w_down, g_q, g_kv)
    kr = _rope(kr, *rope)
    ckv_all = jnp.concatenate([ckv, ckv_ctx], axis=1)
    kr_all = jnp.concatenate([kr, kr_ctx], axis=1)
    return _mla_attend(cq, ckv_all, kr_all, w_uq, w_uk, w_uv, w_o, rope)


def _gqa_qkv(h, w_qkv, g_q, g_k):
    b, s = h.shape[:2]
    q, k, v = jnp.split(h @ w_qkv, [H_B * HD_B, (H_B + KV_B) * HD_B], axis=-1)
    q = _rms(q.reshape(b, s, H_B, HD_B), g_q)
    k = _rms(k.reshape(b, s, KV_B, HD_B), g_k)
    return q, k, v.reshape(b, s, KV_B, HD_B)


def _gqa_out(q, k, v, w_o):
    b, s = q.shape[:2]
    o = _sweep(_gqa_sdpa, q.reshape(b, s, KV_B, H_B // KV_B, HD_B), k, v)
    return o.reshape(b, s, H_B * HD_B) @ w_o


def _diff_qkv(h, w_qkv):
    b, s = h.shape[:2]
    q, k, v = jnp.split(h @ w_qkv, [H_C * 2 * DK_C, 2 * H_C * 2 * DK_C], axis=-1)
    return (q.reshape(b, s, H_C, 2, DK_C), k.reshape(b, s, H_C, 2, DK_C),
            v.reshape(b, s, H_C, DV_C))


def _diff_out(q, k, v, lam, lam_init, g_sub, w_o):
    b, s = q.shape[:2]
    o = _sweep(functools.partial(_diff_sdpa, lam=lam), q, k, v)
    o = _rms(o, g_sub) * (1 - lam_init)
    return o.reshape(b, s, H_C * DV_C) @ w_o


def setup_inputs(seed: int = 0) -> dict:
    key = jax.random.key(seed)
    ks = iter(jax.random.split(key, 48))
    f32 = jnp.float32

    def nrm(shape, scale=1.0):
        return jax.random.normal(next(ks), shape, f32) * scale

    def gain(shape):
        return 1.0 + nrm(shape, 0.05)

    D = D_MODEL
    return {
        "x_prompt": nrm((BATCH, SEQ, D)),
        "x_sample": nrm((DEC_BATCH, DEC_SEQ, D)),
        "cache_mla_ckv": nrm((DEC_BATCH, N_A, PAST_LEN, KV_LORA)),
        "cache_mla_kr": nrm((DEC_BATCH, N_A, PAST_LEN, ROPE_A)),
        "cache_gqa_k": nrm((DEC_BATCH, N_B, PAST_LEN, KV_B, HD_B)),
        "cache_gqa_v": nrm((DEC_BATCH, N_B, PAST_LEN, KV_B, HD_B)),
        "cache_diff_k": nrm((DEC_BATCH, N_C, PAST_LEN, H_C, 2, DK_C)),
        "cache_diff_v": nrm((DEC_BATCH, N_C, PAST_LEN, H_C, DV_C)),
        "c": nrm((DEC_BATCH, D)),
        "c_ctx": nrm((D,)),
        "ada_w": nrm((DEPTH, D, N_MOD * D), 0.5 * D ** -0.5),
        "ada_b": nrm((DEPTH, N_MOD * D), 0.02),
        "norm_ffn1": gain((DEPTH, D)),
        "norm_mix": gain((DEPTH, D)),
        "norm_ffn2": gain((DEPTH, D)),
        "ffn1_w_in": nrm((DEPTH, D, 2 * D_FF), D ** -0.5),
        "ffn1_w_out": nrm((DEPTH, D_FF, D), D_FF ** -0.5),
        "ffn2_w_in": nrm((DEPTH, D, 2 * D_FF), D ** -0.5),
        "ffn2_w_out": nrm((DEPTH, D_FF, D), D_FF ** -0.5),
        "mla_w_down": nrm((N_A, D, Q_LORA + KV_LORA + ROPE_A), D ** -0.5),
        "mla_g_q": gain((N_A, Q_LORA)),
        "mla_g_kv": gain((N_A, KV_LORA)),
        "mla_w_uq": nrm((N_A, Q_LORA, H_A * (NOPE_A + ROPE_A)), Q_LORA ** -0.5),
        "mla_w_uk": nrm((N_A, KV_LORA, H_A * NOPE_A), KV_LORA ** -0.5),
        "mla_w_uv": nrm((N_A, KV_LORA, H_A * V_A), KV_LORA ** -0.5),
        "mla_w_o": nrm((N_A, H_A * V_A, D), (H_A * V_A) ** -0.5),
        "gqa_w_qkv": nrm((N_B, D, (H_B + 2 * KV_B) * HD_B), D ** -0.5),
        "gqa_g_q": gain((N_B, HD_B)),
        "gqa_g_k": gain((N_B, HD_B)),
        "gqa_w_o": nrm((N_B, H_B * HD_B, D), (H_B * HD_B) ** -0.5),
        "diff_w_qkv": nrm((N_C, D, 2 * H_C * 2 * DK_C + H_C * DV_C), D ** -0.5),
        "diff_lq1": nrm((N_C, DK_C), 0.1),
        "diff_lk1": nrm((N_C, DK_C), 0.1),
        "diff_lq2": nrm((N_C, DK_C), 0.1),
        "diff_lk2": nrm((N_C, DK_C), 0.1),
        "diff_g_sub": gain((N_C, DV_C)),
        "diff_w_o": nrm((N_C, H_C * DV_C, D), (H_C * DV_C) ** -0.5),
        "norm_final": gain((D,)),
    }


def reference(x_prompt, x_sample, cache_mla_ckv, cache_mla_kr, cache_gqa_k, cache_gqa_v,
              cache_diff_k, cache_diff_v, c, c_ctx,
              ada_w, ada_b, norm_ffn1, norm_mix, norm_ffn2,
              ffn1_w_in, ffn1_w_out, ffn2_w_in, ffn2_w_out,
              mla_w_down, mla_g_q, mla_g_kv, mla_w_uq, mla_w_uk, mla_w_uv, mla_w_o,
              gqa_w_qkv, gqa_g_q, gqa_g_k, gqa_w_o,
              diff_w_qkv, diff_lq1, diff_lk1, diff_lq2, diff_lk2, diff_g_sub, diff_w_o,
              norm_final):
    n_lat = x_sample.shape[1]
    rope_a = _axial_rope_tables(n_lat, ROPE_A, x_sample.dtype)
    rope_b = _axial_rope_tables(n_lat, HD_B, x_sample.dtype)
    rope_c = _axial_rope_tables(n_lat, DK_C, x_sample.dtype)

    y_ctx, y_lat = x_prompt, x_sample
    act_ctx = jax.nn.silu(c_ctx)[None, :]
    act_lat = jax.nn.silu(c)
    new_ckv, new_kr, new_gk, new_gv, new_dk, new_dv = [], [], [], [], [], []

    for i in range(DEPTH):
        kind, j = i % N_MIXERS, i // N_MIXERS
        mc = _ada(act_ctx, ada_w[i], ada_b[i])
        ml = _ada(act_lat, ada_w[i], ada_b[i])
        y_ctx = _ffn_sub(y_ctx, norm_ffn1[i], mc[0], mc[1], mc[2], ffn1_w_in[i], ffn1_w_out[i])
        y_lat = _ffn_sub(y_lat, norm_ffn1[i], ml[0], ml[1], ml[2], ffn1_w_in[i], ffn1_w_out[i])
        h_ctx = _modulate(y_ctx, norm_mix[i], mc[3], mc[4])
        h_lat = _modulate(y_lat, norm_mix[i], ml[3], ml[4])
        if kind == 0:
            pa = (mla_w_down[j], mla_g_q[j], mla_g_kv[j], mla_w_uq[j], mla_w_uk[j], mla_w_uv[j], mla_w_o[j])
            o_ctx, ckv, kr = _mla_context(h_ctx, *pa)
            o_lat = _mla_latent(h_lat, cache_mla_ckv[:, j], cache_mla_kr[:, j], *pa, rope_a)
            new_ckv.append(ckv)
            new_kr.append(kr)
        elif kind == 1:
            qc, kc, vc = _gqa_qkv(h_ctx, gqa_w_qkv[j], gqa_g_q[j], gqa_g_k[j])
            o_ctx = _gqa_out(qc, kc, vc, gqa_w_o[j])
            new_gk.append(kc)
            new_gv.append(vc)
            ql, kl, vl = _gqa_qkv(h_lat, gqa_w_qkv[j], gqa_g_q[j], gqa_g_k[j])
            ql, kl = _rope(ql, *rope_b), _rope(kl, *rope_b)
            k_all = jnp.concatenate([kl, cache_gqa_k[:, j]], axis=1)
            v_all = jnp.concatenate([vl, cache_gqa_v[:, j]], axis=1)
            o_lat = _gqa_out(ql, k_all, v_all, gqa_w_o[j])
        else:
            lam_init = 0.8 - 0.6 * math.exp(-0.3 * i)
            lam = (jnp.exp(jnp.sum(diff_lq1[j].astype(jnp.float32) * diff_lk1[j].astype(jnp.float32)))
                   - jnp.exp(jnp.sum(diff_lq2[j].astype(jnp.float32) * diff_lk2[j].astype(jnp.float32)))
                   + lam_init)
            qc, kc, vc = _diff_qkv(h_ctx, diff_w_qkv[j])
            o_ctx = _diff_out(qc, kc, vc, lam, lam_init, diff_g_sub[j], diff_w_o[j])
            new_dk.append(kc)
            new_dv.append(vc)
            ql, kl, vl = _diff_qkv(h_lat, diff_w_qkv[j])
            ql, kl = _rope(ql, *rope_c), _rope(kl, *rope_c)
            k_all = jnp.concatenate([kl, cache_diff_k[:, j]], axis=1)
            v_all = jnp.concatenate([vl, cache_diff_v[:, j]], axis=1)
            o_lat = _diff_out(ql, k_all, v_all, lam, lam_init, diff_g_sub[j], diff_w_o[j])
        y_ctx = y_ctx + mc[5][:, None] * o_ctx
        y_lat = y_lat + ml[5][:, None] * o_lat
        y_ctx = _ffn_sub(y_ctx, norm_ffn2[i], mc[6], mc[7], mc[8], ffn2_w_in[i], ffn2_w_out[i])
        y_lat = _ffn_sub(y_lat, norm_ffn2[i], ml[6], ml[7], ml[8], ffn2_w_in[i], ffn2_w_out[i])

    y_prompt = _rms(y_ctx, norm_final)
    y_sample = _rms(y_lat, norm_final)
    new_mla_ckv = jnp.stack(new_ckv, axis=1)
    new_mla_kr = jnp.stack(new_kr, axis=1)
    new_gqa_k = jnp.stack(new_gk, axis=1)
    new_gqa_v = jnp.stack(new_gv, axis=1)
    new_diff_k = jnp.stack(new_dk, axis=1)
    new_diff_v = jnp.stack(new_dv, axis=1)
    return (y_prompt, y_sample, new_mla_ckv, new_mla_kr, new_gqa_k, new_gqa_v, new_diff_k, new_diff_v)
```

```python
import math
import os
from contextlib import ExitStack
KDBG = os.environ.get('KDBG', '').split(',')
import numpy as np
import concourse.bass as bass
import concourse.mybir as mybir
from concourse.bass_utils import run_bass_kernel_spmd

F32 = mybir.dt.float32
BF16 = mybir.dt.bfloat16
AF = mybir.ActivationFunctionType
ALU = mybir.AluOpType

ENGS = ("pe", "act", "dve", "pool", "sp")
ATTR = {"pe": "tensor", "act": "scalar", "dve": "vector", "pool": "gpsimd", "sp": "sync"}
SELF_RAW_DIST = 3
EPS = 1e-6
NSLOT = 3
SLOTW = 2816


class Buf:
    __slots__ = ("w", "r", "excl")

    def __init__(self, excl=False):
        self.w = None
        self.r = {}
        self.excl = excl


class BG:
    def __init__(self):
        self.d = {}

    def __call__(self, *key):
        b = self.d.get(key)
        if b is None:
            b = self.d[key] = Buf()
        return b


class DSem:
    def __init__(self, name, inc=16):
        self.name = name
        self.val = 0
        self.inc = inc
        self.h = None


class Op:
    __slots__ = ("fn", "deps", "marked", "count", "dsem")

    def __init__(self, fn, deps, dsem):
        self.fn = fn
        self.deps = deps
        self.marked = False
        self.count = 0
        self.dsem = dsem


class Sched:
    def __init__(self):
        self.ops = {e: [] for e in ENGS}
        self.clock = {e: {} for e in ENGS}
        self.hist = {e: [] for e in ENGS}
        self.dsems = []

    def dsem(self, name, inc=16):
        d = DSem(name, inc)
        self.dsems.append(d)
        return d

    def op(self, eng, fn, reads=(), writes=(), dsem=None):
        ops = self.ops[eng]
        clock = self.clock[eng]
        seq = len(ops)
        deps = {}

        def need(key, val, raw):
            if key == eng:
                if (not raw) or (seq - val > SELF_RAW_DIST and eng != "pool"):
                    return
                if clock.get(("self", eng), -1) >= val:
                    return
                clock[("self", eng)] = val
                deps[key] = max(deps.get(key, -1), val)
                return
            if clock.get(key, -1) >= val:
                return
            clock[key] = val
            deps[key] = val
            if isinstance(key, str):
                for k2, v2 in self.hist[key][val].items():
                    if isinstance(k2, tuple) or k2 == eng:
                        continue
                    if clock.get(k2, -1) < v2:
                        clock[k2] = v2

        for b in reads:
            if b.w is not None:
                need(b.w[0], b.w[1], True)
            if b.excl:
                for k, v in b.r.items():
                    need(k, v, False)
        for b in writes:
            if b.w is not None:
                need(b.w[0], b.w[1], False)
            for k, v in b.r.items():
                need(k, v, False)
        o = Op(fn, list(deps.items()), dsem)
        ops.append(o)
        self.hist[eng].append(dict(clock))
        if dsem is not None:
            dsem.val += dsem.inc
            key, val = dsem, dsem.val
        else:
            key, val = eng, seq
        for b in reads:
            b.r[key] = val
        for b in writes:
            b.w = (key, val)
            b.r = {}
        return o

    def finalize(self):
        for e in ENGS:
            for o in self.ops[e]:
                for k, v in o.deps:
                    if isinstance(k, str):
                        self.ops[k][v].marked = True
        for e in ENGS:
            c = 0
            for o in self.ops[e]:
                if o.marked:
                    c += 1
                o.count = c

    def emit(self, eng, e, sems):
        for o in self.ops[eng]:
            for k, v in o.deps:
                if isinstance(k, str):
                    e.wait_ge(sems[k], self.ops[k][v].count)
                else:
                    e.wait_ge(k.h, v)
            ins = o.fn(e)
            if o.dsem is not None:
                ins.then_inc(o.dsem.h, o.dsem.inc)
            elif o.marked:
                ins.then_inc(sems[eng], 1)


class KB:
    def __init__(self, nlayers=4, noffn=False):
        self.NL = nlayers
        self.noffn = noffn
        self.nc = bass.Bass("TRN2", target_bir_lowering=False)
        self.S = Sched()
        self.es = ExitStack()
        self.in_names = []
        self.out_names = []
        self.out_dsems = []

    def inp(self, name, shape, dt=F32):
        self.in_names.append(name)
        return self.nc.dram_tensor(name, list(shape), dt, kind="ExternalInput").ap()

    def outp(self, name, shape, dt=F32):
        self.out_names.append(name)
        return self.nc.dram_tensor(name, list(shape), dt, kind="ExternalOutput").ap()

    def dram(self, name, shape, dt):
        return self.nc.dram_tensor(name, list(shape), dt)

    def sb(self, name, shape, dt):
        return self.es.enter_context(self.nc.sbuf_tensor(name, list(shape), dt))

    def mm(self, out, lhsT, rhs, start=True, stop=True, R=(), W=()):
        self.S.op("pe", lambda e: e.matmul(out, lhsT=lhsT, rhs=rhs, start=start, stop=stop), R, W)

    def act(self, out, in_, func, R=(), W=(), bias=None, scale=None):
        kw = {}
        if bias is not None:
            kw["bias"] = bias
        if scale is not None:
            kw["scale"] = scale
        self.S.op("act", lambda e: e.activation(out=out, in_=in_, func=func, **kw), R, W)

    def tt(self, eng, out, in0, in1, op, R=(), W=()):
        self.S.op(eng, lambda e: e.tensor_tensor(out=out, in0=in0, in1=in1, op=op), R, W)

    def ts(self, eng, out, in0, s1, s2, op0, op1, R=(), W=()):
        self.S.op(eng, lambda e: e.tensor_scalar(out=out, in0=in0, scalar1=s1, scalar2=s2, op0=op0, op1=op1), R, W)

    def ts1(self, eng, out, in0, s1, op0, R=(), W=()):
        self.S.op(eng, lambda e: e.tensor_single_scalar(out=out, in_=in0, scalar=s1, op=op0), R, W)

    def stt(self, eng, out, in0, scalar, in1, op0, op1, R=(), W=()):
        self.S.op(eng, lambda e: e.scalar_tensor_tensor(out=out, in0=in0, scalar=scalar, in1=in1, op0=op0, op1=op1), R, W)

    def cp(self, eng, out, in_, R=(), W=()):
        if eng == "act":
            self.S.op("act", lambda e: e.activation(out=out, in_=in_, func=AF.Copy), R, W)
        else:
            self.S.op(eng, lambda e: e.tensor_copy(out=out, in_=in_), R, W)

    def memset(self, eng, ap, val, W=()):
        self.S.op(eng, lambda e: e.memset(ap, val), (), W)

    def dma(self, q, out, in_, ds, R=(), W=()):
        self.S.op(q, lambda e: e.dma_start(out=out, in_=in_), R, W, dsem=ds)

    def barrier_bufs(self, bufs):
        nb = Buf()
        for b in bufs:
            if b.w is not None:
                nb.r[b.w[0]] = max(nb.r.get(b.w[0], -1), b.w[1])
            for k, v in b.r.items():
                nb.r[k] = max(nb.r.get(k, -1), v)
        return nb

    def build(self):
        nc, S = self.nc, self.S
        NL = self.NL
        sb = self.sb
        xT_d = self.inp("xT", [128, 8, 2048])
        cT_d = self.inp("cT", [128, 8, 2])
        ada_d = self.inp("ada_t", [NL, 36, 128, 2048])
        adab_d = self.inp("ada_bT", [128, NL, 72])
        normg_d = self.inp("normg", [128, NL, 3, 8])
        normf_d = self.inp("normf", [128, 8])
        if not self.noffn:
            fin_d = self.inp("ffn_in_t", [NL, 2, 22, 128, 2048])
            fout_d = self.inp("ffn_out_t", [NL, 2, 8, 128, 2816])
        consts_d = self.inp("consts", [128, 3, 128])
        ropeA_d = self.inp("ropeA", [128, 2, 1024])
        ropeB_d = self.inp("ropeB", [128, 2, 1024])
        n_mla = (NL + 2) // 3
        has_gqa = NL >= 2
        has_diff = NL >= 3
        mla_down_d = self.inp("mla_down_t", [n_mla, 3, 128, 2048])
        mla_uq_d = self.inp("mla_uq_t", [n_mla, 4, 128, 1152])
        mla_ukv_d = self.inp("mla_ukv_t", [n_mla, 4, 128, 1024])
        mla_o_d = self.inp("mla_o_t", [n_mla, 8, 128, 1024])
        mla_g_d = self.inp("mla_g", [128, n_mla, 5])
        mla_cckv_d = self.inp("mla_cckvT", [n_mla, 128, 2, 512])
        mla_ckr_d = self.inp("mla_ckrT", [n_mla, 32, 512])
        if has_gqa:
            gqa_w_d = self.inp("gqa_t", [7, 128, 2048])
            gqa_o_d = self.inp("gqa_o_t", [8, 128, 1024])
            gqa_g_d = self.inp("gqa_g", [128, 2])
            gqa_ck_d = self.inp("gqa_ckT", [4, 128, 512])
            gqa_cv_d = self.inp("gqa_cv", [512, 256])
        if has_diff:
            diff_w_d = self.inp("diff_t", [12, 128, 2048])
            diff_o_d = self.inp("diff_o_t", [8, 128, 1024])
            diff_l_d = self.inp("diff_l", [128, 4, 64])
            diff_g_d = self.inp("diff_g", [128, 1])
            diff_ck_d = self.inp("diff_ckT", [8, 128, 512])
            diff_cv_d = self.inp("diff_cv", [512, 1024])

        yout_d = self.outp("yT_out", [128, 8, 2048])
        o_ckv_d = self.outp("o_ckv", [2, 128, 2, 1024])
        o_kr_d = self.outp("o_kr", [2, 32, 1024])
        o_gk_d = self.outp("o_gk", [4, 64, 1024])
        o_gv_d = self.outp("o_gv", [1024, 256])
        o_dk_d = self.outp("o_dk", [128, 8, 1024])
        o_dv_d = self.outp("o_dv", [1024, 1024])

        RG = [[0, 1, 2, 3], [4, 5, 6, 7]]
        gA_in = self.dram("gA_in", [288, 1024], BF16)
        gA_out = self.dram("gA_out", [4 * 288, 1024], BF16)
        gBk_in = self.dram("gBk_in", [512, 1024], BF16)
        gBk_out = self.dram("gBk_out", [4 * 512, 1024], BF16)
        gBv_in = self.dram("gBv_in", [1024, 256], BF16)
        gBv_out = self.dram("gBv_out", [4096, 256], BF16)
        gCk_in = [self.dram(f"gCk_in{i}", [256, 1024], BF16) for i in range(4)]
        gCk_out = [self.dram(f"gCk_out{i}", [1024, 1024], BF16) for i in range(4)]
        gCv_in = [self.dram(f"gCv_in{i}", [1024, 256], BF16) for i in range(4)]
        gCv_out = [self.dram(f"gCv_out{i}", [4096, 256], BF16) for i in range(4)]

        yT = sb("yT", [128, 8, 2048], F32)
        hT = sb("hT", [128, 8, 1024], BF16)
        QT = sb("QT", [128, 8, 1024], BF16)
        arena = sb("arena", [128, 24576], BF16)
        QTf = QT[:].rearrange("p c t -> p (c t)")
        slots = [sb(f"wslot{i}", [128, SLOTW], BF16) for i in range(NSLOT)]
        tmpF = [sb(f"tmpF{i}", [128, 512], F32) for i in range(4)]
        sqt = [sb(f"sqt{i}", [128, 512], BF16) for i in range(2)]
        rstd = sb("rstd", [128, 512], F32)
        ropeA = sb("ropeA_s", [128, 2, 1024], F32)
        ropeB = sb("ropeB_s", [128, 2, 1024], F32)
        consts = sb("consts_s", [128, 3, 128], F32)
        onesb = sb("onesb", [128, 128], BF16)
        bd64 = sb("bd64", [128, 128], BF16)
        onesf = sb("onesf", [128, 128], F32)
        warm_rhs = sb("warm_rhs", [128, 512], BF16)
        cT = sb("cT_s", [128, 8, 2], F32)
        actT = sb("actT", [128, 8, 2], BF16)
        adab = sb("adab", [128, NL, 72], F32)
        modT = sb("modT", [128, NL, 72, 2], F32)
        normg = sb("normg_s", [128, NL, 3, 8], F32)
        normf = sb("normf_s", [128, 8], F32)
        gsT = sb("gsT", [128, NL, 3, 2, 8], F32)
        hgT = sb("hgT", [128, NL, 2, 2, 8], F32)
        mlag = sb("mlag", [128, n_mla, 5], F32)
        gqag = sb("gqag", [128, 2], F32)
        diffg = sb("diffg", [128, 1], F32)
        diffl = sb("diffl", [128, 4, 64], F32)
        lamt = sb("lamt", [128, 8], F32)
        ptile = [sb(f"ptile{i}", [128, 512], BF16) for i in range(5)]
        stage = [sb(f"stage{i}", [128, 512], F32) for i in range(2)]
        pb = [self.es.enter_context(nc.psum_tensor(f"pb{i}", [128, 512], F32)) for i in range(8)]

        sems = {e: self.es.enter_context(nc.semaphore("s_" + e)) for e in ENGS}

        YB, HB = BG(), BG()
        PB = [Buf(excl=True) for _ in range(8)]
        WB = [Buf() for _ in range(NSLOT)]
        TMPB = [Buf() for _ in range(4)]
        SQB = [Buf() for _ in range(2)]
        PTB = [Buf() for _ in range(5)]
        STB = [Buf() for _ in range(2)]
        RSTD, CONST, MOD, ARENA = Buf(), Buf(), Buf(), Buf()
        wds = [S.dsem(f"w{i}") for i in range(NSLOT)]
        d_in = S.dsem("in")
        d_misc = S.dsem("misc")
        d_out = S.dsem("out")
        d_st = [S.dsem(f"st{i}") for i in range(2)]
        d_kv = [S.dsem(f"kv{i}") for i in range(4)]
        d_g = S.dsem("gin")
        d_cc = S.dsem("cc", inc=1)
        self._abufs = []
        self._fence = {}

        def AB():
            b = Buf()
            b.r = dict(self._fence)
            self._abufs.append(b)
            return b

        class ABG(BG):
            def __call__(s_, *key):
                b = s_.d.get(key)
                if b is None:
                    b = s_.d[key] = AB()
                return b

        def arena_fence():
            f = dict(self._fence)
            for b in self._abufs:
                if b.w is not None:
                    f[b.w[0]] = max(f.get(b.w[0], -1), b.w[1])
                for k, v in b.r.items():
                    f[k] = max(f.get(k, -1), v)
            self._fence = f
            self._abufs = []

        d_tmp = [S.dsem(f"tmpo{i}") for i in range(4)]
        GDS = {n: S.dsem("g_" + n) for n in ("a", "bk", "bv", "ck", "cv")}
        self._wi = 0
        self._bank = 0
        self._srot = 0
        self._prot = 0
        self._tmp = 0
        self._stg = 0

        def load_w(dram_ap, n):
            s = self._wi % NSLOT
            self._wi += 1
            self.dma("pool", slots[s][:, 0:n], dram_ap, wds[s], W=[WB[s]])
            return slots[s], WB[s]

        def bank(cands=(5, 6, 7)):
            i = cands[self._bank % len(cands)]
            self._bank += 1
            return i

        def tmpi():
            i = self._tmp % 4
            self._tmp += 1
            return i

        def stg():
            i = self._stg % 2
            self._stg += 1
            return i

        self.dma("sp", yT[:], xT_d, d_in, W=[YB(c, g, b) for c in range(8) for g in range(2) for b in range(2)])
        for (t, d) in ((cT, cT_d), (adab, adab_d), (normg, normg_d), (normf, normf_d), (ropeA, ropeA_d),
                       (ropeB, ropeB_d), (consts, consts_d), (mlag, mla_g_d)):
            self.dma("sp", t[:], d, d_misc, W=[CONST])
        if has_gqa:
            self.dma("sp", gqag[:], gqa_g_d, d_misc, W=[CONST])
        if has_diff:
            self.dma("sp", diffg[:], diff_g_d, d_misc, W=[CONST])
            self.dma("sp", diffl[:], diff_l_d, d_misc, W=[CONST])
        self.memset("dve", onesb[:], 1.0, W=[CONST])
        self.memset("dve", onesf[:], 1.0, W=[CONST])
        self.memset("dve", warm_rhs[:], 0.0, W=[CONST])
        self.memset("dve", bd64[:], 0.0, W=[CONST])
        self.memset("dve", bd64[0:64, 0:64], 1.0, W=[CONST])
        self.memset("dve", bd64[64:128, 64:128], 1.0, W=[CONST])
        self.act(actT[:], cT[:], AF.Silu, R=[CONST], W=[MOD])
        permA = consts[:, 1, :]
        permB = consts[:, 2, :]

        MODL = [Buf() for _ in range(NL)]
        cur = {"l": 0}

        def ada_gen(l):
            for s in range(36):
                slot, WBs = load_w(ada_d[l, s], 2048)
                wv = slot[:, 0:2048].rearrange("p (k n) -> p k n", k=8)
                for h in range(2):
                    ch = s * 2 + h
                    for k in range(8):
                        self.mm(pb[7][:, ch * 2:ch * 2 + 2], wv[:, k, h * 128:(h + 1) * 128], actT[:, k, :],
                                start=(k == 0), stop=(k == 7), R=[WBs, MOD], W=[PB[7]])
                yield
            pv = pb[7][:, 0:144].rearrange("p (c s) -> p c s", s=2)
            ML = MODL[l]
            for st in range(2):
                self.tt("dve", modT[:, l, :, st], pv[:, :, st], adab[:, l, :], ALU.add, R=[PB[7], CONST], W=[ML])
            for sub in range(3):
                for st in range(2):
                    sc = modT[:, l, (3 * sub + 1) * 8:(3 * sub + 2) * 8, st]
                    self.stt("dve", gsT[:, l, sub, st, :], sc, 1.0, normg[:, l, sub, :], ALU.add, ALU.mult, R=[ML, CONST], W=[ML])
                    self.ts1("dve", gsT[:, l, sub, st, :], gsT[:, l, sub, st, :], 32.0, ALU.mult, R=[ML], W=[ML])
            for w in range(2):
                for st in range(2):
                    gt = modT[:, l, (6 * w + 2) * 8:(6 * w + 3) * 8, st]
                    self.ts1("dve", hgT[:, l, w, st, :], gt, 0.5, ALU.mult, R=[ML], W=[ML])
            yield

        for _ in ada_gen(0):
            pass
        self._adag = None

        def ada_step():
            if self._adag is not None:
                try:
                    next(self._adag)
                except StopIteration:
                    self._adag = None

        self.ts1("dve", normf[:], normf[:], 32.0, ALU.mult, R=[CONST], W=[CONST])
        for j in range(n_mla):
            self.ts1("dve", mlag[:, j, 0:3], mlag[:, j, 0:3], math.sqrt(384.0), ALU.mult, R=[CONST], W=[CONST])
            self.ts1("dve", mlag[:, j, 3:5], mlag[:, j, 3:5], 16.0, ALU.mult, R=[CONST], W=[CONST])
        if has_gqa:
            self.ts1("dve", gqag[:], gqag[:], 8.0, ALU.mult, R=[CONST], W=[CONST])
        if has_diff:
            lam_init = 0.8 - 0.6 * math.exp(-0.3 * 2)
            self.ts1("dve", diffg[:], diffg[:], math.sqrt(128.0) * (1.0 - lam_init), ALU.mult, R=[CONST], W=[CONST])
        if has_diff and 'dnolam' not in KDBG:
            self.S.op("dve", lambda e: e.tensor_tensor(out=diffl[:, 0, :], in0=diffl[:, 0, :], in1=diffl[:, 1, :], op=ALU.mult), [CONST], [CONST])
            self.S.op("dve", lambda e: e.tensor_tensor(out=diffl[:, 2, :], in0=diffl[:, 2, :], in1=diffl[:, 3, :], op=ALU.mult), [CONST], [CONST])
            self.S.op("dve", lambda e: e.reduce_sum(out=lamt[:, 0:1], in_=diffl[:, 0, :], axis=mybir.AxisListType.X), [CONST], [CONST])
            self.S.op("dve", lambda e: e.reduce_sum(out=lamt[:, 1:2], in_=diffl[:, 2, :], axis=mybir.AxisListType.X), [CONST], [CONST])
            self.act(lamt[:, 2:4], lamt[:, 0:2], AF.Exp, R=[CONST], W=[CONST])
            self.tt("dve", lamt[:, 4:5], lamt[:, 3:4], lamt[:, 2:3], ALU.subtract, R=[CONST], W=[CONST])
            self.ts1("dve", lamt[:, 5:6], lamt[:, 4:5], -lam_init, ALU.add, R=[CONST], W=[CONST])

        def rsqrt(dst, src, addc, n, R, W):
            ti = tmpi()
            self.act(tmpF[ti][:, 0:n], src, AF.Sqrt, bias=addc, R=R, W=[TMPB[ti]])
            self.S.op("dve", lambda e: e.reciprocal(out=dst, in_=tmpF[ti][:, 0:n]), [TMPB[ti]], W)

        def gs_ap(l, sub, st):
            return lambda c: gsT[:, l, sub, st, c:c + 1]

        def sh_ap(l, sub, st):
            return lambda c: modT[:, l, (3 * sub) * 8 + c, st:st + 1]

        def sumsq(src_fn, nch, n, srcR):
            bi = bank((5, 6))
            for c in range(nch):
                sq = sqt[c % 2]
                self.act(sq[:, 0:n], src_fn(c), AF.Square, R=srcR(c), W=[SQB[c % 2]])
                self.mm(pb[bi][:, 0:n], onesb[:], sq[:, 0:n], start=(c == 0), stop=(c == nch - 1), R=[SQB[c % 2], CONST], W=[PB[bi]])
            return bi

        def normmod(g, gs, sh):
            for blk in range(2):
                t0 = g * 1024 + blk * 512
                bi = sumsq(lambda c: yT[:, c, t0:t0 + 512], 8, 512, lambda c: [YB(c, g, blk)])
                rsqrt(rstd[:], pb[bi][:], 1024.0 * EPS, 512, [PB[bi]], [RSTD])
                for c in range(8):
                    ti = tmpi()
                    self.stt("dve", tmpF[ti][:], yT[:, c, t0:t0 + 512], gs(c), rstd[:], ALU.mult, ALU.mult,
                             R=[YB(c, g, blk), RSTD, MODL[cur['l']]], W=[TMPB[ti]])
                    self.act(hT[:, c, blk * 512:(blk + 1) * 512], tmpF[ti][:], AF.Identity, bias=sh(c),
                             R=[TMPB[ti], MODL[cur['l']]], W=[HB(c, blk)])

        uT = arena[:, 0:22 * 1024].rearrange("p (j t) -> p j t", j=22)

        def ffn(g, l, w):
            if self.noffn:
                return
            sub = 0 if w == 0 else 2
            arena_fence()
            UB = ABG()
            normmod(g, gs_ap(l, sub, g), sh_ap(l, sub, g))
            for j in range(22):
                slot, WBs = load_w(fin_d[l, w, j], 2048)
                wv = slot[:, 0:2048].rearrange("p (k s n) -> p k s n", k=8, s=2)
                for blk in range(2):
                    pi = (j * 2 + blk) % 2
                    for s in range(2):
                        bi = 2 * pi + s
                        for k in range(8):
                            self.mm(pb[bi][:], wv[:, k, s, :], hT[:, k, blk * 512:(blk + 1) * 512], start=(k == 0), stop=(k == 7),
                                    R=[WBs, HB(k, blk)], W=[PB[bi]])
                    ti = tmpi()
                    self.act(tmpF[ti][:], pb[2 * pi][:], AF.Silu, R=[PB[2 * pi]], W=[TMPB[ti]])
                    self.tt("dve", uT[:, j, blk * 512:(blk + 1) * 512], pb[2 * pi + 1][:], tmpF[ti][:], ALU.mult,
                            R=[PB[2 * pi + 1], TMPB[ti]], W=[UB(j, blk)])
                if w == 0:
                    ada_step()
            for m in range(8):
                slot, WBs = load_w(fout_d[l, w, m], 2816)
                wv = slot[:, 0:2816].rearrange("p (k n) -> p k n", k=22)
                for blk in range(2):
                    bi = 4 + (m * 2 + blk) % 2
                    for k in range(22):
                        self.mm(pb[bi][:], wv[:, k, :], uT[:, k, blk * 512:(blk + 1) * 512], start=(k == 0), stop=(k == 21),
                                R=[WBs, UB(k, blk)], W=[PB[bi]])
                    t0 = g * 1024 + blk * 512
                    self.stt("dve", yT[:, m, t0:t0 + 512], pb[bi][:], hgT[:, l, w, g, m:m + 1], yT[:, m, t0:t0 + 512], ALU.mult, ALU.add,
                             R=[PB[bi], YB(m, g, blk), MODL[cur['l']]], W=[YB(m, g, blk)])

        def proj(wv_fn, kc, rhs_fn, M, n, R, cands=(5, 6, 7)):
            bi = bank(cands)
            for k in range(kc):
                self.mm(pb[bi][0:M, 0:n], wv_fn(k), rhs_fn(k), start=(k == 0), stop=(k == kc - 1), R=R(k), W=[PB[bi]])
            return bi

        def rope_combine(A, ABuf, lo, hi, perm, table, t0, n, dst, dstW):
            bi = bank()
            self.mm(pb[bi][0:hi, 0:n], perm[lo:hi, 0:hi], A[lo:hi, 0:n], R=[ABuf, CONST], W=[PB[bi]])
            t2 = tmpi()
            self.tt("dve", tmpF[t2][lo:hi, 0:n], pb[bi][lo:hi, 0:n], table[lo:hi, 1, t0:t0 + n], ALU.mult, R=[PB[bi], CONST], W=[TMPB[t2]])
            self.tt("dve", A[lo:hi, 0:n], A[lo:hi, 0:n], table[lo:hi, 0, t0:t0 + n], ALU.mult, R=[ABuf, CONST], W=[ABuf])
            self.tt("pool", dst, A[lo:hi, 0:n], tmpF[t2][lo:hi, 0:n], ALU.add, R=[ABuf, TMPB[t2]], W=dstW)

        self._pending = []

        def flush_pending():
            p, self._pending = self._pending, []
            for fn in p:
                fn()

        self._units = None

        def begin_units(warm=0):
            for _ in range(warm):
                self.mm(pb[7][:, :], onesb[:], warm_rhs[:], R=[CONST], W=[PB[7]])
            self._units = []

        def end_units():
            us, self._units = self._units, None
            run_units(us)

        def attend(KT, QT_ap, nkb, nq, V_fn, vM, po, scale, RK, RQ, RV, zb=None):
            u = dict(KT=[KT(kb) for kb in range(nkb)], Q=QT_ap, nkb=nkb, nq=nq, V=[V_fn(kb) for kb in range(nkb)], vM=vM, po=po,
                     scale=scale, RK=RK, RQ=RQ, RV=RV, zb=zb, fin=None, imm=False)
            if self._units is not None:
                self._units.append(u)
            else:
                run_units([u])

        def attach_fin(fn, imm=False):
            if self._units:
                self._units[-1]["fin"] = fn
                self._units[-1]["imm"] = imm
            elif imm:
                fn()
            else:
                self._pending.append(fn)

        def run_units(units):
            G = 2

            def issue_qk(u, grp):
                sbanks = (0, 1, 2, 7) if u["zb"] is not None else (0, 1, 2, 5, 6)
                out = {}
                for kb in range(grp * G, min(u["nkb"], (grp + 1) * G)):
                    si = sbanks[self._srot % len(sbanks)]
                    self._srot += 1
                    out[kb] = si
                    self.mm(pb[si][:, 0:u["nq"]], u["KT"][kb], u["Q"], R=u["RK"] + u["RQ"], W=[PB[si]])
                return out

            pre = None
            for ui, u in enumerate(units):
                nxt = units[ui + 1] if ui + 1 < len(units) else None
                nkb, nq, po, zb = u["nkb"], u["nq"], u["po"], u["zb"]
                sbank = pre if pre is not None else issue_qk(u, 0)
                pre = None
                ngrp = (nkb + G - 1) // G
                for grp in range(1, ngrp + 1):
                    if grp < ngrp:
                        sbank.update(issue_qk(u, grp))
                    elif nxt is not None and (nxt["zb"] is None) == (zb is None):
                        pre = issue_qk(nxt, 0)
                    kbs = list(range((grp - 1) * G, min(nkb, grp * G)))
                    pis = []
                    for kb in kbs:
                        si = sbank.pop(kb)
                        pi = self._prot % 5
                        self._prot += 1
                        pis.append(pi)
                        self.act(ptile[pi][:, 0:nq], pb[si][:, 0:nq], AF.Exp, scale=u["scale"], R=[PB[si]], W=[PTB[pi]])
                    for kb, pi in zip(kbs, pis):
                        self.mm(pb[po][0:u["vM"], 0:nq], u["V"][kb], ptile[pi][:, 0:nq], start=(kb == 0), stop=(kb == nkb - 1),
                                R=[PTB[pi]] + u["RV"], W=[PB[po]])
                    if zb is not None:
                        for kb, pi in zip(kbs, pis):
                            self.mm(pb[zb][:, 0:nq], onesb[:], ptile[pi][:, 0:nq], start=(kb == 0), stop=(kb == nkb - 1),
                                    R=[PTB[pi], CONST], W=[PB[zb]])
                    if grp == 1:
                        flush_pending()
                if u["fin"] is not None:
                    if u["imm"]:
                        u["fin"]()
                    else:
                        self._pending.append(u["fin"])

        def finish_head(po, par, nq, dst, dstW):
            attach_fin(lambda: finish_head_now(po, par, nq, dst, dstW))

        def finish_head_now(po, par, nq, dst, dstW):
            zr = 64 if par == 0 else 0
            lo = 0 if par == 0 else 64
            ti = tmpi()
            self.S.op("dve", lambda e: e.reciprocal(out=tmpF[ti][zr:zr + 1, 0:nq], in_=pb[po][zr:zr + 1, 0:nq]), [PB[po]], [TMPB[ti]])
            bi = bank((7,))
            self.mm(pb[bi][:, 0:nq], onesf[zr:zr + 1, :], tmpF[ti][zr:zr + 1, 0:nq], R=[TMPB[ti], CONST], W=[PB[bi]])
            t2 = tmpi()
            self.cp("act", tmpF[t2][lo:lo + 64, 0:nq], pb[bi][lo:lo + 64, 0:nq], R=[PB[bi]], W=[TMPB[t2]])
            self.tt("dve", dst, pb[po][lo:lo + 64, 0:nq], tmpF[t2][lo:lo + 64, 0:nq], ALU.mult, R=[PB[po], TMPB[t2]], W=dstW)

        def oproj(g, l, o_d):
            flush_pending()
            for m in range(8):
                slot, WBs = load_w(o_d[m], 1024)
                wv = slot[:, 0:1024].rearrange("p (k n) -> p k n", k=8)
                for blk in range(2):
                    bi = proj(lambda k: wv[:, k, :], 8, lambda k: hT[:, k, blk * 512:(blk + 1) * 512], 128, 512,
                              lambda k: [WBs, HB(k, blk)], cands=(4, 5))
                    t0 = g * 1024 + blk * 512
                    self.stt("dve", yT[:, m, t0:t0 + 512], pb[bi][:], modT[:, l, 5 * 8 + m, g:g + 1], yT[:, m, t0:t0 + 512], ALU.mult, ALU.add,
                             R=[PB[bi], YB(m, g, blk), MODL[cur['l']]], W=[YB(m, g, blk)])

        def allgather(src, dst, RB, WBf):
            self.S.op("pool", lambda e: e.collective_compute("AllGather", ALU.bypass, replica_groups=RG, ins=[src[:, :]], outs=[dst[:, :]]),
                      RB, WBf, dsem=d_cc)

        A_KT = [arena[:, 0:4608], arena[:, 4608:9216]]
        A_V = [arena[:, 9216:13824].rearrange("p (k d) -> p k d", k=36), arena[:, 13824:18432].rearrange("p (k d) -> p k d", k=36)]
        GIN, GOUT = BG(), BG()

        def gqa(l):
            scale = 64 ** -0.5

            def qk_chunk(wv_half, WBs, g, blk, gain, rope, dst, dstW, outfp=None, dst2=None):
                bi = proj(lambda k: wv_half(k), 8, lambda k: hT[:, k, blk * 512:(blk + 1) * 512], 128, 512, lambda k: [WBs, HB(k, blk)])
                self.act(sqt[0][:], pb[bi][:], AF.Square, R=[PB[bi]], W=[SQB[0]])
                b2 = bank()
                self.mm(pb[b2][:], bd64[:], sqt[0][:], R=[SQB[0], CONST], W=[PB[b2]])
                rsqrt(rstd[:], pb[b2][:], 64.0 * EPS, 512, [PB[b2]], [RSTD])
                ti = tmpi()
                self.stt("dve", tmpF[ti][:], pb[bi][:], gain, rstd[:], ALU.mult, ALU.mult, R=[PB[bi], RSTD, CONST], W=[TMPB[ti]])
                if outfp is not None:
                    self.dma("sp", outfp, tmpF[ti][0:64, :], d_tmp[ti], R=[TMPB[ti]])
                if rope and 'norope' not in KDBG:
                    rope_combine(tmpF[ti], TMPB[ti], 0, 128, permB, ropeB, blk * 512, 512, dst, dstW)
                elif dst2 is not None:
                    self.cp("pool", dst, tmpF[ti][0:64, :], R=[TMPB[ti]], W=dstW)
                    self.cp("pool", dst2, tmpF[ti][64:128, :], R=[TMPB[ti]], W=dstW)
                else:
                    self.cp("pool", dst, tmpF[ti][:], R=[TMPB[ti]], W=dstW)

            def run_group(g):
                lat = (g == 1)
                arena_fence()
                KTB = [AB(), AB()]
                QB = ABG()
                normmod(g, gs_ap(l, 1, g), sh_ap(l, 1, g))
                if lat:
                    klat = arena[:, 18432:18432 + 4096].rearrange("p (c t) -> p c t", c=4)
                    KL = AB()
                else:
                    KTc = arena[:, 0:4096].rearrange("p (c t) -> p c t", c=4)
                    KTc1 = arena[:, 12288:16384].rearrange("p (c t) -> p c t", c=4)
                    VE = arena[:, 4096:4096 + 8 * 4 * 66].rearrange("p (k h d) -> p k h d", k=8, h=4)
                    VO = arena[:, 8192:8192 + 8 * 4 * 128].rearrange("p (k h d) -> p k h d", k=8, h=4)
                    KC_, VC_ = ABG(), AB()
                    for kv_ in range(4):
                        self.memset("pool", KTc[64:128, kv_, :], 0.0, W=[KC_(kv_)])
                        self.memset("pool", KTc1[0:64, kv_, :], 0.0, W=[KC_(kv_)])
                    if 'nomemset' not in KDBG:
                        self.memset("pool", VE[:, :, :, 64:65], 1.0, W=[VC_])
                        self.memset("pool", VO[:, :, :, 0:64], 0.0, W=[VC_])
                        self.memset("pool", VO[:, :, :, 0:1], 1.0, W=[VC_])
                for u in range(0 if 'nok' in KDBG else 2):
                    slot, WBs = load_w(gqa_w_d[4 + u], 2048)
                    wv = slot[:, 0:2048].rearrange("p (k n) -> p k n", k=8)
                    for h in range(2):
                        kvh = u * 2 + h
                        for blk in range(2):
                            if lat:
                                qk_chunk(lambda k: wv[:, k, h * 128:(h + 1) * 128], WBs, g, blk, gqag[:, 1:2], True,
                                         klat[:, kvh, blk * 512:(blk + 1) * 512], [KL])
                            else:
                                qk_chunk(lambda k: wv[:, k, h * 128:(h + 1) * 128], WBs, g, blk, gqag[:, 1:2], False,
                                         KTc[0:64, kvh, blk * 512:(blk + 1) * 512], [KC_(kvh)],
                                         outfp=o_gk_d[kvh, :, blk * 512:(blk + 1) * 512],
                                         dst2=KTc1[64:128, kvh, blk * 512:(blk + 1) * 512])
                slot, WBs = load_w(gqa_w_d[6], 2048)
                wv = slot[:, 0:2048].rearrange("p (k n) -> p k n", k=8)
                if lat:
                    vlat = arena[:, 22528:22528 + 2048].rearrange("p (k d) -> p k d", k=8)
                    VL = AB()
                for tb in range(0 if 'nov' in KDBG else 8):
                    bi = proj(lambda k: hT[:, k, tb * 128:(tb + 1) * 128], 8, lambda k: wv[:, k, :], 128, 256,
                              lambda k: [WBs, HB(k, tb // 4)])
                    if lat:
                        self.cp("dve", vlat[:, tb, :], pb[bi][:, 0:256], R=[PB[bi]], W=[VL])
                    else:
                        si = stg()
                        if 'v1' not in KDBG:
                            self.cp("act", stage[si][:, 0:256], pb[bi][:, 0:256], R=[PB[bi]], W=[STB[si]])
                        if 'v2' not in KDBG:
                            self.dma("sp", o_gv_d[tb * 128:(tb + 1) * 128, :], stage[si][:, 0:256], d_st[si], R=[STB[si]])
                        pv = pb[bi][:, 0:256].rearrange("p (h d) -> p h d", h=4)
                        if 'v3' not in KDBG:
                            self.cp("act", VE[:, tb, :, 0:64], pv, R=[PB[bi]], W=[VC_])
                            self.cp("act", VO[:, tb, :, 64:128], pv, R=[PB[bi]], W=[VC_])
                if lat and 'nogather' not in KDBG:
                    self.dma("sp", gBk_in.ap().rearrange("(c p) t -> p c t", p=128), klat, GDS["bk"], R=[KL], W=[GIN("bk")])
                    self.dma("sp", gBv_in.ap().rearrange("(k p) d -> p k d", p=128), vlat, GDS["bv"], R=[VL], W=[GIN("bv")])
                    allgather(gBk_in, gBk_out, [GIN("bk")], [GOUT("bk")])
                    allgather(gBv_in, gBv_out, [GIN("bv")], [GOUT("bv")])
                for u in range(0 if 'noq' in KDBG else 4):
                    slot, WBs = load_w(gqa_w_d[u], 2048)
                    wv = slot[:, 0:2048].rearrange("p (k n) -> p k n", k=8)
                    for h in range(2):
                        m = u * 2 + h
                        for blk in range(2):
                            qk_chunk(lambda k: wv[:, k, h * 128:(h + 1) * 128], WBs, g, blk, gqag[:, 0:1], lat,
                                     QT[:, m, blk * 512:(blk + 1) * 512], [QB(m, blk)])
                if (not lat) and 'noctxattn' in KDBG:
                    pass
                elif not lat:
                    begin_units()
                    for s in range(4):
                        for hd in range(16):
                            kvh, par, m = hd // 4, hd % 2, hd // 2
                            r0 = par * 64
                            po = 3 + (hd % 2)
                            Vt = VE if par == 0 else VO
                            vM = 65 if par == 0 else 128
                            Kz = KTc if par == 0 else KTc1
                            attend(lambda kb: Kz[:, kvh, s * 256 + kb * 128:s * 256 + (kb + 1) * 128],
                                   QT[:, m, s * 256:(s + 1) * 256], 2, 256,
                                   lambda kb: Vt[:, s * 2 + kb, kvh, 0:vM], vM, po, scale,
                                   [KC_(kvh)], [QB(m, s // 2)], [VC_])
                            finish_head(po, par, 256, hT[r0:r0 + 64, m, s * 256:(s + 1) * 256], [HB(m, s // 2)])
                    end_units()
                elif 'nolatattn' in KDBG:
                    pass
                else:
                    VEl = arena[:, 9216:9216 + 36 * 66].rearrange("p (k d) -> p k d", k=36)
                    VOl = arena[:, 13824:18432].rearrange("p (k d) -> p k d", k=36)
                    VEB, VOB = AB(), AB()
                    self.memset("pool", VEl[:, :, 64:65], 1.0, W=[VEB])
                    self.memset("pool", VOl[:, :, 0:64], 0.0, W=[VOB])
                    self.memset("pool", VOl[:, :, 0:1], 1.0, W=[VOB])
                    gk = gBk_out.ap().rearrange("(r c p) t -> c p r t", r=4, c=4)
                    gv = gBv_out.ap().rearrange("(k p) (h d) -> h p k d", p=128, h=4)
                    cv = gqa_cv_d.rearrange("(k p) (h d) -> h p k d", p=128, h=4)
                    self.memset("pool", A_KT[0][64:128, :], 0.0, W=[KTB[0]])
                    self.memset("pool", A_KT[1][0:64, :], 0.0, W=[KTB[1]])
                    for kvh in range(4):
                        for ks in range(2):
                            rs = slice(ks * 64, ks * 64 + 64)
                            self.dma("sp", A_KT[ks][rs, 0:4096].rearrange("p (r t) -> p r t", r=4), gk[kvh][rs], d_kv[ks], R=[GOUT("bk")], W=[KTB[ks]])
                            self.dma("pool", A_KT[ks][rs, 4096:4608], gqa_ck_d[kvh][rs], d_kv[ks], W=[KTB[ks]])
                        self.dma("sp", VEl[:, 0:32, 0:64], gv[kvh], d_kv[2], R=[GOUT("bv")], W=[VEB])
                        self.dma("pool", VEl[:, 32:36, 0:64], cv[kvh], d_kv[2], W=[VEB])
                        self.dma("sp", VOl[:, 0:32, 64:128], gv[kvh], d_kv[3], R=[GOUT("bv")], W=[VOB])
                        self.dma("pool", VOl[:, 32:36, 64:128], cv[kvh], d_kv[3], W=[VOB])
                        begin_units()
                        for hh in range(4):
                            hd = kvh * 4 + hh
                            par, m = hd % 2, hd // 2
                            r0 = par * 64
                            Vt = VEl if par == 0 else VOl
                            VtB = VEB if par == 0 else VOB
                            vM = 65 if par == 0 else 128
                            for qb in range(2):
                                po = 3 + (qb % 2)
                                attend(lambda kb: A_KT[par][:, kb * 128:(kb + 1) * 128],
                                       QT[:, m, qb * 512:(qb + 1) * 512], 36, 512,
                                       lambda kb: Vt[:, kb, 0:vM], vM, po, scale, [KTB[par]], [QB(m, qb)], [VtB])
                                finish_head(po, par, 512, hT[r0:r0 + 64, m, qb * 512:(qb + 1) * 512], [HB(m, qb)])
                        end_units()
                if 'noo' not in KDBG:
                    oproj(g, l, gqa_o_d)

            run_group(0)
            run_group(1)

        def diff(l):
            scale = 64 ** -0.5

            def run_group(g):
                lat = (g == 1)
                arena_fence()
                KTB = [AB(), AB()]
                VB = [AB(), AB()]
                QB = ABG()
                normmod(g, gs_ap(l, 1, g), sh_ap(l, 1, g))
                if lat:
                    KL = AB()
                    VL = AB()
                    klat = arena[:, 18432:18432 + 1024]
                    vlat = arena[:, 19456:19456 + 2048].rearrange("p (k d) -> p k d", k=8)
                else:
                    KTc = arena[:, 0:8192].rearrange("p (c t) -> p c t", c=8)
                    KTc1 = arena[:, 16384:24576].rearrange("p (c t) -> p c t", c=8)
                    Vc = arena[:, 8192:16384].rearrange("p (k d) -> p k d", k=8)
                    KC_, VC_ = ABG(), ABG()
                    for c_ in range(8):
                        self.memset("pool", KTc[64:128, c_, :], 0.0, W=[KC_(c_)])
                        self.memset("pool", KTc1[0:64, c_, :], 0.0, W=[KC_(c_)])

                def qk_unit(u, isq):
                    slot, WBs = load_w(diff_w_d[u], 2048)
                    wv = slot[:, 0:2048].rearrange("p (k n) -> p k n", k=8)
                    for h in range(2):
                        ch = (u % 4) * 2 + h
                        for blk in range(2):
                            bi = proj(lambda k: wv[:, k, h * 128:(h + 1) * 128], 8, lambda k: hT[:, k, blk * 512:(blk + 1) * 512], 128, 512,
                                      lambda k: [WBs, HB(k, blk)])
                            if isq:
                                dst, dstW = QT[:, ch, blk * 512:(blk + 1) * 512], [QB(ch, blk)]
                            elif lat:
                                dst, dstW = klat[:, blk * 512:(blk + 1) * 512], [KL]
                            else:
                                dst, dstW = KTc[:, ch, blk * 512:(blk + 1) * 512], [KC_(ch)]
                            if lat:
                                ti = tmpi()
                                self.cp("act", tmpF[ti][:], pb[bi][:], R=[PB[bi]], W=[TMPB[ti]])
                                rope_combine(tmpF[ti], TMPB[ti], 0, 128, permB, ropeB, blk * 512, 512, dst, dstW)
                            else:
                                if isq:
                                    self.cp("act", dst, pb[bi][:], R=[PB[bi]], W=dstW)
                                else:
                                    self.cp("act", KTc[0:64, ch, blk * 512:(blk + 1) * 512], pb[bi][0:64, :], R=[PB[bi]], W=dstW)
                                    self.cp("act", KTc1[64:128, ch, blk * 512:(blk + 1) * 512], pb[bi][64:128, :], R=[PB[bi]], W=dstW)
                                if not isq:
                                    si = stg()
                                    self.cp("dve", stage[si][:], pb[bi][:], R=[PB[bi]], W=[STB[si]])
                                    self.dma("sp", o_dk_d[:, ch, blk * 512:(blk + 1) * 512], stage[si][:], d_st[si], R=[STB[si]])
                        if (not isq) and lat:
                            self.dma("sp", gCk_in[ch // 2][(ch % 2) * 128:(ch % 2 + 1) * 128, :], klat, GDS["ck"], R=[KL], W=[GIN("ck", ch // 2)])
                            if ch % 2 == 1 and 'dnogather' not in KDBG:
                                allgather(gCk_in[ch // 2], gCk_out[ch // 2], [GIN("ck", ch // 2)], [GOUT("ck", ch // 2)])

                for u in range(4, 4 if 'dnok' in KDBG else 8):
                    qk_unit(u, False)
                for u in range(8, 8 if 'dnov' in KDBG else 12):
                    slot, WBs = load_w(diff_w_d[u], 2048)
                    wv = slot[:, 0:2048].rearrange("p (k n) -> p k n", k=8)
                    for tb in range(8):
                        bi = proj(lambda k: hT[:, k, tb * 128:(tb + 1) * 128], 8, lambda k: wv[:, k, :], 128, 256,
                                  lambda k: [WBs, HB(k, tb // 4)])
                        c0 = (u - 8) * 256
                        if lat:
                            self.cp("dve", vlat[:, tb, :], pb[bi][:, 0:256], R=[PB[bi]], W=[VL])
                        else:
                            si = stg()
                            self.cp("act", stage[si][:, 0:256], pb[bi][:, 0:256], R=[PB[bi]], W=[STB[si]])
                            self.dma("sp", o_dv_d[tb * 128:(tb + 1) * 128, c0:c0 + 256], stage[si][:, 0:256], d_st[si], R=[STB[si]])
                            self.cp("dve", Vc[:, tb, c0:c0 + 256], pb[bi][:, 0:256], R=[PB[bi]], W=[VC_(u - 8)])
                    if lat:
                        self.dma("sp", gCv_in[u - 8].ap().rearrange("(k p) d -> p k d", p=128), vlat, GDS["cv"], R=[VL], W=[GIN("cv", u - 8)])
                        if 'dnogather' not in KDBG:
                            allgather(gCv_in[u - 8], gCv_out[u - 8], [GIN("cv", u - 8)], [GOUT("cv", u - 8)])
                for u in range(0, 0 if 'dnoq' in KDBG else 4):
                    qk_unit(u, True)

                def head_finish(hd, q0, nq, OW):
                    t1, t2, t3 = tmpi(), tmpi(), tmpi()
                    self.S.op("dve", lambda e: e.reciprocal(out=tmpF[t1][:, 0:nq], in_=pb[4][:, 0:nq]), [PB[4]], [TMPB[t1]])
                    self.tt("dve", tmpF[t1][:, 0:nq], pb[3][:, 0:nq], tmpF[t1][:, 0:nq], ALU.mult, R=[PB[3], TMPB[t1]], W=[TMPB[t1]])
                    self.S.op("dve", lambda e: e.reciprocal(out=tmpF[t2][:, 0:nq], in_=pb[6][:, 0:nq]), [PB[6]], [TMPB[t2]])
                    self.tt("dve", tmpF[t2][:, 0:nq], pb[5][:, 0:nq], tmpF[t2][:, 0:nq], ALU.mult, R=[PB[5], TMPB[t2]], W=[TMPB[t2]])
                    self.stt("dve", tmpF[t3][:, 0:nq], tmpF[t2][:, 0:nq], lamt[:, 5:6], tmpF[t1][:, 0:nq], ALU.mult, ALU.add,
                             R=[TMPB[t1], TMPB[t2], CONST], W=[TMPB[t3]])
                    self.act(sqt[0][:, 0:nq], tmpF[t3][:, 0:nq], AF.Square, R=[TMPB[t3]], W=[SQB[0]])
                    self.mm(pb[7][:, 0:nq], onesb[:], sqt[0][:, 0:nq], R=[SQB[0], CONST], W=[PB[7]])
                    rsqrt(rstd[:, 0:nq], pb[7][:, 0:nq], 128.0 * EPS, nq, [PB[7]], [RSTD])
                    self.stt("dve", hT[:, hd, q0:q0 + nq], tmpF[t3][:, 0:nq], diffg[:, 0:1], rstd[:, 0:nq], ALU.mult, ALU.mult,
                             R=[TMPB[t3], RSTD, CONST], W=OW)

                if (not lat and 'dnoctx' in KDBG) or (lat and 'dnolat' in KDBG):
                    pass
                elif not lat:
                    begin_units()
                    for s in range(4):
                        for hd in range(8):
                            for mp in range(2):
                                r0 = mp * 64
                                Kz = KTc if mp == 0 else KTc1
                                attend(lambda kb: Kz[:, hd, s * 256 + kb * 128:s * 256 + (kb + 1) * 128],
                                       QT[:, hd, s * 256:(s + 1) * 256], 2, 256,
                                       lambda kb: Vc[:, s * 2 + kb, hd * 128:(hd + 1) * 128], 128, 3 + 2 * mp, scale,
                                       [KC_(hd)], [QB(hd, s // 2)], [VC_(hd // 2)], zb=4 + 2 * mp)
                            attach_fin(lambda hd=hd, s=s: head_finish(hd, s * 256, 256, [HB(hd, s // 2)]), imm=True)
                    end_units()
                else:
                    cv = diff_cv_d.rearrange("(k p) (h d) -> h p k d", p=128, h=8)
                    self.memset("pool", A_KT[0][64:128, :], 0.0, W=[KTB[0]])
                    self.memset("pool", A_KT[1][0:64, :], 0.0, W=[KTB[1]])
                    for hd in range(8):
                        ks = hd % 2
                        gk = gCk_out[hd // 2].ap().rearrange("(r c p) t -> c p r t", r=4, c=2)[hd % 2]
                        gv = gCv_out[hd // 2].ap().rearrange("(k p) (h d) -> h p k d", p=128, h=2)[hd % 2]
                        for mp_ in range(2):
                            rs = slice(mp_ * 64, mp_ * 64 + 64)
                            self.dma("sp", A_KT[mp_][rs, 0:4096].rearrange("p (r t) -> p r t", r=4), gk[rs], d_kv[mp_], R=[GOUT("ck", hd // 2)], W=[KTB[mp_]])
                            self.dma("pool", A_KT[mp_][rs, 4096:4608], diff_ck_d[hd][rs], d_kv[mp_], W=[KTB[mp_]])
                        self.dma("sp", A_V[ks][:, 0:32, :], gv, d_kv[2 + ks], R=[GOUT("cv", hd // 2)], W=[VB[ks]])
                        self.dma("pool", A_V[ks][:, 32:36, :], cv[hd], d_kv[2 + ks], W=[VB[ks]])
                        begin_units()
                        for qb in range(2):
                            for mp in range(2):
                                r0 = mp * 64
                                attend(lambda kb: A_KT[mp][:, kb * 128:(kb + 1) * 128],
                                       QT[:, hd, qb * 512:(qb + 1) * 512], 36, 512,
                                       lambda kb: A_V[ks][:, kb, :], 128, 3 + 2 * mp, scale,
                                       [KTB[mp]], [QB(hd, qb)], [VB[ks]], zb=4 + 2 * mp)
                            attach_fin(lambda hd=hd, qb=qb: head_finish(hd, qb * 512, 512, [HB(hd, qb)]), imm=True)
                        end_units()
                if 'dnoo' not in KDBG:
                    oproj(g, l, diff_o_d)

            run_group(0)
            run_group(1)

        def mla(l, j):
            scale = 96 ** -0.5
            cqT = QTf[:, 0:3072].rearrange("p (c t) -> p c t", c=3)
            qsl = [QTf[:, 3072:4096], QTf[:, 4096:5120]]

            def run_group(g):
                lat = (g == 1)
                arena_fence()
                QSB = [AB(), AB()]
                CQB = ABG()
                nk = 4608 if lat else 1024
                nkb = nk // 128
                normmod(g, gs_ap(l, 1, g), sh_ap(l, 1, g))
                if lat:
                    ckvT = arena[:, 0:9216].rearrange("p (c t) -> p c t", c=2)
                    KT1 = arena[:, 9216:13824]
                    VEl = arena[:, 13824:13824 + 36 * 66].rearrange("p (k d) -> p k d", k=36)
                    VOl = arena[:, 17408:17408 + 4608].rearrange("p (k d) -> p k d", k=36)
                    cl = arena[:, 22016:22016 + 2048].rearrange("p (c t) -> p c t", c=2)
                    krl = arena[:, 16384:17408]
                else:
                    ckvT = arena[:, 0:2048].rearrange("p (c t) -> p c t", c=2)
                    KT1 = arena[:, 9216:9216 + 1024]
                    VEl = arena[:, 13824:13824 + 8 * 66].rearrange("p (k d) -> p k d", k=8)
                    VOl = arena[:, 17408:17408 + 1024].rearrange("p (k d) -> p k d", k=8)
                CKB, K1B, VEB, VOB, CLB, KRB = ABG(), AB(), AB(), AB(), AB(), AB()
                self.memset("pool", VEl[:, :, 64:65], 1.0, W=[VEB])
                self.memset("pool", VOl[:, :, 0:64], 0.0, W=[VOB])
                self.memset("pool", VOl[:, :, 0:1], 1.0, W=[VOB])
                self.memset("pool", KT1[64:128, :], 0.0, W=[K1B])
                self.memset("pool", qsl[0][64:128, :], 0.0, W=[QSB[0]])
                self.memset("pool", qsl[1][64:128, :], 0.0, W=[QSB[1]])
                units = []
                for u in range(3):
                    nn_ = 2048 if u < 2 else 1280
                    slot, WBs = load_w(mla_down_d[j, u][:, 0:nn_], nn_)
                    n = 256 if u < 2 else 160
                    units.append((slot[:, 0:8 * n].rearrange("p (k n) -> p k n", k=8), WBs))

                def wcol(c0, M):
                    u = c0 // 256
                    wv, WBs = units[u]
                    o = c0 - u * 256
                    return (lambda k: wv[:, k, o:o + M]), WBs

                for blk in range(2):
                    hr = lambda k: hT[:, k, blk * 512:(blk + 1) * 512]
                    bis = []
                    for c in range(3):
                        wf, WBs = wcol(c * 128, 128)
                        bis.append(proj(wf, 8, hr, 128, 512, lambda k, WBs=WBs: [WBs, HB(k, blk)], cands=(0, 1, 2)))
                    sb_ = sumsq(lambda c: pb[bis[c]][:], 3, 512, lambda c: [PB[bis[c]]])
                    rsqrt(rstd[:], pb[sb_][:], 384.0 * EPS, 512, [PB[sb_]], [RSTD])
                    for c in range(3):
                        self.stt("dve", cqT[:, c, blk * 512:(blk + 1) * 512], pb[bis[c]][:], mlag[:, j, c:c + 1], rstd[:], ALU.mult, ALU.mult,
                                 R=[PB[bis[c]], RSTD, CONST], W=[CQB(c, blk)])
                    bis = []
                    for c in range(2):
                        wf, WBs = wcol(384 + c * 128, 128)
                        bis.append(proj(wf, 8, hr, 128, 512, lambda k, WBs=WBs: [WBs, HB(k, blk)], cands=(0, 1, 2)))
                    sb_ = sumsq(lambda c: pb[bis[c]][:], 2, 512, lambda c: [PB[bis[c]]])
                    rsqrt(rstd[:], pb[sb_][:], 256.0 * EPS, 512, [PB[sb_]], [RSTD])
                    for c in range(2):
                        ti = tmpi()
                        self.stt("dve", tmpF[ti][:], pb[bis[c]][:], mlag[:, j, 3 + c:4 + c], rstd[:], ALU.mult, ALU.mult,
                                 R=[PB[bis[c]], RSTD, CONST], W=[TMPB[ti]])
                        if lat:
                            self.cp("act", cl[:, c, blk * 512:(blk + 1) * 512], tmpF[ti][:], R=[TMPB[ti]], W=[CLB])
                        else:
                            self.cp("act", ckvT[:, c, blk * 512:(blk + 1) * 512], tmpF[ti][:], R=[TMPB[ti]], W=[CKB(c)])
                            self.dma("sp", o_ckv_d[j, :, c, blk * 512:(blk + 1) * 512], tmpF[ti][:], d_tmp[ti], R=[TMPB[ti]])
                    wf, WBs = wcol(576, 96)
                    bi = proj(wf, 8, hr, 96, 512, lambda k, WBs=WBs: [WBs, HB(k, blk)], cands=(0, 1, 2))
                    ti = tmpi()
                    self.cp("act", tmpF[ti][64:96, :], pb[bi][64:96, :], R=[PB[bi]], W=[TMPB[ti]])
                    if lat:
                        rope_combine(tmpF[ti], TMPB[ti], 64, 96, permA, ropeA, blk * 512, 512, krl[64:96, blk * 512:(blk + 1) * 512], [KRB])
                    else:
                        self.dma("sp", o_kr_d[j, :, blk * 512:(blk + 1) * 512], tmpF[ti][64:96, :], d_tmp[ti], R=[TMPB[ti]])
                        self.cp("pool", KT1[64:96, blk * 512:(blk + 1) * 512], tmpF[ti][64:96, :], R=[TMPB[ti]], W=[K1B])
                if lat:
                    ga = gA_in.ap()
                    self.dma("sp", ga[0:256, :].rearrange("(c p) t -> p c t", p=128), cl, GDS["a"], R=[CLB], W=[GIN("a")])
                    self.dma("sp", ga[256:288, :], krl[64:96, :], GDS["a"], R=[KRB], W=[GIN("a")])
                    allgather(gA_in, gA_out, [GIN("a")], [GOUT("a")])
                    go = gA_out.ap().rearrange("(r f) t -> f r t", r=4)
                    for c in range(2):
                        self.dma("sp", ckvT[:, c, 0:4096].rearrange("p (r t) -> p r t", r=4), go[c * 128:(c + 1) * 128], d_kv[c], R=[GOUT("a")], W=[CKB(c)])
                        self.dma("pool", ckvT[:, c, 4096:4608], mla_cckv_d[j, :, c, :], d_kv[c], W=[CKB(c)])
                    self.dma("sp", KT1[64:96, 0:4096].rearrange("p (r t) -> p r t", r=4), go[256:288], d_kv[2], R=[GOUT("a")], W=[K1B])
                    self.dma("pool", KT1[64:96, 4096:4608], mla_ckr_d[j], d_kv[2], W=[K1B])
                for hg in range(4):
                    sq_, WBq = load_w(mla_uq_d[j, hg], 1152)
                    wq = sq_[:, 0:1152].rearrange("p (h k n) -> p h k n", h=4, k=3)
                    skv, WBkv = load_w(mla_ukv_d[j, hg], 1024)
                    wkv = skv[:, 0:1024].rearrange("p (h k n) -> p h k n", h=4, k=2)
                    for hh in range(4):
                        hd = hg * 4 + hh
                        par, m = hd % 2, hd // 2
                        for kb4 in range(nk // 512):
                            bi = proj(lambda k: wkv[:, hh, k, 0:64], 2, lambda k: ckvT[:, k, kb4 * 512:(kb4 + 1) * 512], 64, 512,
                                      lambda k: [WBkv, CKB(k)])
                            self.cp("dve", KT1[0:64, kb4 * 512:(kb4 + 1) * 512], pb[bi][0:64, :], R=[PB[bi]], W=[K1B])
                        Vt, VtB = (VEl, VEB) if par == 0 else (VOl, VOB)
                        c0 = 0 if par == 0 else 64
                        for kb8 in range((nkb + 7) // 8):
                            nb = min(8, nkb - kb8 * 8)
                            bi = bank()
                            for q in range(nb):
                                kb = kb8 * 8 + q
                                for k in range(2):
                                    self.mm(pb[bi][:, q * 64:(q + 1) * 64], ckvT[:, k, kb * 128:(kb + 1) * 128], wkv[:, hh, k, 64:128],
                                            start=(k == 0), stop=(k == 1), R=[CKB(k), WBkv], W=[PB[bi]])
                            self.cp("act", Vt[:, kb8 * 8:kb8 * 8 + nb, c0:c0 + 64], pb[bi][:, 0:nb * 64].rearrange("p (q d) -> p q d", d=64),
                                    R=[PB[bi]], W=[VtB])
                        qs, QSb = qsl[hd % 2], QSB[hd % 2]
                        for blk in range(2):
                            bi = proj(lambda k: wq[:, hh, k, :], 3, lambda k: cqT[:, k, blk * 512:(blk + 1) * 512], 96, 512,
                                      lambda k: [WBq, CQB(k, blk)])
                            self.cp("act", qs[0:64, blk * 512:(blk + 1) * 512], pb[bi][0:64, :], R=[PB[bi]], W=[QSb])
                            if lat:
                                ti = tmpi()
                                self.cp("dve", tmpF[ti][64:96, :], pb[bi][64:96, :], R=[PB[bi]], W=[TMPB[ti]])
                                rope_combine(tmpF[ti], TMPB[ti], 64, 96, permA, ropeA, blk * 512, 512, qs[64:96, blk * 512:(blk + 1) * 512], [QSb])
                            else:
                                self.cp("dve", qs[64:96, blk * 512:(blk + 1) * 512], pb[bi][64:96, :], R=[PB[bi]], W=[QSb])
                        vM = 65 if par == 0 else 128
                        r0 = par * 64
                        begin_units()
                        if lat:
                            for qb in range(2):
                                po = 3 + (qb % 2)
                                attend(lambda kb: KT1[:, kb * 128:(kb + 1) * 128], qs[:, qb * 512:(qb + 1) * 512], 36, 512,
                                       lambda kb: Vt[:, kb, 0:vM], vM, po, scale, [K1B], [QSb], [VtB])
                                finish_head(po, par, 512, hT[r0:r0 + 64, m, qb * 512:(qb + 1) * 512], [HB(m, qb)])
                        else:
                            for s in range(4):
                                po = 3 + (s % 2)
                                attend(lambda kb: KT1[:, s * 256 + kb * 128:s * 256 + (kb + 1) * 128], qs[:, s * 256:(s + 1) * 256], 2, 256,
                                       lambda kb: Vt[:, s * 2 + kb, 0:vM], vM, po, scale, [K1B], [QSb], [VtB])
                                finish_head(po, par, 256, hT[r0:r0 + 64, m, s * 256:(s + 1) * 256], [HB(m, s // 2)])
                        end_units()
                oproj(g, l, mla_o_d[j])

            run_group(0)
            run_group(1)

        for l in range(NL):
            cur["l"] = l
            if l + 1 < NL:
                self._adag = ada_gen(l + 1)
            ffn(0, l, 0)
            ffn(1, l, 0)
            while self._adag is not None:
                ada_step()
            kind = l % 3
            if kind == 0:
                mla(l, l // 3)
            elif kind == 1:
                gqa(l)
            else:
                diff(l)
            ffn(0, l, 1)
            ffn(1, l, 1)

        for g in range(2):
            for blk in range(2):
                t0 = g * 1024 + blk * 512
                bi = sumsq(lambda c: yT[:, c, t0:t0 + 512], 8, 512, lambda c: [YB(c, g, blk)])
                rsqrt(rstd[:], pb[bi][:], 1024.0 * EPS, 512, [PB[bi]], [RSTD])
                for c in range(8):
                    self.stt("dve", yT[:, c, t0:t0 + 512], yT[:, c, t0:t0 + 512], normf[:, c:c + 1], rstd[:], ALU.mult, ALU.mult,
                             R=[YB(c, g, blk), RSTD, CONST], W=[YB(c, g, blk)])
                self.dma("sp", yout_d[:, :, t0:t0 + 512], yT[:, :, t0:t0 + 512], d_out, R=[YB(c, g, blk) for c in range(8)])
        fin = Buf()
        fin.r = {d: d.val for d in [d_out] + d_st + d_tmp if d.val > 0}
        self.S.op("sp", lambda e: e.nop(), writes=[fin])

        for d in S.dsems:
            d.h = self.es.enter_context(nc.semaphore("d_" + d.name))
        S.finalize()
        block = self.es.enter_context(nc.Block())
        for name in ENGS:
            def mk(name=name):
                def f(e):
                    S.emit(name, e, sems)
                return f
            getattr(block, ATTR[name])(mk())
        self.es.close()
        return nc


def _fm(x2d):
    T, Dd = x2d.shape
    return np.ascontiguousarray(x2d.T.reshape(Dd // 128, 128, T).transpose(1, 0, 2))


def _wt(W):
    K, N = W.shape
    return np.ascontiguousarray(W.reshape(K // 128, 128, N).transpose(1, 0, 2).reshape(128, (K // 128) * N))


def _pad(a, n):
    out = np.zeros((a.shape[0], n), np.float32)
    out[:, :a.shape[1]] = a
    return out


def _rope_tables(pos, rot_dim):
    GRID_W = 64
    r = (pos // GRID_W).astype(np.float32)
    cidx = (pos % GRID_W).astype(np.float32)
    n_f = rot_dim // 4
    freqs = (np.float32(10000.0) ** (-np.arange(n_f, dtype=np.float32) / np.float32(n_f))).astype(np.float32)
    ang = np.concatenate([r[:, None] * freqs, cidx[:, None] * freqs], axis=-1).astype(np.float32)
    cos, sin = np.cos(ang).astype(np.float32), np.sin(ang).astype(np.float32)
    cos2 = np.concatenate([cos, cos], axis=1).T
    sinp = np.concatenate([-sin, sin], axis=1).T
    return cos2, sinp


_CACHE = {}


def _get_nc(nl, noffn=False):
    if (nl, noffn) not in _CACHE:
        _CACHE[(nl, noffn)] = KB(nl, noffn).build()
    return _CACHE[(nl, noffn)]


def kernel(x_prompt, x_sample, cache_mla_ckv, cache_mla_kr, cache_gqa_k, cache_gqa_v,
           cache_diff_k, cache_diff_v, c, c_ctx,
           ada_w, ada_b, norm_ffn1, norm_mix, norm_ffn2,
           ffn1_w_in, ffn1_w_out, ffn2_w_in, ffn2_w_out,
           mla_w_down, mla_g_q, mla_g_kv, mla_w_uq, mla_w_uk, mla_w_uv, mla_w_o,
           gqa_w_qkv, gqa_g_q, gqa_g_k, gqa_w_o,
           diff_w_qkv, diff_lq1, diff_lk1, diff_lq2, diff_lk2, diff_g_sub, diff_w_o,
           norm_final, _nl=4, _noffn=False, _trace=False):
    f = lambda a: np.asarray(a, dtype=np.float32)
    x_prompt, x_sample = f(x_prompt), f(x_sample)
    NL = _nl
    if 'probe_noffn' in KDBG:
        _noffn = True
    nc = _get_nc(NL, _noffn)
    n_mla = (NL + 2) // 3
    sh = {}
    ada_w = f(ada_w)
    sh["ada_t"] = np.stack([np.stack([_wt(ada_w[l][:, s * 256:(s + 1) * 256]) for s in range(36)]) for l in range(NL)])
    sh["ada_bT"] = np.ascontiguousarray(f(ada_b)[:NL].reshape(NL, 72, 128).transpose(2, 0, 1))
    ng = np.stack([f(norm_ffn1)[:NL], f(norm_mix)[:NL], f(norm_ffn2)[:NL]], axis=1)
    sh["normg"] = np.ascontiguousarray(ng.reshape(NL, 3, 8, 128).transpose(3, 0, 1, 2))
    sh["normf"] = np.ascontiguousarray(f(norm_final).reshape(8, 128).T)
    if not _noffn:
        win = np.stack([f(ffn1_w_in)[:NL], f(ffn2_w_in)[:NL]], axis=1)
        sh["ffn_in_t"] = np.ascontiguousarray(
            win.reshape(NL, 2, 8, 128, 2, 22, 128).transpose(0, 1, 5, 3, 2, 4, 6).reshape(NL, 2, 22, 128, 2048))
        wout = np.stack([f(ffn1_w_out)[:NL], f(ffn2_w_out)[:NL]], axis=1)
        sh["ffn_out_t"] = np.ascontiguousarray(
            wout.reshape(NL, 2, 22, 128, 8, 128).transpose(0, 1, 4, 3, 2, 5).reshape(NL, 2, 8, 128, 2816))
    consts = np.zeros((128, 3, 128), np.float32)
    consts[:, 0, :] = np.eye(128, dtype=np.float32)
    for jj in range(128):
        base = (jj // 64) * 64
        o = jj - base
        consts[base + (o + 32) % 64, 2, jj] = 1.0
    for jj in range(64, 96):
        o = jj - 64
        consts[64 + (o + 16) % 32, 1, jj] = 1.0
    sh["consts"] = consts
    wd = f(mla_w_down)[:n_mla]
    sh["mla_down_t"] = np.stack([np.stack([_pad(_wt(wd[j][:, 0:256]), 2048), _pad(_wt(wd[j][:, 256:512]), 2048),
                                           _pad(_wt(wd[j][:, 512:672]), 2048)]) for j in range(n_mla)])
    wuq = f(mla_w_uq)[:n_mla]
    sh["mla_uq_t"] = np.ascontiguousarray(
        wuq.reshape(n_mla, 3, 128, 4, 4, 96).transpose(0, 3, 2, 4, 1, 5).reshape(n_mla, 4, 128, 1152))
    wuk = f(mla_w_uk)[:n_mla].reshape(n_mla, 2, 128, 4, 4, 64)
    wuv = f(mla_w_uv)[:n_mla].reshape(n_mla, 2, 128, 4, 4, 64)
    ukv = np.concatenate([wuk, wuv], axis=-1)
    sh["mla_ukv_t"] = np.ascontiguousarray(ukv.transpose(0, 3, 2, 4, 1, 5).reshape(n_mla, 4, 128, 1024))
    wo = f(mla_w_o)[:n_mla]
    sh["mla_o_t"] = np.stack([np.stack([_wt(wo[j][:, m * 128:(m + 1) * 128]) for m in range(8)]) for j in range(n_mla)])
    gq = f(mla_g_q)[:n_mla].reshape(n_mla, 3, 128)
    gkv = f(mla_g_kv)[:n_mla].reshape(n_mla, 2, 128)
    sh["mla_g"] = np.ascontiguousarray(np.concatenate([gq, gkv], axis=1).transpose(2, 0, 1))
    if NL >= 2:
        wq = f(gqa_w_qkv)[0]
        units = [_wt(wq[:, u * 256:(u + 1) * 256]) for u in range(4)]
        for u in range(2):
            cols = []
            for h in range(2):
                kvh = u * 2 + h
                blk = wq[:, 1024 + kvh * 64:1024 + (kvh + 1) * 64]
                cols += [blk, blk]
            units.append(_wt(np.concatenate(cols, axis=1)))
        units.append(_wt(wq[:, 1280:1536]))
        sh["gqa_t"] = np.stack(units)
        go = f(gqa_w_o)[0]
        sh["gqa_o_t"] = np.stack([_wt(go[:, m * 128:(m + 1) * 128]) for m in range(8)])
        sh["gqa_g"] = np.ascontiguousarray(np.stack([np.tile(f(gqa_g_q)[0], 2), np.tile(f(gqa_g_k)[0], 2)], axis=1))
    if NL >= 3:
        wq = f(diff_w_qkv)[0]
        sh["diff_t"] = np.stack([_wt(wq[:, u * 256:(u + 1) * 256]) for u in range(12)])
        do = f(diff_w_o)[0]
        sh["diff_o_t"] = np.stack([_wt(do[:, m * 128:(m + 1) * 128]) for m in range(8)])
        sh["diff_l"] = np.ascontiguousarray(np.broadcast_to(np.stack([f(diff_lq1)[0], f(diff_lk1)[0], f(diff_lq2)[0], f(diff_lk2)[0]])[None], (128, 4, 64)))
        sh["diff_g"] = np.ascontiguousarray(f(diff_g_sub)[0].reshape(128, 1))

    in_maps = []
    for core in range(8):
        b, r = core // 4, core % 4
        m = dict(sh)
        xc = x_prompt[4 * core:4 * core + 4].reshape(1024, 1024)
        xl = x_sample[b, r * 1024:(r + 1) * 1024]
        m["xT"] = np.concatenate([_fm(xc), _fm(xl)], axis=2)
        m["cT"] = np.ascontiguousarray(np.stack([f(c_ctx), f(c)[b]], axis=1).reshape(8, 128, 2).transpose(1, 0, 2))
        pos = np.arange(r * 1024, (r + 1) * 1024)
        ca, sa = _rope_tables(pos, 32)
        ra = np.zeros((128, 2, 1024), np.float32)
        ra[64:96, 0], ra[64:96, 1] = ca, sa
        m["ropeA"] = ra
        cb, sb_ = _rope_tables(pos, 64)
        m["ropeB"] = np.ascontiguousarray(np.stack([np.tile(cb, (2, 1)), np.tile(sb_, (2, 1))], axis=1))
        ck = f(cache_mla_ckv)[b][:n_mla]
        m["mla_cckvT"] = np.ascontiguousarray(ck.transpose(0, 2, 1).reshape(n_mla, 2, 128, 512).transpose(0, 2, 1, 3))
        m["mla_ckrT"] = np.ascontiguousarray(f(cache_mla_kr)[b][:n_mla].transpose(0, 2, 1))
        if NL >= 2:
            gk = f(cache_gqa_k)[b, 0]
            kt = gk.transpose(1, 2, 0)
            m["gqa_ckT"] = np.ascontiguousarray(np.concatenate([kt, kt], axis=1))
            m["gqa_cv"] = np.ascontiguousarray(f(cache_gqa_v)[b, 0].reshape(512, 256))
        if NL >= 3:
            dk = f(cache_diff_k)[b, 0].reshape(512, 8, 128)
            m["diff_ckT"] = np.ascontiguousarray(dk.transpose(1, 2, 0))
            m["diff_cv"] = np.ascontiguousarray(f(cache_diff_v)[b, 0].reshape(512, 1024))
        in_maps.append(m)

    if _trace:
        res = run_bass_kernel_spmd(nc, in_maps, core_ids=list(range(8)), trace=True)
        _CACHE['last_res'] = res
    else:
        res = run_bass_kernel_spmd(nc, in_maps, core_ids=list(range(8)))
    R = res.results
    y_prompt = np.zeros((32, 256, 1024), np.float32)
    y_sample = np.zeros((2, 4096, 1024), np.float32)
    n_ckv = np.zeros((32, 2, 256, 256), np.float32)
    n_kr = np.zeros((32, 2, 256, 32), np.float32)
    n_gk = np.zeros((32, 1, 256, 4, 64), np.float32)
    n_gv = np.zeros((32, 1, 256, 4, 64), np.float32)
    n_dk = np.zeros((32, 1, 256, 8, 2, 64), np.float32)
    n_dv = np.zeros((32, 1, 256, 8, 128), np.float32)
    for core in range(8):
        b, r = core // 4, core % 4
        o = R[core]
        yt = np.asarray(o["yT_out"]).transpose(2, 1, 0).reshape(2048, 1024)
        y_prompt[4 * core:4 * core + 4] = yt[:1024].reshape(4, 256, 1024)
        y_sample[b, r * 1024:(r + 1) * 1024] = yt[1024:]
        ck = np.asarray(o["o_ckv"]).transpose(0, 3, 2, 1).reshape(2, 4, 256, 256)
        n_ckv[4 * core:4 * core + 4] = ck.transpose(1, 0, 2, 3)
        kr = np.asarray(o["o_kr"]).transpose(0, 2, 1).reshape(2, 4, 256, 32)
        n_kr[4 * core:4 * core + 4] = kr.transpose(1, 0, 2, 3)
        gk = np.asarray(o["o_gk"]).transpose(2, 0, 1).reshape(4, 256, 4, 64)
        n_gk[4 * core:4 * core + 4, 0] = gk
        n_gv[4 * core:4 * core + 4, 0] = np.asarray(o["o_gv"]).reshape(4, 256, 4, 64)
        dk = np.asarray(o["o_dk"]).transpose(2, 1, 0).reshape(4, 256, 8, 2, 64)
        n_dk[4 * core:4 * core + 4, 0] = dk
        n_dv[4 * core:4 * core + 4, 0] = np.asarray(o["o_dv"]).reshape(4, 256, 8, 128)
    return (y_prompt, y_sample, n_ckv, n_kr, n_gk, n_gv, n_dk, n_dv)
```

```python
import math
import os
from contextlib import ExitStack
KDBG = os.environ.get('KDBG', '').split(',')
import numpy as np
import concourse.bass as bass
import concourse.mybir as mybir
from concourse.bass_utils import run_bass_kernel_spmd

F32 = mybir.dt.float32
BF16 = mybir.dt.bfloat16
AF = mybir.ActivationFunctionType
ALU = mybir.AluOpType

ENGS = ("pe", "act", "dve", "pool", "sp")
ATTR = {"pe": "tensor", "act": "scalar", "dve": "vector", "pool": "gpsimd", "sp": "sync"}
SELF_RAW_DIST = 3
EPS = 1e-6
NSLOT = 3
SLOTW = 2816


class Buf:
    __slots__ = ("w", "r", "excl")

    def __init__(self, excl=False):
        self.w = None
        self.r = {}
        self.excl = excl


class BG:
    def __init__(self):
        self.d = {}

    def __call__(self, *key):
        b = self.d.get(key)
        if b is None:
            b = self.d[key] = Buf()
        return b


class DSem:
    def __init__(self, name, inc=16):
        self.name = name
        self.val = 0
        self.inc = inc
        self.h = None


class Op:
    __slots__ = ("fn", "deps", "marked", "count", "dsem")

    def __init__(self, fn, deps, dsem):
        self.fn = fn
        self.deps = deps
        self.marked = False
        self.count = 0
        self.dsem = dsem


class Sched:
    def __init__(self):
        self.ops = {e: [] for e in ENGS}
        self.clock = {e: {} for e in ENGS}
        self.hist = {e: [] for e in ENGS}
        self.dsems = []

    def dsem(self, name, inc=16):
        d = DSem(name, inc)
        self.dsems.append(d)
        return d

    def op(self, eng, fn, reads=(), writes=(), dsem=None):
        ops = self.ops[eng]
        clock = self.clock[eng]
        seq = len(ops)
        deps = {}

        def need(key, val, raw):
            if key == eng:
                if (not raw) or (seq - val > SELF_RAW_DIST and eng != "pool"):
                    return
                if clock.get(("self", eng), -1) >= val:
                    return
                clock[("self", eng)] = val
                deps[key] = max(deps.get(key, -1), val)
                return
            if clock.get(key, -1) >= val:
                return
            clock[key] = val
            deps[key] = val
            if isinstance(key, str):
                for k2, v2 in self.hist[key][val].items():
                    if isinstance(k2, tuple) or k2 == eng:
                        continue
                    if clock.get(k2, -1) < v2:
                        clock[k2] = v2

        for b in reads:
            if b.w is not None:
                need(b.w[0], b.w[1], True)
            if b.excl:
                for k, v in b.r.items():
                    need(k, v, False)
        for b in writes:
            if b.w is not None:
                need(b.w[0], b.w[1], False)
            for k, v in b.r.items():
                need(k, v, False)
        o = Op(fn, list(deps.items()), dsem)
        ops.append(o)
        self.hist[eng].append(dict(clock))
        if dsem is not None:
            dsem.val += dsem.inc
            key, val = dsem, dsem.val
        else:
            key, val = eng, seq
        for b in reads:
            b.r[key] = val
        for b in writes:
            b.w = (key, val)
            b.r = {}
        return o

    def finalize(self):
        for e in ENGS:
            for o in self.ops[e]:
                for k, v in o.deps:
                    if isinstance(k, str):
                        self.ops[k][v].marked = True
        for e in ENGS:
            c = 0
            for o in self.ops[e]:
                if o.marked:
                    c += 1
                o.count = c

    def emit(self, eng, e, sems):
        for o in self.ops[eng]:
            for k, v in o.deps:
                if isinstance(k, str):
                    e.wait_ge(sems[k], self.ops[k][v].count)
                else:
                    e.wait_ge(k.h, v)
            ins = o.fn(e)
            if o.dsem is not None:
                ins.then_inc(o.dsem.h, o.dsem.inc)
            elif o.marked:
                ins.then_inc(sems[eng], 1)


class KB:
    def __init__(self, nlayers=4, noffn=False):
        self.NL = nlayers
        self.noffn = noffn
        self.nc = bass.Bass("TRN2", target_bir_lowering=False)
        self.S = Sched()
        self.es = ExitStack()
        self.in_names = []
        self.out_names = []
        self.out_dsems = []

    def inp(self, name, shape, dt=F32):
        self.in_names.append(name)
        return self.nc.dram_tensor(name, list(shape), dt, kind="ExternalInput").ap()

    def outp(self, name, shape, dt=F32):
        self.out_names.append(name)
        return self.nc.dram_tensor(name, list(shape), dt, kind="ExternalOutput").ap()

    def dram(self, name, shape, dt):
        return self.nc.dram_tensor(name, list(shape), dt)

    def sb(self, name, shape, dt):
        return self.es.enter_context(self.nc.sbuf_tensor(name, list(shape), dt))

    def mm(self, out, lhsT, rhs, start=True, stop=True, R=(), W=()):
        self.S.op("pe", lambda e: e.matmul(out, lhsT=lhsT, rhs=rhs, start=start, stop=stop), R, W)

    def act(self, out, in_, func, R=(), W=(), bias=None, scale=None):
        kw = {}
        if bias is not None:
            kw["bias"] = bias
        if scale is not None:
            kw["scale"] = scale
        self.S.op("act", lambda e: e.activation(out=out, in_=in_, func=func, **kw), R, W)

    def tt(self, eng, out, in0, in1, op, R=(), W=()):
        self.S.op(eng, lambda e: e.tensor_tensor(out=out, in0=in0, in1=in1, op=op), R, W)

    def ts(self, eng, out, in0, s1, s2, op0, op1, R=(), W=()):
        self.S.op(eng, lambda e: e.tensor_scalar(out=out, in0=in0, scalar1=s1, scalar2=s2, op0=op0, op1=op1), R, W)

    def ts1(self, eng, out, in0, s1, op0, R=(), W=()):
        self.S.op(eng, lambda e: e.tensor_single_scalar(out=out, in_=in0, scalar=s1, op=op0), R, W)

    def stt(self, eng, out, in0, scalar, in1, op0, op1, R=(), W=()):
        self.S.op(eng, lambda e: e.scalar_tensor_tensor(out=out, in0=in0, scalar=scalar, in1=in1, op0=op0, op1=op1), R, W)

    def cp(self, eng, out, in_, R=(), W=()):
        if eng == "act":
            self.S.op("act", lambda e: e.activation(out=out, in_=in_, func=AF.Copy), R, W)
        else:
            self.S.op(eng, lambda e: e.tensor_copy(out=out, in_=in_), R, W)

    def memset(self, eng, ap, val, W=()):
        self.S.op(eng, lambda e: e.memset(ap, val), (), W)

    def dma(self, q, out, in_, ds, R=(), W=()):
        self.S.op(q, lambda e: e.dma_start(out=out, in_=in_), R, W, dsem=ds)

    def barrier_bufs(self, bufs):
        nb = Buf()
        for b in bufs:
            if b.w is not None:
                nb.r[b.w[0]] = max(nb.r.get(b.w[0], -1), b.w[1])
            for k, v in b.r.items():
                nb.r[k] = max(nb.r.get(k, -1), v)
        return nb

    def build(self):
        nc, S = self.nc, self.S
        NL = self.NL
        sb = self.sb
        xT_d = self.inp("xT", [128, 8, 2048])
        cT_d = self.inp("cT", [128, 8, 2])
        ada_d = self.inp("ada_t", [NL, 36, 128, 2048])
        adab_d = self.inp("ada_bT", [128, NL, 72])
        normg_d = self.inp("normg", [128, NL, 3, 8])
        normf_d = self.inp("normf", [128, 8])
        if not self.noffn:
            fin_d = self.inp("ffn_in_t", [NL, 2, 22, 128, 2048])
            fout_d = self.inp("ffn_out_t", [NL, 2, 8, 128, 2816])
        consts_d = self.inp("consts", [128, 3, 128])
        ropeA_d = self.inp("ropeA", [128, 2, 1024])
        ropeB_d = self.inp("ropeB", [128, 2, 1024])
        n_mla = (NL + 2) // 3
        has_gqa = NL >= 2
        has_diff = NL >= 3
        mla_down_d = self.inp("mla_down_t", [n_mla, 3, 128, 2048])
        mla_uq_d = self.inp("mla_uq_t", [n_mla, 4, 128, 1152])
        mla_ukv_d = self.inp("mla_ukv_t", [n_mla, 4, 128, 1024])
        mla_o_d = self.inp("mla_o_t", [n_mla, 8, 128, 1024])
        mla_g_d = self.inp("mla_g", [128, n_mla, 5])
        mla_cckv_d = self.inp("mla_cckvT", [n_mla, 128, 2, 512])
        mla_ckr_d = self.inp("mla_ckrT", [n_mla, 32, 512])
        if has_gqa:
            gqa_w_d = self.inp("gqa_t", [7, 128, 2048])
            gqa_o_d = self.inp("gqa_o_t", [8, 128, 1024])
            gqa_g_d = self.inp("gqa_g", [128, 2])
            gqa_ck_d = self.inp("gqa_ckT", [4, 128, 512])
            gqa_cv_d = self.inp("gqa_cv", [512, 256])
        if has_diff:
            diff_w_d = self.inp("diff_t", [12, 128, 2048])
            diff_o_d = self.inp("diff_o_t", [8, 128, 1024])
            diff_l_d = self.inp("diff_l", [128, 4, 64])
            diff_g_d = self.inp("diff_g", [128, 1])
            diff_ck_d = self.inp("diff_ckT", [8, 128, 512])
            diff_cv_d = self.inp("diff_cv", [512, 1024])

        yout_d = self.outp("yT_out", [128, 8, 2048])
        o_ckv_d = self.outp("o_ckv", [2, 128, 2, 1024])
        o_kr_d = self.outp("o_kr", [2, 32, 1024])
        o_gk_d = self.outp("o_gk", [4, 64, 1024])
        o_gv_d = self.outp("o_gv", [1024, 256])
        o_dk_d = self.outp("o_dk", [128, 8, 1024])
        o_dv_d = self.outp("o_dv", [1024, 1024])

        RG = [[0, 1, 2, 3], [4, 5, 6, 7]]
        gA_in = self.dram("gA_in", [288, 1024], BF16)
        gA_out = self.dram("gA_out", [4 * 288, 1024], BF16)
        gBk_in = self.dram("gBk_in", [512, 1024], BF16)
        gBk_out = self.dram("gBk_out", [4 * 512, 1024], BF16)
        gBv_in = self.dram("gBv_in", [1024, 256], BF16)
        gBv_out = self.dram("gBv_out", [4096, 256], BF16)
        gCk_in = [self.dram(f"gCk_in{i}", [256, 1024], BF16) for i in range(4)]
        gCk_out = [self.dram(f"gCk_out{i}", [1024, 1024], BF16) for i in range(4)]
        gCv_in = [self.dram(f"gCv_in{i}", [1024, 256], BF16) for i in range(4)]
        gCv_out = [self.dram(f"gCv_out{i}", [4096, 256], BF16) for i in range(4)]

        yT = sb("yT", [128, 8, 2048], F32)
        hT = sb("hT", [128, 8, 1024], BF16)
        QT = sb("QT", [128, 8, 1024], BF16)
        arena = sb("arena", [128, 24576], BF16)
        QTf = QT[:].rearrange("p c t -> p (c t)")
        slots = [sb(f"wslot{i}", [128, SLOTW], BF16) for i in range(NSLOT)]
        tmpF = [sb(f"tmpF{i}", [128, 512], F32) for i in range(4)]
        sqt = [sb(f"sqt{i}", [128, 512], BF16) for i in range(2)]
        rstd = sb("rstd", [128, 512], F32)
        rstd2 = sb("rstd2", [128, 512], F32)
        ropeA = sb("ropeA_s", [128, 2, 1024], F32)
        ropeB = sb("ropeB_s", [128, 2, 1024], F32)
        consts = sb("consts_s", [128, 3, 128], F32)
        onesb = sb("onesb", [128, 128], BF16)
        bd64 = sb("bd64", [128, 128], BF16)
        onesf = sb("onesf", [128, 128], F32)
        cT = sb("cT_s", [128, 8, 2], F32)
        actT = sb("actT", [128, 8, 2], BF16)
        adab = sb("adab", [128, NL, 72], F32)
        modT = sb("modT", [128, NL, 72, 2], F32)
        normg = sb("normg_s", [128, NL, 3, 8], F32)
        normf = sb("normf_s", [128, 8], F32)
        gsT = sb("gsT", [128, NL, 3, 2, 8], F32)
        hgT = sb("hgT", [128, NL, 2, 2, 8], F32)
        mlag = sb("mlag", [128, n_mla, 5], F32)
        gqag = sb("gqag", [128, 2], F32)
        diffg = sb("diffg", [128, 1], F32)
        diffl = sb("diffl", [128, 4, 64], F32)
        lamt = sb("lamt", [128, 8], F32)
        ptile = [sb(f"ptile{i}", [128, 512], BF16) for i in range(4)]
        stage = [sb(f"stage{i}", [128, 512], F32) for i in range(2)]
        pb = [self.es.enter_context(nc.psum_tensor(f"pb{i}", [128, 512], F32)) for i in range(8)]

        sems = {e: self.es.enter_context(nc.semaphore("s_" + e)) for e in ENGS}

        YB, HB = BG(), BG()
        PB = [Buf(excl=True) for _ in range(8)]
        WB = [Buf() for _ in range(NSLOT)]
        TMPB = [Buf() for _ in range(4)]
        SQB = [Buf() for _ in range(2)]
        PTB = [Buf() for _ in range(4)]
        STB = [Buf() for _ in range(2)]
        RSTD, CONST, MOD, ARENA = Buf(), Buf(), Buf(), Buf()
        RSTD2 = Buf()
        wds = [S.dsem(f"w{i}") for i in range(NSLOT)]
        d_in = S.dsem("in")
        d_misc = S.dsem("misc")
        d_out = S.dsem("out")
        d_st = [S.dsem(f"st{i}") for i in range(2)]
        d_kv = [S.dsem(f"kv{i}") for i in range(4)]
        d_g = S.dsem("gin")
        d_cc = S.dsem("cc", inc=1)
        self._abufs = []
        self._fence = {}

        def AB():
            b = Buf()
            b.r = dict(self._fence)
            self._abufs.append(b)
            return b

        class ABG(BG):
            def __call__(s_, *key):
                b = s_.d.get(key)
                if b is None:
                    b = s_.d[key] = AB()
                return b

        def arena_fence():
            f = dict(self._fence)
            for b in self._abufs:
                if b.w is not None:
                    f[b.w[0]] = max(f.get(b.w[0], -1), b.w[1])
                for k, v in b.r.items():
                    f[k] = max(f.get(k, -1), v)
            self._fence = f
            self._abufs = []

        d_tmp = [S.dsem(f"tmpo{i}") for i in range(4)]
        GDS = {n: S.dsem("g_" + n) for n in ("a", "bk", "bv", "ck", "cv")}
        self._wi = 0
        self._bank = 0
        self._srot = 0
        self._prot = 0
        self._tmp = 0
        self._stg = 0

        def load_w(dram_ap, n):
            s = self._wi % NSLOT
            self._wi += 1
            self.dma("pool", slots[s][:, 0:n], dram_ap, wds[s], W=[WB[s]])
            return slots[s], WB[s]

        def bank(cands=(5, 6, 7)):
            i = cands[self._bank % len(cands)]
            self._bank += 1
            return i

        def tmpi():
            i = self._tmp % 4
            self._tmp += 1
            return i

        def stg():
            i = self._stg % 2
            self._stg += 1
            return i

        self.dma("sp", yT[:], xT_d, d_in, W=[YB(c, g, b) for c in range(8) for g in range(2) for b in range(2)])
        for (t, d) in ((cT, cT_d), (adab, adab_d), (normg, normg_d), (normf, normf_d), (ropeA, ropeA_d),
                       (ropeB, ropeB_d), (consts, consts_d), (mlag, mla_g_d)):
            self.dma("sp", t[:], d, d_misc, W=[CONST])
        if has_gqa:
            self.dma("sp", gqag[:], gqa_g_d, d_misc, W=[CONST])
        if has_diff:
            self.dma("sp", diffg[:], diff_g_d, d_misc, W=[CONST])
            self.dma("sp", diffl[:], diff_l_d, d_misc, W=[CONST])
        self.memset("dve", onesb[:], 1.0, W=[CONST])
        self.memset("dve", onesf[:], 1.0, W=[CONST])
        self.memset("dve", bd64[:], 0.0, W=[CONST])
        self.memset("dve", bd64[0:64, 0:64], 1.0, W=[CONST])
        self.memset("dve", bd64[64:128, 64:128], 1.0, W=[CONST])
        self.act(actT[:], cT[:], AF.Silu, R=[CONST], W=[MOD])
        permA = consts[:, 1, :]
        permB = consts[:, 2, :]

        MODL = [Buf() for _ in range(NL)]
        cur = {"l": 0}

        def ada_gen(l):
            for s in range(36):
                slot, WBs = load_w(ada_d[l, s], 2048)
                wv = slot[:, 0:2048].rearrange("p (k n) -> p k n", k=8)
                for h in range(2):
                    ch = s * 2 + h
                    for k in range(8):
                        self.mm(pb[7][:, ch * 2:ch * 2 + 2], wv[:, k, h * 128:(h + 1) * 128], actT[:, k, :],
                                start=(k == 0), stop=(k == 7), R=[WBs, MOD], W=[PB[7]])
                yield
            pv = pb[7][:, 0:144].rearrange("p (c s) -> p c s", s=2)
            ML = MODL[l]
            for st in range(2):
                self.tt("dve", modT[:, l, :, st], pv[:, :, st], adab[:, l, :], ALU.add, R=[PB[7], CONST], W=[ML])
            for sub in range(3):
                for st in range(2):
                    sc = modT[:, l, (3 * sub + 1) * 8:(3 * sub + 2) * 8, st]
                    self.stt("dve", gsT[:, l, sub, st, :], sc, 1.0, normg[:, l, sub, :], ALU.add, ALU.mult, R=[ML, CONST], W=[ML])
                    self.ts1("dve", gsT[:, l, sub, st, :], gsT[:, l, sub, st, :], 32.0, ALU.mult, R=[ML], W=[ML])
            for w in range(2):
                for st in range(2):
                    gt = modT[:, l, (6 * w + 2) * 8:(6 * w + 3) * 8, st]
                    self.ts1("dve", hgT[:, l, w, st, :], gt, 0.5, ALU.mult, R=[ML], W=[ML])
            yield

        for _ in ada_gen(0):
            pass
        self._adag = None

        def ada_step():
            if self._adag is not None:
                try:
                    next(self._adag)
                except StopIteration:
                    self._adag = None

        self.ts1("dve", normf[:], normf[:], 32.0, ALU.mult, R=[CONST], W=[CONST])
        for j in range(n_mla):
            self.ts1("dve", mlag[:, j, 0:3], mlag[:, j, 0:3], math.sqrt(384.0), ALU.mult, R=[CONST], W=[CONST])
            self.ts1("dve", mlag[:, j, 3:5], mlag[:, j, 3:5], 16.0, ALU.mult, R=[CONST], W=[CONST])
        if has_gqa:
            self.ts1("dve", gqag[:], gqag[:], 8.0, ALU.mult, R=[CONST], W=[CONST])
        if has_diff:
            lam_init = 0.8 - 0.6 * math.exp(-0.3 * 2)
            self.ts1("dve", diffg[:], diffg[:], math.sqrt(128.0) * (1.0 - lam_init), ALU.mult, R=[CONST], W=[CONST])
        if has_diff and 'dnolam' not in KDBG:
            self.S.op("dve", lambda e: e.tensor_tensor(out=diffl[:, 0, :], in0=diffl[:, 0, :], in1=diffl[:, 1, :], op=ALU.mult), [CONST], [CONST])
            self.S.op("dve", lambda e: e.tensor_tensor(out=diffl[:, 2, :], in0=diffl[:, 2, :], in1=diffl[:, 3, :], op=ALU.mult), [CONST], [CONST])
            self.S.op("dve", lambda e: e.reduce_sum(out=lamt[:, 0:1], in_=diffl[:, 0, :], axis=mybir.AxisListType.X), [CONST], [CONST])
            self.S.op("dve", lambda e: e.reduce_sum(out=lamt[:, 1:2], in_=diffl[:, 2, :], axis=mybir.AxisListType.X), [CONST], [CONST])
            self.act(lamt[:, 2:4], lamt[:, 0:2], AF.Exp, R=[CONST], W=[CONST])
            self.tt("dve", lamt[:, 4:5], lamt[:, 3:4], lamt[:, 2:3], ALU.subtract, R=[CONST], W=[CONST])
            self.ts1("dve", lamt[:, 5:6], lamt[:, 4:5], -lam_init, ALU.add, R=[CONST], W=[CONST])

        def rsqrt(dst, src, addc, n, R, W):
            ti = tmpi()
            self.act(tmpF[ti][:, 0:n], src, AF.Sqrt, bias=addc, R=R, W=[TMPB[ti]])
            self.S.op("dve", lambda e: e.reciprocal(out=dst, in_=tmpF[ti][:, 0:n]), [TMPB[ti]], W)

        def gs_ap(l, sub, st):
            return lambda c: gsT[:, l, sub, st, c:c + 1]

        def sh_ap(l, sub, st):
            return lambda c: modT[:, l, (3 * sub) * 8 + c, st:st + 1]

        def sumsq(src_fn, nch, n, srcR):
            bi = bank((5, 6))
            for c in range(nch):
                sq = sqt[c % 2]
                self.act(sq[:, 0:n], src_fn(c), AF.Square, R=srcR(c), W=[SQB[c % 2]])
                self.mm(pb[bi][:, 0:n], onesb[:], sq[:, 0:n], start=(c == 0), stop=(c == nch - 1), R=[SQB[c % 2], CONST], W=[PB[bi]])
            return bi

        def normmod(g, gs, sh):
            for blk in range(2):
                t0 = g * 1024 + blk * 512
                bi = sumsq(lambda c: yT[:, c, t0:t0 + 512], 8, 512, lambda c: [YB(c, g, blk)])
                rsqrt(rstd[:], pb[bi][:], 1024.0 * EPS, 512, [PB[bi]], [RSTD])
                for c in range(8):
                    ti = tmpi()
                    self.stt("dve", tmpF[ti][:], yT[:, c, t0:t0 + 512], gs(c), rstd[:], ALU.mult, ALU.mult,
                             R=[YB(c, g, blk), RSTD, MODL[cur['l']]], W=[TMPB[ti]])
                    self.act(hT[:, c, blk * 512:(blk + 1) * 512], tmpF[ti][:], AF.Identity, bias=sh(c),
                             R=[TMPB[ti], MODL[cur['l']]], W=[HB(c, blk)])

        uT = arena[:, 0:22 * 1024].rearrange("p (j t) -> p j t", j=22)

        def ffn(g, l, w):
            if self.noffn:
                return
            sub = 0 if w == 0 else 2
            arena_fence()
            UB = ABG()
            normmod(g, gs_ap(l, sub, g), sh_ap(l, sub, g))
            for j in range(22):
                slot, WBs = load_w(fin_d[l, w, j], 2048)
                wv = slot[:, 0:2048].rearrange("p (k s n) -> p k s n", k=8, s=2)
                for blk in range(2):
                    pi = (j * 2 + blk) % 2
                    for s in range(2):
                        bi = 2 * pi + s
                        for k in range(8):
                            self.mm(pb[bi][:], wv[:, k, s, :], hT[:, k, blk * 512:(blk + 1) * 512], start=(k == 0), stop=(k == 7),
                                    R=[WBs, HB(k, blk)], W=[PB[bi]])
                    ti = tmpi()
                    self.act(tmpF[ti][:], pb[2 * pi][:], AF.Silu, R=[PB[2 * pi]], W=[TMPB[ti]])
                    self.tt("dve", uT[:, j, blk * 512:(blk + 1) * 512], pb[2 * pi + 1][:], tmpF[ti][:], ALU.mult,
                            R=[PB[2 * pi + 1], TMPB[ti]], W=[UB(j, blk)])
                if w == 0:
                    ada_step()
            for m in range(8):
                slot, WBs = load_w(fout_d[l, w, m], 2816)
                wv = slot[:, 0:2816].rearrange("p (k n) -> p k n", k=22)
                for blk in range(2):
                    bi = 4 + (m * 2 + blk) % 2
                    for k in range(22):
                        self.mm(pb[bi][:], wv[:, k, :], uT[:, k, blk * 512:(blk + 1) * 512], start=(k == 0), stop=(k == 21),
                                R=[WBs, UB(k, blk)], W=[PB[bi]])
                    t0 = g * 1024 + blk * 512
                    self.stt("dve", yT[:, m, t0:t0 + 512], pb[bi][:], hgT[:, l, w, g, m:m + 1], yT[:, m, t0:t0 + 512], ALU.mult, ALU.add,
                             R=[PB[bi], YB(m, g, blk), MODL[cur['l']]], W=[YB(m, g, blk)])

        def proj(wv_fn, kc, rhs_fn, M, n, R, cands=(5, 6, 7)):
            bi = bank(cands)
            for k in range(kc):
                self.mm(pb[bi][0:M, 0:n], wv_fn(k), rhs_fn(k), start=(k == 0), stop=(k == kc - 1), R=R(k), W=[PB[bi]])
            return bi

        def rope_combine(A, ABuf, lo, hi, perm, table, t0, n, dst, dstW, cands=(5, 6, 7)):
            bi = bank(cands)
            self.mm(pb[bi][0:hi, 0:n], perm[lo:hi, 0:hi], A[lo:hi, 0:n], R=[ABuf, CONST], W=[PB[bi]])
            t2 = tmpi()
            self.tt("dve", tmpF[t2][lo:hi, 0:n], pb[bi][lo:hi, 0:n], table[lo:hi, 1, t0:t0 + n], ALU.mult, R=[PB[bi], CONST], W=[TMPB[t2]])
            self.tt("dve", A[lo:hi, 0:n], A[lo:hi, 0:n], table[lo:hi, 0, t0:t0 + n], ALU.mult, R=[ABuf, CONST], W=[ABuf])
            self.tt("pool", dst, A[lo:hi, 0:n], tmpF[t2][lo:hi, 0:n], ALU.add, R=[ABuf, TMPB[t2]], W=dstW)

        self._pending = []

        def flush_pending():
            p, self._pending = self._pending, []
            for fn in p:
                fn()

        self._units = None

        def begin_units(warm=0):
            self._units = []

        def end_units():
            us, self._units = self._units, None
            run_units(us)

        def attend(KT, QT_ap, nkb, nq, V_fn, vM, po, scale, RK, RQ, RV, zb=None):
            u = dict(KT=[KT(kb) for kb in range(nkb)], Q=QT_ap, nkb=nkb, nq=nq, V=[V_fn(kb) for kb in range(nkb)], vM=vM, po=po,
                     scale=scale, RK=RK, RQ=RQ, RV=RV, zb=zb, fin=None, imm=False)
            if self._units is not None:
                self._units.append(u)
            else:
                run_units([u])

        def attach_fin(fn, imm=False):
            if self._units:
                self._units[-1]["fin"] = fn
                self._units[-1]["imm"] = imm
            elif imm:
                fn()
            else:
                self._pending.append(fn)

        def run_units(units):
            G = 2

            def issue_qk(u, grp):
                sbanks = (0, 1, 2, 7) if u["zb"] is not None else (0, 1, 2, 5, 6)
                out = {}
                for kb in range(grp * G, min(u["nkb"], (grp + 1) * G)):
                    si = sbanks[self._srot % len(sbanks)]
                    self._srot += 1
                    out[kb] = si
                    self.mm(pb[si][:, 0:u["nq"]], u["KT"][kb], u["Q"], R=u["RK"] + u["RQ"], W=[PB[si]])
                return out

            pre = None
            for ui, u in enumerate(units):
                nxt = units[ui + 1] if ui + 1 < len(units) else None
                nkb, nq, po, zb = u["nkb"], u["nq"], u["po"], u["zb"]
                sbank = pre if pre is not None else issue_qk(u, 0)
                pre = None
                ngrp = (nkb + G - 1) // G
                for grp in range(1, ngrp + 1):
                    if grp < ngrp:
                        sbank.update(issue_qk(u, grp))
                    elif nxt is not None and (nxt["zb"] is None) == (zb is None):
                        pre = issue_qk(nxt, 0)
                    kbs = list(range((grp - 1) * G, min(nkb, grp * G)))
                    pis = []
                    for kb in kbs:
                        si = sbank.pop(kb)
                        pi = self._prot % 4
                        self._prot += 1
                        pis.append(pi)
                        self.act(ptile[pi][:, 0:nq], pb[si][:, 0:nq], AF.Exp, scale=u["scale"], R=[PB[si]], W=[PTB[pi]])
                    for kb, pi in zip(kbs, pis):
                        self.mm(pb[po][0:u["vM"], 0:nq], u["V"][kb], ptile[pi][:, 0:nq], start=(kb == 0), stop=(kb == nkb - 1),
                                R=[PTB[pi]] + u["RV"], W=[PB[po]])
                    if zb is not None:
                        for kb, pi in zip(kbs, pis):
                            self.mm(pb[zb][:, 0:nq], onesb[:], ptile[pi][:, 0:nq], start=(kb == 0), stop=(kb == nkb - 1),
                                    R=[PTB[pi], CONST], W=[PB[zb]])
                    if grp == 1:
                        flush_pending()
                if u["fin"] is not None:
                    if u["imm"]:
                        u["fin"]()
                    else:
                        self._pending.append(u["fin"])

        def finish_head(po, par, nq, dst, dstW):
            attach_fin(lambda: finish_head_now(po, par, nq, dst, dstW))

        def finish_head_now(po, par, nq, dst, dstW):
            zr = 64 if par == 0 else 0
            lo = 0 if par == 0 else 64
            ti = tmpi()
            self.S.op("dve", lambda e: e.reciprocal(out=tmpF[ti][zr:zr + 1, 0:nq], in_=pb[po][zr:zr + 1, 0:nq]), [PB[po]], [TMPB[ti]])
            bi = bank((7,))
            self.mm(pb[bi][:, 0:nq], onesf[zr:zr + 1, :], tmpF[ti][zr:zr + 1, 0:nq], R=[TMPB[ti], CONST], W=[PB[bi]])
            t2 = tmpi()
            self.cp("act", tmpF[t2][lo:lo + 64, 0:nq], pb[bi][lo:lo + 64, 0:nq], R=[PB[bi]], W=[TMPB[t2]])
            self.tt("dve", dst, pb[po][lo:lo + 64, 0:nq], tmpF[t2][lo:lo + 64, 0:nq], ALU.mult, R=[PB[po], TMPB[t2]], W=dstW)

        def oproj(g, l, o_d):
            flush_pending()
            for m in range(8):
                slot, WBs = load_w(o_d[m], 1024)
                wv = slot[:, 0:1024].rearrange("p (k n) -> p k n", k=8)
                for blk in range(2):
                    bi = proj(lambda k: wv[:, k, :], 8, lambda k: hT[:, k, blk * 512:(blk + 1) * 512], 128, 512,
                              lambda k: [WBs, HB(k, blk)], cands=(4, 5))
                    t0 = g * 1024 + blk * 512
                    self.stt("dve", yT[:, m, t0:t0 + 512], pb[bi][:], modT[:, l, 5 * 8 + m, g:g + 1], yT[:, m, t0:t0 + 512], ALU.mult, ALU.add,
                             R=[PB[bi], YB(m, g, blk), MODL[cur['l']]], W=[YB(m, g, blk)])

        def allgather(src, dst, RB, WBf):
            self.S.op("pool", lambda e: e.collective_compute("AllGather", ALU.bypass, replica_groups=RG, ins=[src[:, :]], outs=[dst[:, :]]),
                      RB, WBf, dsem=d_cc)

        A_KT = [arena[:, 0:4608], arena[:, 4608:9216]]
        A_V = [arena[:, 9216:13824].rearrange("p (k d) -> p k d", k=36), arena[:, 13824:18432].rearrange("p (k d) -> p k d", k=36)]
        GIN, GOUT = BG(), BG()

        def run_pipelined(items):
            active = []
            for g in items:
                for a in list(active):
                    try:
                        next(a)
                    except StopIteration:
                        active.remove(a)
                try:
                    next(g)
                    active.append(g)
                except StopIteration:
                    pass
            while active:
                for a in list(active):
                    try:
                        next(a)
                    except StopIteration:
                        active.remove(a)

        def gqa(l):
            scale = 64 ** -0.5

            self._qi = 0

            def qk_chunk(wv_half, WBs, g, blk, gain, rope, dst, dstW, outfp=None, dst2=None):
                idx = self._qi
                self._qi += 1
                sq, SQ = sqt[idx % 2], SQB[idx % 2]
                rs, RS = (rstd, RSTD) if idx % 2 == 0 else (rstd2, RSTD2)
                bi = proj(lambda k: wv_half(k), 8, lambda k: hT[:, k, blk * 512:(blk + 1) * 512], 128, 512, lambda k: [WBs, HB(k, blk)],
                          cands=(3, 4, 5, 6, 7))
                self.act(sq[:], pb[bi][:], AF.Square, R=[PB[bi]], W=[SQ])
                yield
                b2 = bank((3, 4, 5, 6, 7))
                self.mm(pb[b2][:], bd64[:], sq[:], R=[SQ, CONST], W=[PB[b2]])
                rsqrt(rs[:], pb[b2][:], 64.0 * EPS, 512, [PB[b2]], [RS])
                ti = tmpi()
                self.stt("dve", tmpF[ti][:], pb[bi][:], gain, rs[:], ALU.mult, ALU.mult, R=[PB[bi], RS, CONST], W=[TMPB[ti]])
                if outfp is not None:
                    self.dma("sp", outfp, tmpF[ti][0:64, :], d_tmp[ti], R=[TMPB[ti]])
                yield
                if rope and 'norope' not in KDBG:
                    rope_combine(tmpF[ti], TMPB[ti], 0, 128, permB, ropeB, blk * 512, 512, dst, dstW, cands=(3, 4, 5, 6, 7))
                elif dst2 is not None:
                    self.cp("pool", dst, tmpF[ti][0:64, :], R=[TMPB[ti]], W=dstW)
                    self.cp("pool", dst2, tmpF[ti][64:128, :], R=[TMPB[ti]], W=dstW)
                else:
                    self.cp("pool", dst, tmpF[ti][:], R=[TMPB[ti]], W=dstW)

            def run_group(g):
                lat = (g == 1)
                arena_fence()
                KTB = [AB(), AB()]
                QB = ABG()
                normmod(g, gs_ap(l, 1, g), sh_ap(l, 1, g))
                if lat:
                    klat = arena[:, 18432:18432 + 4096].rearrange("p (c t) -> p c t", c=4)
                    KL = AB()
                else:
                    KTc = arena[:, 0:4096].rearrange("p (c t) -> p c t", c=4)
                    KTc1 = arena[:, 12288:16384].rearrange("p (c t) -> p c t", c=4)
                    VE = arena[:, 4096:4096 + 8 * 4 * 66].rearrange("p (k h d) -> p k h d", k=8, h=4)
                    VO = arena[:, 8192:8192 + 8 * 4 * 128].rearrange("p (k h d) -> p k h d", k=8, h=4)
                    KC_, VC_ = ABG(), AB()
                    for kv_ in range(4):
                        self.memset("pool", KTc[64:128, kv_, :], 0.0, W=[KC_(kv_)])
                        self.memset("pool", KTc1[0:64, kv_, :], 0.0, W=[KC_(kv_)])
                    if 'nomemset' not in KDBG:
                        self.memset("pool", VE[:, :, :, 64:65], 1.0, W=[VC_])
                        self.memset("pool", VO[:, :, :, 0:64], 0.0, W=[VC_])
                        self.memset("pool", VO[:, :, :, 0:1], 1.0, W=[VC_])
                def k_items():
                    for u in range(0 if 'nok' in KDBG else 2):
                        slot, WBs = load_w(gqa_w_d[4 + u], 2048)
                        wv = slot[:, 0:2048].rearrange("p (k n) -> p k n", k=8)
                        for h in range(2):
                            kvh = u * 2 + h
                            for blk in range(2):
                                if lat:
                                    yield qk_chunk(lambda k, wv=wv, h=h: wv[:, k, h * 128:(h + 1) * 128], WBs, g, blk, gqag[:, 1:2], True,
                                                   klat[:, kvh, blk * 512:(blk + 1) * 512], [KL])
                                else:
                                    yield qk_chunk(lambda k, wv=wv, h=h: wv[:, k, h * 128:(h + 1) * 128], WBs, g, blk, gqag[:, 1:2], False,
                                                   KTc[0:64, kvh, blk * 512:(blk + 1) * 512], [KC_(kvh)],
                                                   outfp=o_gk_d[kvh, :, blk * 512:(blk + 1) * 512],
                                                   dst2=KTc1[64:128, kvh, blk * 512:(blk + 1) * 512])
                run_pipelined(k_items())
                slot, WBs = load_w(gqa_w_d[6], 2048)
                wv = slot[:, 0:2048].rearrange("p (k n) -> p k n", k=8)
                if lat:
                    vlat = arena[:, 22528:22528 + 2048].rearrange("p (k d) -> p k d", k=8)
                    VL = AB()
                for tb in range(0 if 'nov' in KDBG else 8):
                    bi = proj(lambda k: hT[:, k, tb * 128:(tb + 1) * 128], 8, lambda k: wv[:, k, :], 128, 256,
                              lambda k: [WBs, HB(k, tb // 4)])
                    if lat:
                        self.cp("dve", vlat[:, tb, :], pb[bi][:, 0:256], R=[PB[bi]], W=[VL])
                    else:
                        si = stg()
                        if 'v1' not in KDBG:
                            self.cp("act", stage[si][:, 0:256], pb[bi][:, 0:256], R=[PB[bi]], W=[STB[si]])
                        if 'v2' not in KDBG:
                            self.dma("sp", o_gv_d[tb * 128:(tb + 1) * 128, :], stage[si][:, 0:256], d_st[si], R=[STB[si]])
                        pv = pb[bi][:, 0:256].rearrange("p (h d) -> p h d", h=4)
                        if 'v3' not in KDBG:
                            self.cp("act", VE[:, tb, :, 0:64], pv, R=[PB[bi]], W=[VC_])
                            self.cp("act", VO[:, tb, :, 64:128], pv, R=[PB[bi]], W=[VC_])
                if lat and 'nogather' not in KDBG:
                    self.dma("sp", gBk_in.ap().rearrange("(c p) t -> p c t", p=128), klat, GDS["bk"], R=[KL], W=[GIN("bk")])
                    self.dma("sp", gBv_in.ap().rearrange("(k p) d -> p k d", p=128), vlat, GDS["bv"], R=[VL], W=[GIN("bv")])
                    allgather(gBk_in, gBk_out, [GIN("bk")], [GOUT("bk")])
                    allgather(gBv_in, gBv_out, [GIN("bv")], [GOUT("bv")])
                def q_items():
                    for u in range(0 if 'noq' in KDBG else 4):
                        slot, WBs = load_w(gqa_w_d[u], 2048)
                        wv = slot[:, 0:2048].rearrange("p (k n) -> p k n", k=8)
                        for h in range(2):
                            m = u * 2 + h
                            for blk in range(2):
                                yield qk_chunk(lambda k, wv=wv, h=h: wv[:, k, h * 128:(h + 1) * 128], WBs, g, blk, gqag[:, 0:1], lat,
                                               QT[:, m, blk * 512:(blk + 1) * 512], [QB(m, blk)])
                run_pipelined(q_items())
                if (not lat) and 'noctxattn' in KDBG:
                    pass
                elif not lat:
                    begin_units()
                    for s in range(4):
                        for hd in range(16):
                            kvh, par, m = hd // 4, hd % 2, hd // 2
                            r0 = par * 64
                            po = 3 + (hd % 2)
                            Vt = VE if par == 0 else VO
                            vM = 65 if par == 0 else 128
                            Kz = KTc if par == 0 else KTc1
                            attend(lambda kb: Kz[:, kvh, s * 256 + kb * 128:s * 256 + (kb + 1) * 128],
                                   QT[:, m, s * 256:(s + 1) * 256], 2, 256,
                                   lambda kb: Vt[:, s * 2 + kb, kvh, 0:vM], vM, po, scale,
                                   [KC_(kvh)], [QB(m, s // 2)], [VC_])
                            finish_head(po, par, 256, hT[r0:r0 + 64, m, s * 256:(s + 1) * 256], [HB(m, s // 2)])
                    end_units()
                elif 'nolatattn' in KDBG:
                    pass
                else:
                    VEl = arena[:, 9216:9216 + 36 * 66].rearrange("p (k d) -> p k d", k=36)
                    VOl = arena[:, 13824:18432].rearrange("p (k d) -> p k d", k=36)
                    VEB, VOB = AB(), AB()
                    self.memset("pool", VEl[:, :, 64:65], 1.0, W=[VEB])
                    self.memset("pool", VOl[:, :, 0:64], 0.0, W=[VOB])
                    self.memset("pool", VOl[:, :, 0:1], 1.0, W=[VOB])
                    gk = gBk_out.ap().rearrange("(r c p) t -> c p r t", r=4, c=4)
                    gv = gBv_out.ap().rearrange("(k p) (h d) -> h p k d", p=128, h=4)
                    cv = gqa_cv_d.rearrange("(k p) (h d) -> h p k d", p=128, h=4)
                    self.memset("pool", A_KT[0][64:128, :], 0.0, W=[KTB[0]])
                    self.memset("pool", A_KT[1][0:64, :], 0.0, W=[KTB[1]])
                    for kvh in range(4):
                        for ks in range(2):
                            rs = slice(ks * 64, ks * 64 + 64)
                            self.dma("sp", A_KT[ks][rs, 0:4096].rearrange("p (r t) -> p r t", r=4), gk[kvh][rs], d_kv[ks], R=[GOUT("bk")], W=[KTB[ks]])
                            self.dma("pool", A_KT[ks][rs, 4096:4608], gqa_ck_d[kvh][rs], d_kv[ks], W=[KTB[ks]])
                        self.dma("sp", VEl[:, 0:32, 0:64], gv[kvh], d_kv[2], R=[GOUT("bv")], W=[VEB])
                        self.dma("pool", VEl[:, 32:36, 0:64], cv[kvh], d_kv[2], W=[VEB])
                        self.dma("sp", VOl[:, 0:32, 64:128], gv[kvh], d_kv[3], R=[GOUT("bv")], W=[VOB])
                        self.dma("pool", VOl[:, 32:36, 64:128], cv[kvh], d_kv[3], W=[VOB])
                        begin_units()
                        for hh in range(4):
                            hd = kvh * 4 + hh
                            par, m = hd % 2, hd // 2
                            r0 = par * 64
                            Vt = VEl if par == 0 else VOl
                            VtB = VEB if par == 0 else VOB
                            vM = 65 if par == 0 else 128
                            for qb in range(2):
                                po = 3 + (qb % 2)
                                attend(lambda kb: A_KT[par][:, kb * 128:(kb + 1) * 128],
                                       QT[:, m, qb * 512:(qb + 1) * 512], 36, 512,
                                       lambda kb: Vt[:, kb, 0:vM], vM, po, scale, [KTB[par]], [QB(m, qb)], [VtB])
                                finish_head(po, par, 512, hT[r0:r0 + 64, m, qb * 512:(qb + 1) * 512], [HB(m, qb)])
                        end_units()
                if 'noo' not in KDBG:
                    oproj(g, l, gqa_o_d)

            run_group(0)
            run_group(1)

        def diff(l):
            scale = 64 ** -0.5

            def run_group(g):
                lat = (g == 1)
                arena_fence()
                KTB = [AB(), AB()]
                VB = [AB(), AB()]
                QB = ABG()
                normmod(g, gs_ap(l, 1, g), sh_ap(l, 1, g))
                if lat:
                    KL = AB()
                    VL = AB()
                    klat = arena[:, 18432:18432 + 1024]
                    vlat = arena[:, 19456:19456 + 2048].rearrange("p (k d) -> p k d", k=8)
                else:
                    KTc = arena[:, 0:8192].rearrange("p (c t) -> p c t", c=8)
                    KTc1 = arena[:, 16384:24576].rearrange("p (c t) -> p c t", c=8)
                    Vc = arena[:, 8192:16384].rearrange("p (k d) -> p k d", k=8)
                    KC_, VC_ = ABG(), ABG()
                    for c_ in range(8):
                        self.memset("pool", KTc[64:128, c_, :], 0.0, W=[KC_(c_)])
                        self.memset("pool", KTc1[0:64, c_, :], 0.0, W=[KC_(c_)])

                def qk_unit(u, isq):
                    slot, WBs = load_w(diff_w_d[u], 2048)
                    wv = slot[:, 0:2048].rearrange("p (k n) -> p k n", k=8)
                    for h in range(2):
                        ch = (u % 4) * 2 + h
                        for blk in range(2):
                            bi = proj(lambda k: wv[:, k, h * 128:(h + 1) * 128], 8, lambda k: hT[:, k, blk * 512:(blk + 1) * 512], 128, 512,
                                      lambda k: [WBs, HB(k, blk)])
                            if isq:
                                dst, dstW = QT[:, ch, blk * 512:(blk + 1) * 512], [QB(ch, blk)]
                            elif lat:
                                dst, dstW = klat[:, blk * 512:(blk + 1) * 512], [KL]
                            else:
                                dst, dstW = KTc[:, ch, blk * 512:(blk + 1) * 512], [KC_(ch)]
                            if lat:
                                ti = tmpi()
                                self.cp("act", tmpF[ti][:], pb[bi][:], R=[PB[bi]], W=[TMPB[ti]])
                                rope_combine(tmpF[ti], TMPB[ti], 0, 128, permB, ropeB, blk * 512, 512, dst, dstW)
                            else:
                                if isq:
                                    self.cp("act", dst, pb[bi][:], R=[PB[bi]], W=dstW)
                                else:
                                    self.cp("act", KTc[0:64, ch, blk * 512:(blk + 1) * 512], pb[bi][0:64, :], R=[PB[bi]], W=dstW)
                                    self.cp("act", KTc1[64:128, ch, blk * 512:(blk + 1) * 512], pb[bi][64:128, :], R=[PB[bi]], W=dstW)
                                if not isq:
                                    si = stg()
                                    self.cp("dve", stage[si][:], pb[bi][:], R=[PB[bi]], W=[STB[si]])
                                    self.dma("sp", o_dk_d[:, ch, blk * 512:(blk + 1) * 512], stage[si][:], d_st[si], R=[STB[si]])
                        if (not isq) and lat:
                            self.dma("sp", gCk_in[ch // 2][(ch % 2) * 128:(ch % 2 + 1) * 128, :], klat, GDS["ck"], R=[KL], W=[GIN("ck", ch // 2)])
                            if ch % 2 == 1 and 'dnogather' not in KDBG:
                                allgather(gCk_in[ch // 2], gCk_out[ch // 2], [GIN("ck", ch // 2)], [GOUT("ck", ch // 2)])

                for u in range(4, 4 if 'dnok' in KDBG else 8):
                    qk_unit(u, False)
                for u in range(8, 8 if 'dnov' in KDBG else 12):
                    slot, WBs = load_w(diff_w_d[u], 2048)
                    wv = slot[:, 0:2048].rearrange("p (k n) -> p k n", k=8)
                    for tb in range(8):
                        bi = proj(lambda k: hT[:, k, tb * 128:(tb + 1) * 128], 8, lambda k: wv[:, k, :], 128, 256,
                                  lambda k: [WBs, HB(k, tb // 4)])
                        c0 = (u - 8) * 256
                        if lat:
                            self.cp("dve", vlat[:, tb, :], pb[bi][:, 0:256], R=[PB[bi]], W=[VL])
                        else:
                            si = stg()
                            self.cp("act", stage[si][:, 0:256], pb[bi][:, 0:256], R=[PB[bi]], W=[STB[si]])
                            self.dma("sp", o_dv_d[tb * 128:(tb + 1) * 128, c0:c0 + 256], stage[si][:, 0:256], d_st[si], R=[STB[si]])
                            self.cp("dve", Vc[:, tb, c0:c0 + 256], pb[bi][:, 0:256], R=[PB[bi]], W=[VC_(u - 8)])
                    if lat:
                        self.dma("sp", gCv_in[u - 8].ap().rearrange("(k p) d -> p k d", p=128), vlat, GDS["cv"], R=[VL], W=[GIN("cv", u - 8)])
                        if 'dnogather' not in KDBG:
                            allgather(gCv_in[u - 8], gCv_out[u - 8], [GIN("cv", u - 8)], [GOUT("cv", u - 8)])
                for u in range(0, 0 if 'dnoq' in KDBG else 4):
                    qk_unit(u, True)

                def head_finish(hd, q0, nq, OW):
                    t1, t2, t3 = tmpi(), tmpi(), tmpi()
                    self.S.op("dve", lambda e: e.reciprocal(out=tmpF[t1][:, 0:nq], in_=pb[4][:, 0:nq]), [PB[4]], [TMPB[t1]])
                    self.tt("dve", tmpF[t1][:, 0:nq], pb[3][:, 0:nq], tmpF[t1][:, 0:nq], ALU.mult, R=[PB[3], TMPB[t1]], W=[TMPB[t1]])
                    self.S.op("dve", lambda e: e.reciprocal(out=tmpF[t2][:, 0:nq], in_=pb[6][:, 0:nq]), [PB[6]], [TMPB[t2]])
                    self.tt("dve", tmpF[t2][:, 0:nq], pb[5][:, 0:nq], tmpF[t2][:, 0:nq], ALU.mult, R=[PB[5], TMPB[t2]], W=[TMPB[t2]])
                    self.stt("dve", tmpF[t3][:, 0:nq], tmpF[t2][:, 0:nq], lamt[:, 5:6], tmpF[t1][:, 0:nq], ALU.mult, ALU.add,
                             R=[TMPB[t1], TMPB[t2], CONST], W=[TMPB[t3]])
                    self.act(sqt[0][:, 0:nq], tmpF[t3][:, 0:nq], AF.Square, R=[TMPB[t3]], W=[SQB[0]])
                    self.mm(pb[7][:, 0:nq], onesb[:], sqt[0][:, 0:nq], R=[SQB[0], CONST], W=[PB[7]])
                    rsqrt(rstd[:, 0:nq], pb[7][:, 0:nq], 128.0 * EPS, nq, [PB[7]], [RSTD])
                    self.stt("dve", hT[:, hd, q0:q0 + nq], tmpF[t3][:, 0:nq], diffg[:, 0:1], rstd[:, 0:nq], ALU.mult, ALU.mult,
                             R=[TMPB[t3], RSTD, CONST], W=OW)

                if (not lat and 'dnoctx' in KDBG) or (lat and 'dnolat' in KDBG):
                    pass
                elif not lat:
                    begin_units()
                    for s in range(4):
                        for hd in range(8):
                            for mp in range(2):
                                r0 = mp * 64
                                Kz = KTc if mp == 0 else KTc1
                                attend(lambda kb: Kz[:, hd, s * 256 + kb * 128:s * 256 + (kb + 1) * 128],
                                       QT[:, hd, s * 256:(s + 1) * 256], 2, 256,
                                       lambda kb: Vc[:, s * 2 + kb, hd * 128:(hd + 1) * 128], 128, 3 + 2 * mp, scale,
                                       [KC_(hd)], [QB(hd, s // 2)], [VC_(hd // 2)], zb=4 + 2 * mp)
                            attach_fin(lambda hd=hd, s=s: head_finish(hd, s * 256, 256, [HB(hd, s // 2)]), imm=True)
                    end_units()
                else:
                    cv = diff_cv_d.rearrange("(k p) (h d) -> h p k d", p=128, h=8)
                    self.memset("pool", A_KT[0][64:128, :], 0.0, W=[KTB[0]])
                    self.memset("pool", A_KT[1][0:64, :], 0.0, W=[KTB[1]])
                    for hd in range(8):
                        ks = hd % 2
                        gk = gCk_out[hd // 2].ap().rearrange("(r c p) t -> c p r t", r=4, c=2)[hd % 2]
                        gv = gCv_out[hd // 2].ap().rearrange("(k p) (h d) -> h p k d", p=128, h=2)[hd % 2]
                        for mp_ in range(2):
                            rs = slice(mp_ * 64, mp_ * 64 + 64)
                            self.dma("sp", A_KT[mp_][rs, 0:4096].rearrange("p (r t) -> p r t", r=4), gk[rs], d_kv[mp_], R=[GOUT("ck", hd // 2)], W=[KTB[mp_]])
                            self.dma("pool", A_KT[mp_][rs, 4096:4608], diff_ck_d[hd][rs], d_kv[mp_], W=[KTB[mp_]])
                        self.dma("sp", A_V[ks][:, 0:32, :], gv, d_kv[2 + ks], R=[GOUT("cv", hd // 2)], W=[VB[ks]])
                        self.dma("pool", A_V[ks][:, 32:36, :], cv[hd], d_kv[2 + ks], W=[VB[ks]])
                        begin_units()
                        for qb in range(2):
                            for mp in range(2):
                                r0 = mp * 64
                                attend(lambda kb: A_KT[mp][:, kb * 128:(kb + 1) * 128],
                                       QT[:, hd, qb * 512:(qb + 1) * 512], 36, 512,
                                       lambda kb: A_V[ks][:, kb, :], 128, 3 + 2 * mp, scale,
                                       [KTB[mp]], [QB(hd, qb)], [VB[ks]], zb=4 + 2 * mp)
                            attach_fin(lambda hd=hd, qb=qb: head_finish(hd, qb * 512, 512, [HB(hd, qb)]), imm=True)
                        end_units()
                if 'dnoo' not in KDBG:
                    oproj(g, l, diff_o_d)

            run_group(0)
            run_group(1)

        def mla(l, j):
            scale = 96 ** -0.5
            cqT = QTf[:, 0:3072].rearrange("p (c t) -> p c t", c=3)
            qsl = [QTf[:, 3072:4096], QTf[:, 4096:5120]]

            def run_group(g):
                lat = (g == 1)
                arena_fence()
                QSB = [AB(), AB()]
                CQB = ABG()
                nk = 4608 if lat else 1024
                nkb = nk // 128
                normmod(g, gs_ap(l, 1, g), sh_ap(l, 1, g))
                if lat:
                    ckvT = arena[:, 0:9216].rearrange("p (c t) -> p c t", c=2)
                    KT1 = arena[:, 9216:13824]
                    VEl = arena[:, 13824:13824 + 36 * 66].rearrange("p (k d) -> p k d", k=36)
                    VOl = arena[:, 17408:17408 + 4608].rearrange("p (k d) -> p k d", k=36)
                    cl = arena[:, 22016:22016 + 2048].rearrange("p (c t) -> p c t", c=2)
                    krl = arena[:, 16384:17408]
                else:
                    ckvT = arena[:, 0:2048].rearrange("p (c t) -> p c t", c=2)
                    KT1 = arena[:, 9216:9216 + 1024]
                    VEl = arena[:, 13824:13824 + 8 * 66].rearrange("p (k d) -> p k d", k=8)
                    VOl = arena[:, 17408:17408 + 1024].rearrange("p (k d) -> p k d", k=8)
                CKB, K1B, VEB, VOB, CLB, KRB = ABG(), AB(), AB(), AB(), AB(), AB()
                self.memset("pool", VEl[:, :, 64:65], 1.0, W=[VEB])
                self.memset("pool", VOl[:, :, 0:64], 0.0, W=[VOB])
                self.memset("pool", VOl[:, :, 0:1], 1.0, W=[VOB])
                self.memset("pool", KT1[64:128, :], 0.0, W=[K1B])
                self.memset("pool", qsl[0][64:128, :], 0.0, W=[QSB[0]])
                self.memset("pool", qsl[1][64:128, :], 0.0, W=[QSB[1]])
                units = []
                for u in range(3):
                    nn_ = 2048 if u < 2 else 1280
                    slot, WBs = load_w(mla_down_d[j, u][:, 0:nn_], nn_)
                    n = 256 if u < 2 else 160
                    units.append((slot[:, 0:8 * n].rearrange("p (k n) -> p k n", k=8), WBs))

                def wcol(c0, M):
                    u = c0 // 256
                    wv, WBs = units[u]
                    o = c0 - u * 256
                    return (lambda k: wv[:, k, o:o + M]), WBs

                for blk in range(2):
                    hr = lambda k: hT[:, k, blk * 512:(blk + 1) * 512]
                    bis = []
                    for c in range(3):
                        wf, WBs = wcol(c * 128, 128)
                        bis.append(proj(wf, 8, hr, 128, 512, lambda k, WBs=WBs: [WBs, HB(k, blk)], cands=(0, 1, 2)))
                    sb_ = sumsq(lambda c: pb[bis[c]][:], 3, 512, lambda c: [PB[bis[c]]])
                    rsqrt(rstd[:], pb[sb_][:], 384.0 * EPS, 512, [PB[sb_]], [RSTD])
                    for c in range(3):
                        self.stt("dve", cqT[:, c, blk * 512:(blk + 1) * 512], pb[bis[c]][:], mlag[:, j, c:c + 1], rstd[:], ALU.mult, ALU.mult,
                                 R=[PB[bis[c]], RSTD, CONST], W=[CQB(c, blk)])
                    bis = []
                    for c in range(2):
                        wf, WBs = wcol(384 + c * 128, 128)
                        bis.append(proj(wf, 8, hr, 128, 512, lambda k, WBs=WBs: [WBs, HB(k, blk)], cands=(0, 1, 2)))
                    sb_ = sumsq(lambda c: pb[bis[c]][:], 2, 512, lambda c: [PB[bis[c]]])
                    rsqrt(rstd[:], pb[sb_][:], 256.0 * EPS, 512, [PB[sb_]], [RSTD])
                    for c in range(2):
                        ti = tmpi()
                        self.stt("dve", tmpF[ti][:], pb[bis[c]][:], mlag[:, j, 3 + c:4 + c], rstd[:], ALU.mult, ALU.mult,
                                 R=[PB[bis[c]], RSTD, CONST], W=[TMPB[ti]])
                        if lat:
                            self.cp("act", cl[:, c, blk * 512:(blk + 1) * 512], tmpF[ti][:], R=[TMPB[ti]], W=[CLB])
                        else:
                            self.cp("act", ckvT[:, c, blk * 512:(blk + 1) * 512], tmpF[ti][:], R=[TMPB[ti]], W=[CKB(c)])
                            self.dma("sp", o_ckv_d[j, :, c, blk * 512:(blk + 1) * 512], tmpF[ti][:], d_tmp[ti], R=[TMPB[ti]])
                    wf, WBs = wcol(576, 96)
                    bi = proj(wf, 8, hr, 96, 512, lambda k, WBs=WBs: [WBs, HB(k, blk)], cands=(0, 1, 2))
                    ti = tmpi()
                    self.cp("act", tmpF[ti][64:96, :], pb[bi][64:96, :], R=[PB[bi]], W=[TMPB[ti]])
                    if lat:
                        rope_combine(tmpF[ti], TMPB[ti], 64, 96, permA, ropeA, blk * 512, 512, krl[64:96, blk * 512:(blk + 1) * 512], [KRB])
                    else:
                        self.dma("sp", o_kr_d[j, :, blk * 512:(blk + 1) * 512], tmpF[ti][64:96, :], d_tmp[ti], R=[TMPB[ti]])
                        self.cp("pool", KT1[64:96, blk * 512:(blk + 1) * 512], tmpF[ti][64:96, :], R=[TMPB[ti]], W=[K1B])
                if lat:
                    ga = gA_in.ap()
                    self.dma("sp", ga[0:256, :].rearrange("(c p) t -> p c t", p=128), cl, GDS["a"], R=[CLB], W=[GIN("a")])
                    self.dma("sp", ga[256:288, :], krl[64:96, :], GDS["a"], R=[KRB], W=[GIN("a")])
                    allgather(gA_in, gA_out, [GIN("a")], [GOUT("a")])
                    go = gA_out.ap().rearrange("(r f) t -> f r t", r=4)
                    for c in range(2):
                        self.dma("sp", ckvT[:, c, 0:4096].rearrange("p (r t) -> p r t", r=4), go[c * 128:(c + 1) * 128], d_kv[c], R=[GOUT("a")], W=[CKB(c)])
                        self.dma("pool", ckvT[:, c, 4096:4608], mla_cckv_d[j, :, c, :], d_kv[c], W=[CKB(c)])
                    self.dma("sp", KT1[64:96, 0:4096].rearrange("p (r t) -> p r t", r=4), go[256:288], d_kv[2], R=[GOUT("a")], W=[K1B])
                    self.dma("pool", KT1[64:96, 4096:4608], mla_ckr_d[j], d_kv[2], W=[K1B])
                for hg in range(4):
                    sq_, WBq = load_w(mla_uq_d[j, hg], 1152)
                    wq = sq_[:, 0:1152].rearrange("p (h k n) -> p h k n", h=4, k=3)
                    skv, WBkv = load_w(mla_ukv_d[j, hg], 1024)
                    wkv = skv[:, 0:1024].rearrange("p (h k n) -> p h k n", h=4, k=2)
                    for hh in range(4):
                        hd = hg * 4 + hh
                        par, m = hd % 2, hd // 2
                        for kb4 in range(nk // 512):
                            bi = proj(lambda k: wkv[:, hh, k, 0:64], 2, lambda k: ckvT[:, k, kb4 * 512:(kb4 + 1) * 512], 64, 512,
                                      lambda k: [WBkv, CKB(k)])
                            self.cp("dve", KT1[0:64, kb4 * 512:(kb4 + 1) * 512], pb[bi][0:64, :], R=[PB[bi]], W=[K1B])
                        Vt, VtB = (VEl, VEB) if par == 0 else (VOl, VOB)
                        c0 = 0 if par == 0 else 64
                        for kb8 in range((nkb + 7) // 8):
                            nb = min(8, nkb - kb8 * 8)
                            bi = bank()
                            for q in range(nb):
                                kb = kb8 * 8 + q
                                for k in range(2):
                                    self.mm(pb[bi][:, q * 64:(q + 1) * 64], ckvT[:, k, kb * 128:(kb + 1) * 128], wkv[:, hh, k, 64:128],
                                            start=(k == 0), stop=(k == 1), R=[CKB(k), WBkv], W=[PB[bi]])
                            self.cp("act", Vt[:, kb8 * 8:kb8 * 8 + nb, c0:c0 + 64], pb[bi][:, 0:nb * 64].rearrange("p (q d) -> p q d", d=64),
                                    R=[PB[bi]], W=[VtB])
                        qs, QSb = qsl[hd % 2], QSB[hd % 2]
                        for blk in range(2):
                            bi = proj(lambda k: wq[:, hh, k, :], 3, lambda k: cqT[:, k, blk * 512:(blk + 1) * 512], 96, 512,
                                      lambda k: [WBq, CQB(k, blk)])
                            self.cp("act", qs[0:64, blk * 512:(blk + 1) * 512], pb[bi][0:64, :], R=[PB[bi]], W=[QSb])
                            if lat:
                                ti = tmpi()
                                self.cp("dve", tmpF[ti][64:96, :], pb[bi][64:96, :], R=[PB[bi]], W=[TMPB[ti]])
                                rope_combine(tmpF[ti], TMPB[ti], 64, 96, permA, ropeA, blk * 512, 512, qs[64:96, blk * 512:(blk + 1) * 512], [QSb])
                            else:
                                self.cp("dve", qs[64:96, blk * 512:(blk + 1) * 512], pb[bi][64:96, :], R=[PB[bi]], W=[QSb])
                        vM = 65 if par == 0 else 128
                        r0 = par * 64
                        begin_units()
                        if lat:
                            for qb in range(2):
                                po = 3 + (qb % 2)
                                attend(lambda kb: KT1[:, kb * 128:(kb + 1) * 128], qs[:, qb * 512:(qb + 1) * 512], 36, 512,
                                       lambda kb: Vt[:, kb, 0:vM], vM, po, scale, [K1B], [QSb], [VtB])
                                finish_head(po, par, 512, hT[r0:r0 + 64, m, qb * 512:(qb + 1) * 512], [HB(m, qb)])
                        else:
                            for s in range(4):
                                po = 3 + (s % 2)
                                attend(lambda kb: KT1[:, s * 256 + kb * 128:s * 256 + (kb + 1) * 128], qs[:, s * 256:(s + 1) * 256], 2, 256,
                                       lambda kb: Vt[:, s * 2 + kb, 0:vM], vM, po, scale, [K1B], [QSb], [VtB])
                                finish_head(po, par, 256, hT[r0:r0 + 64, m, s * 256:(s + 1) * 256], [HB(m, s // 2)])
                        end_units()
                oproj(g, l, mla_o_d[j])

            run_group(0)
            run_group(1)

        for l in range(NL):
            cur["l"] = l
            if l + 1 < NL:
                self._adag = ada_gen(l + 1)
            ffn(0, l, 0)
            ffn(1, l, 0)
            while self._adag is not None:
                ada_step()
            kind = l % 3
            if kind == 0:
                mla(l, l // 3)
            elif kind == 1:
                gqa(l)
            else:
                diff(l)
            ffn(0, l, 1)
            ffn(1, l, 1)

        for g in range(2):
            for blk in range(2):
                t0 = g * 1024 + blk * 512
                bi = sumsq(lambda c: yT[:, c, t0:t0 + 512], 8, 512, lambda c: [YB(c, g, blk)])
                rsqrt(rstd[:], pb[bi][:], 1024.0 * EPS, 512, [PB[bi]], [RSTD])
                for c in range(8):
                    self.stt("dve", yT[:, c, t0:t0 + 512], yT[:, c, t0:t0 + 512], normf[:, c:c + 1], rstd[:], ALU.mult, ALU.mult,
                             R=[YB(c, g, blk), RSTD, CONST], W=[YB(c, g, blk)])
                self.dma("sp", yout_d[:, :, t0:t0 + 512], yT[:, :, t0:t0 + 512], d_out, R=[YB(c, g, blk) for c in range(8)])
        fin = Buf()
        fin.r = {d: d.val for d in [d_out] + d_st + d_tmp if d.val > 0}
        self.S.op("sp", lambda e: e.nop(), writes=[fin])

        for d in S.dsems:
            d.h = self.es.enter_context(nc.semaphore("d_" + d.name))
        S.finalize()
        block = self.es.enter_context(nc.Block())
        for name in ENGS:
            def mk(name=name):
                def f(e):
                    S.emit(name, e, sems)
                return f
            getattr(block, ATTR[name])(mk())
        self.es.close()
        return nc


def _fm(x2d):
    T, Dd = x2d.shape
    return np.ascontiguousarray(x2d.T.reshape(Dd // 128, 128, T).transpose(1, 0, 2))


def _wt(W):
    K, N = W.shape
    return np.ascontiguousarray(W.reshape(K // 128, 128, N).transpose(1, 0, 2).reshape(128, (K // 128) * N))


def _pad(a, n):
    out = np.zeros((a.shape[0], n), np.float32)
    out[:, :a.shape[1]] = a
    return out


def _rope_tables(pos, rot_dim):
    GRID_W = 64
    r = (pos // GRID_W).astype(np.float32)
    cidx = (pos % GRID_W).astype(np.float32)
    n_f = rot_dim // 4
    freqs = (np.float32(10000.0) ** (-np.arange(n_f, dtype=np.float32) / np.float32(n_f))).astype(np.float32)
    ang = np.concatenate([r[:, None] * freqs, cidx[:, None] * freqs], axis=-1).astype(np.float32)
    cos, sin = np.cos(ang).astype(np.float32), np.sin(ang).astype(np.float32)
    cos2 = np.concatenate([cos, cos], axis=1).T
    sinp = np.concatenate([-sin, sin], axis=1).T
    return cos2, sinp


_CACHE = {}


def _get_nc(nl, noffn=False):
    if (nl, noffn) not in _CACHE:
        _CACHE[(nl, noffn)] = KB(nl, noffn).build()
    return _CACHE[(nl, noffn)]


def kernel(x_prompt, x_sample, cache_mla_ckv, cache_mla_kr, cache_gqa_k, cache_gqa_v,
           cache_diff_k, cache_diff_v, c, c_ctx,
           ada_w, ada_b, norm_ffn1, norm_mix, norm_ffn2,
           ffn1_w_in, ffn1_w_out, ffn2_w_in, ffn2_w_out,
           mla_w_down, mla_g_q, mla_g_kv, mla_w_uq, mla_w_uk, mla_w_uv, mla_w_o,
           gqa_w_qkv, gqa_g_q, gqa_g_k, gqa_w_o,
           diff_w_qkv, diff_lq1, diff_lk1, diff_lq2, diff_lk2, diff_g_sub, diff_w_o,
           norm_final, _nl=4, _noffn=False, _trace=False):
    f = lambda a: np.asarray(a, dtype=np.float32)
    x_prompt, x_sample = f(x_prompt), f(x_sample)
    NL = _nl
    if 'probe_noffn' in KDBG:
        _noffn = True
    nc = _get_nc(NL, _noffn)
    n_mla = (NL + 2) // 3
    sh = {}
    ada_w = f(ada_w)
    sh["ada_t"] = np.stack([np.stack([_wt(ada_w[l][:, s * 256:(s + 1) * 256]) for s in range(36)]) for l in range(NL)])
    sh["ada_bT"] = np.ascontiguousarray(f(ada_b)[:NL].reshape(NL, 72, 128).transpose(2, 0, 1))
    ng = np.stack([f(norm_ffn1)[:NL], f(norm_mix)[:NL], f(norm_ffn2)[:NL]], axis=1)
    sh["normg"] = np.ascontiguousarray(ng.reshape(NL, 3, 8, 128).transpose(3, 0, 1, 2))
    sh["normf"] = np.ascontiguousarray(f(norm_final).reshape(8, 128).T)
    if not _noffn:
        win = np.stack([f(ffn1_w_in)[:NL], f(ffn2_w_in)[:NL]], axis=1)
        sh["ffn_in_t"] = np.ascontiguousarray(
            win.reshape(NL, 2, 8, 128, 2, 22, 128).transpose(0, 1, 5, 3, 2, 4, 6).reshape(NL, 2, 22, 128, 2048))
        wout = np.stack([f(ffn1_w_out)[:NL], f(ffn2_w_out)[:NL]], axis=1)
        sh["ffn_out_t"] = np.ascontiguousarray(
            wout.reshape(NL, 2, 22, 128, 8, 128).transpose(0, 1, 4, 3, 2, 5).reshape(NL, 2, 8, 128, 2816))
    consts = np.zeros((128, 3, 128), np.float32)
    consts[:, 0, :] = np.eye(128, dtype=np.float32)
    for jj in range(128):
        base = (jj // 64) * 64
        o = jj - base
        consts[base + (o + 32) % 64, 2, jj] = 1.0
    for jj in range(64, 96):
        o = jj - 64
        consts[64 + (o + 16) % 32, 1, jj] = 1.0
    sh["consts"] = consts
    wd = f(mla_w_down)[:n_mla]
    sh["mla_down_t"] = np.stack([np.stack([_pad(_wt(wd[j][:, 0:256]), 2048), _pad(_wt(wd[j][:, 256:512]), 2048),
                                           _pad(_wt(wd[j][:, 512:672]), 2048)]) for j in range(n_mla)])
    wuq = f(mla_w_uq)[:n_mla]
    sh["mla_uq_t"] = np.ascontiguousarray(
        wuq.reshape(n_mla, 3, 128, 4, 4, 96).transpose(0, 3, 2, 4, 1, 5).reshape(n_mla, 4, 128, 1152))
    wuk = f(mla_w_uk)[:n_mla].reshape(n_mla, 2, 128, 4, 4, 64)
    wuv = f(mla_w_uv)[:n_mla].reshape(n_mla, 2, 128, 4, 4, 64)
    ukv = np.concatenate([wuk, wuv], axis=-1)
    sh["mla_ukv_t"] = np.ascontiguousarray(ukv.transpose(0, 3, 2, 4, 1, 5).reshape(n_mla, 4, 128, 1024))
    wo = f(mla_w_o)[:n_mla]
    sh["mla_o_t"] = np.stack([np.stack([_wt(wo[j][:, m * 128:(m + 1) * 128]) for m in range(8)]) for j in range(n_mla)])
    gq = f(mla_g_q)[:n_mla].reshape(n_mla, 3, 128)
    gkv = f(mla_g_kv)[:n_mla].reshape(n_mla, 2, 128)
    sh["mla_g"] = np.ascontiguousarray(np.concatenate([gq, gkv], axis=1).transpose(2, 0, 1))
    if NL >= 2:
        wq = f(gqa_w_qkv)[0]
        units = [_wt(wq[:, u * 256:(u + 1) * 256]) for u in range(4)]
        for u in range(2):
            cols = []
            for h in range(2):
                kvh = u * 2 + h
                blk = wq[:, 1024 + kvh * 64:1024 + (kvh + 1) * 64]
                cols += [blk, blk]
            units.append(_wt(np.concatenate(cols, axis=1)))
        units.append(_wt(wq[:, 1280:1536]))
        sh["gqa_t"] = np.stack(units)
        go = f(gqa_w_o)[0]
        sh["gqa_o_t"] = np.stack([_wt(go[:, m * 128:(m + 1) * 128]) for m in range(8)])
        sh["gqa_g"] = np.ascontiguousarray(np.stack([np.tile(f(gqa_g_q)[0], 2), np.tile(f(gqa_g_k)[0], 2)], axis=1))
    if NL >= 3:
        wq = f(diff_w_qkv)[0]
        sh["diff_t"] = np.stack([_wt(wq[:, u * 256:(u + 1) * 256]) for u in range(12)])
        do = f(diff_w_o)[0]
        sh["diff_o_t"] = np.stack([_wt(do[:, m * 128:(m + 1) * 128]) for m in range(8)])
        sh["diff_l"] = np.ascontiguousarray(np.broadcast_to(np.stack([f(diff_lq1)[0], f(diff_lk1)[0], f(diff_lq2)[0], f(diff_lk2)[0]])[None], (128, 4, 64)))
        sh["diff_g"] = np.ascontiguousarray(f(diff_g_sub)[0].reshape(128, 1))

    in_maps = []
    for core in range(8):
        b, r = core // 4, core % 4
        m = dict(sh)
        xc = x_prompt[4 * core:4 * core + 4].reshape(1024, 1024)
        xl = x_sample[b, r * 1024:(r + 1) * 1024]
        m["xT"] = np.concatenate([_fm(xc), _fm(xl)], axis=2)
        m["cT"] = np.ascontiguousarray(np.stack([f(c_ctx), f(c)[b]], axis=1).reshape(8, 128, 2).transpose(1, 0, 2))
        pos = np.arange(r * 1024, (r + 1) * 1024)
        ca, sa = _rope_tables(pos, 32)
        ra = np.zeros((128, 2, 1024), np.float32)
        ra[64:96, 0], ra[64:96, 1] = ca, sa
        m["ropeA"] = ra
        cb, sb_ = _rope_tables(pos, 64)
        m["ropeB"] = np.ascontiguousarray(np.stack([np.tile(cb, (2, 1)), np.tile(sb_, (2, 1))], axis=1))
        ck = f(cache_mla_ckv)[b][:n_mla]
        m["mla_cckvT"] = np.ascontiguousarray(ck.transpose(0, 2, 1).reshape(n_mla, 2, 128, 512).transpose(0, 2, 1, 3))
        m["mla_ckrT"] = np.ascontiguousarray(f(cache_mla_kr)[b][:n_mla].transpose(0, 2, 1))
        if NL >= 2:
            gk = f(cache_gqa_k)[b, 0]
            kt = gk.transpose(1, 2, 0)
            m["gqa_ckT"] = np.ascontiguousarray(np.concatenate([kt, kt], axis=1))
            m["gqa_cv"] = np.ascontiguousarray(f(cache_gqa_v)[b, 0].reshape(512, 256))
        if NL >= 3:
            dk = f(cache_diff_k)[b, 0].reshape(512, 8, 128)
            m["diff_ckT"] = np.ascontiguousarray(dk.transpose(1, 2, 0))
            m["diff_cv"] = np.ascontiguousarray(f(cache_diff_v)[b, 0].reshape(512, 1024))
        in_maps.append(m)

    if _trace:
        res = run_bass_kernel_spmd(nc, in_maps, core_ids=list(range(8)), trace=True)
        _CACHE['last_res'] = res
    else:
        res = run_bass_kernel_spmd(nc, in_maps, core_ids=list(range(8)))
    R = res.results
    y_prompt = np.zeros((32, 256, 1024), np.float32)
    y_sample = np.zeros((2, 4096, 1024), np.float32)
    n_ckv = np.zeros((32, 2, 256, 256), np.float32)
    n_kr = np.zeros((32, 2, 256, 32), np.float32)
    n_gk = np.zeros((32, 1, 256, 4, 64), np.float32)
    n_gv = np.zeros((32, 1, 256, 4, 64), np.float32)
    n_dk = np.zeros((32, 1, 256, 8, 2, 64), np.float32)
    n_dv = np.zeros((32, 1, 256, 8, 128), np.float32)
    for core in range(8):
        b, r = core // 4, core % 4
        o = R[core]
        yt = np.asarray(o["yT_out"]).transpose(2, 1, 0).reshape(2048, 1024)
        y_prompt[4 * core:4 * core + 4] = yt[:1024].reshape(4, 256, 1024)
        y_sample[b, r * 1024:(r + 1) * 1024] = yt[1024:]
        ck = np.asarray(o["o_ckv"]).transpose(0, 3, 2, 1).reshape(2, 4, 256, 256)
        n_ckv[4 * core:4 * core + 4] = ck.transpose(1, 0, 2, 3)
        kr = np.asarray(o["o_kr"]).transpose(0, 2, 1).reshape(2, 4, 256, 32)
        n_kr[4 * core:4 * core + 4] = kr.transpose(1, 0, 2, 3)
        gk = np.asarray(o["o_gk"]).transpose(2, 0, 1).reshape(4, 256, 4, 64)
        n_gk[4 * core:4 * core + 4, 0] = gk
        n_gv[4 * core:4 * core + 4, 0] = np.asarray(o["o_gv"]).reshape(4, 256, 4, 64)
        dk = np.asarray(o["o_dk"]).transpose(2, 1, 0).reshape(4, 256, 8, 2, 64)
        n_dk[4 * core:4 * core + 4, 0] = dk
        n_dv[4 * core:4 * core + 4, 0] = np.asarray(o["o_dv"]).reshape(4, 256, 8, 128)
    return (y_prompt, y_sample, n_ckv, n_kr, n_gk, n_gv, n_dk, n_dv)
```

```python
import math
import os
from contextlib import ExitStack
KDBG = os.environ.get('KDBG', '').split(',')
import numpy as np
import concourse.bass as bass
import concourse.mybir as mybir
from concourse.bass_utils import run_bass_kernel_spmd

F32 = mybir.dt.float32
BF16 = mybir.dt.bfloat16
AF = mybir.ActivationFunctionType
ALU = mybir.AluOpType

ENGS = ("pe", "act", "dve", "pool", "sp")
ATTR = {"pe": "tensor", "act": "scalar", "dve": "vector", "pool": "gpsimd", "sp": "sync"}
SELF_RAW_DIST = 3
EPS = 1e-6
NSLOT = 3
SLOTW = 2816


class Buf:
    __slots__ = ("w", "r", "excl")

    def __init__(self, excl=False):
        self.w = None
        self.r = {}
        self.excl = excl


class BG:
    def __init__(self):
        self.d = {}

    def __call__(self, *key):
        b = self.d.get(key)
        if b is None:
            b = self.d[key] = Buf()
        return b


class DSem:
    def __init__(self, name, inc=16):
        self.name = name
        self.val = 0
        self.inc = inc
        self.h = None


class Op:
    __slots__ = ("fn", "deps", "marked", "count", "dsem")

    def __init__(self, fn, deps, dsem):
        self.fn = fn
        self.deps = deps
        self.marked = False
        self.count = 0
        self.dsem = dsem


class Sched:
    def __init__(self):
        self.ops = {e: [] for e in ENGS}
        self.clock = {e: {} for e in ENGS}
        self.hist = {e: [] for e in ENGS}
        self.dsems = []

    def dsem(self, name, inc=16):
        d = DSem(name, inc)
        self.dsems.append(d)
        return d

    def op(self, eng, fn, reads=(), writes=(), dsem=None):
        ops = self.ops[eng]
        clock = self.clock[eng]
        seq = len(ops)
        deps = {}

        def need(key, val, raw):
            if key == eng:
                if (not raw) or (seq - val > SELF_RAW_DIST and eng != "pool"):
                    return
                if clock.get(("self", eng), -1) >= val:
                    return
                clock[("self", eng)] = val
                deps[key] = max(deps.get(key, -1), val)
                return
            if clock.get(key, -1) >= val:
                return
            clock[key] = val
            deps[key] = val
            if isinstance(key, str):
                for k2, v2 in self.hist[key][val].items():
                    if isinstance(k2, tuple) or k2 == eng:
                        continue
                    if clock.get(k2, -1) < v2:
                        clock[k2] = v2

        for b in reads:
            if b.w is not None:
                need(b.w[0], b.w[1], True)
            if b.excl:
                for k, v in b.r.items():
                    need(k, v, False)
        for b in writes:
            if b.w is not None:
                need(b.w[0], b.w[1], False)
            for k, v in b.r.items():
                need(k, v, False)
        o = Op(fn, list(deps.items()), dsem)
        ops.append(o)
        self.hist[eng].append(dict(clock))
        if dsem is not None:
            dsem.val += dsem.inc
            key, val = dsem, dsem.val
        else:
            key, val = eng, seq
        for b in reads:
            b.r[key] = val
        for b in writes:
            b.w = (key, val)
            b.r = {}
        return o

    def finalize(self):
        for e in ENGS:
            for o in self.ops[e]:
                for k, v in o.deps:
                    if isinstance(k, str):
                        self.ops[k][v].marked = True
        for e in ENGS:
            c = 0
            for o in self.ops[e]:
                if o.marked:
                    c += 1
                o.count = c

    def emit(self, eng, e, sems):
        for o in self.ops[eng]:
            for k, v in o.deps:
                if isinstance(k, str):
                    e.wait_ge(sems[k], self.ops[k][v].count)
                else:
                    e.wait_ge(k.h, v)
            ins = o.fn(e)
            if o.dsem is not None:
                ins.then_inc(o.dsem.h, o.dsem.inc)
            elif o.marked:
                ins.then_inc(sems[eng], 1)


class KB:
    def __init__(self, nlayers=4, noffn=False):
        self.NL = nlayers
        self.noffn = noffn
        self.nc = bass.Bass("TRN2", target_bir_lowering=False)
        self.S = Sched()
        self.es = ExitStack()
        self.in_names = []
        self.out_names = []
        self.out_dsems = []

    def inp(self, name, shape, dt=F32):
        self.in_names.append(name)
        return self.nc.dram_tensor(name, list(shape), dt, kind="ExternalInput").ap()

    def outp(self, name, shape, dt=F32):
        self.out_names.append(name)
        return self.nc.dram_tensor(name, list(shape), dt, kind="ExternalOutput").ap()

    def dram(self, name, shape, dt):
        return self.nc.dram_tensor(name, list(shape), dt)

    def sb(self, name, shape, dt):
        return self.es.enter_context(self.nc.sbuf_tensor(name, list(shape), dt))

    def mm(self, out, lhsT, rhs, start=True, stop=True, R=(), W=()):
        self.S.op("pe", lambda e: e.matmul(out, lhsT=lhsT, rhs=rhs, start=start, stop=stop), R, W)

    def act(self, out, in_, func, R=(), W=(), bias=None, scale=None):
        kw = {}
        if bias is not None:
            kw["bias"] = bias
        if scale is not None:
            kw["scale"] = scale
        self.S.op("act", lambda e: e.activation(out=out, in_=in_, func=func, **kw), R, W)

    def tt(self, eng, out, in0, in1, op, R=(), W=()):
        self.S.op(eng, lambda e: e.tensor_tensor(out=out, in0=in0, in1=in1, op=op), R, W)

    def ts(self, eng, out, in0, s1, s2, op0, op1, R=(), W=()):
        self.S.op(eng, lambda e: e.tensor_scalar(out=out, in0=in0, scalar1=s1, scalar2=s2, op0=op0, op1=op1), R, W)

    def ts1(self, eng, out, in0, s1, op0, R=(), W=()):
        self.S.op(eng, lambda e: e.tensor_single_scalar(out=out, in_=in0, scalar=s1, op=op0), R, W)

    def stt(self, eng, out, in0, scalar, in1, op0, op1, R=(), W=()):
        self.S.op(eng, lambda e: e.scalar_tensor_tensor(out=out, in0=in0, scalar=scalar, in1=in1, op0=op0, op1=op1), R, W)

    def cp(self, eng, out, in_, R=(), W=()):
        if eng == "act":
            self.S.op("act", lambda e: e.activation(out=out, in_=in_, func=AF.Copy), R, W)
        else:
            self.S.op(eng, lambda e: e.tensor_copy(out=out, in_=in_), R, W)

    def memset(self, eng, ap, val, W=()):
        self.S.op(eng, lambda e: e.memset(ap, val), (), W)

    def dma(self, q, out, in_, ds, R=(), W=()):
        self.S.op(q, lambda e: e.dma_start(out=out, in_=in_), R, W, dsem=ds)

    def barrier_bufs(self, bufs):
        nb = Buf()
        for b in bufs:
            if b.w is not None:
                nb.r[b.w[0]] = max(nb.r.get(b.w[0], -1), b.w[1])
            for k, v in b.r.items():
                nb.r[k] = max(nb.r.get(k, -1), v)
        return nb

    def build(self):
        nc, S = self.nc, self.S
        NL = self.NL
        sb = self.sb
        xT_d = self.inp("xT", [128, 8, 2048])
        cT_d = self.inp("cT", [128, 8, 2])
        ada_d = self.inp("ada_t", [NL, 36, 128, 2048])
        adab_d = self.inp("ada_bT", [128, NL, 72])
        normg_d = self.inp("normg", [128, NL, 3, 8])
        normf_d = self.inp("normf", [128, 8])
        if not self.noffn:
            fin_d = self.inp("ffn_in_t", [NL, 2, 22, 128, 2048])
            fout_d = self.inp("ffn_out_t", [NL, 2, 8, 128, 2816])
        consts_d = self.inp("consts", [128, 3, 128])
        ropeA_d = self.inp("ropeA", [128, 2, 1024])
        ropeB_d = self.inp("ropeB", [128, 2, 1024])
        n_mla = (NL + 2) // 3
        has_gqa = NL >= 2
        has_diff = NL >= 3
        mla_down_d = self.inp("mla_down_t", [n_mla, 3, 128, 2048])
        mla_uq_d = self.inp("mla_uq_t", [n_mla, 4, 128, 1152])
        mla_ukv_d = self.inp("mla_ukv_t", [n_mla, 4, 128, 1024])
        mla_o_d = self.inp("mla_o_t", [n_mla, 8, 128, 1024])
        mla_g_d = self.inp("mla_g", [128, n_mla, 5])
        mla_cckv_d = self.inp("mla_cckvT", [n_mla, 128, 2, 512])
        mla_ckr_d = self.inp("mla_ckrT", [n_mla, 32, 512])
        if has_gqa:
            gqa_w_d = self.inp("gqa_t", [7, 128, 2048])
            gqa_o_d = self.inp("gqa_o_t", [8, 128, 1024])
            gqa_g_d = self.inp("gqa_g", [128, 2])
            gqa_ck_d = self.inp("gqa_ckT", [4, 128, 512])
            gqa_cv_d = self.inp("gqa_cv", [512, 256])
        if has_diff:
            diff_w_d = self.inp("diff_t", [12, 128, 2048])
            diff_o_d = self.inp("diff_o_t", [8, 128, 1024])
            diff_l_d = self.inp("diff_l", [128, 4, 64])
            diff_g_d = self.inp("diff_g", [128, 1])
            diff_ck_d = self.inp("diff_ckT", [8, 128, 512])
            diff_cv_d = self.inp("diff_cv", [512, 1024])

        yout_d = self.outp("yT_out", [128, 8, 2048])
        o_ckv_d = self.outp("o_ckv", [2, 128, 2, 1024])
        o_kr_d = self.outp("o_kr", [2, 32, 1024])
        o_gk_d = self.outp("o_gk", [4, 64, 1024])
        o_gv_d = self.outp("o_gv", [1024, 256])
        o_dk_d = self.outp("o_dk", [128, 8, 1024])
        o_dv_d = self.outp("o_dv", [1024, 1024])

        RG = [[0, 1, 2, 3], [4, 5, 6, 7]]
        gA_in = self.dram("gA_in", [288, 1024], BF16)
        gA_out = self.dram("gA_out", [4 * 288, 1024], BF16)
        gBk_in = self.dram("gBk_in", [512, 1024], BF16)
        gBk_out = self.dram("gBk_out", [4 * 512, 1024], BF16)
        gBv_in = self.dram("gBv_in", [1024, 256], BF16)
        gBv_out = self.dram("gBv_out", [4096, 256], BF16)
        gCk_in = [self.dram(f"gCk_in{i}", [256, 1024], BF16) for i in range(4)]
        gCk_out = [self.dram(f"gCk_out{i}", [1024, 1024], BF16) for i in range(4)]
        gCv_in = [self.dram(f"gCv_in{i}", [1024, 256], BF16) for i in range(4)]
        gCv_out = [self.dram(f"gCv_out{i}", [4096, 256], BF16) for i in range(4)]

        yT = sb("yT", [128, 8, 2048], F32)
        hT = sb("hT", [128, 8, 1024], BF16)
        QT = sb("QT", [128, 8, 1024], BF16)
        arena = sb("arena", [128, 24576], BF16)
        QTf = QT[:].rearrange("p c t -> p (c t)")
        slots = [sb(f"wslot{i}", [128, SLOTW], BF16) for i in range(NSLOT)]
        tmpF = [sb(f"tmpF{i}", [128, 512], F32) for i in range(4)]
        sqt = [sb(f"sqt{i}", [128, 512], BF16) for i in range(2)]
        rstd = sb("rstd", [128, 512], F32)
        rstd2 = sb("rstd2", [128, 512], F32)
        ropeA = sb("ropeA_s", [128, 2, 1024], F32)
        ropeB = sb("ropeB_s", [128, 2, 1024], F32)
        consts = sb("consts_s", [128, 3, 128], F32)
        onesb = sb("onesb", [128, 128], BF16)
        bd64 = sb("bd64", [128, 128], BF16)
        onesf = sb("onesf", [128, 128], F32)
        cT = sb("cT_s", [128, 8, 2], F32)
        actT = sb("actT", [128, 8, 2], BF16)
        adab = sb("adab", [128, NL, 72], F32)
        modT = sb("modT", [128, NL, 72, 2], F32)
        normg = sb("normg_s", [128, NL, 3, 8], F32)
        normf = sb("normf_s", [128, 8], F32)
        gsT = sb("gsT", [128, NL, 3, 2, 8], F32)
        hgT = sb("hgT", [128, NL, 2, 2, 8], F32)
        mlag = sb("mlag", [128, n_mla, 5], F32)
        gqag = sb("gqag", [128, 2], F32)
        diffg = sb("diffg", [128, 1], F32)
        diffl = sb("diffl", [128, 4, 64], F32)
        lamt = sb("lamt", [128, 8], F32)
        ptile = [sb(f"ptile{i}", [128, 512], BF16) for i in range(4)]
        stage = [sb(f"stage{i}", [128, 512], F32) for i in range(2)]
        pb = [self.es.enter_context(nc.psum_tensor(f"pb{i}", [128, 512], F32)) for i in range(8)]

        sems = {e: self.es.enter_context(nc.semaphore("s_" + e)) for e in ENGS}

        YB, HB = BG(), BG()
        PB = [Buf(excl=True) for _ in range(8)]
        WB = [Buf() for _ in range(NSLOT)]
        TMPB = [Buf() for _ in range(4)]
        SQB = [Buf() for _ in range(2)]
        PTB = [Buf() for _ in range(4)]
        STB = [Buf() for _ in range(2)]
        RSTD, CONST, MOD, ARENA = Buf(), Buf(), Buf(), Buf()
        RSTD2 = Buf()
        wds = [S.dsem(f"w{i}") for i in range(NSLOT)]
        d_in = S.dsem("in")
        d_misc = S.dsem("misc")
        d_out = S.dsem("out")
        d_st = [S.dsem(f"st{i}") for i in range(2)]
        d_kv = [S.dsem(f"kv{i}") for i in range(4)]
        d_g = S.dsem("gin")
        d_cc = S.dsem("cc", inc=1)
        self._abufs = []
        self._fence = {}

        def AB():
            b = Buf()
            b.r = dict(self._fence)
            self._abufs.append(b)
            return b

        class ABG(BG):
            def __call__(s_, *key):
                b = s_.d.get(key)
                if b is None:
                    b = s_.d[key] = AB()
                return b

        def arena_fence():
            f = dict(self._fence)
            for b in self._abufs:
                if b.w is not None:
                    f[b.w[0]] = max(f.get(b.w[0], -1), b.w[1])
                for k, v in b.r.items():
                    f[k] = max(f.get(k, -1), v)
            self._fence = f
            self._abufs = []

        d_tmp = [S.dsem(f"tmpo{i}") for i in range(4)]
        GDS = {n: S.dsem("g_" + n) for n in ("a", "bk", "bv", "ck", "cv")}
        self._wi = 0
        self._bank = 0
        self._srot = 0
        self._prot = 0
        self._tmp = 0
        self._stg = 0

        def load_w(dram_ap, n):
            s = self._wi % NSLOT
            self._wi += 1
            self.dma("pool", slots[s][:, 0:n], dram_ap, wds[s], W=[WB[s]])
            return slots[s], WB[s]

        def bank(cands=(5, 6, 7)):
            i = cands[self._bank % len(cands)]
            self._bank += 1
            return i

        def tmpi():
            i = self._tmp % 4
            self._tmp += 1
            return i

        def stg():
            i = self._stg % 2
            self._stg += 1
            return i

        self.dma("sp", yT[:], xT_d, d_in, W=[YB(c, g, b) for c in range(8) for g in range(2) for b in range(2)])
        for (t, d) in ((cT, cT_d), (adab, adab_d), (normg, normg_d), (normf, normf_d), (ropeA, ropeA_d),
                       (ropeB, ropeB_d), (consts, consts_d), (mlag, mla_g_d)):
            self.dma("sp", t[:], d, d_misc, W=[CONST])
        if has_gqa:
            self.dma("sp", gqag[:], gqa_g_d, d_misc, W=[CONST])
        if has_diff:
            self.dma("sp", diffg[:], diff_g_d, d_misc, W=[CONST])
            self.dma("sp", diffl[:], diff_l_d, d_misc, W=[CONST])
        self.memset("dve", onesb[:], 1.0, W=[CONST])
        self.memset("dve", onesf[:], 1.0, W=[CONST])
        self.memset("dve", bd64[:], 0.0, W=[CONST])
        self.memset("dve", bd64[0:64, 0:64], 1.0, W=[CONST])
        self.memset("dve", bd64[64:128, 64:128], 1.0, W=[CONST])
        self.act(actT[:], cT[:], AF.Silu, R=[CONST], W=[MOD])
        permA = consts[:, 1, :]
        permB = consts[:, 2, :]

        MODL = [Buf() for _ in range(NL)]
        cur = {"l": 0}

        def ada_gen(l):
            for s in range(36):
                slot, WBs = load_w(ada_d[l, s], 2048)
                wv = slot[:, 0:2048].rearrange("p (k n) -> p k n", k=8)
                for h in range(2):
                    ch = s * 2 + h
                    for k in range(8):
                        self.mm(pb[7][:, ch * 2:ch * 2 + 2], wv[:, k, h * 128:(h + 1) * 128], actT[:, k, :],
                                start=(k == 0), stop=(k == 7), R=[WBs, MOD], W=[PB[7]])
                yield
            pv = pb[7][:, 0:144].rearrange("p (c s) -> p c s", s=2)
            ML = MODL[l]
            for st in range(2):
                self.tt("dve", modT[:, l, :, st], pv[:, :, st], adab[:, l, :], ALU.add, R=[PB[7], CONST], W=[ML])
            for sub in range(3):
                for st in range(2):
                    sc = modT[:, l, (3 * sub + 1) * 8:(3 * sub + 2) * 8, st]
                    self.stt("dve", gsT[:, l, sub, st, :], sc, 1.0, normg[:, l, sub, :], ALU.add, ALU.mult, R=[ML, CONST], W=[ML])
                    self.ts1("dve", gsT[:, l, sub, st, :], gsT[:, l, sub, st, :], 32.0, ALU.mult, R=[ML], W=[ML])
            for w in range(2):
                for st in range(2):
                    gt = modT[:, l, (6 * w + 2) * 8:(6 * w + 3) * 8, st]
                    self.ts1("dve", hgT[:, l, w, st, :], gt, 0.5, ALU.mult, R=[ML], W=[ML])
            yield

        for _ in ada_gen(0):
            pass
        self._adag = None

        def ada_step():
            if self._adag is not None:
                try:
                    next(self._adag)
                except StopIteration:
                    self._adag = None

        self.ts1("dve", normf[:], normf[:], 32.0, ALU.mult, R=[CONST], W=[CONST])
        for j in range(n_mla):
            self.ts1("dve", mlag[:, j, 0:3], mlag[:, j, 0:3], math.sqrt(384.0), ALU.mult, R=[CONST], W=[CONST])
            self.ts1("dve", mlag[:, j, 3:5], mlag[:, j, 3:5], 16.0, ALU.mult, R=[CONST], W=[CONST])
        if has_gqa:
            self.ts1("dve", gqag[:], gqag[:], 8.0, ALU.mult, R=[CONST], W=[CONST])
        if has_diff:
            lam_init = 0.8 - 0.6 * math.exp(-0.3 * 2)
            self.ts1("dve", diffg[:], diffg[:], math.sqrt(128.0) * (1.0 - lam_init), ALU.mult, R=[CONST], W=[CONST])
        if has_diff and 'dnolam' not in KDBG:
            self.S.op("dve", lambda e: e.tensor_tensor(out=diffl[:, 0, :], in0=diffl[:, 0, :], in1=diffl[:, 1, :], op=ALU.mult), [CONST], [CONST])
            self.S.op("dve", lambda e: e.tensor_tensor(out=diffl[:, 2, :], in0=diffl[:, 2, :], in1=diffl[:, 3, :], op=ALU.mult), [CONST], [CONST])
            self.S.op("dve", lambda e: e.reduce_sum(out=lamt[:, 0:1], in_=diffl[:, 0, :], axis=mybir.AxisListType.X), [CONST], [CONST])
            self.S.op("dve", lambda e: e.reduce_sum(out=lamt[:, 1:2], in_=diffl[:, 2, :], axis=mybir.AxisListType.X), [CONST], [CONST])
            self.act(lamt[:, 2:4], lamt[:, 0:2], AF.Exp, R=[CONST], W=[CONST])
            self.tt("dve", lamt[:, 4:5], lamt[:, 3:4], lamt[:, 2:3], ALU.subtract, R=[CONST], W=[CONST])
            self.ts1("dve", lamt[:, 5:6], lamt[:, 4:5], -lam_init, ALU.add, R=[CONST], W=[CONST])

        def rsqrt(dst, src, addc, n, R, W):
            ti = tmpi()
            self.act(tmpF[ti][:, 0:n], src, AF.Sqrt, bias=addc, R=R, W=[TMPB[ti]])
            self.S.op("dve", lambda e: e.reciprocal(out=dst, in_=tmpF[ti][:, 0:n]), [TMPB[ti]], W)

        def gs_ap(l, sub, st):
            return lambda c: gsT[:, l, sub, st, c:c + 1]

        def sh_ap(l, sub, st):
            return lambda c: modT[:, l, (3 * sub) * 8 + c, st:st + 1]

        def sumsq(src_fn, nch, n, srcR):
            bi = bank((5, 6))
            for c in range(nch):
                sq = sqt[c % 2]
                self.act(sq[:, 0:n], src_fn(c), AF.Square, R=srcR(c), W=[SQB[c % 2]])
                self.mm(pb[bi][:, 0:n], onesb[:], sq[:, 0:n], start=(c == 0), stop=(c == nch - 1), R=[SQB[c % 2], CONST], W=[PB[bi]])
            return bi

        def normmod(g, gs, sh):
            for blk in range(2):
                t0 = g * 1024 + blk * 512
                bi = sumsq(lambda c: yT[:, c, t0:t0 + 512], 8, 512, lambda c: [YB(c, g, blk)])
                rsqrt(rstd[:], pb[bi][:], 1024.0 * EPS, 512, [PB[bi]], [RSTD])
                for c in range(8):
                    ti = tmpi()
                    self.stt("dve", tmpF[ti][:], yT[:, c, t0:t0 + 512], gs(c), rstd[:], ALU.mult, ALU.mult,
                             R=[YB(c, g, blk), RSTD, MODL[cur['l']]], W=[TMPB[ti]])
                    self.act(hT[:, c, blk * 512:(blk + 1) * 512], tmpF[ti][:], AF.Identity, bias=sh(c),
                             R=[TMPB[ti], MODL[cur['l']]], W=[HB(c, blk)])

        uT = arena[:, 0:22 * 1024].rearrange("p (j t) -> p j t", j=22)

        def ffn(g, l, w):
            if self.noffn:
                return
            sub = 0 if w == 0 else 2
            arena_fence()
            UB = ABG()
            normmod(g, gs_ap(l, sub, g), sh_ap(l, sub, g))
            for j in range(22):
                slot, WBs = load_w(fin_d[l, w, j], 2048)
                wv = slot[:, 0:2048].rearrange("p (k s n) -> p k s n", k=8, s=2)
                for blk in range(2):
                    pi = (j * 2 + blk) % 2
                    for s in range(2):
                        bi = 2 * pi + s
                        for k in range(8):
                            self.mm(pb[bi][:], wv[:, k, s, :], hT[:, k, blk * 512:(blk + 1) * 512], start=(k == 0), stop=(k == 7),
                                    R=[WBs, HB(k, blk)], W=[PB[bi]])
                    ti = tmpi()
                    self.act(tmpF[ti][:], pb[2 * pi][:], AF.Silu, R=[PB[2 * pi]], W=[TMPB[ti]])
                    self.tt("dve", uT[:, j, blk * 512:(blk + 1) * 512], pb[2 * pi + 1][:], tmpF[ti][:], ALU.mult,
                            R=[PB[2 * pi + 1], TMPB[ti]], W=[UB(j, blk)])
                if w == 0:
                    ada_step()
            for m in range(8):
                slot, WBs = load_w(fout_d[l, w, m], 2816)
                wv = slot[:, 0:2816].rearrange("p (k n) -> p k n", k=22)
                for blk in range(2):
                    bi = 4 + (m * 2 + blk) % 2
                    for k in range(22):
                        self.mm(pb[bi][:], wv[:, k, :], uT[:, k, blk * 512:(blk + 1) * 512], start=(k == 0), stop=(k == 21),
                                R=[WBs, UB(k, blk)], W=[PB[bi]])
                    t0 = g * 1024 + blk * 512
                    self.stt("dve", yT[:, m, t0:t0 + 512], pb[bi][:], hgT[:, l, w, g, m:m + 1], yT[:, m, t0:t0 + 512], ALU.mult, ALU.add,
                             R=[PB[bi], YB(m, g, blk), MODL[cur['l']]], W=[YB(m, g, blk)])

        def proj(wv_fn, kc, rhs_fn, M, n, R, cands=(5, 6, 7)):
            bi = bank(cands)
            for k in range(kc):
                self.mm(pb[bi][0:M, 0:n], wv_fn(k), rhs_fn(k), start=(k == 0), stop=(k == kc - 1), R=R(k), W=[PB[bi]])
            return bi

        def rope_combine(A, ABuf, lo, hi, perm, table, t0, n, dst, dstW, cands=(5, 6, 7)):
            bi = bank(cands)
            self.mm(pb[bi][0:hi, 0:n], perm[lo:hi, 0:hi], A[lo:hi, 0:n], R=[ABuf, CONST], W=[PB[bi]])
            t2 = tmpi()
            self.tt("dve", tmpF[t2][lo:hi, 0:n], pb[bi][lo:hi, 0:n], table[lo:hi, 1, t0:t0 + n], ALU.mult, R=[PB[bi], CONST], W=[TMPB[t2]])
            self.tt("dve", A[lo:hi, 0:n], A[lo:hi, 0:n], table[lo:hi, 0, t0:t0 + n], ALU.mult, R=[ABuf, CONST], W=[ABuf])
            self.tt("pool", dst, A[lo:hi, 0:n], tmpF[t2][lo:hi, 0:n], ALU.add, R=[ABuf, TMPB[t2]], W=dstW)

        self._pending = []

        def flush_pending():
            p, self._pending = self._pending, []
            for fn in p:
                fn()

        self._units = None

        def begin_units(warm=0):
            self._units = []

        def end_units():
            us, self._units = self._units, None
            run_units(us)

        def attend(KT, QT_ap, nkb, nq, V_fn, vM, po, scale, RK, RQ, RV, zb=None):
            u = dict(KT=[KT(kb) for kb in range(nkb)], Q=QT_ap, nkb=nkb, nq=nq, V=[V_fn(kb) for kb in range(nkb)], vM=vM, po=po,
                     scale=scale, RK=RK, RQ=RQ, RV=RV, zb=zb, fin=None, imm=False)
            if self._units is not None:
                self._units.append(u)
            else:
                run_units([u])

        def attach_fin(fn, imm=False):
            if self._units:
                self._units[-1]["fin"] = fn
                self._units[-1]["imm"] = imm
            elif imm:
                fn()
            else:
                self._pending.append(fn)

        def run_units(units):
            G = 2

            def issue_qk(u, grp):
                sbanks = (0, 1, 2, 7) if u["zb"] is not None else (0, 1, 2, 5, 6)
                out = {}
                for kb in range(grp * G, min(u["nkb"], (grp + 1) * G)):
                    si = sbanks[self._srot % len(sbanks)]
                    self._srot += 1
                    out[kb] = si
                    self.mm(pb[si][:, 0:u["nq"]], u["KT"][kb], u["Q"], R=u["RK"] + u["RQ"], W=[PB[si]])
                return out

            pre = None
            for ui, u in enumerate(units):
                nxt = units[ui + 1] if ui + 1 < len(units) else None
                nkb, nq, po, zb = u["nkb"], u["nq"], u["po"], u["zb"]
                sbank = pre if pre is not None else issue_qk(u, 0)
                pre = None
                ngrp = (nkb + G - 1) // G
                for grp in range(1, ngrp + 1):
                    if grp < ngrp:
                        sbank.update(issue_qk(u, grp))
                    elif nxt is not None and (nxt["zb"] is None) == (zb is None):
                        pre = issue_qk(nxt, 0)
                    kbs = list(range((grp - 1) * G, min(nkb, grp * G)))
                    pis = []
                    for kb in kbs:
                        si = sbank.pop(kb)
                        pi = self._prot % 4
                        self._prot += 1
                        pis.append(pi)
                        self.act(ptile[pi][:, 0:nq], pb[si][:, 0:nq], AF.Exp, scale=u["scale"], R=[PB[si]], W=[PTB[pi]])
                    for kb, pi in zip(kbs, pis):
                        self.mm(pb[po][0:u["vM"], 0:nq], u["V"][kb], ptile[pi][:, 0:nq], start=(kb == 0), stop=(kb == nkb - 1),
                                R=[PTB[pi]] + u["RV"], W=[PB[po]])
                    if zb is not None:
                        for kb, pi in zip(kbs, pis):
                            self.mm(pb[zb][:, 0:nq], onesb[:], ptile[pi][:, 0:nq], start=(kb == 0), stop=(kb == nkb - 1),
                                    R=[PTB[pi], CONST], W=[PB[zb]])
                    if grp == 1:
                        flush_pending()
                if u["fin"] is not None:
                    if u["imm"]:
                        u["fin"]()
                    else:
                        self._pending.append(u["fin"])

        def finish_head(po, par, nq, dst, dstW):
            attach_fin(lambda: finish_head_now(po, par, nq, dst, dstW))

        def finish_head_now(po, par, nq, dst, dstW):
            zr = 64 if par == 0 else 0
            lo = 0 if par == 0 else 64
            ti = tmpi()
            self.S.op("dve", lambda e: e.reciprocal(out=tmpF[ti][zr:zr + 1, 0:nq], in_=pb[po][zr:zr + 1, 0:nq]), [PB[po]], [TMPB[ti]])
            bi = bank((7,))
            self.mm(pb[bi][:, 0:nq], onesf[zr:zr + 1, :], tmpF[ti][zr:zr + 1, 0:nq], R=[TMPB[ti], CONST], W=[PB[bi]])
            t2 = tmpi()
            self.cp("act", tmpF[t2][lo:lo + 64, 0:nq], pb[bi][lo:lo + 64, 0:nq], R=[PB[bi]], W=[TMPB[t2]])
            self.tt("dve", dst, pb[po][lo:lo + 64, 0:nq], tmpF[t2][lo:lo + 64, 0:nq], ALU.mult, R=[PB[po], TMPB[t2]], W=dstW)

        def oproj(g, l, o_d):
            flush_pending()
            for m in range(8):
                slot, WBs = load_w(o_d[m], 1024)
                wv = slot[:, 0:1024].rearrange("p (k n) -> p k n", k=8)
                for blk in range(2):
                    bi = proj(lambda k: wv[:, k, :], 8, lambda k: hT[:, k, blk * 512:(blk + 1) * 512], 128, 512,
                              lambda k: [WBs, HB(k, blk)], cands=(4, 5))
                    t0 = g * 1024 + blk * 512
                    self.stt("dve", yT[:, m, t0:t0 + 512], pb[bi][:], modT[:, l, 5 * 8 + m, g:g + 1], yT[:, m, t0:t0 + 512], ALU.mult, ALU.add,
                             R=[PB[bi], YB(m, g, blk), MODL[cur['l']]], W=[YB(m, g, blk)])

        def allgather(src, dst, RB, WBf):
            self.S.op("pool", lambda e: e.collective_compute("AllGather", ALU.bypass, replica_groups=RG, ins=[src[:, :]], outs=[dst[:, :]]),
                      RB, WBf, dsem=d_cc)

        A_KT = [arena[:, 0:4608], arena[:, 4608:9216]]
        A_V = [arena[:, 9216:13824].rearrange("p (k d) -> p k d", k=36), arena[:, 13824:18432].rearrange("p (k d) -> p k d", k=36)]
        GIN, GOUT = BG(), BG()

        def run_pipelined(items):
            active = []
            for g in items:
                for a in list(active):
                    try:
                        next(a)
                    except StopIteration:
                        active.remove(a)
                try:
                    next(g)
                    active.append(g)
                except StopIteration:
                    pass
            while active:
                for a in list(active):
                    try:
                        next(a)
                    except StopIteration:
                        active.remove(a)

        def gqa(l):
            scale = 64 ** -0.5

            self._qi = 0

            def qk_chunk(wv_half, WBs, g, blk, gain, rope, dst, dstW, outfp=None, dst2=None):
                idx = self._qi
                self._qi += 1
                sq, SQ = sqt[idx % 2], SQB[idx % 2]
                rs, RS = (rstd, RSTD) if idx % 2 == 0 else (rstd2, RSTD2)
                bi = proj(lambda k: wv_half(k), 8, lambda k: hT[:, k, blk * 512:(blk + 1) * 512], 128, 512, lambda k: [WBs, HB(k, blk)],
                          cands=(3, 4, 5, 6, 7))
                self.act(sq[:], pb[bi][:], AF.Square, R=[PB[bi]], W=[SQ])
                yield
                b2 = bank((3, 4, 5, 6, 7))
                self.mm(pb[b2][:], bd64[:], sq[:], R=[SQ, CONST], W=[PB[b2]])
                rsqrt(rs[:], pb[b2][:], 64.0 * EPS, 512, [PB[b2]], [RS])
                ti = tmpi()
                self.stt("dve", tmpF[ti][:], pb[bi][:], gain, rs[:], ALU.mult, ALU.mult, R=[PB[bi], RS, CONST], W=[TMPB[ti]])
                if outfp is not None:
                    self.dma("sp", outfp, tmpF[ti][0:64, :], d_tmp[ti], R=[TMPB[ti]])
                yield
                if rope and 'norope' not in KDBG:
                    rope_combine(tmpF[ti], TMPB[ti], 0, 128, permB, ropeB, blk * 512, 512, dst, dstW, cands=(3, 4, 5, 6, 7))
                elif dst2 is not None:
                    self.cp("pool", dst, tmpF[ti][0:64, :], R=[TMPB[ti]], W=dstW)
                    self.cp("pool", dst2, tmpF[ti][64:128, :], R=[TMPB[ti]], W=dstW)
                else:
                    self.cp("pool", dst, tmpF[ti][:], R=[TMPB[ti]], W=dstW)

            def run_group(g):
                lat = (g == 1)
                arena_fence()
                KTB = [AB(), AB()]
                QB = ABG()
                normmod(g, gs_ap(l, 1, g), sh_ap(l, 1, g))
                if lat:
                    klat = arena[:, 18432:18432 + 4096].rearrange("p (c t) -> p c t", c=4)
                    KL = AB()
                else:
                    KTc = arena[:, 0:4096].rearrange("p (c t) -> p c t", c=4)
                    KTc1 = arena[:, 12288:16384].rearrange("p (c t) -> p c t", c=4)
                    VE = arena[:, 4096:4096 + 8 * 4 * 66].rearrange("p (k h d) -> p k h d", k=8, h=4)
                    VO = arena[:, 8192:8192 + 8 * 4 * 128].rearrange("p (k h d) -> p k h d", k=8, h=4)
                    KC_, VC_ = ABG(), AB()
                    for kv_ in range(4):
                        self.memset("pool", KTc[64:128, kv_, :], 0.0, W=[KC_(kv_)])
                        self.memset("pool", KTc1[0:64, kv_, :], 0.0, W=[KC_(kv_)])
                    if 'nomemset' not in KDBG:
                        self.memset("pool", VE[:, :, :, 64:65], 1.0, W=[VC_])
                        self.memset("pool", VO[:, :, :, 0:64], 0.0, W=[VC_])
                        self.memset("pool", VO[:, :, :, 0:1], 1.0, W=[VC_])
                def k_items():
                    for u in range(0 if 'nok' in KDBG else 2):
                        slot, WBs = load_w(gqa_w_d[4 + u], 2048)
                        wv = slot[:, 0:2048].rearrange("p (k n) -> p k n", k=8)
                        for h in range(2):
                            kvh = u * 2 + h
                            for blk in range(2):
                                if lat:
                                    yield qk_chunk(lambda k, wv=wv, h=h: wv[:, k, h * 128:(h + 1) * 128], WBs, g, blk, gqag[:, 1:2], True,
                                                   klat[:, kvh, blk * 512:(blk + 1) * 512], [KL])
                                else:
                                    yield qk_chunk(lambda k, wv=wv, h=h: wv[:, k, h * 128:(h + 1) * 128], WBs, g, blk, gqag[:, 1:2], False,
                                                   KTc[0:64, kvh, blk * 512:(blk + 1) * 512], [KC_(kvh)],
                                                   outfp=o_gk_d[kvh, :, blk * 512:(blk + 1) * 512],
                                                   dst2=KTc1[64:128, kvh, blk * 512:(blk + 1) * 512])
                run_pipelined(k_items())
                slot, WBs = load_w(gqa_w_d[6], 2048)
                wv = slot[:, 0:2048].rearrange("p (k n) -> p k n", k=8)
                if lat:
                    vlat = arena[:, 22528:22528 + 2048].rearrange("p (k d) -> p k d", k=8)
                    VL = AB()
                for tb in range(0 if 'nov' in KDBG else 8):
                    bi = proj(lambda k: hT[:, k, tb * 128:(tb + 1) * 128], 8, lambda k: wv[:, k, :], 128, 256,
                              lambda k: [WBs, HB(k, tb // 4)])
                    if lat:
                        self.cp("dve", vlat[:, tb, :], pb[bi][:, 0:256], R=[PB[bi]], W=[VL])
                    else:
                        si = stg()
                        if 'v1' not in KDBG:
                            self.cp("act", stage[si][:, 0:256], pb[bi][:, 0:256], R=[PB[bi]], W=[STB[si]])
                        if 'v2' not in KDBG:
                            self.dma("sp", o_gv_d[tb * 128:(tb + 1) * 128, :], stage[si][:, 0:256], d_st[si], R=[STB[si]])
                        pv = pb[bi][:, 0:256].rearrange("p (h d) -> p h d", h=4)
                        if 'v3' not in KDBG:
                            self.cp("act", VE[:, tb, :, 0:64], pv, R=[PB[bi]], W=[VC_])
                            self.cp("act", VO[:, tb, :, 64:128], pv, R=[PB[bi]], W=[VC_])
                if lat and 'nogather' not in KDBG:
                    self.dma("sp", gBk_in.ap().rearrange("(c p) t -> p c t", p=128), klat, GDS["bk"], R=[KL], W=[GIN("bk")])
                    self.dma("sp", gBv_in.ap().rearrange("(k p) d -> p k d", p=128), vlat, GDS["bv"], R=[VL], W=[GIN("bv")])
                    allgather(gBk_in, gBk_out, [GIN("bk")], [GOUT("bk")])
                    allgather(gBv_in, gBv_out, [GIN("bv")], [GOUT("bv")])
                def q_items():
                    for u in range(0 if 'noq' in KDBG else 4):
                        slot, WBs = load_w(gqa_w_d[u], 2048)
                        wv = slot[:, 0:2048].rearrange("p (k n) -> p k n", k=8)
                        for h in range(2):
                            m = u * 2 + h
                            for blk in range(2):
                                yield qk_chunk(lambda k, wv=wv, h=h: wv[:, k, h * 128:(h + 1) * 128], WBs, g, blk, gqag[:, 0:1], lat,
                                               QT[:, m, blk * 512:(blk + 1) * 512], [QB(m, blk)])
                run_pipelined(q_items())
                if (not lat) and 'noctxattn' in KDBG:
                    pass
                elif not lat:
                    begin_units()
                    for s in range(4):
                        for hd in range(16):
                            kvh, par, m = hd // 4, hd % 2, hd // 2
                            r0 = par * 64
                            po = 3 + (hd % 2)
                            Vt = VE if par == 0 else VO
                            vM = 65 if par == 0 else 128
                            Kz = KTc if par == 0 else KTc1
                            attend(lambda kb: Kz[:, kvh, s * 256 + kb * 128:s * 256 + (kb + 1) * 128],
                                   QT[:, m, s * 256:(s + 1) * 256], 2, 256,
                                   lambda kb: Vt[:, s * 2 + kb, kvh, 0:vM], vM, po, scale,
                                   [KC_(kvh)], [QB(m, s // 2)], [VC_])
                            finish_head(po, par, 256, hT[r0:r0 + 64, m, s * 256:(s + 1) * 256], [HB(m, s // 2)])
                    end_units()
                elif 'nolatattn' in KDBG:
                    pass
                else:
                    VEl = arena[:, 9216:9216 + 36 * 66].rearrange("p (k d) -> p k d", k=36)
                    VOl = arena[:, 13824:18432].rearrange("p (k d) -> p k d", k=36)
                    VEB, VOB = AB(), AB()
                    self.memset("pool", VEl[:, :, 64:65], 1.0, W=[VEB])
                    self.memset("pool", VOl[:, :, 0:64], 0.0, W=[VOB])
                    self.memset("pool", VOl[:, :, 0:1], 1.0, W=[VOB])
                    gk = gBk_out.ap().rearrange("(r c p) t -> c p r t", r=4, c=4)
                    gv = gBv_out.ap().rearrange("(k p) (h d) -> h p k d", p=128, h=4)
                    cv = gqa_cv_d.rearrange("(k p) (h d) -> h p k d", p=128, h=4)
                    self.memset("pool", A_KT[0][64:128, :], 0.0, W=[KTB[0]])
                    self.memset("pool", A_KT[1][0:64, :], 0.0, W=[KTB[1]])
                    for kvh in range(4):
                        for ks in range(2):
                            rs = slice(ks * 64, ks * 64 + 64)
                            self.dma("sp", A_KT[ks][rs, 0:4096].rearrange("p (r t) -> p r t", r=4), gk[kvh][rs], d_kv[ks], R=[GOUT("bk")], W=[KTB[ks]])
                            self.dma("pool", A_KT[ks][rs, 4096:4608], gqa_ck_d[kvh][rs], d_kv[ks], W=[KTB[ks]])
                        self.dma("sp", VEl[:, 0:32, 0:64], gv[kvh], d_kv[2], R=[GOUT("bv")], W=[VEB])
                        self.dma("pool", VEl[:, 32:36, 0:64], cv[kvh], d_kv[2], W=[VEB])
                        self.dma("sp", VOl[:, 0:32, 64:128], gv[kvh], d_kv[3], R=[GOUT("bv")], W=[VOB])
                        self.dma("pool", VOl[:, 32:36, 64:128], cv[kvh], d_kv[3], W=[VOB])
                        begin_units()
                        for hh in range(4):
                            hd = kvh * 4 + hh
                            par, m = hd % 2, hd // 2
                            r0 = par * 64
                            Vt = VEl if par == 0 else VOl
                            VtB = VEB if par == 0 else VOB
                            vM = 65 if par == 0 else 128
                            for qb in range(2):
                                po = 3 + (qb % 2)
                                attend(lambda kb: A_KT[par][:, kb * 128:(kb + 1) * 128],
                                       QT[:, m, qb * 512:(qb + 1) * 512], 36, 512,
                                       lambda kb: Vt[:, kb, 0:vM], vM, po, scale, [KTB[par]], [QB(m, qb)], [VtB])
                                finish_head(po, par, 512, hT[r0:r0 + 64, m, qb * 512:(qb + 1) * 512], [HB(m, qb)])
                        end_units()
                if 'noo' not in KDBG:
                    oproj(g, l, gqa_o_d)

            run_group(0)
            run_group(1)

        def diff(l):
            scale = 64 ** -0.5

            def run_group(g):
                lat = (g == 1)
                arena_fence()
                KTB = [AB(), AB()]
                VB = [AB(), AB()]
                QB = ABG()
                normmod(g, gs_ap(l, 1, g), sh_ap(l, 1, g))
                if lat:
                    KL = AB()
                    VL = AB()
                    klat = arena[:, 18432:18432 + 1024]
                    vlat = arena[:, 19456:19456 + 2048].rearrange("p (k d) -> p k d", k=8)
                else:
                    KTc = arena[:, 0:8192].rearrange("p (c t) -> p c t", c=8)
                    KTc1 = arena[:, 16384:24576].rearrange("p (c t) -> p c t", c=8)
                    Vc = arena[:, 8192:16384].rearrange("p (k d) -> p k d", k=8)
                    KC_, VC_ = ABG(), ABG()
                    for c_ in range(8):
                        self.memset("pool", KTc[64:128, c_, :], 0.0, W=[KC_(c_)])
                        self.memset("pool", KTc1[0:64, c_, :], 0.0, W=[KC_(c_)])

                def qk_item(wv, WBs, h, ch, blk, isq):
                    bi = proj(lambda k: wv[:, k, h * 128:(h + 1) * 128], 8, lambda k: hT[:, k, blk * 512:(blk + 1) * 512], 128, 512,
                              lambda k: [WBs, HB(k, blk)], cands=(3, 4, 5, 6, 7))
                    if isq:
                        dst, dstW = QT[:, ch, blk * 512:(blk + 1) * 512], [QB(ch, blk)]
                    elif lat:
                        dst, dstW = klat[:, blk * 512:(blk + 1) * 512], [KL]
                    else:
                        dst, dstW = KTc[:, ch, blk * 512:(blk + 1) * 512], [KC_(ch)]
                    if lat:
                        ti = tmpi()
                        self.cp("act", tmpF[ti][:], pb[bi][:], R=[PB[bi]], W=[TMPB[ti]])
                        yield
                        rope_combine(tmpF[ti], TMPB[ti], 0, 128, permB, ropeB, blk * 512, 512, dst, dstW, cands=(3, 4, 5, 6, 7))
                        if (not isq) and blk == 1:
                            self.dma("sp", gCk_in[ch // 2][(ch % 2) * 128:(ch % 2 + 1) * 128, :], klat, GDS["ck"], R=[KL], W=[GIN("ck", ch // 2)])
                            if ch % 2 == 1 and 'dnogather' not in KDBG:
                                allgather(gCk_in[ch // 2], gCk_out[ch // 2], [GIN("ck", ch // 2)], [GOUT("ck", ch // 2)])
                    else:
                        if isq:
                            self.cp("act", dst, pb[bi][:], R=[PB[bi]], W=dstW)
                        else:
                            self.cp("act", KTc[0:64, ch, blk * 512:(blk + 1) * 512], pb[bi][0:64, :], R=[PB[bi]], W=dstW)
                            self.cp("act", KTc1[64:128, ch, blk * 512:(blk + 1) * 512], pb[bi][64:128, :], R=[PB[bi]], W=dstW)
                            si = stg()
                            self.cp("dve", stage[si][:], pb[bi][:], R=[PB[bi]], W=[STB[si]])
                            self.dma("sp", o_dk_d[:, ch, blk * 512:(blk + 1) * 512], stage[si][:], d_st[si], R=[STB[si]])

                def qk_items(us, isq):
                    for u in us:
                        slot, WBs = load_w(diff_w_d[u], 2048)
                        wv = slot[:, 0:2048].rearrange("p (k n) -> p k n", k=8)
                        for h in range(2):
                            ch = (u % 4) * 2 + h
                            for blk in range(2):
                                yield qk_item(wv, WBs, h, ch, blk, isq)

                def qk_unit(u, isq):
                    run_pipelined(qk_items([u], isq))

                run_pipelined(qk_items(range(4, 4 if 'dnok' in KDBG else 8), False))
                for u in range(8, 8 if 'dnov' in KDBG else 12):
                    slot, WBs = load_w(diff_w_d[u], 2048)
                    wv = slot[:, 0:2048].rearrange("p (k n) -> p k n", k=8)
                    for tb in range(8):
                        bi = proj(lambda k: hT[:, k, tb * 128:(tb + 1) * 128], 8, lambda k: wv[:, k, :], 128, 256,
                                  lambda k: [WBs, HB(k, tb // 4)])
                        c0 = (u - 8) * 256
                        if lat:
                            self.cp("dve", vlat[:, tb, :], pb[bi][:, 0:256], R=[PB[bi]], W=[VL])
                        else:
                            si = stg()
                            self.cp("act", stage[si][:, 0:256], pb[bi][:, 0:256], R=[PB[bi]], W=[STB[si]])
                            self.dma("sp", o_dv_d[tb * 128:(tb + 1) * 128, c0:c0 + 256], stage[si][:, 0:256], d_st[si], R=[STB[si]])
                            self.cp("dve", Vc[:, tb, c0:c0 + 256], pb[bi][:, 0:256], R=[PB[bi]], W=[VC_(u - 8)])
                    if lat:
                        self.dma("sp", gCv_in[u - 8].ap().rearrange("(k p) d -> p k d", p=128), vlat, GDS["cv"], R=[VL], W=[GIN("cv", u - 8)])
                        if 'dnogather' not in KDBG:
                            allgather(gCv_in[u - 8], gCv_out[u - 8], [GIN("cv", u - 8)], [GOUT("cv", u - 8)])
                run_pipelined(qk_items(range(0, 0 if 'dnoq' in KDBG else 4), True))

                def head_finish(hd, q0, nq, OW):
                    t1, t2, t3 = tmpi(), tmpi(), tmpi()
                    self.S.op("dve", lambda e: e.reciprocal(out=tmpF[t1][:, 0:nq], in_=pb[4][:, 0:nq]), [PB[4]], [TMPB[t1]])
                    self.tt("dve", tmpF[t1][:, 0:nq], pb[3][:, 0:nq], tmpF[t1][:, 0:nq], ALU.mult, R=[PB[3], TMPB[t1]], W=[TMPB[t1]])
                    self.S.op("dve", lambda e: e.reciprocal(out=tmpF[t2][:, 0:nq], in_=pb[6][:, 0:nq]), [PB[6]], [TMPB[t2]])
                    self.tt("dve", tmpF[t2][:, 0:nq], pb[5][:, 0:nq], tmpF[t2][:, 0:nq], ALU.mult, R=[PB[5], TMPB[t2]], W=[TMPB[t2]])
                    self.stt("dve", tmpF[t3][:, 0:nq], tmpF[t2][:, 0:nq], lamt[:, 5:6], tmpF[t1][:, 0:nq], ALU.mult, ALU.add,
                             R=[TMPB[t1], TMPB[t2], CONST], W=[TMPB[t3]])
                    self.act(sqt[0][:, 0:nq], tmpF[t3][:, 0:nq], AF.Square, R=[TMPB[t3]], W=[SQB[0]])
                    self.mm(pb[7][:, 0:nq], onesb[:], sqt[0][:, 0:nq], R=[SQB[0], CONST], W=[PB[7]])
                    rsqrt(rstd[:, 0:nq], pb[7][:, 0:nq], 128.0 * EPS, nq, [PB[7]], [RSTD])
                    self.stt("dve", hT[:, hd, q0:q0 + nq], tmpF[t3][:, 0:nq], diffg[:, 0:1], rstd[:, 0:nq], ALU.mult, ALU.mult,
                             R=[TMPB[t3], RSTD, CONST], W=OW)

                if (not lat and 'dnoctx' in KDBG) or (lat and 'dnolat' in KDBG):
                    pass
                elif not lat:
                    begin_units()
                    for s in range(4):
                        for hd in range(8):
                            for mp in range(2):
                                r0 = mp * 64
                                Kz = KTc if mp == 0 else KTc1
                                attend(lambda kb: Kz[:, hd, s * 256 + kb * 128:s * 256 + (kb + 1) * 128],
                                       QT[:, hd, s * 256:(s + 1) * 256], 2, 256,
                                       lambda kb: Vc[:, s * 2 + kb, hd * 128:(hd + 1) * 128], 128, 3 + 2 * mp, scale,
                                       [KC_(hd)], [QB(hd, s // 2)], [VC_(hd // 2)], zb=4 + 2 * mp)
                            attach_fin(lambda hd=hd, s=s: head_finish(hd, s * 256, 256, [HB(hd, s // 2)]), imm=True)
                    end_units()
                else:
                    cv = diff_cv_d.rearrange("(k p) (h d) -> h p k d", p=128, h=8)
                    self.memset("pool", A_KT[0][64:128, :], 0.0, W=[KTB[0]])
                    self.memset("pool", A_KT[1][0:64, :], 0.0, W=[KTB[1]])
                    for hd in range(8):
                        ks = hd % 2
                        gk = gCk_out[hd // 2].ap().rearrange("(r c p) t -> c p r t", r=4, c=2)[hd % 2]
                        gv = gCv_out[hd // 2].ap().rearrange("(k p) (h d) -> h p k d", p=128, h=2)[hd % 2]
                        for mp_ in range(2):
                            rs = slice(mp_ * 64, mp_ * 64 + 64)
                            self.dma("sp", A_KT[mp_][rs, 0:4096].rearrange("p (r t) -> p r t", r=4), gk[rs], d_kv[mp_], R=[GOUT("ck", hd // 2)], W=[KTB[mp_]])
                            self.dma("pool", A_KT[mp_][rs, 4096:4608], diff_ck_d[hd][rs], d_kv[mp_], W=[KTB[mp_]])
                        self.dma("sp", A_V[ks][:, 0:32, :], gv, d_kv[2 + ks], R=[GOUT("cv", hd // 2)], W=[VB[ks]])
                        self.dma("pool", A_V[ks][:, 32:36, :], cv[hd], d_kv[2 + ks], W=[VB[ks]])
                        begin_units()
                        for qb in range(2):
                            for mp in range(2):
                                r0 = mp * 64
                                attend(lambda kb: A_KT[mp][:, kb * 128:(kb + 1) * 128],
                                       QT[:, hd, qb * 512:(qb + 1) * 512], 36, 512,
                                       lambda kb: A_V[ks][:, kb, :], 128, 3 + 2 * mp, scale,
                                       [KTB[mp]], [QB(hd, qb)], [VB[ks]], zb=4 + 2 * mp)
                            attach_fin(lambda hd=hd, qb=qb: head_finish(hd, qb * 512, 512, [HB(hd, qb)]), imm=True)
                        end_units()
                if 'dnoo' not in KDBG:
                    oproj(g, l, diff_o_d)

            run_group(0)
            run_group(1)

        def mla(l, j):
            scale = 96 ** -0.5
            cqT = QTf[:, 0:3072].rearrange("p (c t) -> p c t", c=3)
            qsl = [QTf[:, 3072:4096], QTf[:, 4096:5120]]

            def run_group(g):
                lat = (g == 1)
                arena_fence()
                QSB = [AB(), AB()]
                CQB = ABG()
                nk = 4608 if lat else 1024
                nkb = nk // 128
                normmod(g, gs_ap(l, 1, g), sh_ap(l, 1, g))
                if lat:
                    ckvT = arena[:, 0:9216].rearrange("p (c t) -> p c t", c=2)
                    KT1 = arena[:, 9216:13824]
                    VEl = arena[:, 13824:13824 + 36 * 66].rearrange("p (k d) -> p k d", k=36)
                    VOl = arena[:, 17408:17408 + 4608].rearrange("p (k d) -> p k d", k=36)
                    cl = arena[:, 22016:22016 + 2048].rearrange("p (c t) -> p c t", c=2)
                    krl = arena[:, 16384:17408]
                else:
                    ckvT = arena[:, 0:2048].rearrange("p (c t) -> p c t", c=2)
                    KT1 = arena[:, 9216:9216 + 1024]
                    VEl = arena[:, 13824:13824 + 8 * 66].rearrange("p (k d) -> p k d", k=8)
                    VOl = arena[:, 17408:17408 + 1024].rearrange("p (k d) -> p k d", k=8)
                CKB, K1B, VEB, VOB, CLB, KRB = ABG(), AB(), AB(), AB(), AB(), AB()
                self.memset("pool", VEl[:, :, 64:65], 1.0, W=[VEB])
                self.memset("pool", VOl[:, :, 0:64], 0.0, W=[VOB])
                self.memset("pool", VOl[:, :, 0:1], 1.0, W=[VOB])
                self.memset("pool", KT1[64:128, :], 0.0, W=[K1B])
                self.memset("pool", qsl[0][64:128, :], 0.0, W=[QSB[0]])
                self.memset("pool", qsl[1][64:128, :], 0.0, W=[QSB[1]])
                units = []
                for u in range(3):
                    nn_ = 2048 if u < 2 else 1280
                    slot, WBs = load_w(mla_down_d[j, u][:, 0:nn_], nn_)
                    n = 256 if u < 2 else 160
                    units.append((slot[:, 0:8 * n].rearrange("p (k n) -> p k n", k=8), WBs))

                def wcol(c0, M):
                    u = c0 // 256
                    wv, WBs = units[u]
                    o = c0 - u * 256
                    return (lambda k: wv[:, k, o:o + M]), WBs

                for blk in range(2):
                    hr = lambda k: hT[:, k, blk * 512:(blk + 1) * 512]
                    bis = []
                    for c in range(3):
                        wf, WBs = wcol(c * 128, 128)
                        bis.append(proj(wf, 8, hr, 128, 512, lambda k, WBs=WBs: [WBs, HB(k, blk)], cands=(0, 1, 2)))
                    sb_ = sumsq(lambda c: pb[bis[c]][:], 3, 512, lambda c: [PB[bis[c]]])
                    rsqrt(rstd[:], pb[sb_][:], 384.0 * EPS, 512, [PB[sb_]], [RSTD])
                    for c in range(3):
                        self.stt("dve", cqT[:, c, blk * 512:(blk + 1) * 512], pb[bis[c]][:], mlag[:, j, c:c + 1], rstd[:], ALU.mult, ALU.mult,
                                 R=[PB[bis[c]], RSTD, CONST], W=[CQB(c, blk)])
                    bis = []
                    for c in range(2):
                        wf, WBs = wcol(384 + c * 128, 128)
                        bis.append(proj(wf, 8, hr, 128, 512, lambda k, WBs=WBs: [WBs, HB(k, blk)], cands=(0, 1, 2)))
                    sb_ = sumsq(lambda c: pb[bis[c]][:], 2, 512, lambda c: [PB[bis[c]]])
                    rsqrt(rstd[:], pb[sb_][:], 256.0 * EPS, 512, [PB[sb_]], [RSTD])
                    for c in range(2):
                        ti = tmpi()
                        self.stt("dve", tmpF[ti][:], pb[bis[c]][:], mlag[:, j, 3 + c:4 + c], rstd[:], ALU.mult, ALU.mult,
                                 R=[PB[bis[c]], RSTD, CONST], W=[TMPB[ti]])
                        if lat:
                            self.cp("act", cl[:, c, blk * 512:(blk + 1) * 512], tmpF[ti][:], R=[TMPB[ti]], W=[CLB])
                        else:
                            self.cp("act", ckvT[:, c, blk * 512:(blk + 1) * 512], tmpF[ti][:], R=[TMPB[ti]], W=[CKB(c)])
                            self.dma("sp", o_ckv_d[j, :, c, blk * 512:(blk + 1) * 512], tmpF[ti][:], d_tmp[ti], R=[TMPB[ti]])
                    wf, WBs = wcol(576, 96)
                    bi = proj(wf, 8, hr, 96, 512, lambda k, WBs=WBs: [WBs, HB(k, blk)], cands=(0, 1, 2))
                    ti = tmpi()
                    self.cp("act", tmpF[ti][64:96, :], pb[bi][64:96, :], R=[PB[bi]], W=[TMPB[ti]])
                    if lat:
                        rope_combine(tmpF[ti], TMPB[ti], 64, 96, permA, ropeA, blk * 512, 512, krl[64:96, blk * 512:(blk + 1) * 512], [KRB])
                    else:
                        self.dma("sp", o_kr_d[j, :, blk * 512:(blk + 1) * 512], tmpF[ti][64:96, :], d_tmp[ti], R=[TMPB[ti]])
                        self.cp("pool", KT1[64:96, blk * 512:(blk + 1) * 512], tmpF[ti][64:96, :], R=[TMPB[ti]], W=[K1B])
                if lat:
                    ga = gA_in.ap()
                    self.dma("sp", ga[0:256, :].rearrange("(c p) t -> p c t", p=128), cl, GDS["a"], R=[CLB], W=[GIN("a")])
                    self.dma("sp", ga[256:288, :], krl[64:96, :], GDS["a"], R=[KRB], W=[GIN("a")])
                    allgather(gA_in, gA_out, [GIN("a")], [GOUT("a")])
                    go = gA_out.ap().rearrange("(r f) t -> f r t", r=4)
                    for c in range(2):
                        self.dma("sp", ckvT[:, c, 0:4096].rearrange("p (r t) -> p r t", r=4), go[c * 128:(c + 1) * 128], d_kv[c], R=[GOUT("a")], W=[CKB(c)])
                        self.dma("pool", ckvT[:, c, 4096:4608], mla_cckv_d[j, :, c, :], d_kv[c], W=[CKB(c)])
                    self.dma("sp", KT1[64:96, 0:4096].rearrange("p (r t) -> p r t", r=4), go[256:288], d_kv[2], R=[GOUT("a")], W=[K1B])
                    self.dma("pool", KT1[64:96, 4096:4608], mla_ckr_d[j], d_kv[2], W=[K1B])
                for hg in range(4):
                    sq_, WBq = load_w(mla_uq_d[j, hg], 1152)
                    wq = sq_[:, 0:1152].rearrange("p (h k n) -> p h k n", h=4, k=3)
                    skv, WBkv = load_w(mla_ukv_d[j, hg], 1024)
                    wkv = skv[:, 0:1024].rearrange("p (h k n) -> p h k n", h=4, k=2)
                    for hh in range(4):
                        hd = hg * 4 + hh
                        par, m = hd % 2, hd // 2
                        for kb4 in range(nk // 512):
                            bi = proj(lambda k: wkv[:, hh, k, 0:64], 2, lambda k: ckvT[:, k, kb4 * 512:(kb4 + 1) * 512], 64, 512,
                                      lambda k: [WBkv, CKB(k)])
                            self.cp("dve", KT1[0:64, kb4 * 512:(kb4 + 1) * 512], pb[bi][0:64, :], R=[PB[bi]], W=[K1B])
                        Vt, VtB = (VEl, VEB) if par == 0 else (VOl, VOB)
                        c0 = 0 if par == 0 else 64
                        for kb8 in range((nkb + 7) // 8):
                            nb = min(8, nkb - kb8 * 8)
                            bi = bank()
                            for q in range(nb):
                                kb = kb8 * 8 + q
                                for k in range(2):
                                    self.mm(pb[bi][:, q * 64:(q + 1) * 64], ckvT[:, k, kb * 128:(kb + 1) * 128], wkv[:, hh, k, 64:128],
                                            start=(k == 0), stop=(k == 1), R=[CKB(k), WBkv], W=[PB[bi]])
                            self.cp("act", Vt[:, kb8 * 8:kb8 * 8 + nb, c0:c0 + 64], pb[bi][:, 0:nb * 64].rearrange("p (q d) -> p q d", d=64),
                                    R=[PB[bi]], W=[VtB])
                        qs, QSb = qsl[hd % 2], QSB[hd % 2]
                        for blk in range(2):
                            bi = proj(lambda k: wq[:, hh, k, :], 3, lambda k: cqT[:, k, blk * 512:(blk + 1) * 512], 96, 512,
                                      lambda k: [WBq, CQB(k, blk)])
                            self.cp("act", qs[0:64, blk * 512:(blk + 1) * 512], pb[bi][0:64, :], R=[PB[bi]], W=[QSb])
                            if lat:
                                ti = tmpi()
                                self.cp("dve", tmpF[ti][64:96, :], pb[bi][64:96, :], R=[PB[bi]], W=[TMPB[ti]])
                                rope_combine(tmpF[ti], TMPB[ti], 64, 96, permA, ropeA, blk * 512, 512, qs[64:96, blk * 512:(blk + 1) * 512], [QSb])
                            else:
                                self.cp("dve", qs[64:96, blk * 512:(blk + 1) * 512], pb[bi][64:96, :], R=[PB[bi]], W=[QSb])
                        vM = 65 if par == 0 else 128
                        r0 = par * 64
                        begin_units()
                        if lat:
                            for qb in range(2):
                                po = 3 + (qb % 2)
                                attend(lambda kb: KT1[:, kb * 128:(kb + 1) * 128], qs[:, qb * 512:(qb + 1) * 512], 36, 512,
                                       lambda kb: Vt[:, kb, 0:vM], vM, po, scale, [K1B], [QSb], [VtB])
                                finish_head(po, par, 512, hT[r0:r0 + 64, m, qb * 512:(qb + 1) * 512], [HB(m, qb)])
                        else:
                            for s in range(4):
                                po = 3 + (s % 2)
                                attend(lambda kb: KT1[:, s * 256 + kb * 128:s * 256 + (kb + 1) * 128], qs[:, s * 256:(s + 1) * 256], 2, 256,
                                       lambda kb: Vt[:, s * 2 + kb, 0:vM], vM, po, scale, [K1B], [QSb], [VtB])
                                finish_head(po, par, 256, hT[r0:r0 + 64, m, s * 256:(s + 1) * 256], [HB(m, s // 2)])
                        end_units()
                oproj(g, l, mla_o_d[j])

            run_group(0)
            run_group(1)

        for l in range(NL):
            cur["l"] = l
            if l + 1 < NL:
                self._adag = ada_gen(l + 1)
            ffn(0, l, 0)
            ffn(1, l, 0)
            while self._adag is not None:
                ada_step()
            kind = l % 3
            if kind == 0:
                mla(l, l // 3)
            elif kind == 1:
                gqa(l)
            else:
                diff(l)
            ffn(0, l, 1)
            ffn(1, l, 1)

        for g in range(2):
            for blk in range(2):
                t0 = g * 1024 + blk * 512
                bi = sumsq(lambda c: yT[:, c, t0:t0 + 512], 8, 512, lambda c: [YB(c, g, blk)])
                rsqrt(rstd[:], pb[bi][:], 1024.0 * EPS, 512, [PB[bi]], [RSTD])
                for c in range(8):
                    self.stt("dve", yT[:, c, t0:t0 + 512], yT[:, c, t0:t0 + 512], normf[:, c:c + 1], rstd[:], ALU.mult, ALU.mult,
                             R=[YB(c, g, blk), RSTD, CONST], W=[YB(c, g, blk)])
                self.dma("sp", yout_d[:, :, t0:t0 + 512], yT[:, :, t0:t0 + 512], d_out, R=[YB(c, g, blk) for c in range(8)])
        fin = Buf()
        fin.r = {d: d.val for d in [d_out] + d_st + d_tmp if d.val > 0}
        self.S.op("sp", lambda e: e.nop(), writes=[fin])

        for d in S.dsems:
            d.h = self.es.enter_context(nc.semaphore("d_" + d.name))
        S.finalize()
        block = self.es.enter_context(nc.Block())
        for name in ENGS:
            def mk(name=name):
                def f(e):
                    S.emit(name, e, sems)
                return f
            getattr(block, ATTR[name])(mk())
        self.es.close()
        return nc


def _fm(x2d):
    T, Dd = x2d.shape
    return np.ascontiguousarray(x2d.T.reshape(Dd // 128, 128, T).transpose(1, 0, 2))


def _wt(W):
    K, N = W.shape
    return np.ascontiguousarray(W.reshape(K // 128, 128, N).transpose(1, 0, 2).reshape(128, (K // 128) * N))


def _pad(a, n):
    out = np.zeros((a.shape[0], n), np.float32)
    out[:, :a.shape[1]] = a
    return out


def _rope_tables(pos, rot_dim):
    GRID_W = 64
    r = (pos // GRID_W).astype(np.float32)
    cidx = (pos % GRID_W).astype(np.float32)
    n_f = rot_dim // 4
    freqs = (np.float32(10000.0) ** (-np.arange(n_f, dtype=np.float32) / np.float32(n_f))).astype(np.float32)
    ang = np.concatenate([r[:, None] * freqs, cidx[:, None] * freqs], axis=-1).astype(np.float32)
    cos, sin = np.cos(ang).astype(np.float32), np.sin(ang).astype(np.float32)
    cos2 = np.concatenate([cos, cos], axis=1).T
    sinp = np.concatenate([-sin, sin], axis=1).T
    return cos2, sinp


_CACHE = {}


def _get_nc(nl, noffn=False):
    if (nl, noffn) not in _CACHE:
        _CACHE[(nl, noffn)] = KB(nl, noffn).build()
    return _CACHE[(nl, noffn)]


def kernel(x_prompt, x_sample, cache_mla_ckv, cache_mla_kr, cache_gqa_k, cache_gqa_v,
           cache_diff_k, cache_diff_v, c, c_ctx,
           ada_w, ada_b, norm_ffn1, norm_mix, norm_ffn2,
           ffn1_w_in, ffn1_w_out, ffn2_w_in, ffn2_w_out,
           mla_w_down, mla_g_q, mla_g_kv, mla_w_uq, mla_w_uk, mla_w_uv, mla_w_o,
           gqa_w_qkv, gqa_g_q, gqa_g_k, gqa_w_o,
           diff_w_qkv, diff_lq1, diff_lk1, diff_lq2, diff_lk2, diff_g_sub, diff_w_o,
           norm_final, _nl=4, _noffn=False, _trace=False):
    f = lambda a: np.asarray(a, dtype=np.float32)
    x_prompt, x_sample = f(x_prompt), f(x_sample)
    NL = _nl
    if 'probe_noffn' in KDBG:
        _noffn = True
    nc = _get_nc(NL, _noffn)
    n_mla = (NL + 2) // 3
    sh = {}
    ada_w = f(ada_w)
    sh["ada_t"] = np.stack([np.stack([_wt(ada_w[l][:, s * 256:(s + 1) * 256]) for s in range(36)]) for l in range(NL)])
    sh["ada_bT"] = np.ascontiguousarray(f(ada_b)[:NL].reshape(NL, 72, 128).transpose(2, 0, 1))
    ng = np.stack([f(norm_ffn1)[:NL], f(norm_mix)[:NL], f(norm_ffn2)[:NL]], axis=1)
    sh["normg"] = np.ascontiguousarray(ng.reshape(NL, 3, 8, 128).transpose(3, 0, 1, 2))
    sh["normf"] = np.ascontiguousarray(f(norm_final).reshape(8, 128).T)
    if not _noffn:
        win = np.stack([f(ffn1_w_in)[:NL], f(ffn2_w_in)[:NL]], axis=1)
        sh["ffn_in_t"] = np.ascontiguousarray(
            win.reshape(NL, 2, 8, 128, 2, 22, 128).transpose(0, 1, 5, 3, 2, 4, 6).reshape(NL, 2, 22, 128, 2048))
        wout = np.stack([f(ffn1_w_out)[:NL], f(ffn2_w_out)[:NL]], axis=1)
        sh["ffn_out_t"] = np.ascontiguousarray(
            wout.reshape(NL, 2, 22, 128, 8, 128).transpose(0, 1, 4, 3, 2, 5).reshape(NL, 2, 8, 128, 2816))
    consts = np.zeros((128, 3, 128), np.float32)
    consts[:, 0, :] = np.eye(128, dtype=np.float32)
    for jj in range(128):
        base = (jj // 64) * 64
        o = jj - base
        consts[base + (o + 32) % 64, 2, jj] = 1.0
    for jj in range(64, 96):
        o = jj - 64
        consts[64 + (o + 16) % 32, 1, jj] = 1.0
    sh["consts"] = consts
    wd = f(mla_w_down)[:n_mla]
    sh["mla_down_t"] = np.stack([np.stack([_pad(_wt(wd[j][:, 0:256]), 2048), _pad(_wt(wd[j][:, 256:512]), 2048),
                                           _pad(_wt(wd[j][:, 512:672]), 2048)]) for j in range(n_mla)])
    wuq = f(mla_w_uq)[:n_mla]
    sh["mla_uq_t"] = np.ascontiguousarray(
        wuq.reshape(n_mla, 3, 128, 4, 4, 96).transpose(0, 3, 2, 4, 1, 5).reshape(n_mla, 4, 128, 1152))
    wuk = f(mla_w_uk)[:n_mla].reshape(n_mla, 2, 128, 4, 4, 64)
    wuv = f(mla_w_uv)[:n_mla].reshape(n_mla, 2, 128, 4, 4, 64)
    ukv = np.concatenate([wuk, wuv], axis=-1)
    sh["mla_ukv_t"] = np.ascontiguousarray(ukv.transpose(0, 3, 2, 4, 1, 5).reshape(n_mla, 4, 128, 1024))
    wo = f(mla_w_o)[:n_mla]
    sh["mla_o_t"] = np.stack([np.stack([_wt(wo[j][:, m * 128:(m + 1) * 128]) for m in range(8)]) for j in range(n_mla)])
    gq = f(mla_g_q)[:n_mla].reshape(n_mla, 3, 128)
    gkv = f(mla_g_kv)[:n_mla].reshape(n_mla, 2, 128)
    sh["mla_g"] = np.ascontiguousarray(np.concatenate([gq, gkv], axis=1).transpose(2, 0, 1))
    if NL >= 2:
        wq = f(gqa_w_qkv)[0]
        units = [_wt(wq[:, u * 256:(u + 1) * 256]) for u in range(4)]
        for u in range(2):
            cols = []
            for h in range(2):
                kvh = u * 2 + h
                blk = wq[:, 1024 + kvh * 64:1024 + (kvh + 1) * 64]
                cols += [blk, blk]
            units.append(_wt(np.concatenate(cols, axis=1)))
        units.append(_wt(wq[:, 1280:1536]))
        sh["gqa_t"] = np.stack(units)
        go = f(gqa_w_o)[0]
        sh["gqa_o_t"] = np.stack([_wt(go[:, m * 128:(m + 1) * 128]) for m in range(8)])
        sh["gqa_g"] = np.ascontiguousarray(np.stack([np.tile(f(gqa_g_q)[0], 2), np.tile(f(gqa_g_k)[0], 2)], axis=1))
    if NL >= 3:
        wq = f(diff_w_qkv)[0]
        sh["diff_t"] = np.stack([_wt(wq[:, u * 256:(u + 1) * 256]) for u in range(12)])
        do = f(diff_w_o)[0]
        sh["diff_o_t"] = np.stack([_wt(do[:, m * 128:(m + 1) * 128]) for m in range(8)])
        sh["diff_l"] = np.ascontiguousarray(np.broadcast_to(np.stack([f(diff_lq1)[0], f(diff_lk1)[0], f(diff_lq2)[0], f(diff_lk2)[0]])[None], (128, 4, 64)))
        sh["diff_g"] = np.ascontiguousarray(f(diff_g_sub)[0].reshape(128, 1))

    in_maps = []
    for core in range(8):
        b, r = core // 4, core % 4
        m = dict(sh)
        xc = x_prompt[4 * core:4 * core + 4].reshape(1024, 1024)
        xl = x_sample[b, r * 1024:(r + 1) * 1024]
        m["xT"] = np.concatenate([_fm(xc), _fm(xl)], axis=2)
        m["cT"] = np.ascontiguousarray(np.stack([f(c_ctx), f(c)[b]], axis=1).reshape(8, 128, 2).transpose(1, 0, 2))
        pos = np.arange(r * 1024, (r + 1) * 1024)
        ca, sa = _rope_tables(pos, 32)
        ra = np.zeros((128, 2, 1024), np.float32)
        ra[64:96, 0], ra[64:96, 1] = ca, sa
        m["ropeA"] = ra
        cb, sb_ = _rope_tables(pos, 64)
        m["ropeB"] = np.ascontiguousarray(np.stack([np.tile(cb, (2, 1)), np.tile(sb_, (2, 1))], axis=1))
        ck = f(cache_mla_ckv)[b][:n_mla]
        m["mla_cckvT"] = np.ascontiguousarray(ck.transpose(0, 2, 1).reshape(n_mla, 2, 128, 512).transpose(0, 2, 1, 3))
        m["mla_ckrT"] = np.ascontiguousarray(f(cache_mla_kr)[b][:n_mla].transpose(0, 2, 1))
        if NL >= 2:
            gk = f(cache_gqa_k)[b, 0]
            kt = gk.transpose(1, 2, 0)
            m["gqa_ckT"] = np.ascontiguousarray(np.concatenate([kt, kt], axis=1))
            m["gqa_cv"] = np.ascontiguousarray(f(cache_gqa_v)[b, 0].reshape(512, 256))
        if NL >= 3:
            dk = f(cache_diff_k)[b, 0].reshape(512, 8, 128)
            m["diff_ckT"] = np.ascontiguousarray(dk.transpose(1, 2, 0))
            m["diff_cv"] = np.ascontiguousarray(f(cache_diff_v)[b, 0].reshape(512, 1024))
        in_maps.append(m)

    if _trace:
        res = run_bass_kernel_spmd(nc, in_maps, core_ids=list(range(8)), trace=True)
        _CACHE['last_res'] = res
    else:
        res = run_bass_kernel_spmd(nc, in_maps, core_ids=list(range(8)))
    R = res.results
    y_prompt = np.zeros((32, 256, 1024), np.float32)
    y_sample = np.zeros((2, 4096, 1024), np.float32)
    n_ckv = np.zeros((32, 2, 256, 256), np.float32)
    n_kr = np.zeros((32, 2, 256, 32), np.float32)
    n_gk = np.zeros((32, 1, 256, 4, 64), np.float32)
    n_gv = np.zeros((32, 1, 256, 4, 64), np.float32)
    n_dk = np.zeros((32, 1, 256, 8, 2, 64), np.float32)
    n_dv = np.zeros((32, 1, 256, 8, 128), np.float32)
    for core in range(8):
        b, r = core // 4, core % 4
        o = R[core]
        yt = np.asarray(o["yT_out"]).transpose(2, 1, 0).reshape(2048, 1024)
        y_prompt[4 * core:4 * core + 4] = yt[:1024].reshape(4, 256, 1024)
        y_sample[b, r * 1024:(r + 1) * 1024] = yt[1024:]
        ck = np.asarray(o["o_ckv"]).transpose(0, 3, 2, 1).reshape(2, 4, 256, 256)
        n_ckv[4 * core:4 * core + 4] = ck.transpose(1, 0, 2, 3)
        kr = np.asarray(o["o_kr"]).transpose(0, 2, 1).reshape(2, 4, 256, 32)
        n_kr[4 * core:4 * core + 4] = kr.transpose(1, 0, 2, 3)
        gk = np.asarray(o["o_gk"]).transpose(2, 0, 1).reshape(4, 256, 4, 64)
        n_gk[4 * core:4 * core + 4, 0] = gk
        n_gv[4 * core:4 * core + 4, 0] = np.asarray(o["o_gv"]).reshape(4, 256, 4, 64)
        dk = np.asarray(o["o_dk"]).transpose(2, 1, 0).reshape(4, 256, 8, 2, 64)
        n_dk[4 * core:4 * core + 4, 0] = dk
        n_dv[4 * core:4 * core + 4, 0] = np.asarray(o["o_dv"]).reshape(4, 256, 8, 128)
    return (y_prompt, y_sample, n_ckv, n_kr, n_gk, n_gv, n_dk, n_dv)
```

```python
import math
import os
from contextlib import ExitStack
KDBG = os.environ.get('KDBG', '').split(',')
import numpy as np
import concourse.bass as bass
import concourse.mybir as mybir
from concourse.bass_utils import run_bass_kernel_spmd

F32 = mybir.dt.float32
BF16 = mybir.dt.bfloat16
AF = mybir.ActivationFunctionType
ALU = mybir.AluOpType

ENGS = ("pe", "act", "dve", "pool", "sp")
ATTR = {"pe": "tensor", "act": "scalar", "dve": "vector", "pool": "gpsimd", "sp": "sync"}
SELF_RAW_DIST = 3
EPS = 1e-6
NSLOT = 3
SLOTW = 2816


class Buf:
    __slots__ = ("w", "r", "excl")

    def __init__(self, excl=False):
        self.w = None
        self.r = {}
        self.excl = excl


class BG:
    def __init__(self):
        self.d = {}

    def __call__(self, *key):
        b = self.d.get(key)
        if b is None:
            b = self.d[key] = Buf()
        return b


class DSem:
    def __init__(self, name, inc=16):
        self.name = name
        self.val = 0
        self.inc = inc
        self.h = None


class Op:
    __slots__ = ("fn", "deps", "marked", "count", "dsem")

    def __init__(self, fn, deps, dsem):
        self.fn = fn
        self.deps = deps
        self.marked = False
        self.count = 0
        self.dsem = dsem


class Sched:
    def __init__(self):
        self.ops = {e: [] for e in ENGS}
        self.clock = {e: {} for e in ENGS}
        self.hist = {e: [] for e in ENGS}
        self.dsems = []

    def dsem(self, name, inc=16):
        d = DSem(name, inc)
        self.dsems.append(d)
        return d

    def op(self, eng, fn, reads=(), writes=(), dsem=None):
        ops = self.ops[eng]
        clock = self.clock[eng]
        seq = len(ops)
        deps = {}

        def need(key, val, raw):
            if key == eng:
                if (not raw) or (seq - val > SELF_RAW_DIST and eng != "pool"):
                    return
                if clock.get(("self", eng), -1) >= val:
                    return
                clock[("self", eng)] = val
                deps[key] = max(deps.get(key, -1), val)
                return
            if clock.get(key, -1) >= val:
                return
            clock[key] = val
            deps[key] = val
            if isinstance(key, str):
                for k2, v2 in self.hist[key][val].items():
                    if isinstance(k2, tuple) or k2 == eng:
                        continue
                    if clock.get(k2, -1) < v2:
                        clock[k2] = v2

        for b in reads:
            if b.w is not None:
                need(b.w[0], b.w[1], True)
            if b.excl:
                for k, v in b.r.items():
                    need(k, v, False)
        for b in writes:
            if b.w is not None:
                need(b.w[0], b.w[1], False)
            for k, v in b.r.items():
                need(k, v, False)
        o = Op(fn, list(deps.items()), dsem)
        ops.append(o)
        self.hist[eng].append(dict(clock))
        if dsem is not None:
            dsem.val += dsem.inc
            key, val = dsem, dsem.val
        else:
            key, val = eng, seq
        for b in reads:
            b.r[key] = val
        for b in writes:
            b.w = (key, val)
            b.r = {}
        return o

    def finalize(self):
        for e in ENGS:
            for o in self.ops[e]:
                for k, v in o.deps:
                    if isinstance(k, str):
                        self.ops[k][v].marked = True
        for e in ENGS:
            c = 0
            for o in self.ops[e]:
                if o.marked:
                    c += 1
                o.count = c

    def emit(self, eng, e, sems):
        for o in self.ops[eng]:
            for k, v in o.deps:
                if isinstance(k, str):
                    e.wait_ge(sems[k], self.ops[k][v].count)
                else:
                    e.wait_ge(k.h, v)
            ins = o.fn(e)
            if o.dsem is not None:
                ins.then_inc(o.dsem.h, o.dsem.inc)
            elif o.marked:
                ins.then_inc(sems[eng], 1)


class KB:
    def __init__(self, nlayers=4, noffn=False):
        self.NL = nlayers
        self.noffn = noffn
        self.nc = bass.Bass("TRN2", target_bir_lowering=False)
        self.S = Sched()
        self.es = ExitStack()
        self.in_names = []
        self.out_names = []
        self.out_dsems = []

    def inp(self, name, shape, dt=F32):
        self.in_names.append(name)
        return self.nc.dram_tensor(name, list(shape), dt, kind="ExternalInput").ap()

    def outp(self, name, shape, dt=F32):
        self.out_names.append(name)
        return self.nc.dram_tensor(name, list(shape), dt, kind="ExternalOutput").ap()

    def dram(self, name, shape, dt):
        return self.nc.dram_tensor(name, list(shape), dt)

    def sb(self, name, shape, dt):
        return self.es.enter_context(self.nc.sbuf_tensor(name, list(shape), dt))

    def mm(self, out, lhsT, rhs, start=True, stop=True, R=(), W=()):
        self.S.op("pe", lambda e: e.matmul(out, lhsT=lhsT, rhs=rhs, start=start, stop=stop), R, W)

    def act(self, out, in_, func, R=(), W=(), bias=None, scale=None):
        kw = {}
        if bias is not None:
            kw["bias"] = bias
        if scale is not None:
            kw["scale"] = scale
        self.S.op("act", lambda e: e.activation(out=out, in_=in_, func=func, **kw), R, W)

    def tt(self, eng, out, in0, in1, op, R=(), W=()):
        self.S.op(eng, lambda e: e.tensor_tensor(out=out, in0=in0, in1=in1, op=op), R, W)

    def ts(self, eng, out, in0, s1, s2, op0, op1, R=(), W=()):
        self.S.op(eng, lambda e: e.tensor_scalar(out=out, in0=in0, scalar1=s1, scalar2=s2, op0=op0, op1=op1), R, W)

    def ts1(self, eng, out, in0, s1, op0, R=(), W=()):
        self.S.op(eng, lambda e: e.tensor_single_scalar(out=out, in_=in0, scalar=s1, op=op0), R, W)

    def stt(self, eng, out, in0, scalar, in1, op0, op1, R=(), W=()):
        self.S.op(eng, lambda e: e.scalar_tensor_tensor(out=out, in0=in0, scalar=scalar, in1=in1, op0=op0, op1=op1), R, W)

    def cp(self, eng, out, in_, R=(), W=()):
        if eng == "act":
            self.S.op("act", lambda e: e.activation(out=out, in_=in_, func=AF.Copy), R, W)
        else:
            self.S.op(eng, lambda e: e.tensor_copy(out=out, in_=in_), R, W)

    def memset(self, eng, ap, val, W=()):
        self.S.op(eng, lambda e: e.memset(ap, val), (), W)

    def dma(self, q, out, in_, ds, R=(), W=()):
        self.S.op(q, lambda e: e.dma_start(out=out, in_=in_), R, W, dsem=ds)

    def barrier_bufs(self, bufs):
        nb = Buf()
        for b in bufs:
            if b.w is not None:
                nb.r[b.w[0]] = max(nb.r.get(b.w[0], -1), b.w[1])
            for k, v in b.r.items():
                nb.r[k] = max(nb.r.get(k, -1), v)
        return nb

    def build(self):
        nc, S = self.nc, self.S
        NL = self.NL
        sb = self.sb
        xT_d = self.inp("xT", [128, 8, 2048])
        cT_d = self.inp("cT", [128, 8, 2])
        ada_d = self.inp("ada_t", [NL, 36, 128, 2048])
        adab_d = self.inp("ada_bT", [128, NL, 72])
        normg_d = self.inp("normg", [128, NL, 3, 8])
        normf_d = self.inp("normf", [128, 8])
        if not self.noffn:
            fin_d = self.inp("ffn_in_t", [NL, 2, 22, 128, 2048])
            fout_d = self.inp("ffn_out_t", [NL, 2, 8, 128, 2816])
        consts_d = self.inp("consts", [128, 3, 128])
        ropeA_d = self.inp("ropeA", [128, 2, 1024])
        ropeB_d = self.inp("ropeB", [128, 2, 1024])
        n_mla = (NL + 2) // 3
        has_gqa = NL >= 2
        has_diff = NL >= 3
        mla_down_d = self.inp("mla_down_t", [n_mla, 3, 128, 2048])
        mla_uq_d = self.inp("mla_uq_t", [n_mla, 4, 128, 1152])
        mla_ukv_d = self.inp("mla_ukv_t", [n_mla, 4, 128, 1024])
        mla_o_d = self.inp("mla_o_t", [n_mla, 8, 128, 1024])
        mla_g_d = self.inp("mla_g", [128, n_mla, 5])
        mla_cckv_d = self.inp("mla_cckvT", [n_mla, 128, 2, 512])
        mla_ckr_d = self.inp("mla_ckrT", [n_mla, 32, 512])
        if has_gqa:
            gqa_w_d = self.inp("gqa_t", [7, 128, 2048])
            gqa_o_d = self.inp("gqa_o_t", [8, 128, 1024])
            gqa_g_d = self.inp("gqa_g", [128, 2])
            gqa_ck_d = self.inp("gqa_ckT", [4, 128, 512])
            gqa_cv_d = self.inp("gqa_cv", [512, 256])
        if has_diff:
            diff_w_d = self.inp("diff_t", [12, 128, 2048])
            diff_o_d = self.inp("diff_o_t", [8, 128, 1024])
            diff_l_d = self.inp("diff_l", [128, 4, 64])
            diff_g_d = self.inp("diff_g", [128, 1])
            diff_ck_d = self.inp("diff_ckT", [8, 128, 512])
            diff_cv_d = self.inp("diff_cv", [512, 1024])

        yout_d = self.outp("yT_out", [128, 8, 2048])
        o_ckv_d = self.outp("o_ckv", [2, 128, 2, 1024])
        o_kr_d = self.outp("o_kr", [2, 32, 1024])
        o_gk_d = self.outp("o_gk", [4, 64, 1024])
        o_gv_d = self.outp("o_gv", [1024, 256])
        o_dk_d = self.outp("o_dk", [128, 8, 1024])
        o_dv_d = self.outp("o_dv", [1024, 1024])

        RG = [[0, 1, 2, 3], [4, 5, 6, 7]]
        gA_in = self.dram("gA_in", [288, 1024], BF16)
        gA_out = self.dram("gA_out", [4 * 288, 1024], BF16)
        gBk_in = self.dram("gBk_in", [512, 1024], BF16)
        gBk_out = self.dram("gBk_out", [4 * 512, 1024], BF16)
        gBv_in = self.dram("gBv_in", [1024, 256], BF16)
        gBv_out = self.dram("gBv_out", [4096, 256], BF16)
        gCk_in = [self.dram(f"gCk_in{i}", [256, 1024], BF16) for i in range(4)]
        gCk_out = [self.dram(f"gCk_out{i}", [1024, 1024], BF16) for i in range(4)]
        gCv_in = [self.dram(f"gCv_in{i}", [1024, 256], BF16) for i in range(4)]
        gCv_out = [self.dram(f"gCv_out{i}", [4096, 256], BF16) for i in range(4)]

        yT = sb("yT", [128, 8, 2048], F32)
        hT = sb("hT", [128, 8, 1024], BF16)
        QT = sb("QT", [128, 8, 1024], BF16)
        arena = sb("arena", [128, 24576], BF16)
        QTf = QT[:].rearrange("p c t -> p (c t)")
        slots = [sb(f"wslot{i}", [128, SLOTW], BF16) for i in range(NSLOT)]
        tmpF = [sb(f"tmpF{i}", [128, 512], F32) for i in range(4)]
        sqt = [sb(f"sqt{i}", [128, 512], BF16) for i in range(2)]
        rstd = sb("rstd", [128, 512], F32)
        rstd2 = sb("rstd2", [128, 512], F32)
        ropeA = sb("ropeA_s", [128, 2, 1024], F32)
        ropeB = sb("ropeB_s", [128, 2, 1024], F32)
        consts = sb("consts_s", [128, 3, 128], F32)
        onesb = sb("onesb", [128, 128], BF16)
        bd64 = sb("bd64", [128, 128], BF16)
        onesf = sb("onesf", [128, 128], F32)
        cT = sb("cT_s", [128, 8, 2], F32)
        actT = sb("actT", [128, 8, 2], BF16)
        adab = sb("adab", [128, NL, 72], F32)
        modT = sb("modT", [128, NL, 72, 2], F32)
        normg = sb("normg_s", [128, NL, 3, 8], F32)
        normf = sb("normf_s", [128, 8], F32)
        gsT = sb("gsT", [128, NL, 3, 2, 8], F32)
        hgT = sb("hgT", [128, NL, 2, 2, 8], F32)
        mlag = sb("mlag", [128, n_mla, 5], F32)
        gqag = sb("gqag", [128, 2], F32)
        diffg = sb("diffg", [128, 1], F32)
        diffl = sb("diffl", [128, 4, 64], F32)
        lamt = sb("lamt", [128, 8], F32)
        ptile = [sb(f"ptile{i}", [128, 512], BF16) for i in range(4)]
        stage = [sb(f"stage{i}", [128, 512], F32) for i in range(2)]
        pb = [self.es.enter_context(nc.psum_tensor(f"pb{i}", [128, 512], F32)) for i in range(8)]

        sems = {e: self.es.enter_context(nc.semaphore("s_" + e)) for e in ENGS}

        YB, HB = BG(), BG()
        PB = [Buf(excl=True) for _ in range(8)]
        WB = [Buf() for _ in range(NSLOT)]
        TMPB = [Buf() for _ in range(4)]
        SQB = [Buf() for _ in range(2)]
        PTB = [Buf() for _ in range(4)]
        STB = [Buf() for _ in range(2)]
        RSTD, CONST, MOD, ARENA = Buf(), Buf(), Buf(), Buf()
        RSTD2 = Buf()
        wds = [S.dsem(f"w{i}") for i in range(NSLOT)]
        d_in = S.dsem("in")
        d_misc = S.dsem("misc")
        d_out = S.dsem("out")
        d_st = [S.dsem(f"st{i}") for i in range(2)]
        d_kv = [S.dsem(f"kv{i}") for i in range(4)]
        d_g = S.dsem("gin")
        d_cc = S.dsem("cc", inc=1)
        self._abufs = []
        self._fence = {}

        def AB():
            b = Buf()
            b.r = dict(self._fence)
            self._abufs.append(b)
            return b

        class ABG(BG):
            def __call__(s_, *key):
                b = s_.d.get(key)
                if b is None:
                    b = s_.d[key] = AB()
                return b

        def arena_fence():
            f = dict(self._fence)
            for b in self._abufs:
                if b.w is not None:
                    f[b.w[0]] = max(f.get(b.w[0], -1), b.w[1])
                for k, v in b.r.items():
                    f[k] = max(f.get(k, -1), v)
            self._fence = f
            self._abufs = []

        d_tmp = [S.dsem(f"tmpo{i}") for i in range(4)]
        GDS = {n: S.dsem("g_" + n) for n in ("a", "bk", "bv", "ck", "cv")}
        self._wi = 0
        self._bank = 0
        self._srot = 0
        self._prot = 0
        self._tmp = 0
        self._stg = 0

        def load_w(dram_ap, n):
            s = self._wi % NSLOT
            self._wi += 1
            self.dma("pool", slots[s][:, 0:n], dram_ap, wds[s], W=[WB[s]])
            return slots[s], WB[s]

        def bank(cands=(5, 6, 7)):
            i = cands[self._bank % len(cands)]
            self._bank += 1
            return i

        def tmpi():
            i = self._tmp % 4
            self._tmp += 1
            return i

        def stg():
            i = self._stg % 2
            self._stg += 1
            return i

        self.dma("sp", yT[:], xT_d, d_in, W=[YB(c, g, b) for c in range(8) for g in range(2) for b in range(2)])
        for (t, d) in ((cT, cT_d), (adab, adab_d), (normg, normg_d), (normf, normf_d), (ropeA, ropeA_d),
                       (ropeB, ropeB_d), (consts, consts_d), (mlag, mla_g_d)):
            self.dma("sp", t[:], d, d_misc, W=[CONST])
        if has_gqa:
            self.dma("sp", gqag[:], gqa_g_d, d_misc, W=[CONST])
        if has_diff:
            self.dma("sp", diffg[:], diff_g_d, d_misc, W=[CONST])
            self.dma("sp", diffl[:], diff_l_d, d_misc, W=[CONST])
        self.memset("dve", onesb[:], 1.0, W=[CONST])
        self.memset("dve", onesf[:], 1.0, W=[CONST])
        self.memset("dve", bd64[:], 0.0, W=[CONST])
        self.memset("dve", bd64[0:64, 0:64], 1.0, W=[CONST])
        self.memset("dve", bd64[64:128, 64:128], 1.0, W=[CONST])
        self.act(actT[:], cT[:], AF.Silu, R=[CONST], W=[MOD])
        permA = consts[:, 1, :]
        permB = consts[:, 2, :]

        MODL = [Buf() for _ in range(NL)]
        cur = {"l": 0}

        def ada_gen(l):
            for s in range(36):
                slot, WBs = load_w(ada_d[l, s], 2048)
                wv = slot[:, 0:2048].rearrange("p (k n) -> p k n", k=8)
                for h in range(2):
                    ch = s * 2 + h
                    for k in range(8):
                        self.mm(pb[7][:, ch * 2:ch * 2 + 2], wv[:, k, h * 128:(h + 1) * 128], actT[:, k, :],
                                start=(k == 0), stop=(k == 7), R=[WBs, MOD], W=[PB[7]])
                yield
            pv = pb[7][:, 0:144].rearrange("p (c s) -> p c s", s=2)
            ML = MODL[l]
            for st in range(2):
                self.tt("dve", modT[:, l, :, st], pv[:, :, st], adab[:, l, :], ALU.add, R=[PB[7], CONST], W=[ML])
            for sub in range(3):
                for st in range(2):
                    sc = modT[:, l, (3 * sub + 1) * 8:(3 * sub + 2) * 8, st]
                    self.stt("dve", gsT[:, l, sub, st, :], sc, 1.0, normg[:, l, sub, :], ALU.add, ALU.mult, R=[ML, CONST], W=[ML])
                    self.ts1("dve", gsT[:, l, sub, st, :], gsT[:, l, sub, st, :], 32.0, ALU.mult, R=[ML], W=[ML])
            for w in range(2):
                for st in range(2):
                    gt = modT[:, l, (6 * w + 2) * 8:(6 * w + 3) * 8, st]
                    self.ts1("dve", hgT[:, l, w, st, :], gt, 0.5, ALU.mult, R=[ML], W=[ML])
            yield

        for _ in ada_gen(0):
            pass
        self._adag = None

        def ada_step():
            if self._adag is not None:
                try:
                    next(self._adag)
                except StopIteration:
                    self._adag = None

        self.ts1("dve", normf[:], normf[:], 32.0, ALU.mult, R=[CONST], W=[CONST])
        for j in range(n_mla):
            self.ts1("dve", mlag[:, j, 0:3], mlag[:, j, 0:3], math.sqrt(384.0), ALU.mult, R=[CONST], W=[CONST])
            self.ts1("dve", mlag[:, j, 3:5], mlag[:, j, 3:5], 16.0, ALU.mult, R=[CONST], W=[CONST])
        if has_gqa:
            self.ts1("dve", gqag[:], gqag[:], 8.0, ALU.mult, R=[CONST], W=[CONST])
        if has_diff:
            lam_init = 0.8 - 0.6 * math.exp(-0.3 * 2)
            self.ts1("dve", diffg[:], diffg[:], math.sqrt(128.0) * (1.0 - lam_init), ALU.mult, R=[CONST], W=[CONST])
        if has_diff and 'dnolam' not in KDBG:
            self.S.op("dve", lambda e: e.tensor_tensor(out=diffl[:, 0, :], in0=diffl[:, 0, :], in1=diffl[:, 1, :], op=ALU.mult), [CONST], [CONST])
            self.S.op("dve", lambda e: e.tensor_tensor(out=diffl[:, 2, :], in0=diffl[:, 2, :], in1=diffl[:, 3, :], op=ALU.mult), [CONST], [CONST])
            self.S.op("dve", lambda e: e.reduce_sum(out=lamt[:, 0:1], in_=diffl[:, 0, :], axis=mybir.AxisListType.X), [CONST], [CONST])
            self.S.op("dve", lambda e: e.reduce_sum(out=lamt[:, 1:2], in_=diffl[:, 2, :], axis=mybir.AxisListType.X), [CONST], [CONST])
            self.act(lamt[:, 2:4], lamt[:, 0:2], AF.Exp, R=[CONST], W=[CONST])
            self.tt("dve", lamt[:, 4:5], lamt[:, 3:4], lamt[:, 2:3], ALU.subtract, R=[CONST], W=[CONST])
            self.ts1("dve", lamt[:, 5:6], lamt[:, 4:5], -lam_init, ALU.add, R=[CONST], W=[CONST])

        def rsqrt(dst, src, addc, n, R, W):
            ti = tmpi()
            self.act(tmpF[ti][:, 0:n], src, AF.Sqrt, bias=addc, R=R, W=[TMPB[ti]])
            self.S.op("dve", lambda e: e.reciprocal(out=dst, in_=tmpF[ti][:, 0:n]), [TMPB[ti]], W)

        def gs_ap(l, sub, st):
            return lambda c: gsT[:, l, sub, st, c:c + 1]

        def sh_ap(l, sub, st):
            return lambda c: modT[:, l, (3 * sub) * 8 + c, st:st + 1]

        def sumsq(src_fn, nch, n, srcR):
            bi = bank((5, 6))
            for c in range(nch):
                sq = sqt[c % 2]
                self.act(sq[:, 0:n], src_fn(c), AF.Square, R=srcR(c), W=[SQB[c % 2]])
                self.mm(pb[bi][:, 0:n], onesb[:], sq[:, 0:n], start=(c == 0), stop=(c == nch - 1), R=[SQB[c % 2], CONST], W=[PB[bi]])
            return bi

        def normmod(g, gs, sh):
            for blk in range(2):
                t0 = g * 1024 + blk * 512
                bi = sumsq(lambda c: yT[:, c, t0:t0 + 512], 8, 512, lambda c: [YB(c, g, blk)])
                rsqrt(rstd[:], pb[bi][:], 1024.0 * EPS, 512, [PB[bi]], [RSTD])
                for c in range(8):
                    ti = tmpi()
                    self.stt("dve", tmpF[ti][:], yT[:, c, t0:t0 + 512], gs(c), rstd[:], ALU.mult, ALU.mult,
                             R=[YB(c, g, blk), RSTD, MODL[cur['l']]], W=[TMPB[ti]])
                    self.act(hT[:, c, blk * 512:(blk + 1) * 512], tmpF[ti][:], AF.Identity, bias=sh(c),
                             R=[TMPB[ti], MODL[cur['l']]], W=[HB(c, blk)])

        uT = arena[:, 0:22 * 1024].rearrange("p (j t) -> p j t", j=22)

        def ffn(g, l, w):
            if self.noffn:
                return
            sub = 0 if w == 0 else 2
            arena_fence()
            UB = ABG()
            normmod(g, gs_ap(l, sub, g), sh_ap(l, sub, g))
            for j in range(22):
                slot, WBs = load_w(fin_d[l, w, j], 2048)
                wv = slot[:, 0:2048].rearrange("p (k s n) -> p k s n", k=8, s=2)
                for blk in range(2):
                    pi = (j * 2 + blk) % 2
                    for s in range(2):
                        bi = 2 * pi + s
                        for k in range(8):
                            self.mm(pb[bi][:], wv[:, k, s, :], hT[:, k, blk * 512:(blk + 1) * 512], start=(k == 0), stop=(k == 7),
                                    R=[WBs, HB(k, blk)], W=[PB[bi]])
                    ti = tmpi()
                    self.act(tmpF[ti][:], pb[2 * pi][:], AF.Silu, R=[PB[2 * pi]], W=[TMPB[ti]])
                    self.tt("dve", uT[:, j, blk * 512:(blk + 1) * 512], pb[2 * pi + 1][:], tmpF[ti][:], ALU.mult,
                            R=[PB[2 * pi + 1], TMPB[ti]], W=[UB(j, blk)])
                if w == 0:
                    ada_step()
            for m in range(8):
                slot, WBs = load_w(fout_d[l, w, m], 2816)
                wv = slot[:, 0:2816].rearrange("p (k n) -> p k n", k=22)
                for blk in range(2):
                    bi = 4 + (m * 2 + blk) % 2
                    for k in range(22):
                        self.mm(pb[bi][:], wv[:, k, :], uT[:, k, blk * 512:(blk + 1) * 512], start=(k == 0), stop=(k == 21),
                                R=[WBs, UB(k, blk)], W=[PB[bi]])
                    t0 = g * 1024 + blk * 512
                    self.stt("dve", yT[:, m, t0:t0 + 512], pb[bi][:], hgT[:, l, w, g, m:m + 1], yT[:, m, t0:t0 + 512], ALU.mult, ALU.add,
                             R=[PB[bi], YB(m, g, blk), MODL[cur['l']]], W=[YB(m, g, blk)])

        def proj(wv_fn, kc, rhs_fn, M, n, R, cands=(5, 6, 7)):
            bi = bank(cands)
            for k in range(kc):
                self.mm(pb[bi][0:M, 0:n], wv_fn(k), rhs_fn(k), start=(k == 0), stop=(k == kc - 1), R=R(k), W=[PB[bi]])
            return bi

        def rope_combine(A, ABuf, lo, hi, perm, table, t0, n, dst, dstW, cands=(5, 6, 7)):
            bi = bank(cands)
            self.mm(pb[bi][0:hi, 0:n], perm[lo:hi, 0:hi], A[lo:hi, 0:n], R=[ABuf, CONST], W=[PB[bi]])
            t2 = tmpi()
            self.tt("dve", tmpF[t2][lo:hi, 0:n], pb[bi][lo:hi, 0:n], table[lo:hi, 1, t0:t0 + n], ALU.mult, R=[PB[bi], CONST], W=[TMPB[t2]])
            self.tt("dve", A[lo:hi, 0:n], A[lo:hi, 0:n], table[lo:hi, 0, t0:t0 + n], ALU.mult, R=[ABuf, CONST], W=[ABuf])
            self.tt("pool", dst, A[lo:hi, 0:n], tmpF[t2][lo:hi, 0:n], ALU.add, R=[ABuf, TMPB[t2]], W=dstW)

        self._pending = []

        def flush_pending():
            p, self._pending = self._pending, []
            for fn in p:
                fn()

        self._units = None

        def begin_units(warm=0):
            self._units = []

        def end_units():
            us, self._units = self._units, None
            run_units(us)

        def attend(KT, QT_ap, nkb, nq, V_fn, vM, po, scale, RK, RQ, RV, zb=None):
            u = dict(KT=[KT(kb) for kb in range(nkb)], Q=QT_ap, nkb=nkb, nq=nq, V=[V_fn(kb) for kb in range(nkb)], vM=vM, po=po,
                     scale=scale, RK=RK, RQ=RQ, RV=RV, zb=zb, fin=None, imm=False)
            if self._units is not None:
                self._units.append(u)
            else:
                run_units([u])

        def attach_fin(fn, imm=False):
            if self._units:
                self._units[-1]["fin"] = fn
                self._units[-1]["imm"] = imm
            elif imm:
                fn()
            else:
                self._pending.append(fn)

        def run_units(units):
            G = 2

            def issue_qk(u, grp):
                sbanks = (0, 1, 2, 7) if u["zb"] is not None else (0, 1, 2, 5, 6)
                out = {}
                for kb in range(grp * G, min(u["nkb"], (grp + 1) * G)):
                    si = sbanks[self._srot % len(sbanks)]
                    self._srot += 1
                    out[kb] = si
                    self.mm(pb[si][:, 0:u["nq"]], u["KT"][kb], u["Q"], R=u["RK"] + u["RQ"], W=[PB[si]])
                return out

            pre = None
            for ui, u in enumerate(units):
                nxt = units[ui + 1] if ui + 1 < len(units) else None
                nkb, nq, po, zb = u["nkb"], u["nq"], u["po"], u["zb"]
                sbank = pre if pre is not None else issue_qk(u, 0)
                pre = None
                ngrp = (nkb + G - 1) // G
                for grp in range(1, ngrp + 1):
                    if grp < ngrp:
                        sbank.update(issue_qk(u, grp))
                    elif nxt is not None and (nxt["zb"] is None) == (zb is None):
                        pre = issue_qk(nxt, 0)
                    kbs = list(range((grp - 1) * G, min(nkb, grp * G)))
                    pis = []
                    for kb in kbs:
                        si = sbank.pop(kb)
                        pi = self._prot % 4
                        self._prot += 1
                        pis.append(pi)
                        self.act(ptile[pi][:, 0:nq], pb[si][:, 0:nq], AF.Exp, scale=u["scale"], R=[PB[si]], W=[PTB[pi]])
                    for kb, pi in zip(kbs, pis):
                        self.mm(pb[po][0:u["vM"], 0:nq], u["V"][kb], ptile[pi][:, 0:nq], start=(kb == 0), stop=(kb == nkb - 1),
                                R=[PTB[pi]] + u["RV"], W=[PB[po]])
                    if zb is not None:
                        for kb, pi in zip(kbs, pis):
                            self.mm(pb[zb][:, 0:nq], onesb[:], ptile[pi][:, 0:nq], start=(kb == 0), stop=(kb == nkb - 1),
                                    R=[PTB[pi], CONST], W=[PB[zb]])
                    if grp == 1:
                        flush_pending()
                if u["fin"] is not None:
                    if u["imm"]:
                        u["fin"]()
                    else:
                        self._pending.append(u["fin"])

        def finish_head(po, par, nq, dst, dstW):
            attach_fin(lambda: finish_head_now(po, par, nq, dst, dstW))

        def finish_head_now(po, par, nq, dst, dstW):
            zr = 64 if par == 0 else 0
            lo = 0 if par == 0 else 64
            ti = tmpi()
            self.memset("dve", tmpF[ti][:, 0:nq], 0.0, W=[TMPB[ti]])
            self.S.op("dve", lambda e: e.reciprocal(out=tmpF[ti][zr:zr + 1, 0:nq], in_=pb[po][zr:zr + 1, 0:nq]), [PB[po]], [TMPB[ti]])
            bi = bank((7,))
            self.mm(pb[bi][:, 0:nq], onesf[:], tmpF[ti][:, 0:nq], R=[TMPB[ti], CONST], W=[PB[bi]])
            t2 = tmpi()
            self.cp("act", tmpF[t2][lo:lo + 64, 0:nq], pb[bi][lo:lo + 64, 0:nq], R=[PB[bi]], W=[TMPB[t2]])
            self.tt("dve", dst, pb[po][lo:lo + 64, 0:nq], tmpF[t2][lo:lo + 64, 0:nq], ALU.mult, R=[PB[po], TMPB[t2]], W=dstW)

        def oproj(g, l, o_d):
            flush_pending()
            for m in range(8):
                slot, WBs = load_w(o_d[m], 1024)
                wv = slot[:, 0:1024].rearrange("p (k n) -> p k n", k=8)
                for blk in range(2):
                    bi = proj(lambda k: wv[:, k, :], 8, lambda k: hT[:, k, blk * 512:(blk + 1) * 512], 128, 512,
                              lambda k: [WBs, HB(k, blk)], cands=(4, 5))
                    t0 = g * 1024 + blk * 512
                    self.stt("dve", yT[:, m, t0:t0 + 512], pb[bi][:], modT[:, l, 5 * 8 + m, g:g + 1], yT[:, m, t0:t0 + 512], ALU.mult, ALU.add,
                             R=[PB[bi], YB(m, g, blk), MODL[cur['l']]], W=[YB(m, g, blk)])

        def allgather(src, dst, RB, WBf):
            self.S.op("pool", lambda e: e.collective_compute("AllGather", ALU.bypass, replica_groups=RG, ins=[src[:, :]], outs=[dst[:, :]]),
                      RB, WBf, dsem=d_cc)

        A_KT = [arena[:, 0:4608], arena[:, 4608:9216]]
        A_V = [arena[:, 9216:13824].rearrange("p (k d) -> p k d", k=36), arena[:, 13824:18432].rearrange("p (k d) -> p k d", k=36)]
        GIN, GOUT = BG(), BG()

        def run_pipelined(items):
            active = []
            for g in items:
                for a in list(active):
                    try:
                        next(a)
                    except StopIteration:
                        active.remove(a)
                try:
                    next(g)
                    active.append(g)
                except StopIteration:
                    pass
            while active:
                for a in list(active):
                    try:
                        next(a)
                    except StopIteration:
                        active.remove(a)

        def gqa(l):
            scale = 64 ** -0.5

            self._qi = 0

            def qk_chunk(wv_half, WBs, g, blk, gain, rope, dst, dstW, outfp=None, dst2=None):
                idx = self._qi
                self._qi += 1
                sq, SQ = sqt[idx % 2], SQB[idx % 2]
                rs, RS = (rstd, RSTD) if idx % 2 == 0 else (rstd2, RSTD2)
                bi = proj(lambda k: wv_half(k), 8, lambda k: hT[:, k, blk * 512:(blk + 1) * 512], 128, 512, lambda k: [WBs, HB(k, blk)],
                          cands=(3, 4, 5, 6, 7))
                self.act(sq[:], pb[bi][:], AF.Square, R=[PB[bi]], W=[SQ])
                yield
                b2 = bank((3, 4, 5, 6, 7))
                self.mm(pb[b2][:], bd64[:], sq[:], R=[SQ, CONST], W=[PB[b2]])
                rsqrt(rs[:], pb[b2][:], 64.0 * EPS, 512, [PB[b2]], [RS])
                ti = tmpi()
                self.stt("dve", tmpF[ti][:], pb[bi][:], gain, rs[:], ALU.mult, ALU.mult, R=[PB[bi], RS, CONST], W=[TMPB[ti]])
                if outfp is not None:
                    self.dma("sp", outfp, tmpF[ti][0:64, :], d_tmp[ti], R=[TMPB[ti]])
                yield
                if rope and 'norope' not in KDBG:
                    rope_combine(tmpF[ti], TMPB[ti], 0, 128, permB, ropeB, blk * 512, 512, dst, dstW, cands=(3, 4, 5, 6, 7))
                elif dst2 is not None:
                    self.cp("pool", dst, tmpF[ti][0:64, :], R=[TMPB[ti]], W=dstW)
                    self.cp("pool", dst2, tmpF[ti][64:128, :], R=[TMPB[ti]], W=dstW)
                else:
                    self.cp("pool", dst, tmpF[ti][:], R=[TMPB[ti]], W=dstW)

            def run_group(g):
                lat = (g == 1)
                arena_fence()
                KTB = [AB(), AB()]
                QB = ABG()
                normmod(g, gs_ap(l, 1, g), sh_ap(l, 1, g))
                if lat:
                    klat = arena[:, 18432:18432 + 4096].rearrange("p (c t) -> p c t", c=4)
                    KL = AB()
                else:
                    KTc = arena[:, 0:4096].rearrange("p (c t) -> p c t", c=4)
                    KTc1 = arena[:, 12288:16384].rearrange("p (c t) -> p c t", c=4)
                    VE = arena[:, 4096:4096 + 8 * 4 * 66].rearrange("p (k h d) -> p k h d", k=8, h=4)
                    VO = arena[:, 8192:8192 + 8 * 4 * 128].rearrange("p (k h d) -> p k h d", k=8, h=4)
                    KC_, VC_ = ABG(), AB()
                    for kv_ in range(4):
                        self.memset("pool", KTc[64:128, kv_, :], 0.0, W=[KC_(kv_)])
                        self.memset("pool", KTc1[0:64, kv_, :], 0.0, W=[KC_(kv_)])
                    if 'nomemset' not in KDBG:
                        self.memset("pool", VE[:, :, :, 64:65], 1.0, W=[VC_])
                        self.memset("pool", VO[:, :, :, 0:64], 0.0, W=[VC_])
                        self.memset("pool", VO[:, :, :, 0:1], 1.0, W=[VC_])
                def k_items():
                    for u in range(0 if 'nok' in KDBG else 2):
                        slot, WBs = load_w(gqa_w_d[4 + u], 2048)
                        wv = slot[:, 0:2048].rearrange("p (k n) -> p k n", k=8)
                        for h in range(2):
                            kvh = u * 2 + h
                            for blk in range(2):
                                if lat:
                                    yield qk_chunk(lambda k, wv=wv, h=h: wv[:, k, h * 128:(h + 1) * 128], WBs, g, blk, gqag[:, 1:2], True,
                                                   klat[:, kvh, blk * 512:(blk + 1) * 512], [KL])
                                else:
                                    yield qk_chunk(lambda k, wv=wv, h=h: wv[:, k, h * 128:(h + 1) * 128], WBs, g, blk, gqag[:, 1:2], False,
                                                   KTc[0:64, kvh, blk * 512:(blk + 1) * 512], [KC_(kvh)],
                                                   outfp=o_gk_d[kvh, :, blk * 512:(blk + 1) * 512],
                                                   dst2=KTc1[64:128, kvh, blk * 512:(blk + 1) * 512])
                run_pipelined(k_items())
                slot, WBs = load_w(gqa_w_d[6], 2048)
                wv = slot[:, 0:2048].rearrange("p (k n) -> p k n", k=8)
                if lat:
                    vlat = arena[:, 22528:22528 + 2048].rearrange("p (k d) -> p k d", k=8)
                    VL = AB()
                for tb in range(0 if 'nov' in KDBG else 8):
                    bi = proj(lambda k: hT[:, k, tb * 128:(tb + 1) * 128], 8, lambda k: wv[:, k, :], 128, 256,
                              lambda k: [WBs, HB(k, tb // 4)])
                    if lat:
                        self.cp("dve", vlat[:, tb, :], pb[bi][:, 0:256], R=[PB[bi]], W=[VL])
                    else:
                        si = stg()
                        if 'v1' not in KDBG:
                            self.cp("act", stage[si][:, 0:256], pb[bi][:, 0:256], R=[PB[bi]], W=[STB[si]])
                        if 'v2' not in KDBG:
                            self.dma("sp", o_gv_d[tb * 128:(tb + 1) * 128, :], stage[si][:, 0:256], d_st[si], R=[STB[si]])
                        pv = pb[bi][:, 0:256].rearrange("p (h d) -> p h d", h=4)
                        if 'v3' not in KDBG:
                            self.cp("act", VE[:, tb, :, 0:64], pv, R=[PB[bi]], W=[VC_])
                            self.cp("act", VO[:, tb, :, 64:128], pv, R=[PB[bi]], W=[VC_])
                if lat and 'nogather' not in KDBG:
                    self.dma("sp", gBk_in.ap().rearrange("(c p) t -> p c t", p=128), klat, GDS["bk"], R=[KL], W=[GIN("bk")])
                    self.dma("sp", gBv_in.ap().rearrange("(k p) d -> p k d", p=128), vlat, GDS["bv"], R=[VL], W=[GIN("bv")])
                    allgather(gBk_in, gBk_out, [GIN("bk")], [GOUT("bk")])
                    allgather(gBv_in, gBv_out, [GIN("bv")], [GOUT("bv")])
                def q_items():
                    for u in range(0 if 'noq' in KDBG else 4):
                        slot, WBs = load_w(gqa_w_d[u], 2048)
                        wv = slot[:, 0:2048].rearrange("p (k n) -> p k n", k=8)
                        for h in range(2):
                            m = u * 2 + h
                            for blk in range(2):
                                yield qk_chunk(lambda k, wv=wv, h=h: wv[:, k, h * 128:(h + 1) * 128], WBs, g, blk, gqag[:, 0:1], lat,
                                               QT[:, m, blk * 512:(blk + 1) * 512], [QB(m, blk)])
                run_pipelined(q_items())
                if (not lat) and 'noctxattn' in KDBG:
                    pass
                elif not lat:
                    begin_units()
                    for s in range(4):
                        for hd in range(16):
                            kvh, par, m = hd // 4, hd % 2, hd // 2
                            r0 = par * 64
                            po = 3 + (hd % 2)
                            Vt = VE if par == 0 else VO
                            vM = 65 if par == 0 else 128
                            Kz = KTc if par == 0 else KTc1
                            attend(lambda kb: Kz[:, kvh, s * 256 + kb * 128:s * 256 + (kb + 1) * 128],
                                   QT[:, m, s * 256:(s + 1) * 256], 2, 256,
                                   lambda kb: Vt[:, s * 2 + kb, kvh, 0:vM], vM, po, scale,
                                   [KC_(kvh)], [QB(m, s // 2)], [VC_])
                            finish_head(po, par, 256, hT[r0:r0 + 64, m, s * 256:(s + 1) * 256], [HB(m, s // 2)])
                    end_units()
                elif 'nolatattn' in KDBG:
                    pass
                else:
                    VEl = arena[:, 9216:9216 + 36 * 66].rearrange("p (k d) -> p k d", k=36)
                    VOl = arena[:, 13824:18432].rearrange("p (k d) -> p k d", k=36)
                    VEB, VOB = AB(), AB()
                    self.memset("pool", VEl[:, :, 64:65], 1.0, W=[VEB])
                    self.memset("pool", VOl[:, :, 0:64], 0.0, W=[VOB])
                    self.memset("pool", VOl[:, :, 0:1], 1.0, W=[VOB])
                    gk = gBk_out.ap().rearrange("(r c p) t -> c p r t", r=4, c=4)
                    gv = gBv_out.ap().rearrange("(k p) (h d) -> h p k d", p=128, h=4)
                    cv = gqa_cv_d.rearrange("(k p) (h d) -> h p k d", p=128, h=4)
                    self.memset("pool", A_KT[0][64:128, :], 0.0, W=[KTB[0]])
                    self.memset("pool", A_KT[1][0:64, :], 0.0, W=[KTB[1]])
                    for kvh in range(4):
                        for ks in range(2):
                            rs = slice(ks * 64, ks * 64 + 64)
                            self.dma("sp", A_KT[ks][rs, 0:4096].rearrange("p (r t) -> p r t", r=4), gk[kvh][rs], d_kv[ks], R=[GOUT("bk")], W=[KTB[ks]])
                            self.dma("pool", A_KT[ks][rs, 4096:4608], gqa_ck_d[kvh][rs], d_kv[ks], W=[KTB[ks]])
                        self.dma("sp", VEl[:, 0:32, 0:64], gv[kvh], d_kv[2], R=[GOUT("bv")], W=[VEB])
                        self.dma("pool", VEl[:, 32:36, 0:64], cv[kvh], d_kv[2], W=[VEB])
                        self.dma("sp", VOl[:, 0:32, 64:128], gv[kvh], d_kv[3], R=[GOUT("bv")], W=[VOB])
                        self.dma("pool", VOl[:, 32:36, 64:128], cv[kvh], d_kv[3], W=[VOB])
                        begin_units()
                        for hh in range(4):
                            hd = kvh * 4 + hh
                            par, m = hd % 2, hd // 2
                            r0 = par * 64
                            Vt = VEl if par == 0 else VOl
                            VtB = VEB if par == 0 else VOB
                            vM = 65 if par == 0 else 128
                            for qb in range(2):
                                po = 3 + (qb % 2)
                                attend(lambda kb: A_KT[par][:, kb * 128:(kb + 1) * 128],
                                       QT[:, m, qb * 512:(qb + 1) * 512], 36, 512,
                                       lambda kb: Vt[:, kb, 0:vM], vM, po, scale, [KTB[par]], [QB(m, qb)], [VtB])
                                finish_head(po, par, 512, hT[r0:r0 + 64, m, qb * 512:(qb + 1) * 512], [HB(m, qb)])
                        end_units()
                if 'noo' not in KDBG:
                    oproj(g, l, gqa_o_d)

            run_group(0)
            run_group(1)

        def diff(l):
            scale = 64 ** -0.5

            def run_group(g):
                lat = (g == 1)
                arena_fence()
                KTB = [AB(), AB()]
                VB = [AB(), AB()]
                QB = ABG()
                normmod(g, gs_ap(l, 1, g), sh_ap(l, 1, g))
                if lat:
                    KL = AB()
                    VL = AB()
                    klat = arena[:, 18432:18432 + 1024]
                    vlat = arena[:, 19456:19456 + 2048].rearrange("p (k d) -> p k d", k=8)
                else:
                    KTc = arena[:, 0:8192].rearrange("p (c t) -> p c t", c=8)
                    KTc1 = arena[:, 16384:24576].rearrange("p (c t) -> p c t", c=8)
                    Vc = arena[:, 8192:16384].rearrange("p (k d) -> p k d", k=8)
                    KC_, VC_ = ABG(), ABG()
                    for c_ in range(8):
                        self.memset("pool", KTc[64:128, c_, :], 0.0, W=[KC_(c_)])
                        self.memset("pool", KTc1[0:64, c_, :], 0.0, W=[KC_(c_)])

                def qk_item(wv, WBs, h, ch, blk, isq):
                    bi = proj(lambda k: wv[:, k, h * 128:(h + 1) * 128], 8, lambda k: hT[:, k, blk * 512:(blk + 1) * 512], 128, 512,
                              lambda k: [WBs, HB(k, blk)], cands=(3, 4, 5, 6, 7))
                    if isq:
                        dst, dstW = QT[:, ch, blk * 512:(blk + 1) * 512], [QB(ch, blk)]
                    elif lat:
                        dst, dstW = klat[:, blk * 512:(blk + 1) * 512], [KL]
                    else:
                        dst, dstW = KTc[:, ch, blk * 512:(blk + 1) * 512], [KC_(ch)]
                    if lat:
                        ti = tmpi()
                        self.cp("act", tmpF[ti][:], pb[bi][:], R=[PB[bi]], W=[TMPB[ti]])
                        yield
                        rope_combine(tmpF[ti], TMPB[ti], 0, 128, permB, ropeB, blk * 512, 512, dst, dstW, cands=(3, 4, 5, 6, 7))
                        if (not isq) and blk == 1:
                            self.dma("sp", gCk_in[ch // 2][(ch % 2) * 128:(ch % 2 + 1) * 128, :], klat, GDS["ck"], R=[KL], W=[GIN("ck", ch // 2)])
                            if ch % 2 == 1 and 'dnogather' not in KDBG:
                                allgather(gCk_in[ch // 2], gCk_out[ch // 2], [GIN("ck", ch // 2)], [GOUT("ck", ch // 2)])
                    else:
                        if isq:
                            self.cp("act", dst, pb[bi][:], R=[PB[bi]], W=dstW)
                        else:
                            self.cp("act", KTc[0:64, ch, blk * 512:(blk + 1) * 512], pb[bi][0:64, :], R=[PB[bi]], W=dstW)
                            self.cp("act", KTc1[64:128, ch, blk * 512:(blk + 1) * 512], pb[bi][64:128, :], R=[PB[bi]], W=dstW)
                            si = stg()
                            self.cp("dve", stage[si][:], pb[bi][:], R=[PB[bi]], W=[STB[si]])
                            self.dma("sp", o_dk_d[:, ch, blk * 512:(blk + 1) * 512], stage[si][:], d_st[si], R=[STB[si]])

                def qk_items(us, isq):
                    for u in us:
                        slot, WBs = load_w(diff_w_d[u], 2048)
                        wv = slot[:, 0:2048].rearrange("p (k n) -> p k n", k=8)
                        for h in range(2):
                            ch = (u % 4) * 2 + h
                            for blk in range(2):
                                yield qk_item(wv, WBs, h, ch, blk, isq)

                def qk_unit(u, isq):
                    run_pipelined(qk_items([u], isq))

                run_pipelined(qk_items(range(4, 4 if 'dnok' in KDBG else 8), False))
                for u in range(8, 8 if 'dnov' in KDBG else 12):
                    slot, WBs = load_w(diff_w_d[u], 2048)
                    wv = slot[:, 0:2048].rearrange("p (k n) -> p k n", k=8)
                    for tb in range(8):
                        bi = proj(lambda k: hT[:, k, tb * 128:(tb + 1) * 128], 8, lambda k: wv[:, k, :], 128, 256,
                                  lambda k: [WBs, HB(k, tb // 4)])
                        c0 = (u - 8) * 256
                        if lat:
                            self.cp("dve", vlat[:, tb, :], pb[bi][:, 0:256], R=[PB[bi]], W=[VL])
                        else:
                            si = stg()
                            self.cp("act", stage[si][:, 0:256], pb[bi][:, 0:256], R=[PB[bi]], W=[STB[si]])
                            self.dma("sp", o_dv_d[tb * 128:(tb + 1) * 128, c0:c0 + 256], stage[si][:, 0:256], d_st[si], R=[STB[si]])
                            self.cp("dve", Vc[:, tb, c0:c0 + 256], pb[bi][:, 0:256], R=[PB[bi]], W=[VC_(u - 8)])
                    if lat:
                        self.dma("sp", gCv_in[u - 8].ap().rearrange("(k p) d -> p k d", p=128), vlat, GDS["cv"], R=[VL], W=[GIN("cv", u - 8)])
                        if 'dnogather' not in KDBG:
                            allgather(gCv_in[u - 8], gCv_out[u - 8], [GIN("cv", u - 8)], [GOUT("cv", u - 8)])
                run_pipelined(qk_items(range(0, 0 if 'dnoq' in KDBG else 4), True))

                def head_finish(hd, q0, nq, OW):
                    t1, t2, t3 = tmpi(), tmpi(), tmpi()
                    self.S.op("dve", lambda e: e.reciprocal(out=tmpF[t1][:, 0:nq], in_=pb[4][:, 0:nq]), [PB[4]], [TMPB[t1]])
                    self.tt("dve", tmpF[t1][:, 0:nq], pb[3][:, 0:nq], tmpF[t1][:, 0:nq], ALU.mult, R=[PB[3], TMPB[t1]], W=[TMPB[t1]])
                    self.S.op("dve", lambda e: e.reciprocal(out=tmpF[t2][:, 0:nq], in_=pb[6][:, 0:nq]), [PB[6]], [TMPB[t2]])
                    self.tt("dve", tmpF[t2][:, 0:nq], pb[5][:, 0:nq], tmpF[t2][:, 0:nq], ALU.mult, R=[PB[5], TMPB[t2]], W=[TMPB[t2]])
                    self.stt("dve", tmpF[t3][:, 0:nq], tmpF[t2][:, 0:nq], lamt[:, 5:6], tmpF[t1][:, 0:nq], ALU.mult, ALU.add,
                             R=[TMPB[t1], TMPB[t2], CONST], W=[TMPB[t3]])
                    self.act(sqt[0][:, 0:nq], tmpF[t3][:, 0:nq], AF.Square, R=[TMPB[t3]], W=[SQB[0]])
                    self.mm(pb[7][:, 0:nq], onesb[:], sqt[0][:, 0:nq], R=[SQB[0], CONST], W=[PB[7]])
                    rsqrt(rstd[:, 0:nq], pb[7][:, 0:nq], 128.0 * EPS, nq, [PB[7]], [RSTD])
                    self.stt("dve", hT[:, hd, q0:q0 + nq], tmpF[t3][:, 0:nq], diffg[:, 0:1], rstd[:, 0:nq], ALU.mult, ALU.mult,
                             R=[TMPB[t3], RSTD, CONST], W=OW)

                if (not lat and 'dnoctx' in KDBG) or (lat and 'dnolat' in KDBG):
                    pass
                elif not lat:
                    begin_units()
                    for s in range(4):
                        for hd in range(8):
                            for mp in range(2):
                                r0 = mp * 64
                                Kz = KTc if mp == 0 else KTc1
                                attend(lambda kb: Kz[:, hd, s * 256 + kb * 128:s * 256 + (kb + 1) * 128],
                                       QT[:, hd, s * 256:(s + 1) * 256], 2, 256,
                                       lambda kb: Vc[:, s * 2 + kb, hd * 128:(hd + 1) * 128], 128, 3 + 2 * mp, scale,
                                       [KC_(hd)], [QB(hd, s // 2)], [VC_(hd // 2)], zb=4 + 2 * mp)
                            attach_fin(lambda hd=hd, s=s: head_finish(hd, s * 256, 256, [HB(hd, s // 2)]), imm=True)
                    end_units()
                else:
                    cv = diff_cv_d.rearrange("(k p) (h d) -> h p k d", p=128, h=8)
                    self.memset("pool", A_KT[0][64:128, :], 0.0, W=[KTB[0]])
                    self.memset("pool", A_KT[1][0:64, :], 0.0, W=[KTB[1]])
                    for hd in range(8):
                        ks = hd % 2
                        gk = gCk_out[hd // 2].ap().rearrange("(r c p) t -> c p r t", r=4, c=2)[hd % 2]
                        gv = gCv_out[hd // 2].ap().rearrange("(k p) (h d) -> h p k d", p=128, h=2)[hd % 2]
                        for mp_ in range(2):
                            rs = slice(mp_ * 64, mp_ * 64 + 64)
                            self.dma("sp", A_KT[mp_][rs, 0:4096].rearrange("p (r t) -> p r t", r=4), gk[rs], d_kv[mp_], R=[GOUT("ck", hd // 2)], W=[KTB[mp_]])
                            self.dma("pool", A_KT[mp_][rs, 4096:4608], diff_ck_d[hd][rs], d_kv[mp_], W=[KTB[mp_]])
                        self.dma("sp", A_V[ks][:, 0:32, :], gv, d_kv[2 + ks], R=[GOUT("cv", hd // 2)], W=[VB[ks]])
                        self.dma("pool", A_V[ks][:, 32:36, :], cv[hd], d_kv[2 + ks], W=[VB[ks]])
                        begin_units()
                        for qb in range(2):
                            for mp in range(2):
                                r0 = mp * 64
                                attend(lambda kb: A_KT[mp][:, kb * 128:(kb + 1) * 128],
                                       QT[:, hd, qb * 512:(qb + 1) * 512], 36, 512,
                                       lambda kb: A_V[ks][:, kb, :], 128, 3 + 2 * mp, scale,
                                       [KTB[mp]], [QB(hd, qb)], [VB[ks]], zb=4 + 2 * mp)
                            attach_fin(lambda hd=hd, qb=qb: head_finish(hd, qb * 512, 512, [HB(hd, qb)]), imm=True)
                        end_units()
                if 'dnoo' not in KDBG:
                    oproj(g, l, diff_o_d)

            run_group(0)
            run_group(1)

        def mla(l, j):
            scale = 96 ** -0.5
            cqT = QTf[:, 0:3072].rearrange("p (c t) -> p c t", c=3)
            qsl = [QTf[:, 3072:4096], QTf[:, 4096:5120]]

            def run_group(g):
                lat = (g == 1)
                arena_fence()
                QSB = [AB(), AB()]
                CQB = ABG()
                nk = 4608 if lat else 1024
                nkb = nk // 128
                normmod(g, gs_ap(l, 1, g), sh_ap(l, 1, g))
                if lat:
                    ckvT = arena[:, 0:9216].rearrange("p (c t) -> p c t", c=2)
                    KT1 = arena[:, 9216:13824]
                    VEl = arena[:, 13824:13824 + 36 * 66].rearrange("p (k d) -> p k d", k=36)
                    VOl = arena[:, 17408:17408 + 4608].rearrange("p (k d) -> p k d", k=36)
                    cl = arena[:, 22016:22016 + 2048].rearrange("p (c t) -> p c t", c=2)
                    krl = arena[:, 16384:17408]
                else:
                    ckvT = arena[:, 0:2048].rearrange("p (c t) -> p c t", c=2)
                    KT1 = arena[:, 9216:9216 + 1024]
                    VEl = arena[:, 13824:13824 + 8 * 66].rearrange("p (k d) -> p k d", k=8)
                    VOl = arena[:, 17408:17408 + 1024].rearrange("p (k d) -> p k d", k=8)
                CKB, K1B, VEB, VOB, CLB, KRB = ABG(), AB(), AB(), AB(), AB(), AB()
                self.memset("pool", VEl[:, :, 64:65], 1.0, W=[VEB])
                self.memset("pool", VOl[:, :, 0:64], 0.0, W=[VOB])
                self.memset("pool", VOl[:, :, 0:1], 1.0, W=[VOB])
                self.memset("pool", KT1[64:128, :], 0.0, W=[K1B])
                self.memset("pool", qsl[0][64:128, :], 0.0, W=[QSB[0]])
                self.memset("pool", qsl[1][64:128, :], 0.0, W=[QSB[1]])
                units = []
                for u in range(3):
                    nn_ = 2048 if u < 2 else 1280
                    slot, WBs = load_w(mla_down_d[j, u][:, 0:nn_], nn_)
                    n = 256 if u < 2 else 160
                    units.append((slot[:, 0:8 * n].rearrange("p (k n) -> p k n", k=8), WBs))

                def wcol(c0, M):
                    u = c0 // 256
                    wv, WBs = units[u]
                    o = c0 - u * 256
                    return (lambda k: wv[:, k, o:o + M]), WBs

                for blk in range(2):
                    hr = lambda k: hT[:, k, blk * 512:(blk + 1) * 512]
                    bis = []
                    for c in range(3):
                        wf, WBs = wcol(c * 128, 128)
                        bis.append(proj(wf, 8, hr, 128, 512, lambda k, WBs=WBs: [WBs, HB(k, blk)], cands=(0, 1, 2)))
                    sb_ = sumsq(lambda c: pb[bis[c]][:], 3, 512, lambda c: [PB[bis[c]]])
                    rsqrt(rstd[:], pb[sb_][:], 384.0 * EPS, 512, [PB[sb_]], [RSTD])
                    for c in range(3):
                        self.stt("dve", cqT[:, c, blk * 512:(blk + 1) * 512], pb[bis[c]][:], mlag[:, j, c:c + 1], rstd[:], ALU.mult, ALU.mult,
                                 R=[PB[bis[c]], RSTD, CONST], W=[CQB(c, blk)])
                    bis = []
                    for c in range(2):
                        wf, WBs = wcol(384 + c * 128, 128)
                        bis.append(proj(wf, 8, hr, 128, 512, lambda k, WBs=WBs: [WBs, HB(k, blk)], cands=(0, 1, 2)))
                    sb_ = sumsq(lambda c: pb[bis[c]][:], 2, 512, lambda c: [PB[bis[c]]])
                    rsqrt(rstd[:], pb[sb_][:], 256.0 * EPS, 512, [PB[sb_]], [RSTD])
                    for c in range(2):
                        ti = tmpi()
                        self.stt("dve", tmpF[ti][:], pb[bis[c]][:], mlag[:, j, 3 + c:4 + c], rstd[:], ALU.mult, ALU.mult,
                                 R=[PB[bis[c]], RSTD, CONST], W=[TMPB[ti]])
                        if lat:
                            self.cp("act", cl[:, c, blk * 512:(blk + 1) * 512], tmpF[ti][:], R=[TMPB[ti]], W=[CLB])
                        else:
                            self.cp("act", ckvT[:, c, blk * 512:(blk + 1) * 512], tmpF[ti][:], R=[TMPB[ti]], W=[CKB(c)])
                            self.dma("sp", o_ckv_d[j, :, c, blk * 512:(blk + 1) * 512], tmpF[ti][:], d_tmp[ti], R=[TMPB[ti]])
                    wf, WBs = wcol(576, 96)
                    bi = proj(wf, 8, hr, 96, 512, lambda k, WBs=WBs: [WBs, HB(k, blk)], cands=(0, 1, 2))
                    ti = tmpi()
                    self.cp("act", tmpF[ti][64:96, :], pb[bi][64:96, :], R=[PB[bi]], W=[TMPB[ti]])
                    if lat:
                        rope_combine(tmpF[ti], TMPB[ti], 64, 96, permA, ropeA, blk * 512, 512, krl[64:96, blk * 512:(blk + 1) * 512], [KRB])
                    else:
                        self.dma("sp", o_kr_d[j, :, blk * 512:(blk + 1) * 512], tmpF[ti][64:96, :], d_tmp[ti], R=[TMPB[ti]])
                        self.cp("pool", KT1[64:96, blk * 512:(blk + 1) * 512], tmpF[ti][64:96, :], R=[TMPB[ti]], W=[K1B])
                if lat:
                    ga = gA_in.ap()
                    self.dma("sp", ga[0:256, :].rearrange("(c p) t -> p c t", p=128), cl, GDS["a"], R=[CLB], W=[GIN("a")])
                    self.dma("sp", ga[256:288, :], krl[64:96, :], GDS["a"], R=[KRB], W=[GIN("a")])
                    allgather(gA_in, gA_out, [GIN("a")], [GOUT("a")])
                    go = gA_out.ap().rearrange("(r f) t -> f r t", r=4)
                    for c in range(2):
                        self.dma("sp", ckvT[:, c, 0:4096].rearrange("p (r t) -> p r t", r=4), go[c * 128:(c + 1) * 128], d_kv[c], R=[GOUT("a")], W=[CKB(c)])
                        self.dma("pool", ckvT[:, c, 4096:4608], mla_cckv_d[j, :, c, :], d_kv[c], W=[CKB(c)])
                    self.dma("sp", KT1[64:96, 0:4096].rearrange("p (r t) -> p r t", r=4), go[256:288], d_kv[2], R=[GOUT("a")], W=[K1B])
                    self.dma("pool", KT1[64:96, 4096:4608], mla_ckr_d[j], d_kv[2], W=[K1B])
                for hg in range(4):
                    sq_, WBq = load_w(mla_uq_d[j, hg], 1152)
                    wq = sq_[:, 0:1152].rearrange("p (h k n) -> p h k n", h=4, k=3)
                    skv, WBkv = load_w(mla_ukv_d[j, hg], 1024)
                    wkv = skv[:, 0:1024].rearrange("p (h k n) -> p h k n", h=4, k=2)
                    for hh in range(4):
                        hd = hg * 4 + hh
                        par, m = hd % 2, hd // 2
                        for kb4 in range(nk // 512):
                            bi = proj(lambda k: wkv[:, hh, k, 0:64], 2, lambda k: ckvT[:, k, kb4 * 512:(kb4 + 1) * 512], 64, 512,
                                      lambda k: [WBkv, CKB(k)])
                            self.cp("dve", KT1[0:64, kb4 * 512:(kb4 + 1) * 512], pb[bi][0:64, :], R=[PB[bi]], W=[K1B])
                        Vt, VtB = (VEl, VEB) if par == 0 else (VOl, VOB)
                        c0 = 0 if par == 0 else 64
                        for kb8 in range((nkb + 7) // 8):
                            nb = min(8, nkb - kb8 * 8)
                            bi = bank()
                            for q in range(nb):
                                kb = kb8 * 8 + q
                                for k in range(2):
                                    self.mm(pb[bi][:, q * 64:(q + 1) * 64], ckvT[:, k, kb * 128:(kb + 1) * 128], wkv[:, hh, k, 64:128],
                                            start=(k == 0), stop=(k == 1), R=[CKB(k), WBkv], W=[PB[bi]])
                            self.cp("act", Vt[:, kb8 * 8:kb8 * 8 + nb, c0:c0 + 64], pb[bi][:, 0:nb * 64].rearrange("p (q d) -> p q d", d=64),
                                    R=[PB[bi]], W=[VtB])
                        qs, QSb = qsl[hd % 2], QSB[hd % 2]
                        for blk in range(2):
                            bi = proj(lambda k: wq[:, hh, k, :], 3, lambda k: cqT[:, k, blk * 512:(blk + 1) * 512], 96, 512,
                                      lambda k: [WBq, CQB(k, blk)])
                            self.cp("act", qs[0:64, blk * 512:(blk + 1) * 512], pb[bi][0:64, :], R=[PB[bi]], W=[QSb])
                            if lat:
                                ti = tmpi()
                                self.cp("dve", tmpF[ti][64:96, :], pb[bi][64:96, :], R=[PB[bi]], W=[TMPB[ti]])
                                rope_combine(tmpF[ti], TMPB[ti], 64, 96, permA, ropeA, blk * 512, 512, qs[64:96, blk * 512:(blk + 1) * 512], [QSb])
                            else:
                                self.cp("dve", qs[64:96, blk * 512:(blk + 1) * 512], pb[bi][64:96, :], R=[PB[bi]], W=[QSb])
                        vM = 65 if par == 0 else 128
                        r0 = par * 64
                        begin_units()
                        if lat:
                            for qb in range(2):
                                po = 3 + (qb % 2)
                                attend(lambda kb: KT1[:, kb * 128:(kb + 1) * 128], qs[:, qb * 512:(qb + 1) * 512], 36, 512,
                                       lambda kb: Vt[:, kb, 0:vM], vM, po, scale, [K1B], [QSb], [VtB])
                                finish_head(po, par, 512, hT[r0:r0 + 64, m, qb * 512:(qb + 1) * 512], [HB(m, qb)])
                        else:
                            for s in range(4):
                                po = 3 + (s % 2)
                                attend(lambda kb: KT1[:, s * 256 + kb * 128:s * 256 + (kb + 1) * 128], qs[:, s * 256:(s + 1) * 256], 2, 256,
                                       lambda kb: Vt[:, s * 2 + kb, 0:vM], vM, po, scale, [K1B], [QSb], [VtB])
                                finish_head(po, par, 256, hT[r0:r0 + 64, m, s * 256:(s + 1) * 256], [HB(m, s // 2)])
                        end_units()
                oproj(g, l, mla_o_d[j])

            run_group(0)
            run_group(1)

        for l in range(NL):
            cur["l"] = l
            if l + 1 < NL:
                self._adag = ada_gen(l + 1)
            ffn(0, l, 0)
            ffn(1, l, 0)
            while self._adag is not None:
                ada_step()
            kind = l % 3
            if kind == 0:
                mla(l, l // 3)
            elif kind == 1:
                gqa(l)
            else:
                diff(l)
            ffn(0, l, 1)
            ffn(1, l, 1)

        for g in range(2):
            for blk in range(2):
                t0 = g * 1024 + blk * 512
                bi = sumsq(lambda c: yT[:, c, t0:t0 + 512], 8, 512, lambda c: [YB(c, g, blk)])
                rsqrt(rstd[:], pb[bi][:], 1024.0 * EPS, 512, [PB[bi]], [RSTD])
                for c in range(8):
                    self.stt("dve", yT[:, c, t0:t0 + 512], yT[:, c, t0:t0 + 512], normf[:, c:c + 1], rstd[:], ALU.mult, ALU.mult,
                             R=[YB(c, g, blk), RSTD, CONST], W=[YB(c, g, blk)])
                self.dma("sp", yout_d[:, :, t0:t0 + 512], yT[:, :, t0:t0 + 512], d_out, R=[YB(c, g, blk) for c in range(8)])
        fin = Buf()
        fin.r = {d: d.val for d in [d_out] + d_st + d_tmp if d.val > 0}
        self.S.op("sp", lambda e: e.nop(), writes=[fin])

        for d in S.dsems:
            d.h = self.es.enter_context(nc.semaphore("d_" + d.name))
        S.finalize()
        block = self.es.enter_context(nc.Block())
        for name in ENGS:
            def mk(name=name):
                def f(e):
                    S.emit(name, e, sems)
                return f
            getattr(block, ATTR[name])(mk())
        self.es.close()
        return nc


def _fm(x2d):
    T, Dd = x2d.shape
    return np.ascontiguousarray(x2d.T.reshape(Dd // 128, 128, T).transpose(1, 0, 2))


def _wt(W):
    K, N = W.shape
    return np.ascontiguousarray(W.reshape(K // 128, 128, N).transpose(1, 0, 2).reshape(128, (K // 128) * N))


def _pad(a, n):
    out = np.zeros((a.shape[0], n), np.float32)
    out[:, :a.shape[1]] = a
    return out


def _rope_tables(pos, rot_dim):
    GRID_W = 64
    r = (pos // GRID_W).astype(np.float32)
    cidx = (pos % GRID_W).astype(np.float32)
    n_f = rot_dim // 4
    freqs = (np.float32(10000.0) ** (-np.arange(n_f, dtype=np.float32) / np.float32(n_f))).astype(np.float32)
    ang = np.concatenate([r[:, None] * freqs, cidx[:, None] * freqs], axis=-1).astype(np.float32)
    cos, sin = np.cos(ang).astype(np.float32), np.sin(ang).astype(np.float32)
    cos2 = np.concatenate([cos, cos], axis=1).T
    sinp = np.concatenate([-sin, sin], axis=1).T
    return cos2, sinp


_CACHE = {}


def _get_nc(nl, noffn=False):
    if (nl, noffn) not in _CACHE:
        _CACHE[(nl, noffn)] = KB(nl, noffn).build()
    return _CACHE[(nl, noffn)]


def kernel(x_prompt, x_sample, cache_mla_ckv, cache_mla_kr, cache_gqa_k, cache_gqa_v,
           cache_diff_k, cache_diff_v, c, c_ctx,
           ada_w, ada_b, norm_ffn1, norm_mix, norm_ffn2,
           ffn1_w_in, ffn1_w_out, ffn2_w_in, ffn2_w_out,
           mla_w_down, mla_g_q, mla_g_kv, mla_w_uq, mla_w_uk, mla_w_uv, mla_w_o,
           gqa_w_qkv, gqa_g_q, gqa_g_k, gqa_w_o,
           diff_w_qkv, diff_lq1, diff_lk1, diff_lq2, diff_lk2, diff_g_sub, diff_w_o,
           norm_final, _nl=4, _noffn=False, _trace=False):
    f = lambda a: np.asarray(a, dtype=np.float32)
    x_prompt, x_sample = f(x_prompt), f(x_sample)
    NL = _nl
    if 'probe_noffn' in KDBG:
        _noffn = True
    nc = _get_nc(NL, _noffn)
    n_mla = (NL + 2) // 3
    sh = {}
    ada_w = f(ada_w)
    sh["ada_t"] = np.stack([np.stack([_wt(ada_w[l][:, s * 256:(s + 1) * 256]) for s in range(36)]) for l in range(NL)])
    sh["ada_bT"] = np.ascontiguousarray(f(ada_b)[:NL].reshape(NL, 72, 128).transpose(2, 0, 1))
    ng = np.stack([f(norm_ffn1)[:NL], f(norm_mix)[:NL], f(norm_ffn2)[:NL]], axis=1)
    sh["normg"] = np.ascontiguousarray(ng.reshape(NL, 3, 8, 128).transpose(3, 0, 1, 2))
    sh["normf"] = np.ascontiguousarray(f(norm_final).reshape(8, 128).T)
    if not _noffn:
        win = np.stack([f(ffn1_w_in)[:NL], f(ffn2_w_in)[:NL]], axis=1)
        sh["ffn_in_t"] = np.ascontiguousarray(
            win.reshape(NL, 2, 8, 128, 2, 22, 128).transpose(0, 1, 5, 3, 2, 4, 6).reshape(NL, 2, 22, 128, 2048))
        wout = np.stack([f(ffn1_w_out)[:NL], f(ffn2_w_out)[:NL]], axis=1)
        sh["ffn_out_t"] = np.ascontiguousarray(
            wout.reshape(NL, 2, 22, 128, 8, 128).transpose(0, 1, 4, 3, 2, 5).reshape(NL, 2, 8, 128, 2816))
    consts = np.zeros((128, 3, 128), np.float32)
    consts[:, 0, :] = np.eye(128, dtype=np.float32)
    for jj in range(128):
        base = (jj // 64) * 64
        o = jj - base
        consts[base + (o + 32) % 64, 2, jj] = 1.0
    for jj in range(64, 96):
        o = jj - 64
        consts[64 + (o + 16) % 32, 1, jj] = 1.0
    sh["consts"] = consts
    wd = f(mla_w_down)[:n_mla]
    sh["mla_down_t"] = np.stack([np.stack([_pad(_wt(wd[j][:, 0:256]), 2048), _pad(_wt(wd[j][:, 256:512]), 2048),
                                           _pad(_wt(wd[j][:, 512:672]), 2048)]) for j in range(n_mla)])
    wuq = f(mla_w_uq)[:n_mla]
    sh["mla_uq_t"] = np.ascontiguousarray(
        wuq.reshape(n_mla, 3, 128, 4, 4, 96).transpose(0, 3, 2, 4, 1, 5).reshape(n_mla, 4, 128, 1152))
    wuk = f(mla_w_uk)[:n_mla].reshape(n_mla, 2, 128, 4, 4, 64)
    wuv = f(mla_w_uv)[:n_mla].reshape(n_mla, 2, 128, 4, 4, 64)
    ukv = np.concatenate([wuk, wuv], axis=-1)
    sh["mla_ukv_t"] = np.ascontiguousarray(ukv.transpose(0, 3, 2, 4, 1, 5).reshape(n_mla, 4, 128, 1024))
    wo = f(mla_w_o)[:n_mla]
    sh["mla_o_t"] = np.stack([np.stack([_wt(wo[j][:, m * 128:(m + 1) * 128]) for m in range(8)]) for j in range(n_mla)])
    gq = f(mla_g_q)[:n_mla].reshape(n_mla, 3, 128)
    gkv = f(mla_g_kv)[:n_mla].reshape(n_mla, 2, 128)
    sh["mla_g"] = np.ascontiguousarray(np.concatenate([gq, gkv], axis=1).transpose(2, 0, 1))
    if NL >= 2:
        wq = f(gqa_w_qkv)[0]
        units = [_wt(wq[:, u * 256:(u + 1) * 256]) for u in range(4)]
        for u in range(2):
            cols = []
            for h in range(2):
                kvh = u * 2 + h
                blk = wq[:, 1024 + kvh * 64:1024 + (kvh + 1) * 64]
                cols += [blk, blk]
            units.append(_wt(np.concatenate(cols, axis=1)))
        units.append(_wt(wq[:, 1280:1536]))
        sh["gqa_t"] = np.stack(units)
        go = f(gqa_w_o)[0]
        sh["gqa_o_t"] = np.stack([_wt(go[:, m * 128:(m + 1) * 128]) for m in range(8)])
        sh["gqa_g"] = np.ascontiguousarray(np.stack([np.tile(f(gqa_g_q)[0], 2), np.tile(f(gqa_g_k)[0], 2)], axis=1))
    if NL >= 3:
        wq = f(diff_w_qkv)[0]
        sh["diff_t"] = np.stack([_wt(wq[:, u * 256:(u + 1) * 256]) for u in range(12)])
        do = f(diff_w_o)[0]
        sh["diff_o_t"] = np.stack([_wt(do[:, m * 128:(m + 1) * 128]) for m in range(8)])
        sh["diff_l"] = np.ascontiguousarray(np.broadcast_to(np.stack([f(diff_lq1)[0], f(diff_lk1)[0], f(diff_lq2)[0], f(diff_lk2)[0]])[None], (128, 4, 64)))
        sh["diff_g"] = np.ascontiguousarray(f(diff_g_sub)[0].reshape(128, 1))

    in_maps = []
    for core in range(8):
        b, r = core // 4, core % 4
        m = dict(sh)
        xc = x_prompt[4 * core:4 * core + 4].reshape(1024, 1024)
        xl = x_sample[b, r * 1024:(r + 1) * 1024]
        m["xT"] = np.concatenate([_fm(xc), _fm(xl)], axis=2)
        m["cT"] = np.ascontiguousarray(np.stack([f(c_ctx), f(c)[b]], axis=1).reshape(8, 128, 2).transpose(1, 0, 2))
        pos = np.arange(r * 1024, (r + 1) * 1024)
        ca, sa = _rope_tables(pos, 32)
        ra = np.zeros((128, 2, 1024), np.float32)
        ra[64:96, 0], ra[64:96, 1] = ca, sa
        m["ropeA"] = ra
        cb, sb_ = _rope_tables(pos, 64)
        m["ropeB"] = np.ascontiguousarray(np.stack([np.tile(cb, (2, 1)), np.tile(sb_, (2, 1))], axis=1))
        ck = f(cache_mla_ckv)[b][:n_mla]
        m["mla_cckvT"] = np.ascontiguousarray(ck.transpose(0, 2, 1).reshape(n_mla, 2, 128, 512).transpose(0, 2, 1, 3))
        m["mla_ckrT"] = np.ascontiguousarray(f(cache_mla_kr)[b][:n_mla].transpose(0, 2, 1))
        if NL >= 2:
            gk = f(cache_gqa_k)[b, 0]
            kt = gk.transpose(1, 2, 0)
            m["gqa_ckT"] = np.ascontiguousarray(np.concatenate([kt, kt], axis=1))
            m["gqa_cv"] = np.ascontiguousarray(f(cache_gqa_v)[b, 0].reshape(512, 256))
        if NL >= 3:
            dk = f(cache_diff_k)[b, 0].reshape(512, 8, 128)
            m["diff_ckT"] = np.ascontiguousarray(dk.transpose(1, 2, 0))
            m["diff_cv"] = np.ascontiguousarray(f(cache_diff_v)[b, 0].reshape(512, 1024))
        in_maps.append(m)

    if _trace:
        res = run_bass_kernel_spmd(nc, in_maps, core_ids=list(range(8)), trace=True)
        _CACHE['last_res'] = res
    else:
        res = run_bass_kernel_spmd(nc, in_maps, core_ids=list(range(8)))
    R = res.results
    y_prompt = np.zeros((32, 256, 1024), np.float32)
    y_sample = np.zeros((2, 4096, 1024), np.float32)
    n_ckv = np.zeros((32, 2, 256, 256), np.float32)
    n_kr = np.zeros((32, 2, 256, 32), np.float32)
    n_gk = np.zeros((32, 1, 256, 4, 64), np.float32)
    n_gv = np.zeros((32, 1, 256, 4, 64), np.float32)
    n_dk = np.zeros((32, 1, 256, 8, 2, 64), np.float32)
    n_dv = np.zeros((32, 1, 256, 8, 128), np.float32)
    for core in range(8):
        b, r = core // 4, core % 4
        o = R[core]
        yt = np.asarray(o["yT_out"]).transpose(2, 1, 0).reshape(2048, 1024)
        y_prompt[4 * core:4 * core + 4] = yt[:1024].reshape(4, 256, 1024)
        y_sample[b, r * 1024:(r + 1) * 1024] = yt[1024:]
        ck = np.asarray(o["o_ckv"]).transpose(0, 3, 2, 1).reshape(2, 4, 256, 256)
        n_ckv[4 * core:4 * core + 4] = ck.transpose(1, 0, 2, 3)
        kr = np.asarray(o["o_kr"]).transpose(0, 2, 1).reshape(2, 4, 256, 32)
        n_kr[4 * core:4 * core + 4] = kr.transpose(1, 0, 2, 3)
        gk = np.asarray(o["o_gk"]).transpose(2, 0, 1).reshape(4, 256, 4, 64)
        n_gk[4 * core:4 * core + 4, 0] = gk
        n_gv[4 * core:4 * core + 4, 0] = np.asarray(o["o_gv"]).reshape(4, 256, 4, 64)
        dk = np.asarray(o["o_dk"]).transpose(2, 1, 0).reshape(4, 256, 8, 2, 64)
        n_dk[4 * core:4 * core + 4, 0] = dk
        n_dv[4 * core:4 * core + 4, 0] = np.asarray(o["o_dv"]).reshape(4, 256, 8, 128)
    return (y_prompt, y_sample, n_ckv, n_kr, n_gk, n_gv, n_dk, n_dv)
```

```python
import math
import os
from contextlib import ExitStack
KDBG = os.environ.get('KDBG', '').split(',')
import numpy as np
import concourse.bass as bass
import concourse.mybir as mybir
from concourse.bass_utils import run_bass_kernel_spmd

F32 = mybir.dt.float32
BF16 = mybir.dt.bfloat16
AF = mybir.ActivationFunctionType
ALU = mybir.AluOpType

ENGS = ("pe", "act", "dve", "pool", "sp")
ATTR = {"pe": "tensor", "act": "scalar", "dve": "vector", "pool": "gpsimd", "sp": "sync"}
SELF_RAW_DIST = 3
EPS = 1e-6
NSLOT = 3
SLOTW = 2816


class Buf:
    __slots__ = ("w", "r", "excl")

    def __init__(self, excl=False):
        self.w = None
        self.r = {}
        self.excl = excl


class BG:
    def __init__(self):
        self.d = {}

    def __call__(self, *key):
        b = self.d.get(key)
        if b is None:
            b = self.d[key] = Buf()
        return b


class DSem:
    def __init__(self, name, inc=16):
        self.name = name
        self.val = 0
        self.inc = inc
        self.h = None


class Op:
    __slots__ = ("fn", "deps", "marked", "count", "dsem")

    def __init__(self, fn, deps, dsem):
        self.fn = fn
        self.deps = deps
        self.marked = False
        self.count = 0
        self.dsem = dsem


class Sched:
    def __init__(self):
        self.ops = {e: [] for e in ENGS}
        self.clock = {e: {} for e in ENGS}
        self.hist = {e: [] for e in ENGS}
        self.dsems = []

    def dsem(self, name, inc=16):
        d = DSem(name, inc)
        self.dsems.append(d)
        return d

    def op(self, eng, fn, reads=(), writes=(), dsem=None):
        ops = self.ops[eng]
        clock = self.clock[eng]
        seq = len(ops)
        deps = {}

        def need(key, val, raw):
            if key == eng:
                if (not raw) or (seq - val > SELF_RAW_DIST and eng != "pool"):
                    return
                if clock.get(("self", eng), -1) >= val:
                    return
                clock[("self", eng)] = val
                deps[key] = max(deps.get(key, -1), val)
                return
            if clock.get(key, -1) >= val:
                return
            clock[key] = val
            deps[key] = val
            if isinstance(key, str):
                for k2, v2 in self.hist[key][val].items():
                    if isinstance(k2, tuple) or k2 == eng:
                        continue
                    if clock.get(k2, -1) < v2:
                        clock[k2] = v2

        for b in reads:
            if b.w is not None:
                need(b.w[0], b.w[1], True)
            if b.excl:
                for k, v in b.r.items():
                    need(k, v, False)
        for b in writes:
            if b.w is not None:
                need(b.w[0], b.w[1], False)
            for k, v in b.r.items():
                need(k, v, False)
        o = Op(fn, list(deps.items()), dsem)
        ops.append(o)
        self.hist[eng].append(dict(clock))
        if dsem is not None:
            dsem.val += dsem.inc
            key, val = dsem, dsem.val
        else:
            key, val = eng, seq
        for b in reads:
            b.r[key] = val
        for b in writes:
            b.w = (key, val)
            b.r = {}
        return o

    def finalize(self):
        for e in ENGS:
            for o in self.ops[e]:
                for k, v in o.deps:
                    if isinstance(k, str):
                        self.ops[k][v].marked = True
        for e in ENGS:
            c = 0
            for o in self.ops[e]:
                if o.marked:
                    c += 1
                o.count = c

    def emit(self, eng, e, sems):
        for o in self.ops[eng]:
            for k, v in o.deps:
                if isinstance(k, str):
                    e.wait_ge(sems[k], self.ops[k][v].count)
                else:
                    e.wait_ge(k.h, v)
            ins = o.fn(e)
            if o.dsem is not None:
                ins.then_inc(o.dsem.h, o.dsem.inc)
            elif o.marked:
                ins.then_inc(sems[eng], 1)


class KB:
    def __init__(self, nlayers=4, noffn=False):
        self.NL = nlayers
        self.noffn = noffn
        self.nc = bass.Bass("TRN2", target_bir_lowering=False)
        self.S = Sched()
        self.es = ExitStack()
        self.in_names = []
        self.out_names = []
        self.out_dsems = []

    def inp(self, name, shape, dt=F32):
        self.in_names.append(name)
        return self.nc.dram_tensor(name, list(shape), dt, kind="ExternalInput").ap()

    def outp(self, name, shape, dt=F32):
        self.out_names.append(name)
        return self.nc.dram_tensor(name, list(shape), dt, kind="ExternalOutput").ap()

    def dram(self, name, shape, dt):
        return self.nc.dram_tensor(name, list(shape), dt)

    def sb(self, name, shape, dt):
        return self.es.enter_context(self.nc.sbuf_tensor(name, list(shape), dt))

    def mm(self, out, lhsT, rhs, start=True, stop=True, R=(), W=()):
        self.S.op("pe", lambda e: e.matmul(out, lhsT=lhsT, rhs=rhs, start=start, stop=stop), R, W)

    def act(self, out, in_, func, R=(), W=(), bias=None, scale=None):
        kw = {}
        if bias is not None:
            kw["bias"] = bias
        if scale is not None:
            kw["scale"] = scale
        self.S.op("act", lambda e: e.activation(out=out, in_=in_, func=func, **kw), R, W)

    def tt(self, eng, out, in0, in1, op, R=(), W=()):
        self.S.op(eng, lambda e: e.tensor_tensor(out=out, in0=in0, in1=in1, op=op), R, W)

    def ts(self, eng, out, in0, s1, s2, op0, op1, R=(), W=()):
        self.S.op(eng, lambda e: e.tensor_scalar(out=out, in0=in0, scalar1=s1, scalar2=s2, op0=op0, op1=op1), R, W)

    def ts1(self, eng, out, in0, s1, op0, R=(), W=()):
        self.S.op(eng, lambda e: e.tensor_single_scalar(out=out, in_=in0, scalar=s1, op=op0), R, W)

    def stt(self, eng, out, in0, scalar, in1, op0, op1, R=(), W=()):
        self.S.op(eng, lambda e: e.scalar_tensor_tensor(out=out, in0=in0, scalar=scalar, in1=in1, op0=op0, op1=op1), R, W)

    def cp(self, eng, out, in_, R=(), W=()):
        if eng == "act":
            self.S.op("act", lambda e: e.activation(out=out, in_=in_, func=AF.Copy), R, W)
        else:
            self.S.op(eng, lambda e: e.tensor_copy(out=out, in_=in_), R, W)

    def memset(self, eng, ap, val, W=()):
        self.S.op(eng, lambda e: e.memset(ap, val), (), W)

    def dma(self, q, out, in_, ds, R=(), W=()):
        self.S.op(q, lambda e: e.dma_start(out=out, in_=in_), R, W, dsem=ds)

    def barrier_bufs(self, bufs):
        nb = Buf()
        for b in bufs:
            if b.w is not None:
                nb.r[b.w[0]] = max(nb.r.get(b.w[0], -1), b.w[1])
            for k, v in b.r.items():
                nb.r[k] = max(nb.r.get(k, -1), v)
        return nb

    def build(self):
        nc, S = self.nc, self.S
        NL = self.NL
        sb = self.sb
        xT_d = self.inp("xT", [128, 8, 2048])
        cT_d = self.inp("cT", [128, 8, 2])
        ada_d = self.inp("ada_t", [NL, 36, 128, 2048])
        adab_d = self.inp("ada_bT", [128, NL, 72])
        normg_d = self.inp("normg", [128, NL, 3, 8])
        normf_d = self.inp("normf", [128, 8])
        if not self.noffn:
            fin_d = self.inp("ffn_in_t", [NL, 2, 22, 128, 2048])
            fout_d = self.inp("ffn_out_t", [NL, 2, 8, 128, 2816])
        consts_d = self.inp("consts", [128, 3, 128])
        ropeA_d = self.inp("ropeA", [128, 2, 1024])
        ropeB_d = self.inp("ropeB", [128, 2, 1024])
        n_mla = (NL + 2) // 3
        has_gqa = NL >= 2
        has_diff = NL >= 3
        mla_down_d = self.inp("mla_down_t", [n_mla, 3, 128, 2048])
        mla_uq_d = self.inp("mla_uq_t", [n_mla, 4, 128, 1152])
        mla_ukv_d = self.inp("mla_ukv_t", [n_mla, 4, 128, 1024])
        mla_o_d = self.inp("mla_o_t", [n_mla, 8, 128, 1024])
        mla_g_d = self.inp("mla_g", [128, n_mla, 5])
        mla_cckv_d = self.inp("mla_cckvT", [n_mla, 128, 2, 512])
        mla_ckr_d = self.inp("mla_ckrT", [n_mla, 32, 512])
        if has_gqa:
            gqa_w_d = self.inp("gqa_t", [7, 128, 2048])
            gqa_o_d = self.inp("gqa_o_t", [8, 128, 1024])
            gqa_g_d = self.inp("gqa_g", [128, 2])
            gqa_ck_d = self.inp("gqa_ckT", [4, 128, 512])
            gqa_cv_d = self.inp("gqa_cv", [512, 256])
        if has_diff:
            diff_w_d = self.inp("diff_t", [12, 128, 2048])
            diff_o_d = self.inp("diff_o_t", [8, 128, 1024])
            diff_l_d = self.inp("diff_l", [128, 4, 64])
            diff_g_d = self.inp("diff_g", [128, 1])
            diff_ck_d = self.inp("diff_ckT", [8, 128, 512])
            diff_cv_d = self.inp("diff_cv", [512, 1024])

        yout_d = self.outp("yT_out", [128, 8, 2048])
        o_ckv_d = self.outp("o_ckv", [2, 128, 2, 1024])
        o_kr_d = self.outp("o_kr", [2, 32, 1024])
        o_gk_d = self.outp("o_gk", [4, 64, 1024])
        o_gv_d = self.outp("o_gv", [1024, 256])
        o_dk_d = self.outp("o_dk", [128, 8, 1024])
        o_dv_d = self.outp("o_dv", [1024, 1024])

        RG = [[0, 1, 2, 3], [4, 5, 6, 7]]
        gA_in = self.dram("gA_in", [288, 1024], BF16)
        gA_out = self.dram("gA_out", [4 * 288, 1024], BF16)
        gBk_in = self.dram("gBk_in", [512, 1024], BF16)
        gBk_out = self.dram("gBk_out", [4 * 512, 1024], BF16)
        gBv_in = self.dram("gBv_in", [1024, 256], BF16)
        gBv_out = self.dram("gBv_out", [4096, 256], BF16)
        gCk_in = [self.dram(f"gCk_in{i}", [256, 1024], BF16) for i in range(4)]
        gCk_out = [self.dram(f"gCk_out{i}", [1024, 1024], BF16) for i in range(4)]
        gCv_in = [self.dram(f"gCv_in{i}", [1024, 256], BF16) for i in range(4)]
        gCv_out = [self.dram(f"gCv_out{i}", [4096, 256], BF16) for i in range(4)]

        yT = sb("yT", [128, 8, 2048], F32)
        hT = sb("hT", [128, 8, 1024], BF16)
        QT = sb("QT", [128, 8, 1024], BF16)
        arena = sb("arena", [128, 24576], BF16)
        QTf = QT[:].rearrange("p c t -> p (c t)")
        slots = [sb(f"wslot{i}", [128, SLOTW], BF16) for i in range(NSLOT)]
        tmpF = [sb(f"tmpF{i}", [128, 512], F32) for i in range(4)]
        sqt = [sb(f"sqt{i}", [128, 512], BF16) for i in range(2)]
        rstd = sb("rstd", [128, 512], F32)
        rstd2 = sb("rstd2", [128, 512], F32)
        ropeA = sb("ropeA_s", [128, 2, 1024], F32)
        ropeB = sb("ropeB_s", [128, 2, 1024], F32)
        consts = sb("consts_s", [128, 3, 128], F32)
        onesb = sb("onesb", [128, 128], BF16)
        bd64 = sb("bd64", [128, 128], BF16)
        onesf = sb("onesf", [128, 128], F32)
        cT = sb("cT_s", [128, 8, 2], F32)
        actT = sb("actT", [128, 8, 2], BF16)
        adab = sb("adab", [128, NL, 72], F32)
        modT = sb("modT", [128, NL, 72, 2], F32)
        normg = sb("normg_s", [128, NL, 3, 8], F32)
        normf = sb("normf_s", [128, 8], F32)
        gsT = sb("gsT", [128, NL, 3, 2, 8], F32)
        hgT = sb("hgT", [128, NL, 2, 2, 8], F32)
        mlag = sb("mlag", [128, n_mla, 5], F32)
        gqag = sb("gqag", [128, 2], F32)
        diffg = sb("diffg", [128, 1], F32)
        diffl = sb("diffl", [128, 4, 64], F32)
        lamt = sb("lamt", [128, 8], F32)
        ptile = [sb(f"ptile{i}", [128, 512], BF16) for i in range(4)]
        stage = [sb(f"stage{i}", [128, 512], F32) for i in range(2)]
        pb = [self.es.enter_context(nc.psum_tensor(f"pb{i}", [128, 512], F32)) for i in range(8)]

        sems = {e: self.es.enter_context(nc.semaphore("s_" + e)) for e in ENGS}

        YB, HB = BG(), BG()
        PB = [Buf(excl=True) for _ in range(8)]
        WB = [Buf() for _ in range(NSLOT)]
        TMPB = [Buf() for _ in range(4)]
        SQB = [Buf() for _ in range(2)]
        PTB = [Buf() for _ in range(4)]
        STB = [Buf() for _ in range(2)]
        RSTD, CONST, MOD, ARENA = Buf(), Buf(), Buf(), Buf()
        RSTD2 = Buf()
        wds = [S.dsem(f"w{i}") for i in range(NSLOT)]
        d_in = S.dsem("in")
        d_misc = S.dsem("misc")
        d_out = S.dsem("out")
        d_st = [S.dsem(f"st{i}") for i in range(2)]
        d_kv = [S.dsem(f"kv{i}") for i in range(4)]
        d_g = S.dsem("gin")
        d_cc = S.dsem("cc", inc=1)
        self._abufs = []
        self._fence = {}

        def AB():
            b = Buf()
            b.r = dict(self._fence)
            self._abufs.append(b)
            return b

        class ABG(BG):
            def __call__(s_, *key):
                b = s_.d.get(key)
                if b is None:
                    b = s_.d[key] = AB()
                return b

        def arena_fence():
            f = dict(self._fence)
            for b in self._abufs:
                if b.w is not None:
                    f[b.w[0]] = max(f.get(b.w[0], -1), b.w[1])
                for k, v in b.r.items():
                    f[k] = max(f.get(k, -1), v)
            self._fence = f
            self._abufs = []

        d_tmp = [S.dsem(f"tmpo{i}") for i in range(4)]
        GDS = {n: S.dsem("g_" + n) for n in ("a", "bk", "bv", "ck", "cv")}
        self._wi = 0
        self._bank = 0
        self._srot = 0
        self._prot = 0
        self._tmp = 0
        self._stg = 0

        def load_w(dram_ap, n):
            s = self._wi % NSLOT
            self._wi += 1
            self.dma("pool", slots[s][:, 0:n], dram_ap, wds[s], W=[WB[s]])
            return slots[s], WB[s]

        def bank(cands=(5, 6, 7)):
            i = cands[self._bank % len(cands)]
            self._bank += 1
            return i

        def tmpi():
            i = self._tmp % 4
            self._tmp += 1
            return i

        def stg():
            i = self._stg % 2
            self._stg += 1
            return i

        self.dma("sp", yT[:], xT_d, d_in, W=[YB(c, g, b) for c in range(8) for g in range(2) for b in range(2)])
        for (t, d) in ((cT, cT_d), (adab, adab_d), (normg, normg_d), (normf, normf_d), (ropeA, ropeA_d),
                       (ropeB, ropeB_d), (consts, consts_d), (mlag, mla_g_d)):
            self.dma("sp", t[:], d, d_misc, W=[CONST])
        if has_gqa:
            self.dma("sp", gqag[:], gqa_g_d, d_misc, W=[CONST])
        if has_diff:
            self.dma("sp", diffg[:], diff_g_d, d_misc, W=[CONST])
            self.dma("sp", diffl[:], diff_l_d, d_misc, W=[CONST])
        self.memset("dve", onesb[:], 1.0, W=[CONST])
        self.memset("dve", onesf[:], 1.0, W=[CONST])
        self.memset("dve", bd64[:], 0.0, W=[CONST])
        self.memset("dve", bd64[0:64, 0:64], 1.0, W=[CONST])
        self.memset("dve", bd64[64:128, 64:128], 1.0, W=[CONST])
        self.act(actT[:], cT[:], AF.Silu, R=[CONST], W=[MOD])
        permA = consts[:, 1, :]
        permB = consts[:, 2, :]

        MODL = [Buf() for _ in range(NL)]
        cur = {"l": 0}

        def ada_gen(l):
            for s in range(36):
                slot, WBs = load_w(ada_d[l, s], 2048)
                wv = slot[:, 0:2048].rearrange("p (k n) -> p k n", k=8)
                for h in range(2):
                    ch = s * 2 + h
                    for k in range(8):
                        self.mm(pb[7][:, ch * 2:ch * 2 + 2], wv[:, k, h * 128:(h + 1) * 128], actT[:, k, :],
                                start=(k == 0), stop=(k == 7), R=[WBs, MOD], W=[PB[7]])
                yield
            pv = pb[7][:, 0:144].rearrange("p (c s) -> p c s", s=2)
            ML = MODL[l]
            for st in range(2):
                self.tt("dve", modT[:, l, :, st], pv[:, :, st], adab[:, l, :], ALU.add, R=[PB[7], CONST], W=[ML])
            for sub in range(3):
                for st in range(2):
                    sc = modT[:, l, (3 * sub + 1) * 8:(3 * sub + 2) * 8, st]
                    self.stt("dve", gsT[:, l, sub, st, :], sc, 1.0, normg[:, l, sub, :], ALU.add, ALU.mult, R=[ML, CONST], W=[ML])
                    self.ts1("dve", gsT[:, l, sub, st, :], gsT[:, l, sub, st, :], 32.0, ALU.mult, R=[ML], W=[ML])
            for w in range(2):
                for st in range(2):
                    gt = modT[:, l, (6 * w + 2) * 8:(6 * w + 3) * 8, st]
                    self.ts1("dve", hgT[:, l, w, st, :], gt, 0.5, ALU.mult, R=[ML], W=[ML])
            yield

        for _ in ada_gen(0):
            pass
        self._adag = None

        def ada_step():
            if self._adag is not None:
                try:
                    next(self._adag)
                except StopIteration:
                    self._adag = None

        self.ts1("dve", normf[:], normf[:], 32.0, ALU.mult, R=[CONST], W=[CONST])
        for j in range(n_mla):
            self.ts1("dve", mlag[:, j, 0:3], mlag[:, j, 0:3], math.sqrt(384.0), ALU.mult, R=[CONST], W=[CONST])
            self.ts1("dve", mlag[:, j, 3:5], mlag[:, j, 3:5], 16.0, ALU.mult, R=[CONST], W=[CONST])
        if has_gqa:
            self.ts1("dve", gqag[:], gqag[:], 8.0, ALU.mult, R=[CONST], W=[CONST])
        if has_diff:
            lam_init = 0.8 - 0.6 * math.exp(-0.3 * 2)
            self.ts1("dve", diffg[:], diffg[:], math.sqrt(128.0) * (1.0 - lam_init), ALU.mult, R=[CONST], W=[CONST])
        if has_diff and 'dnolam' not in KDBG:
            self.S.op("dve", lambda e: e.tensor_tensor(out=diffl[:, 0, :], in0=diffl[:, 0, :], in1=diffl[:, 1, :], op=ALU.mult), [CONST], [CONST])
            self.S.op("dve", lambda e: e.tensor_tensor(out=diffl[:, 2, :], in0=diffl[:, 2, :], in1=diffl[:, 3, :], op=ALU.mult), [CONST], [CONST])
            self.S.op("dve", lambda e: e.reduce_sum(out=lamt[:, 0:1], in_=diffl[:, 0, :], axis=mybir.AxisListType.X), [CONST], [CONST])
            self.S.op("dve", lambda e: e.reduce_sum(out=lamt[:, 1:2], in_=diffl[:, 2, :], axis=mybir.AxisListType.X), [CONST], [CONST])
            self.act(lamt[:, 2:4], lamt[:, 0:2], AF.Exp, R=[CONST], W=[CONST])
            self.tt("dve", lamt[:, 4:5], lamt[:, 3:4], lamt[:, 2:3], ALU.subtract, R=[CONST], W=[CONST])
            self.ts1("dve", lamt[:, 5:6], lamt[:, 4:5], -lam_init, ALU.add, R=[CONST], W=[CONST])

        def rsqrt(dst, src, addc, n, R, W):
            ti = tmpi()
            self.act(tmpF[ti][:, 0:n], src, AF.Sqrt, bias=addc, R=R, W=[TMPB[ti]])
            self.S.op("dve", lambda e: e.reciprocal(out=dst, in_=tmpF[ti][:, 0:n]), [TMPB[ti]], W)

        def gs_ap(l, sub, st):
            return lambda c: gsT[:, l, sub, st, c:c + 1]

        def sh_ap(l, sub, st):
            return lambda c: modT[:, l, (3 * sub) * 8 + c, st:st + 1]

        def sumsq(src_fn, nch, n, srcR):
            bi = bank((5, 6))
            for c in range(nch):
                sq = sqt[c % 2]
                self.act(sq[:, 0:n], src_fn(c), AF.Square, R=srcR(c), W=[SQB[c % 2]])
                self.mm(pb[bi][:, 0:n], onesb[:], sq[:, 0:n], start=(c == 0), stop=(c == nch - 1), R=[SQB[c % 2], CONST], W=[PB[bi]])
            return bi

        def normmod(g, gs, sh):
            for blk in range(2):
                t0 = g * 1024 + blk * 512
                bi = sumsq(lambda c: yT[:, c, t0:t0 + 512], 8, 512, lambda c: [YB(c, g, blk)])
                rsqrt(rstd[:], pb[bi][:], 1024.0 * EPS, 512, [PB[bi]], [RSTD])
                for c in range(8):
                    ti = tmpi()
                    self.stt("dve", tmpF[ti][:], yT[:, c, t0:t0 + 512], gs(c), rstd[:], ALU.mult, ALU.mult,
                             R=[YB(c, g, blk), RSTD, MODL[cur['l']]], W=[TMPB[ti]])
                    self.act(hT[:, c, blk * 512:(blk + 1) * 512], tmpF[ti][:], AF.Identity, bias=sh(c),
                             R=[TMPB[ti], MODL[cur['l']]], W=[HB(c, blk)])

        uT = arena[:, 0:22 * 1024].rearrange("p (j t) -> p j t", j=22)

        def ffn(g, l, w):
            if self.noffn:
                return
            sub = 0 if w == 0 else 2
            arena_fence()
            UB = ABG()
            normmod(g, gs_ap(l, sub, g), sh_ap(l, sub, g))
            for j in range(22):
                slot, WBs = load_w(fin_d[l, w, j], 2048)
                wv = slot[:, 0:2048].rearrange("p (k s n) -> p k s n", k=8, s=2)
                for blk in range(2):
                    pi = (j * 2 + blk) % 2
                    for s in range(2):
                        bi = 2 * pi + s
                        for k in range(8):
                            self.mm(pb[bi][:], wv[:, k, s, :], hT[:, k, blk * 512:(blk + 1) * 512], start=(k == 0), stop=(k == 7),
                                    R=[WBs, HB(k, blk)], W=[PB[bi]])
                    ti = tmpi()
                    self.act(tmpF[ti][:], pb[2 * pi][:], AF.Silu, R=[PB[2 * pi]], W=[TMPB[ti]])
                    self.tt("dve", uT[:, j, blk * 512:(blk + 1) * 512], pb[2 * pi + 1][:], tmpF[ti][:], ALU.mult,
                            R=[PB[2 * pi + 1], TMPB[ti]], W=[UB(j, blk)])
                if w == 0:
                    ada_step()
            for m in range(8):
                slot, WBs = load_w(fout_d[l, w, m], 2816)
                wv = slot[:, 0:2816].rearrange("p (k n) -> p k n", k=22)
                for blk in range(2):
                    bi = 4 + (m * 2 + blk) % 2
                    for k in range(22):
                        self.mm(pb[bi][:], wv[:, k, :], uT[:, k, blk * 512:(blk + 1) * 512], start=(k == 0), stop=(k == 21),
                                R=[WBs, UB(k, blk)], W=[PB[bi]])
                    t0 = g * 1024 + blk * 512
                    self.stt("dve", yT[:, m, t0:t0 + 512], pb[bi][:], hgT[:, l, w, g, m:m + 1], yT[:, m, t0:t0 + 512], ALU.mult, ALU.add,
                             R=[PB[bi], YB(m, g, blk), MODL[cur['l']]], W=[YB(m, g, blk)])

        def proj(wv_fn, kc, rhs_fn, M, n, R, cands=(5, 6, 7)):
            bi = bank(cands)
            for k in range(kc):
                self.mm(pb[bi][0:M, 0:n], wv_fn(k), rhs_fn(k), start=(k == 0), stop=(k == kc - 1), R=R(k), W=[PB[bi]])
            return bi

        def rope_combine(A, ABuf, lo, hi, perm, table, t0, n, dst, dstW, cands=(5, 6, 7)):
            bi = bank(cands)
            self.mm(pb[bi][0:hi, 0:n], perm[:, 0:hi], A[:, 0:n], R=[ABuf, CONST], W=[PB[bi]])
            t2 = tmpi()
            self.tt("dve", tmpF[t2][lo:hi, 0:n], pb[bi][lo:hi, 0:n], table[lo:hi, 1, t0:t0 + n], ALU.mult, R=[PB[bi], CONST], W=[TMPB[t2]])
            self.tt("dve", A[lo:hi, 0:n], A[lo:hi, 0:n], table[lo:hi, 0, t0:t0 + n], ALU.mult, R=[ABuf, CONST], W=[ABuf])
            self.tt("pool", dst, A[lo:hi, 0:n], tmpF[t2][lo:hi, 0:n], ALU.add, R=[ABuf, TMPB[t2]], W=dstW)

        self._pending = []

        def flush_pending():
            p, self._pending = self._pending, []
            for fn in p:
                fn()

        self._units = None

        def begin_units(warm=0):
            self._units = []

        def end_units():
            us, self._units = self._units, None
            run_units(us)

        def attend(KT, QT_ap, nkb, nq, V_fn, vM, po, scale, RK, RQ, RV, zb=None):
            u = dict(KT=[KT(kb) for kb in range(nkb)], Q=QT_ap, nkb=nkb, nq=nq, V=[V_fn(kb) for kb in range(nkb)], vM=vM, po=po,
                     scale=scale, RK=RK, RQ=RQ, RV=RV, zb=zb, fin=None, imm=False)
            if self._units is not None:
                self._units.append(u)
            else:
                run_units([u])

        def attach_fin(fn, imm=False):
            if self._units:
                self._units[-1]["fin"] = fn
                self._units[-1]["imm"] = imm
            elif imm:
                fn()
            else:
                self._pending.append(fn)

        def run_units(units):
            G = 2

            def issue_qk(u, grp):
                sbanks = (0, 1, 2, 7) if u["zb"] is not None else (0, 1, 2, 5, 6)
                out = {}
                for kb in range(grp * G, min(u["nkb"], (grp + 1) * G)):
                    si = sbanks[self._srot % len(sbanks)]
                    self._srot += 1
                    out[kb] = si
                    self.mm(pb[si][:, 0:u["nq"]], u["KT"][kb], u["Q"], R=u["RK"] + u["RQ"], W=[PB[si]])
                return out

            pre = None
            for ui, u in enumerate(units):
                nxt = units[ui + 1] if ui + 1 < len(units) else None
                nkb, nq, po, zb = u["nkb"], u["nq"], u["po"], u["zb"]
                sbank = pre if pre is not None else issue_qk(u, 0)
                pre = None
                ngrp = (nkb + G - 1) // G
                for grp in range(1, ngrp + 1):
                    if grp < ngrp:
                        sbank.update(issue_qk(u, grp))
                    elif nxt is not None and (nxt["zb"] is None) == (zb is None):
                        pre = issue_qk(nxt, 0)
                    kbs = list(range((grp - 1) * G, min(nkb, grp * G)))
                    pis = []
                    for kb in kbs:
                        si = sbank.pop(kb)
                        pi = self._prot % 4
                        self._prot += 1
                        pis.append(pi)
                        self.act(ptile[pi][:, 0:nq], pb[si][:, 0:nq], AF.Exp, scale=u["scale"], R=[PB[si]], W=[PTB[pi]])
                    for kb, pi in zip(kbs, pis):
                        self.mm(pb[po][0:u["vM"], 0:nq], u["V"][kb], ptile[pi][:, 0:nq], start=(kb == 0), stop=(kb == nkb - 1),
                                R=[PTB[pi]] + u["RV"], W=[PB[po]])
                    if zb is not None:
                        for kb, pi in zip(kbs, pis):
                            self.mm(pb[zb][:, 0:nq], onesb[:], ptile[pi][:, 0:nq], start=(kb == 0), stop=(kb == nkb - 1),
                                    R=[PTB[pi], CONST], W=[PB[zb]])
                    if grp == 1:
                        flush_pending()
                if u["fin"] is not None:
                    if u["imm"]:
                        u["fin"]()
                    else:
                        self._pending.append(u["fin"])

        def finish_head(po, par, nq, dst, dstW):
            attach_fin(lambda: finish_head_now(po, par, nq, dst, dstW))

        def finish_head_now(po, par, nq, dst, dstW):
            zr = 64 if par == 0 else 0
            lo = 0 if par == 0 else 64
            ti = tmpi()
            self.memset("dve", tmpF[ti][:, 0:nq], 0.0, W=[TMPB[ti]])
            self.S.op("dve", lambda e: e.reciprocal(out=tmpF[ti][zr:zr + 1, 0:nq], in_=pb[po][zr:zr + 1, 0:nq]), [PB[po]], [TMPB[ti]])
            bi = bank((7,))
            self.mm(pb[bi][:, 0:nq], onesf[:], tmpF[ti][:, 0:nq], R=[TMPB[ti], CONST], W=[PB[bi]])
            t2 = tmpi()
            self.cp("act", tmpF[t2][lo:lo + 64, 0:nq], pb[bi][lo:lo + 64, 0:nq], R=[PB[bi]], W=[TMPB[t2]])
            self.tt("dve", dst, pb[po][lo:lo + 64, 0:nq], tmpF[t2][lo:lo + 64, 0:nq], ALU.mult, R=[PB[po], TMPB[t2]], W=dstW)

        def oproj(g, l, o_d):
            flush_pending()
            for m in range(8):
                slot, WBs = load_w(o_d[m], 1024)
                wv = slot[:, 0:1024].rearrange("p (k n) -> p k n", k=8)
                for blk in range(2):
                    bi = proj(lambda k: wv[:, k, :], 8, lambda k: hT[:, k, blk * 512:(blk + 1) * 512], 128, 512,
                              lambda k: [WBs, HB(k, blk)], cands=(4, 5))
                    t0 = g * 1024 + blk * 512
                    self.stt("dve", yT[:, m, t0:t0 + 512], pb[bi][:], modT[:, l, 5 * 8 + m, g:g + 1], yT[:, m, t0:t0 + 512], ALU.mult, ALU.add,
                             R=[PB[bi], YB(m, g, blk), MODL[cur['l']]], W=[YB(m, g, blk)])

        def allgather(src, dst, RB, WBf):
            self.S.op("pool", lambda e: e.collective_compute("AllGather", ALU.bypass, replica_groups=RG, ins=[src[:, :]], outs=[dst[:, :]]),
                      RB, WBf, dsem=d_cc)

        A_KT = [arena[:, 0:4608], arena[:, 4608:9216]]
        A_V = [arena[:, 9216:13824].rearrange("p (k d) -> p k d", k=36), arena[:, 13824:18432].rearrange("p (k d) -> p k d", k=36)]
        GIN, GOUT = BG(), BG()

        def run_pipelined(items):
            active = []
            for g in items:
                for a in list(active):
                    try:
                        next(a)
                    except StopIteration:
                        active.remove(a)
                try:
                    next(g)
                    active.append(g)
                except StopIteration:
                    pass
            while active:
                for a in list(active):
                    try:
                        next(a)
                    except StopIteration:
                        active.remove(a)

        def gqa(l):
            scale = 64 ** -0.5

            self._qi = 0

            def qk_chunk(wv_half, WBs, g, blk, gain, rope, dst, dstW, outfp=None, dst2=None):
                idx = self._qi
                self._qi += 1
                sq, SQ = sqt[idx % 2], SQB[idx % 2]
                rs, RS = (rstd, RSTD) if idx % 2 == 0 else (rstd2, RSTD2)
                bi = proj(lambda k: wv_half(k), 8, lambda k: hT[:, k, blk * 512:(blk + 1) * 512], 128, 512, lambda k: [WBs, HB(k, blk)],
                          cands=(3, 4, 5, 6, 7))
                self.act(sq[:], pb[bi][:], AF.Square, R=[PB[bi]], W=[SQ])
                yield
                b2 = bank((3, 4, 5, 6, 7))
                self.mm(pb[b2][:], bd64[:], sq[:], R=[SQ, CONST], W=[PB[b2]])
                rsqrt(rs[:], pb[b2][:], 64.0 * EPS, 512, [PB[b2]], [RS])
                ti = tmpi()
                self.stt("dve", tmpF[ti][:], pb[bi][:], gain, rs[:], ALU.mult, ALU.mult, R=[PB[bi], RS, CONST], W=[TMPB[ti]])
                if outfp is not None:
                    self.dma("sp", outfp, tmpF[ti][0:64, :], d_tmp[ti], R=[TMPB[ti]])
                yield
                if rope and 'norope' not in KDBG:
                    rope_combine(tmpF[ti], TMPB[ti], 0, 128, permB, ropeB, blk * 512, 512, dst, dstW, cands=(3, 4, 5, 6, 7))
                elif dst2 is not None:
                    self.cp("pool", dst, tmpF[ti][0:64, :], R=[TMPB[ti]], W=dstW)
                    self.cp("pool", dst2, tmpF[ti][64:128, :], R=[TMPB[ti]], W=dstW)
                else:
                    self.cp("pool", dst, tmpF[ti][:], R=[TMPB[ti]], W=dstW)

            def run_group(g):
                lat = (g == 1)
                arena_fence()
                KTB = [AB(), AB()]
                QB = ABG()
                normmod(g, gs_ap(l, 1, g), sh_ap(l, 1, g))
                if lat:
                    klat = arena[:, 18432:18432 + 4096].rearrange("p (c t) -> p c t", c=4)
                    KL = AB()
                else:
                    KTc = arena[:, 0:4096].rearrange("p (c t) -> p c t", c=4)
                    KTc1 = arena[:, 12288:16384].rearrange("p (c t) -> p c t", c=4)
                    VE = arena[:, 4096:4096 + 8 * 4 * 66].rearrange("p (k h d) -> p k h d", k=8, h=4)
                    VO = arena[:, 8192:8192 + 8 * 4 * 128].rearrange("p (k h d) -> p k h d", k=8, h=4)
                    KC_, VC_ = ABG(), AB()
                    for kv_ in range(4):
                        self.memset("pool", KTc[64:128, kv_, :], 0.0, W=[KC_(kv_)])
                        self.memset("pool", KTc1[0:64, kv_, :], 0.0, W=[KC_(kv_)])
                    if 'nomemset' not in KDBG:
                        self.memset("pool", VE[:, :, :, 64:65], 1.0, W=[VC_])
                        self.memset("pool", VO[:, :, :, 0:64], 0.0, W=[VC_])
                        self.memset("pool", VO[:, :, :, 0:1], 1.0, W=[VC_])
                def k_items():
                    for u in range(0 if 'nok' in KDBG else 2):
                        slot, WBs = load_w(gqa_w_d[4 + u], 2048)
                        wv = slot[:, 0:2048].rearrange("p (k n) -> p k n", k=8)
                        for h in range(2):
                            kvh = u * 2 + h
                            for blk in range(2):
                                if lat:
                                    yield qk_chunk(lambda k, wv=wv, h=h: wv[:, k, h * 128:(h + 1) * 128], WBs, g, blk, gqag[:, 1:2], True,
                                                   klat[:, kvh, blk * 512:(blk + 1) * 512], [KL])
                                else:
                                    yield qk_chunk(lambda k, wv=wv, h=h: wv[:, k, h * 128:(h + 1) * 128], WBs, g, blk, gqag[:, 1:2], False,
                                                   KTc[0:64, kvh, blk * 512:(blk + 1) * 512], [KC_(kvh)],
                                                   outfp=o_gk_d[kvh, :, blk * 512:(blk + 1) * 512],
                                                   dst2=KTc1[64:128, kvh, blk * 512:(blk + 1) * 512])
                run_pipelined(k_items())
                slot, WBs = load_w(gqa_w_d[6], 2048)
                wv = slot[:, 0:2048].rearrange("p (k n) -> p k n", k=8)
                if lat:
                    vlat = arena[:, 22528:22528 + 2048].rearrange("p (k d) -> p k d", k=8)
                    VL = AB()
                for tb in range(0 if 'nov' in KDBG else 8):
                    bi = proj(lambda k: hT[:, k, tb * 128:(tb + 1) * 128], 8, lambda k: wv[:, k, :], 128, 256,
                              lambda k: [WBs, HB(k, tb // 4)])
                    if lat:
                        self.cp("dve", vlat[:, tb, :], pb[bi][:, 0:256], R=[PB[bi]], W=[VL])
                    else:
                        si = stg()
                        if 'v1' not in KDBG:
                            self.cp("act", stage[si][:, 0:256], pb[bi][:, 0:256], R=[PB[bi]], W=[STB[si]])
                        if 'v2' not in KDBG:
                            self.dma("sp", o_gv_d[tb * 128:(tb + 1) * 128, :], stage[si][:, 0:256], d_st[si], R=[STB[si]])
                        pv = pb[bi][:, 0:256].rearrange("p (h d) -> p h d", h=4)
                        if 'v3' not in KDBG:
                            self.cp("act", VE[:, tb, :, 0:64], pv, R=[PB[bi]], W=[VC_])
                            self.cp("act", VO[:, tb, :, 64:128], pv, R=[PB[bi]], W=[VC_])
                if lat and 'nogather' not in KDBG:
                    self.dma("sp", gBk_in.ap().rearrange("(c p) t -> p c t", p=128), klat, GDS["bk"], R=[KL], W=[GIN("bk")])
                    self.dma("sp", gBv_in.ap().rearrange("(k p) d -> p k d", p=128), vlat, GDS["bv"], R=[VL], W=[GIN("bv")])
                    allgather(gBk_in, gBk_out, [GIN("bk")], [GOUT("bk")])
                    allgather(gBv_in, gBv_out, [GIN("bv")], [GOUT("bv")])
                def q_items():
                    for u in range(0 if 'noq' in KDBG else 4):
                        slot, WBs = load_w(gqa_w_d[u], 2048)
                        wv = slot[:, 0:2048].rearrange("p (k n) -> p k n", k=8)
                        for h in range(2):
                            m = u * 2 + h
                            for blk in range(2):
                                yield qk_chunk(lambda k, wv=wv, h=h: wv[:, k, h * 128:(h + 1) * 128], WBs, g, blk, gqag[:, 0:1], lat,
                                               QT[:, m, blk * 512:(blk + 1) * 512], [QB(m, blk)])
                run_pipelined(q_items())
                if (not lat) and 'noctxattn' in KDBG:
                    pass
                elif not lat:
                    begin_units()
                    for s in range(4):
                        for hd in range(16):
                            kvh, par, m = hd // 4, hd % 2, hd // 2
                            r0 = par * 64
                            po = 3 + (hd % 2)
                            Vt = VE if par == 0 else VO
                            vM = 65 if par == 0 else 128
                            Kz = KTc if par == 0 else KTc1
                            attend(lambda kb: Kz[:, kvh, s * 256 + kb * 128:s * 256 + (kb + 1) * 128],
                                   QT[:, m, s * 256:(s + 1) * 256], 2, 256,
                                   lambda kb: Vt[:, s * 2 + kb, kvh, 0:vM], vM, po, scale,
                                   [KC_(kvh)], [QB(m, s // 2)], [VC_])
                            finish_head(po, par, 256, hT[r0:r0 + 64, m, s * 256:(s + 1) * 256], [HB(m, s // 2)])
                    end_units()
                elif 'nolatattn' in KDBG:
                    pass
                else:
                    VEl = arena[:, 9216:9216 + 36 * 66].rearrange("p (k d) -> p k d", k=36)
                    VOl = arena[:, 13824:18432].rearrange("p (k d) -> p k d", k=36)
                    VEB, VOB = AB(), AB()
                    self.memset("pool", VEl[:, :, 64:65], 1.0, W=[VEB])
                    self.memset("pool", VOl[:, :, 0:64], 0.0, W=[VOB])
                    self.memset("pool", VOl[:, :, 0:1], 1.0, W=[VOB])
                    gk = gBk_out.ap().rearrange("(r c p) t -> c p r t", r=4, c=4)
                    gv = gBv_out.ap().rearrange("(k p) (h d) -> h p k d", p=128, h=4)
                    cv = gqa_cv_d.rearrange("(k p) (h d) -> h p k d", p=128, h=4)
                    self.memset("pool", A_KT[0][64:128, :], 0.0, W=[KTB[0]])
                    self.memset("pool", A_KT[1][0:64, :], 0.0, W=[KTB[1]])
                    for kvh in range(4):
                        for ks in range(2):
                            rs = slice(ks * 64, ks * 64 + 64)
                            self.dma("sp", A_KT[ks][rs, 0:4096].rearrange("p (r t) -> p r t", r=4), gk[kvh][rs], d_kv[ks], R=[GOUT("bk")], W=[KTB[ks]])
                            self.dma("pool", A_KT[ks][rs, 4096:4608], gqa_ck_d[kvh][rs], d_kv[ks], W=[KTB[ks]])
                        self.dma("sp", VEl[:, 0:32, 0:64], gv[kvh], d_kv[2], R=[GOUT("bv")], W=[VEB])
                        self.dma("pool", VEl[:, 32:36, 0:64], cv[kvh], d_kv[2], W=[VEB])
                        self.dma("sp", VOl[:, 0:32, 64:128], gv[kvh], d_kv[3], R=[GOUT("bv")], W=[VOB])
                        self.dma("pool", VOl[:, 32:36, 64:128], cv[kvh], d_kv[3], W=[VOB])
                        begin_units()
                        for hh in range(4):
                            hd = kvh * 4 + hh
                            par, m = hd % 2, hd // 2
                            r0 = par * 64
                            Vt = VEl if par == 0 else VOl
                            VtB = VEB if par == 0 else VOB
                            vM = 65 if par == 0 else 128
                            for qb in range(2):
                                po = 3 + (qb % 2)
                                attend(lambda kb: A_KT[par][:, kb * 128:(kb + 1) * 128],
                                       QT[:, m, qb * 512:(qb + 1) * 512], 36, 512,
                                       lambda kb: Vt[:, kb, 0:vM], vM, po, scale, [KTB[par]], [QB(m, qb)], [VtB])
                                finish_head(po, par, 512, hT[r0:r0 + 64, m, qb * 512:(qb + 1) * 512], [HB(m, qb)])
                        end_units()
                if 'noo' not in KDBG:
                    oproj(g, l, gqa_o_d)

            run_group(0)
            run_group(1)

        def diff(l):
            scale = 64 ** -0.5

            def run_group(g):
                lat = (g == 1)
                arena_fence()
                KTB = [AB(), AB()]
                VB = [AB(), AB()]
                QB = ABG()
                normmod(g, gs_ap(l, 1, g), sh_ap(l, 1, g))
                if lat:
                    KL = AB()
                    VL = AB()
                    klat = arena[:, 18432:18432 + 1024]
                    vlat = arena[:, 19456:19456 + 2048].rearrange("p (k d) -> p k d", k=8)
                else:
                    KTc = arena[:, 0:8192].rearrange("p (c t) -> p c t", c=8)
                    KTc1 = arena[:, 16384:24576].rearrange("p (c t) -> p c t", c=8)
                    Vc = arena[:, 8192:16384].rearrange("p (k d) -> p k d", k=8)
                    KC_, VC_ = ABG(), ABG()
                    for c_ in range(8):
                        self.memset("pool", KTc[64:128, c_, :], 0.0, W=[KC_(c_)])
                        self.memset("pool", KTc1[0:64, c_, :], 0.0, W=[KC_(c_)])

                def qk_item(wv, WBs, h, ch, blk, isq):
                    bi = proj(lambda k: wv[:, k, h * 128:(h + 1) * 128], 8, lambda k: hT[:, k, blk * 512:(blk + 1) * 512], 128, 512,
                              lambda k: [WBs, HB(k, blk)], cands=(3, 4, 5, 6, 7))
                    if isq:
                        dst, dstW = QT[:, ch, blk * 512:(blk + 1) * 512], [QB(ch, blk)]
                    elif lat:
                        dst, dstW = klat[:, blk * 512:(blk + 1) * 512], [KL]
                    else:
                        dst, dstW = KTc[:, ch, blk * 512:(blk + 1) * 512], [KC_(ch)]
                    if lat:
                        ti = tmpi()
                        self.cp("act", tmpF[ti][:], pb[bi][:], R=[PB[bi]], W=[TMPB[ti]])
                        yield
                        rope_combine(tmpF[ti], TMPB[ti], 0, 128, permB, ropeB, blk * 512, 512, dst, dstW, cands=(3, 4, 5, 6, 7))
                        if (not isq) and blk == 1:
                            self.dma("sp", gCk_in[ch // 2][(ch % 2) * 128:(ch % 2 + 1) * 128, :], klat, GDS["ck"], R=[KL], W=[GIN("ck", ch // 2)])
                            if ch % 2 == 1 and 'dnogather' not in KDBG:
                                allgather(gCk_in[ch // 2], gCk_out[ch // 2], [GIN("ck", ch // 2)], [GOUT("ck", ch // 2)])
                    else:
                        if isq:
                            self.cp("act", dst, pb[bi][:], R=[PB[bi]], W=dstW)
                        else:
                            self.cp("act", KTc[0:64, ch, blk * 512:(blk + 1) * 512], pb[bi][0:64, :], R=[PB[bi]], W=dstW)
                            self.cp("act", KTc1[64:128, ch, blk * 512:(blk + 1) * 512], pb[bi][64:128, :], R=[PB[bi]], W=dstW)
                            si = stg()
                            self.cp("dve", stage[si][:], pb[bi][:], R=[PB[bi]], W=[STB[si]])
                            self.dma("sp", o_dk_d[:, ch, blk * 512:(blk + 1) * 512], stage[si][:], d_st[si], R=[STB[si]])

                def qk_items(us, isq):
                    for u in us:
                        slot, WBs = load_w(diff_w_d[u], 2048)
                        wv = slot[:, 0:2048].rearrange("p (k n) -> p k n", k=8)
                        for h in range(2):
                            ch = (u % 4) * 2 + h
                            for blk in range(2):
                                yield qk_item(wv, WBs, h, ch, blk, isq)

                def qk_unit(u, isq):
                    run_pipelined(qk_items([u], isq))

                run_pipelined(qk_items(range(4, 4 if 'dnok' in KDBG else 8), False))
                for u in range(8, 8 if 'dnov' in KDBG else 12):
                    slot, WBs = load_w(diff_w_d[u], 2048)
                    wv = slot[:, 0:2048].rearrange("p (k n) -> p k n", k=8)
                    for tb in range(8):
                        bi = proj(lambda k: hT[:, k, tb * 128:(tb + 1) * 128], 8, lambda k: wv[:, k, :], 128, 256,
                                  lambda k: [WBs, HB(k, tb // 4)])
                        c0 = (u - 8) * 256
                        if lat:
                            self.cp("dve", vlat[:, tb, :], pb[bi][:, 0:256], R=[PB[bi]], W=[VL])
                        else:
                            si = stg()
                            self.cp("act", stage[si][:, 0:256], pb[bi][:, 0:256], R=[PB[bi]], W=[STB[si]])
                            self.dma("sp", o_dv_d[tb * 128:(tb + 1) * 128, c0:c0 + 256], stage[si][:, 0:256], d_st[si], R=[STB[si]])
                            self.cp("dve", Vc[:, tb, c0:c0 + 256], pb[bi][:, 0:256], R=[PB[bi]], W=[VC_(u - 8)])
                    if lat:
                        self.dma("sp", gCv_in[u - 8].ap().rearrange("(k p) d -> p k d", p=128), vlat, GDS["cv"], R=[VL], W=[GIN("cv", u - 8)])
                        if 'dnogather' not in KDBG:
                            allgather(gCv_in[u - 8], gCv_out[u - 8], [GIN("cv", u - 8)], [GOUT("cv", u - 8)])
                run_pipelined(qk_items(range(0, 0 if 'dnoq' in KDBG else 4), True))

                def head_finish(hd, q0, nq, OW):
                    t1, t2, t3 = tmpi(), tmpi(), tmpi()
                    self.S.op("dve", lambda e: e.reciprocal(out=tmpF[t1][:, 0:nq], in_=pb[4][:, 0:nq]), [PB[4]], [TMPB[t1]])
                    self.tt("dve", tmpF[t1][:, 0:nq], pb[3][:, 0:nq], tmpF[t1][:, 0:nq], ALU.mult, R=[PB[3], TMPB[t1]], W=[TMPB[t1]])
                    self.S.op("dve", lambda e: e.reciprocal(out=tmpF[t2][:, 0:nq], in_=pb[6][:, 0:nq]), [PB[6]], [TMPB[t2]])
                    self.tt("dve", tmpF[t2][:, 0:nq], pb[5][:, 0:nq], tmpF[t2][:, 0:nq], ALU.mult, R=[PB[5], TMPB[t2]], W=[TMPB[t2]])
                    self.stt("dve", tmpF[t3][:, 0:nq], tmpF[t2][:, 0:nq], lamt[:, 5:6], tmpF[t1][:, 0:nq], ALU.mult, ALU.add,
                             R=[TMPB[t1], TMPB[t2], CONST], W=[TMPB[t3]])
                    self.act(sqt[0][:, 0:nq], tmpF[t3][:, 0:nq], AF.Square, R=[TMPB[t3]], W=[SQB[0]])
                    self.mm(pb[7][:, 0:nq], onesb[:], sqt[0][:, 0:nq], R=[SQB[0], CONST], W=[PB[7]])
                    rsqrt(rstd[:, 0:nq], pb[7][:, 0:nq], 128.0 * EPS, nq, [PB[7]], [RSTD])
                    self.stt("dve", hT[:, hd, q0:q0 + nq], tmpF[t3][:, 0:nq], diffg[:, 0:1], rstd[:, 0:nq], ALU.mult, ALU.mult,
                             R=[TMPB[t3], RSTD, CONST], W=OW)

                if (not lat and 'dnoctx' in KDBG) or (lat and 'dnolat' in KDBG):
                    pass
                elif not lat:
                    begin_units()
                    for s in range(4):
                        for hd in range(8):
                            for mp in range(2):
                                r0 = mp * 64
                                Kz = KTc if mp == 0 else KTc1
                                attend(lambda kb: Kz[:, hd, s * 256 + kb * 128:s * 256 + (kb + 1) * 128],
                                       QT[:, hd, s * 256:(s + 1) * 256], 2, 256,
                                       lambda kb: Vc[:, s * 2 + kb, hd * 128:(hd + 1) * 128], 128, 3 + 2 * mp, scale,
                                       [KC_(hd)], [QB(hd, s // 2)], [VC_(hd // 2)], zb=4 + 2 * mp)
                            attach_fin(lambda hd=hd, s=s: head_finish(hd, s * 256, 256, [HB(hd, s // 2)]), imm=True)
                    end_units()
                else:
                    cv = diff_cv_d.rearrange("(k p) (h d) -> h p k d", p=128, h=8)
                    self.memset("pool", A_KT[0][64:128, :], 0.0, W=[KTB[0]])
                    self.memset("pool", A_KT[1][0:64, :], 0.0, W=[KTB[1]])
                    for hd in range(8):
                        ks = hd % 2
                        gk = gCk_out[hd // 2].ap().rearrange("(r c p) t -> c p r t", r=4, c=2)[hd % 2]
                        gv = gCv_out[hd // 2].ap().rearrange("(k p) (h d) -> h p k d", p=128, h=2)[hd % 2]
                        for mp_ in range(2):
                            rs = slice(mp_ * 64, mp_ * 64 + 64)
                            self.dma("sp", A_KT[mp_][rs, 0:4096].rearrange("p (r t) -> p r t", r=4), gk[rs], d_kv[mp_], R=[GOUT("ck", hd // 2)], W=[KTB[mp_]])
                            self.dma("pool", A_KT[mp_][rs, 4096:4608], diff_ck_d[hd][rs], d_kv[mp_], W=[KTB[mp_]])
                        self.dma("sp", A_V[ks][:, 0:32, :], gv, d_kv[2 + ks], R=[GOUT("cv", hd // 2)], W=[VB[ks]])
                        self.dma("pool", A_V[ks][:, 32:36, :], cv[hd], d_kv[2 + ks], W=[VB[ks]])
                        begin_units()
                        for qb in range(2):
                            for mp in range(2):
                                r0 = mp * 64
                                attend(lambda kb: A_KT[mp][:, kb * 128:(kb + 1) * 128],
                                       QT[:, hd, qb * 512:(qb + 1) * 512], 36, 512,
                                       lambda kb: A_V[ks][:, kb, :], 128, 3 + 2 * mp, scale,
                                       [KTB[mp]], [QB(hd, qb)], [VB[ks]], zb=4 + 2 * mp)
                            attach_fin(lambda hd=hd, qb=qb: head_finish(hd, qb * 512, 512, [HB(hd, qb)]), imm=True)
                        end_units()
                if 'dnoo' not in KDBG:
                    oproj(g, l, diff_o_d)

            run_group(0)
            run_group(1)

        def mla(l, j):
            scale = 96 ** -0.5
            cqT = QTf[:, 0:3072].rearrange("p (c t) -> p c t", c=3)
            qsl = [QTf[:, 3072:4096], QTf[:, 4096:5120]]

            def run_group(g):
                lat = (g == 1)
                arena_fence()
                QSB = [AB(), AB()]
                CQB = ABG()
                nk = 4608 if lat else 1024
                nkb = nk // 128
                normmod(g, gs_ap(l, 1, g), sh_ap(l, 1, g))
                if lat:
                    ckvT = arena[:, 0:9216].rearrange("p (c t) -> p c t", c=2)
                    KT1 = arena[:, 9216:13824]
                    VEl = arena[:, 13824:13824 + 36 * 66].rearrange("p (k d) -> p k d", k=36)
                    VOl = arena[:, 17408:17408 + 4608].rearrange("p (k d) -> p k d", k=36)
                    cl = arena[:, 22016:22016 + 2048].rearrange("p (c t) -> p c t", c=2)
                    krl = arena[:, 16384:17408]
                else:
                    ckvT = arena[:, 0:2048].rearrange("p (c t) -> p c t", c=2)
                    KT1 = arena[:, 9216:9216 + 1024]
                    VEl = arena[:, 13824:13824 + 8 * 66].rearrange("p (k d) -> p k d", k=8)
                    VOl = arena[:, 17408:17408 + 1024].rearrange("p (k d) -> p k d", k=8)
                CKB, K1B, VEB, VOB, CLB, KRB = ABG(), AB(), AB(), AB(), AB(), AB()
                self.memset("pool", VEl[:, :, 64:65], 1.0, W=[VEB])
                self.memset("pool", VOl[:, :, 0:64], 0.0, W=[VOB])
                self.memset("pool", VOl[:, :, 0:1], 1.0, W=[VOB])
                self.memset("pool", KT1[64:128, :], 0.0, W=[K1B])
                self.memset("pool", qsl[0][64:128, :], 0.0, W=[QSB[0]])
                self.memset("pool", qsl[1][64:128, :], 0.0, W=[QSB[1]])
                units = []
                for u in range(3):
                    nn_ = 2048 if u < 2 else 1280
                    slot, WBs = load_w(mla_down_d[j, u][:, 0:nn_], nn_)
                    n = 256 if u < 2 else 160
                    units.append((slot[:, 0:8 * n].rearrange("p (k n) -> p k n", k=8), WBs))

                def wcol(c0, M):
                    u = c0 // 256
                    wv, WBs = units[u]
                    o = c0 - u * 256
                    return (lambda k: wv[:, k, o:o + M]), WBs

                for blk in range(2):
                    hr = lambda k: hT[:, k, blk * 512:(blk + 1) * 512]
                    bis = []
                    for c in range(3):
                        wf, WBs = wcol(c * 128, 128)
                        bis.append(proj(wf, 8, hr, 128, 512, lambda k, WBs=WBs: [WBs, HB(k, blk)], cands=(0, 1, 2)))
                    sb_ = sumsq(lambda c: pb[bis[c]][:], 3, 512, lambda c: [PB[bis[c]]])
                    rsqrt(rstd[:], pb[sb_][:], 384.0 * EPS, 512, [PB[sb_]], [RSTD])
                    for c in range(3):
                        self.stt("dve", cqT[:, c, blk * 512:(blk + 1) * 512], pb[bis[c]][:], mlag[:, j, c:c + 1], rstd[:], ALU.mult, ALU.mult,
                                 R=[PB[bis[c]], RSTD, CONST], W=[CQB(c, blk)])
                    bis = []
                    for c in range(2):
                        wf, WBs = wcol(384 + c * 128, 128)
                        bis.append(proj(wf, 8, hr, 128, 512, lambda k, WBs=WBs: [WBs, HB(k, blk)], cands=(0, 1, 2)))
                    sb_ = sumsq(lambda c: pb[bis[c]][:], 2, 512, lambda c: [PB[bis[c]]])
                    rsqrt(rstd[:], pb[sb_][:], 256.0 * EPS, 512, [PB[sb_]], [RSTD])
                    for c in range(2):
                        ti = tmpi()
                        self.stt("dve", tmpF[ti][:], pb[bis[c]][:], mlag[:, j, 3 + c:4 + c], rstd[:], ALU.mult, ALU.mult,
                                 R=[PB[bis[c]], RSTD, CONST], W=[TMPB[ti]])
                        if lat:
                            self.cp("act", cl[:, c, blk * 512:(blk + 1) * 512], tmpF[ti][:], R=[TMPB[ti]], W=[CLB])
                        else:
                            self.cp("act", ckvT[:, c, blk * 512:(blk + 1) * 512], tmpF[ti][:], R=[TMPB[ti]], W=[CKB(c)])
                            self.dma("sp", o_ckv_d[j, :, c, blk * 512:(blk + 1) * 512], tmpF[ti][:], d_tmp[ti], R=[TMPB[ti]])
                    wf, WBs = wcol(576, 96)
                    bi = proj(wf, 8, hr, 96, 512, lambda k, WBs=WBs: [WBs, HB(k, blk)], cands=(0, 1, 2))
                    ti = tmpi()
                    if lat:
                        self.memset("dve", tmpF[ti][:], 0.0, W=[TMPB[ti]])
                    self.cp("act", tmpF[ti][64:96, :], pb[bi][64:96, :], R=[PB[bi]], W=[TMPB[ti]])
                    if lat:
                        rope_combine(tmpF[ti], TMPB[ti], 64, 96, permA, ropeA, blk * 512, 512, krl[64:96, blk * 512:(blk + 1) * 512], [KRB])
                    else:
                        self.dma("sp", o_kr_d[j, :, blk * 512:(blk + 1) * 512], tmpF[ti][64:96, :], d_tmp[ti], R=[TMPB[ti]])
                        self.cp("pool", KT1[64:96, blk * 512:(blk + 1) * 512], tmpF[ti][64:96, :], R=[TMPB[ti]], W=[K1B])
                if lat:
                    ga = gA_in.ap()
                    self.dma("sp", ga[0:256, :].rearrange("(c p) t -> p c t", p=128), cl, GDS["a"], R=[CLB], W=[GIN("a")])
                    self.dma("sp", ga[256:288, :], krl[64:96, :], GDS["a"], R=[KRB], W=[GIN("a")])
                    allgather(gA_in, gA_out, [GIN("a")], [GOUT("a")])
                    go = gA_out.ap().rearrange("(r f) t -> f r t", r=4)
                    for c in range(2):
                        self.dma("sp", ckvT[:, c, 0:4096].rearrange("p (r t) -> p r t", r=4), go[c * 128:(c + 1) * 128], d_kv[c], R=[GOUT("a")], W=[CKB(c)])
                        self.dma("pool", ckvT[:, c, 4096:4608], mla_cckv_d[j, :, c, :], d_kv[c], W=[CKB(c)])
                    self.dma("sp", KT1[64:96, 0:4096].rearrange("p (r t) -> p r t", r=4), go[256:288], d_kv[2], R=[GOUT("a")], W=[K1B])
                    self.dma("pool", KT1[64:96, 4096:4608], mla_ckr_d[j], d_kv[2], W=[K1B])
                for hg in range(4):
                    sq_, WBq = load_w(mla_uq_d[j, hg], 1152)
                    wq = sq_[:, 0:1152].rearrange("p (h k n) -> p h k n", h=4, k=3)
                    skv, WBkv = load_w(mla_ukv_d[j, hg], 1024)
                    wkv = skv[:, 0:1024].rearrange("p (h k n) -> p h k n", h=4, k=2)
                    for hh in range(4):
                        hd = hg * 4 + hh
                        par, m = hd % 2, hd // 2
                        for kb4 in range(nk // 512):
                            bi = proj(lambda k: wkv[:, hh, k, 0:64], 2, lambda k: ckvT[:, k, kb4 * 512:(kb4 + 1) * 512], 64, 512,
                                      lambda k: [WBkv, CKB(k)])
                            self.cp("dve", KT1[0:64, kb4 * 512:(kb4 + 1) * 512], pb[bi][0:64, :], R=[PB[bi]], W=[K1B])
                        Vt, VtB = (VEl, VEB) if par == 0 else (VOl, VOB)
                        c0 = 0 if par == 0 else 64
                        for kb8 in range((nkb + 7) // 8):
                            nb = min(8, nkb - kb8 * 8)
                            bi = bank()
                            for q in range(nb):
                                kb = kb8 * 8 + q
                                for k in range(2):
                                    self.mm(pb[bi][:, q * 64:(q + 1) * 64], ckvT[:, k, kb * 128:(kb + 1) * 128], wkv[:, hh, k, 64:128],
                                            start=(k == 0), stop=(k == 1), R=[CKB(k), WBkv], W=[PB[bi]])
                            self.cp("act", Vt[:, kb8 * 8:kb8 * 8 + nb, c0:c0 + 64], pb[bi][:, 0:nb * 64].rearrange("p (q d) -> p q d", d=64),
                                    R=[PB[bi]], W=[VtB])
                        qs, QSb = qsl[hd % 2], QSB[hd % 2]
                        for blk in range(2):
                            bi = proj(lambda k: wq[:, hh, k, :], 3, lambda k: cqT[:, k, blk * 512:(blk + 1) * 512], 96, 512,
                                      lambda k: [WBq, CQB(k, blk)])
                            self.cp("act", qs[0:64, blk * 512:(blk + 1) * 512], pb[bi][0:64, :], R=[PB[bi]], W=[QSb])
                            if lat:
                                ti = tmpi()
                                self.memset("dve", tmpF[ti][:], 0.0, W=[TMPB[ti]])
                                self.cp("dve", tmpF[ti][64:96, :], pb[bi][64:96, :], R=[PB[bi]], W=[TMPB[ti]])
                                rope_combine(tmpF[ti], TMPB[ti], 64, 96, permA, ropeA, blk * 512, 512, qs[64:96, blk * 512:(blk + 1) * 512], [QSb])
                            else:
                                self.cp("dve", qs[64:96, blk * 512:(blk + 1) * 512], pb[bi][64:96, :], R=[PB[bi]], W=[QSb])
                        vM = 65 if par == 0 else 128
                        r0 = par * 64
                        begin_units()
                        if lat:
                            for qb in range(2):
                                po = 3 + (qb % 2)
                                attend(lambda kb: KT1[:, kb * 128:(kb + 1) * 128], qs[:, qb * 512:(qb + 1) * 512], 36, 512,
                                       lambda kb: Vt[:, kb, 0:vM], vM, po, scale, [K1B], [QSb], [VtB])
                                finish_head(po, par, 512, hT[r0:r0 + 64, m, qb * 512:(qb + 1) * 512], [HB(m, qb)])
                        else:
                            for s in range(4):
                                po = 3 + (s % 2)
                                attend(lambda kb: KT1[:, s * 256 + kb * 128:s * 256 + (kb + 1) * 128], qs[:, s * 256:(s + 1) * 256], 2, 256,
                                       lambda kb: Vt[:, s * 2 + kb, 0:vM], vM, po, scale, [K1B], [QSb], [VtB])
                                finish_head(po, par, 256, hT[r0:r0 + 64, m, s * 256:(s + 1) * 256], [HB(m, s // 2)])
                        end_units()
                oproj(g, l, mla_o_d[j])

            run_group(0)
            run_group(1)

        for l in range(NL):
            cur["l"] = l
            if l + 1 < NL:
                self._adag = ada_gen(l + 1)
            ffn(0, l, 0)
            ffn(1, l, 0)
            while self._adag is not None:
                ada_step()
            kind = l % 3
            if kind == 0:
                mla(l, l // 3)
            elif kind == 1:
                gqa(l)
            else:
                diff(l)
            ffn(0, l, 1)
            ffn(1, l, 1)

        for g in range(2):
            for blk in range(2):
                t0 = g * 1024 + blk * 512
                bi = sumsq(lambda c: yT[:, c, t0:t0 + 512], 8, 512, lambda c: [YB(c, g, blk)])
                rsqrt(rstd[:], pb[bi][:], 1024.0 * EPS, 512, [PB[bi]], [RSTD])
                for c in range(8):
                    self.stt("dve", yT[:, c, t0:t0 + 512], yT[:, c, t0:t0 + 512], normf[:, c:c + 1], rstd[:], ALU.mult, ALU.mult,
                             R=[YB(c, g, blk), RSTD, CONST], W=[YB(c, g, blk)])
                self.dma("sp", yout_d[:, :, t0:t0 + 512], yT[:, :, t0:t0 + 512], d_out, R=[YB(c, g, blk) for c in range(8)])
        fin = Buf()
        fin.r = {d: d.val for d in [d_out] + d_st + d_tmp if d.val > 0}
        self.S.op("sp", lambda e: e.nop(), writes=[fin])

        for d in S.dsems:
            d.h = self.es.enter_context(nc.semaphore("d_" + d.name))
        S.finalize()
        block = self.es.enter_context(nc.Block())
        for name in ENGS:
            def mk(name=name):
                def f(e):
                    S.emit(name, e, sems)
                return f
            getattr(block, ATTR[name])(mk())
        self.es.close()
        return nc


def _fm(x2d):
    T, Dd = x2d.shape
    return np.ascontiguousarray(x2d.T.reshape(Dd // 128, 128, T).transpose(1, 0, 2))


def _wt(W):
    K, N = W.shape
    return np.ascontiguousarray(W.reshape(K // 128, 128, N).transpose(1, 0, 2).reshape(128, (K // 128) * N))


def _pad(a, n):
    out = np.zeros((a.shape[0], n), np.float32)
    out[:, :a.shape[1]] = a
    return out


def _rope_tables(pos, rot_dim):
    GRID_W = 64
    r = (pos // GRID_W).astype(np.float32)
    cidx = (pos % GRID_W).astype(np.float32)
    n_f = rot_dim // 4
    freqs = (np.float32(10000.0) ** (-np.arange(n_f, dtype=np.float32) / np.float32(n_f))).astype(np.float32)
    ang = np.concatenate([r[:, None] * freqs, cidx[:, None] * freqs], axis=-1).astype(np.float32)
    cos, sin = np.cos(ang).astype(np.float32), np.sin(ang).astype(np.float32)
    cos2 = np.concatenate([cos, cos], axis=1).T
    sinp = np.concatenate([-sin, sin], axis=1).T
    return cos2, sinp


_CACHE = {}


def _get_nc(nl, noffn=False):
    if (nl, noffn) not in _CACHE:
        _CACHE[(nl, noffn)] = KB(nl, noffn).build()
    return _CACHE[(nl, noffn)]


def kernel(x_prompt, x_sample, cache_mla_ckv, cache_mla_kr, cache_gqa_k, cache_gqa_v,
           cache_diff_k, cache_diff_v, c, c_ctx,
           ada_w, ada_b, norm_ffn1, norm_mix, norm_ffn2,
           ffn1_w_in, ffn1_w_out, ffn2_w_in, ffn2_w_out,
           mla_w_down, mla_g_q, mla_g_kv, mla_w_uq, mla_w_uk, mla_w_uv, mla_w_o,
           gqa_w_qkv, gqa_g_q, gqa_g_k, gqa_w_o,
           diff_w_qkv, diff_lq1, diff_lk1, diff_lq2, diff_lk2, diff_g_sub, diff_w_o,
           norm_final, _nl=4, _noffn=False, _trace=False):
    f = lambda a: np.asarray(a, dtype=np.float32)
    x_prompt, x_sample = f(x_prompt), f(x_sample)
    NL = _nl
    if 'probe_noffn' in KDBG:
        _noffn = True
    nc = _get_nc(NL, _noffn)
    n_mla = (NL + 2) // 3
    sh = {}
    ada_w = f(ada_w)
    sh["ada_t"] = np.stack([np.stack([_wt(ada_w[l][:, s * 256:(s + 1) * 256]) for s in range(36)]) for l in range(NL)])
    sh["ada_bT"] = np.ascontiguousarray(f(ada_b)[:NL].reshape(NL, 72, 128).transpose(2, 0, 1))
    ng = np.stack([f(norm_ffn1)[:NL], f(norm_mix)[:NL], f(norm_ffn2)[:NL]], axis=1)
    sh["normg"] = np.ascontiguousarray(ng.reshape(NL, 3, 8, 128).transpose(3, 0, 1, 2))
    sh["normf"] = np.ascontiguousarray(f(norm_final).reshape(8, 128).T)
    if not _noffn:
        win = np.stack([f(ffn1_w_in)[:NL], f(ffn2_w_in)[:NL]], axis=1)
        sh["ffn_in_t"] = np.ascontiguousarray(
            win.reshape(NL, 2, 8, 128, 2, 22, 128).transpose(0, 1, 5, 3, 2, 4, 6).reshape(NL, 2, 22, 128, 2048))
        wout = np.stack([f(ffn1_w_out)[:NL], f(ffn2_w_out)[:NL]], axis=1)
        sh["ffn_out_t"] = np.ascontiguousarray(
            wout.reshape(NL, 2, 22, 128, 8, 128).transpose(0, 1, 4, 3, 2, 5).reshape(NL, 2, 8, 128, 2816))
    consts = np.zeros((128, 3, 128), np.float32)
    consts[:, 0, :] = np.eye(128, dtype=np.float32)
    for jj in range(128):
        base = (jj // 64) * 64
        o = jj - base
        consts[base + (o + 32) % 64, 2, jj] = 1.0
    for jj in range(64, 96):
        o = jj - 64
        consts[64 + (o + 16) % 32, 1, jj] = 1.0
    sh["consts"] = consts
    wd = f(mla_w_down)[:n_mla]
    sh["mla_down_t"] = np.stack([np.stack([_pad(_wt(wd[j][:, 0:256]), 2048), _pad(_wt(wd[j][:, 256:512]), 2048),
                                           _pad(_wt(wd[j][:, 512:672]), 2048)]) for j in range(n_mla)])
    wuq = f(mla_w_uq)[:n_mla]
    sh["mla_uq_t"] = np.ascontiguousarray(
        wuq.reshape(n_mla, 3, 128, 4, 4, 96).transpose(0, 3, 2, 4, 1, 5).reshape(n_mla, 4, 128, 1152))
    wuk = f(mla_w_uk)[:n_mla].reshape(n_mla, 2, 128, 4, 4, 64)
    wuv = f(mla_w_uv)[:n_mla].reshape(n_mla, 2, 128, 4, 4, 64)
    ukv = np.concatenate([wuk, wuv], axis=-1)
    sh["mla_ukv_t"] = np.ascontiguousarray(ukv.transpose(0, 3, 2, 4, 1, 5).reshape(n_mla, 4, 128, 1024))
    wo = f(mla_w_o)[:n_mla]
    sh["mla_o_t"] = np.stack([np.stack([_wt(wo[j][:, m * 128:(m + 1) * 128]) for m in range(8)]) for j in range(n_mla)])
    gq = f(mla_g_q)[:n_mla].reshape(n_mla, 3, 128)
    gkv = f(mla_g_kv)[:n_mla].reshape(n_mla, 2, 128)
    sh["mla_g"] = np.ascontiguousarray(np.concatenate([gq, gkv], axis=1).transpose(2, 0, 1))
    if NL >= 2:
        wq = f(gqa_w_qkv)[0]
        units = [_wt(wq[:, u * 256:(u + 1) * 256]) for u in range(4)]
        for u in range(2):
            cols = []
            for h in range(2):
                kvh = u * 2 + h
                blk = wq[:, 1024 + kvh * 64:1024 + (kvh + 1) * 64]
                cols += [blk, blk]
            units.append(_wt(np.concatenate(cols, axis=1)))
        units.append(_wt(wq[:, 1280:1536]))
        sh["gqa_t"] = np.stack(units)
        go = f(gqa_w_o)[0]
        sh["gqa_o_t"] = np.stack([_wt(go[:, m * 128:(m + 1) * 128]) for m in range(8)])
        sh["gqa_g"] = np.ascontiguousarray(np.stack([np.tile(f(gqa_g_q)[0], 2), np.tile(f(gqa_g_k)[0], 2)], axis=1))
    if NL >= 3:
        wq = f(diff_w_qkv)[0]
        sh["diff_t"] = np.stack([_wt(wq[:, u * 256:(u + 1) * 256]) for u in range(12)])
        do = f(diff_w_o)[0]
        sh["diff_o_t"] = np.stack([_wt(do[:, m * 128:(m + 1) * 128]) for m in range(8)])
        sh["diff_l"] = np.ascontiguousarray(np.broadcast_to(np.stack([f(diff_lq1)[0], f(diff_lk1)[0], f(diff_lq2)[0], f(diff_lk2)[0]])[None], (128, 4, 64)))
        sh["diff_g"] = np.ascontiguousarray(f(diff_g_sub)[0].reshape(128, 1))

    in_maps = []
    for core in range(8):
        b, r = core // 4, core % 4
        m = dict(sh)
        xc = x_prompt[4 * core:4 * core + 4].reshape(1024, 1024)
        xl = x_sample[b, r * 1024:(r + 1) * 1024]
        m["xT"] = np.concatenate([_fm(xc), _fm(xl)], axis=2)
        m["cT"] = np.ascontiguousarray(np.stack([f(c_ctx), f(c)[b]], axis=1).reshape(8, 128, 2).transpose(1, 0, 2))
        pos = np.arange(r * 1024, (r + 1) * 1024)
        ca, sa = _rope_tables(pos, 32)
        ra = np.zeros((128, 2, 1024), np.float32)
        ra[64:96, 0], ra[64:96, 1] = ca, sa
        m["ropeA"] = ra
        cb, sb_ = _rope_tables(pos, 64)
        m["ropeB"] = np.ascontiguousarray(np.stack([np.tile(cb, (2, 1)), np.tile(sb_, (2, 1))], axis=1))
        ck = f(cache_mla_ckv)[b][:n_mla]
        m["mla_cckvT"] = np.ascontiguousarray(ck.transpose(0, 2, 1).reshape(n_mla, 2, 128, 512).transpose(0, 2, 1, 3))
        m["mla_ckrT"] = np.ascontiguousarray(f(cache_mla_kr)[b][:n_mla].transpose(0, 2, 1))
        if NL >= 2:
            gk = f(cache_gqa_k)[b, 0]
            kt = gk.transpose(1, 2, 0)
            m["gqa_ckT"] = np.ascontiguousarray(np.concatenate([kt, kt], axis=1))
            m["gqa_cv"] = np.ascontiguousarray(f(cache_gqa_v)[b, 0].reshape(512, 256))
        if NL >= 3:
            dk = f(cache_diff_k)[b, 0].reshape(512, 8, 128)
            m["diff_ckT"] = np.ascontiguousarray(dk.transpose(1, 2, 0))
            m["diff_cv"] = np.ascontiguousarray(f(cache_diff_v)[b, 0].reshape(512, 1024))
        in_maps.append(m)

    if _trace:
        res = run_bass_kernel_spmd(nc, in_maps, core_ids=list(range(8)), trace=True)
        _CACHE['last_res'] = res
    else:
        res = run_bass_kernel_spmd(nc, in_maps, core_ids=list(range(8)))
    R = res.results
    y_prompt = np.zeros((32, 256, 1024), np.float32)
    y_sample = np.zeros((2, 4096, 1024), np.float32)
    n_ckv = np.zeros((32, 2, 256, 256), np.float32)
    n_kr = np.zeros((32, 2, 256, 32), np.float32)
    n_gk = np.zeros((32, 1, 256, 4, 64), np.float32)
    n_gv = np.zeros((32, 1, 256, 4, 64), np.float32)
    n_dk = np.zeros((32, 1, 256, 8, 2, 64), np.float32)
    n_dv = np.zeros((32, 1, 256, 8, 128), np.float32)
    for core in range(8):
        b, r = core // 4, core % 4
        o = R[core]
        yt = np.asarray(o["yT_out"]).transpose(2, 1, 0).reshape(2048, 1024)
        y_prompt[4 * core:4 * core + 4] = yt[:1024].reshape(4, 256, 1024)
        y_sample[b, r * 1024:(r + 1) * 1024] = yt[1024:]
        ck = np.asarray(o["o_ckv"]).transpose(0, 3, 2, 1).reshape(2, 4, 256, 256)
        n_ckv[4 * core:4 * core + 4] = ck.transpose(1, 0, 2, 3)
        kr = np.asarray(o["o_kr"]).transpose(0, 2, 1).reshape(2, 4, 256, 32)
        n_kr[4 * core:4 * core + 4] = kr.transpose(1, 0, 2, 3)
        gk = np.asarray(o["o_gk"]).transpose(2, 0, 1).reshape(4, 256, 4, 64)
        n_gk[4 * core:4 * core + 4, 0] = gk
        n_gv[4 * core:4 * core + 4, 0] = np.asarray(o["o_gv"]).reshape(4, 256, 4, 64)
        dk = np.asarray(o["o_dk"]).transpose(2, 1, 0).reshape(4, 256, 8, 2, 64)
        n_dk[4 * core:4 * core + 4, 0] = dk
        n_dv[4 * core:4 * core + 4, 0] = np.asarray(o["o_dv"]).reshape(4, 256, 8, 128)
    return (y_prompt, y_sample, n_ckv, n_kr, n_gk, n_gv, n_dk, n_dv)
```
